# Optimizing a Trainium2 kernel written in Bass

```python
import jax, jax.numpy as jnp
from jax import lax
import numpy as np


D_MODEL = 1024
BATCH = 16
SEQ = 4096
DEPTH = 4

HEAD_DIM = 64
ROPE_THETA = 500000.0
ROT_DIM = HEAD_DIM // 4
NORM_EPS = 1e-6
BLOCK = 128

DSA_GROUPS = ((128, 1), (512, 4), (2048, 16))
DSA_HEADS_PER_GROUP = 4
DSA_HEADS = DSA_HEADS_PER_GROUP * len(DSA_GROUPS)
DSA_W = DSA_HEADS * HEAD_DIM
DSA_OUT = DSA_HEADS_PER_GROUP * HEAD_DIM

MLA_HEADS = 8
MLA_Q_LORA = 256
MLA_KV_LORA = 128
MLA_NOPE = 64
MLA_ROPE = 32
MLA_QK = MLA_NOPE + MLA_ROPE
MLA_V = 64
MLA_OUT = MLA_HEADS * MLA_V

SB_HEADS = 8
SB_W = SB_HEADS * HEAD_DIM
SB_OUT = SB_W

N_BRANCH = 3
D_FF = 4 * D_MODEL
BRANCH_IN = DSA_OUT + MLA_OUT + SB_OUT

_IN_SIZES = (DSA_W, DSA_W, DSA_W, MLA_Q_LORA, MLA_KV_LORA, MLA_ROPE, SB_W, SB_W, SB_W, N_BRANCH * D_MODEL)
IN_COLS = sum(_IN_SIZES)
IN_SPLITS = tuple(sum(_IN_SIZES[:i + 1]) for i in range(len(_IN_SIZES) - 1))

kernel_name = "hybrid_gated_dilated_mla_stickbreaking"


def _rms_norm(x, g):
    x32 = x.astype(jnp.float32)
    y = x32 * lax.rsqrt(jnp.mean(x32 * x32, axis=-1, keepdims=True) + NORM_EPS)
    return (y * g.astype(jnp.float32)).astype(x.dtype)


def _rope_tables(seq, dim):
    pos = jnp.arange(seq, dtype=jnp.float32)
    inv_freq = ROPE_THETA ** (-jnp.arange(0, dim, 2, dtype=jnp.float32) / dim)
    ang = pos[:, None] * inv_freq[None, :]
    return jnp.cos(ang), jnp.sin(ang)


def _apply_rope(x, cos, sin):
    half = x.shape[-1] // 2
    bshape = (cos.shape[0],) + (1,) * (x.ndim - 3) + (half,)
    c = cos.reshape(bshape)
    s = sin.reshape(bshape)
    x32 = x.astype(jnp.float32)
    x1, x2 = x32[..., :half], x32[..., half:]
    return jnp.concatenate([x1 * c - x2 * s, x2 * c + x1 * s], axis=-1).astype(x.dtype)


def _partial_rope(x, cos, sin):
    return jnp.concatenate([_apply_rope(x[..., :ROT_DIM], cos, sin), x[..., ROT_DIM:]], axis=-1)


def _strided_band_attention(q, k, v, window, dilation):
    B, S, H, Dh = q.shape
    span = window // dilation
    blk = span
    L = S // dilation
    nb = -(-L // blk)
    Lp = nb * blk

    def to_blocks(t):
        t = t.reshape(B, L, dilation, H, Dh).transpose(0, 2, 1, 3, 4)
        t = jnp.pad(t, ((0, 0), (0, 0), (0, Lp - L), (0, 0), (0, 0)))
        return t.reshape(B, dilation, nb, blk, H, Dh)

    def band(t):
        prev = jnp.pad(t, ((0, 0), (0, 0), (1, 0), (0, 0), (0, 0), (0, 0)))[:, :, :-1]
        return jnp.concatenate([prev, t], axis=3)

    qb = to_blocks(q)
    kband = band(to_blocks(k))
    vband = band(to_blocks(v))
    s = jnp.einsum('brnqhd,brnkhd->brnhqk', qb, kband,
                   preferred_element_type=jnp.float32) * (Dh ** -0.5)
    qi = jnp.arange(blk)[:, None]
    kj = jnp.arange(2 * blk)[None, :]
    dist = qi + blk - kj
    first = (jnp.arange(nb) == 0)[:, None, None]
    valid = (dist >= 0)[None] & (dist <= span)[None] & ~(first & (kj < blk)[None])
    s = jnp.where(valid[:, None], s, -jnp.inf)
    m = jnp.max(s, axis=-1, keepdims=True)
    p = jnp.exp(s - m)
    den = jnp.sum(p, axis=-1, keepdims=True)
    o = jnp.einsum('brnhqk,brnkhd->brnqhd', (p / den).astype(v.dtype), vband)
    lse = (m + jnp.log(den))[..., 0]
    o = o.reshape(B, dilation, Lp, H, Dh)[:, :, :L].transpose(0, 2, 1, 3, 4).reshape(B, S, H, Dh)
    lse = lse.transpose(0, 1, 2, 4, 3).reshape(B, dilation, Lp, H)[:, :, :L]
    lse = lse.transpose(0, 2, 1, 3).reshape(B, S, H)
    return o, lse


def _dilated_mixture(q, k, v):
    outs, lses = [], []
    for g, (window, dil) in enumerate(DSA_GROUPS):
        o, lse = _strided_band_attention(q[:, :, g], k[:, :, g], v[:, :, g], window, dil)
        outs.append(o)
        lses.append(lse)
    wts = jax.nn.softmax(jnp.stack(lses, axis=0), axis=0)
    o = jnp.sum(wts[..., None] * jnp.stack(outs, axis=0).astype(jnp.float32), axis=0)
    return o.astype(q.dtype)


def _causal_softmax_attention(q, k, v):
    B, S, H, Dq = q.shape
    nb = S // BLOCK
    qb = q.reshape(B, nb, BLOCK, H, Dq).transpose(1, 0, 2, 3, 4)
    kpos = jnp.arange(S)
    scale = Dq ** -0.5

    def one(args):
        qi, start = args
        s = jnp.einsum('bqhd,bkhd->bhqk', qi, k, preferred_element_type=jnp.float32) * scale
        qpos = start + jnp.arange(BLOCK)
        s = jnp.where(kpos[None, :] <= qpos[:, None], s, -jnp.inf)
        p = jax.nn.softmax(s, axis=-1)
        return jnp.einsum('bhqk,bkhd->bqhd', p.astype(v.dtype), v)

    out = lax.map(one, (qb, jnp.arange(nb) * BLOCK))
    return out.transpose(1, 0, 2, 3, 4).reshape(B, S, H, v.shape[-1])


def _stick_breaking_attention(q, k, v):
    B, S, H, Dh = q.shape
    nb = S // BLOCK
    qb = q.reshape(B, nb, BLOCK, H, Dh).transpose(1, 0, 2, 3, 4)
    kpos = jnp.arange(S)
    scale = Dh ** -0.5

    def one(args):
        qi, start = args
        z = jnp.einsum('bqhd,bkhd->bhqk', qi, k, preferred_element_type=jnp.float32) * scale
        qpos = start + jnp.arange(BLOCK)
        strict = kpos[None, :] < qpos[:, None]
        log_beta = jax.nn.log_sigmoid(z)
        log_one_minus = jnp.where(strict, jax.nn.log_sigmoid(-z), 0.0)
        after = lax.cumsum(log_one_minus, axis=3, reverse=True) - log_one_minus
        weights = jnp.exp(jnp.where(strict, log_beta + after, -jnp.inf))
        return jnp.einsum('bhqk,bkhd->bqhd', weights.astype(v.dtype), v)

    out = lax.map(one, (qb, jnp.arange(nb) * BLOCK))
    return out.transpose(1, 0, 2, 3, 4).reshape(B, S, H, Dh)


def setup_inputs(seed: int = 0) -> dict:
    key = jax.random.key(seed)
    ks = jax.random.split(key, 20)
    f32 = jnp.float32

    def dense(k, shape, fan_in):
        return jax.random.normal(k, shape, f32) * (fan_in ** -0.5)

    def gain(k, n):
        return 1.0 + 0.02 * jax.random.normal(k, (DEPTH, n), f32)

    w_branch = jnp.concatenate([
        dense(ks[11], (DEPTH, DSA_OUT, D_MODEL), DSA_OUT),
        dense(ks[12], (DEPTH, MLA_OUT, D_MODEL), MLA_OUT),
        dense(ks[13], (DEPTH, SB_OUT, D_MODEL), SB_OUT)], axis=1)
    return {
        'x': jax.random.normal(ks[0], (BATCH, SEQ, D_MODEL), f32),
        'attn_norm': gain(ks[1], D_MODEL),
        'w_in': dense(ks[2], (DEPTH, D_MODEL, IN_COLS), D_MODEL),
        'a_q_norm': gain(ks[3], HEAD_DIM),
        'a_k_norm': gain(ks[4], HEAD_DIM),
        'b_q_a_norm': gain(ks[5], MLA_Q_LORA),
        'w_q_b': dense(ks[6], (DEPTH, MLA_Q_LORA, MLA_HEADS, MLA_QK), MLA_Q_LORA),
        'b_kv_a_norm': gain(ks[7], MLA_KV_LORA),
        'w_kv_b': dense(ks[8], (DEPTH, MLA_KV_LORA, MLA_HEADS, MLA_NOPE + MLA_V), MLA_KV_LORA),
        'b_q_norm': gain(ks[9], MLA_QK),
        'b_k_norm': gain(ks[10], MLA_QK),
        'w_branch': w_branch,
        'w_out': dense(ks[14], (DEPTH, D_MODEL, D_MODEL), D_MODEL),
        'mlp_norm': gain(ks[15], D_MODEL),
        'w_ff1': dense(ks[16], (DEPTH, D_MODEL, D_FF), D_MODEL),
        'w_ff2': dense(ks[17], (DEPTH, D_FF, D_MODEL), D_FF),
    }


def reference(x, attn_norm, w_in, a_q_norm, a_k_norm, b_q_a_norm, w_q_b, b_kv_a_norm,
              w_kv_b, b_q_norm, b_k_norm, w_branch, w_out, mlp_norm, w_ff1, w_ff2):
    B, S, _ = x.shape
    cos_p, sin_p = _rope_tables(S, ROT_DIM)
    cos_m, sin_m = _rope_tables(S, MLA_ROPE)
    grp = (B, S, len(DSA_GROUPS), DSA_HEADS_PER_GROUP, HEAD_DIM)
    for l in range(DEPTH):
        h = _rms_norm(x, attn_norm[l])
        proj = h @ w_in[l]
        a_q, a_k, a_v, b_ql, b_kvl, b_kr, c_q, c_k, c_v, gate = jnp.split(proj, IN_SPLITS, axis=-1)

        aq = _partial_rope(_rms_norm(a_q.reshape(grp), a_q_norm[l]), cos_p, sin_p)
        ak = _partial_rope(_rms_norm(a_k.reshape(grp), a_k_norm[l]), cos_p, sin_p)
        o_a = _dilated_mixture(aq, ak, a_v.reshape(grp)).reshape(B, S, DSA_OUT)

        bq = jnp.einsum('bsr,rhe->bshe', _rms_norm(b_ql, b_q_a_norm[l]), w_q_b[l])
        kv = jnp.einsum('bsr,rhe->bshe', _rms_norm(b_kvl, b_kv_a_norm[l]), w_kv_b[l])
        k_nope, bv = kv[..., :MLA_NOPE], kv[..., MLA_NOPE:]
        k_rope = jnp.broadcast_to(b_kr[:, :, None, :], (B, S, MLA_HEADS, MLA_ROPE))
        bk = jnp.concatenate([k_nope, k_rope], axis=-1)
        bq = _rms_norm(bq, b_q_norm[l])
        bk = _rms_norm(bk, b_k_norm[l])
        bq = jnp.concatenate([bq[..., :MLA_NOPE], _apply_rope(bq[..., MLA_NOPE:], cos_m, sin_m)], axis=-1)
        bk = jnp.concatenate([bk[..., :MLA_NOPE], _apply_rope(bk[..., MLA_NOPE:], cos_m, sin_m)], axis=-1)
        o_b = _causal_softmax_attention(bq, bk, bv).reshape(B, S, MLA_OUT)

        sb = (B, S, SB_HEADS, HEAD_DIM)
        o_c = _stick_breaking_attention(c_q.reshape(sb), c_k.reshape(sb), c_v.reshape(sb)).reshape(B, S, SB_OUT)

        gates = jax.nn.sigmoid(gate.reshape(B, S, N_BRANCH, D_MODEL))
        wb = w_branch[l]
        merged = (gates[:, :, 0] * (o_a @ wb[:DSA_OUT])
                  + gates[:, :, 1] * (o_b @ wb[DSA_OUT:DSA_OUT + MLA_OUT])
                  + gates[:, :, 2] * (o_c @ wb[DSA_OUT + MLA_OUT:]))
        x = x + merged @ w_out[l]

        h2 = _rms_norm(x, mlp_norm[l])
        x = x + jnp.square(jax.nn.relu(h2 @ w_ff1[l])) @ w_ff2[l]
    return x
```

```python
import contextlib
import numpy as np
import concourse.bass as bass
import concourse.mybir as mybir
from concourse.alu_op_type import AluOpType as ALU
from concourse.bass_utils import run_bass_kernel_spmd

AF = mybir.ActivationFunctionType
AX = mybir.AxisListType
F32 = mybir.dt.float32
BF16 = mybir.dt.bfloat16

SAME_ENG_SYNC = True
UPTO = 99
STAGE1 = 99
TLIM = 9999
EPS = 1e-6
D = 1024
INC = 7328
DFF = 4096


class T:
    _n = 0

    def __init__(self, t, name=None):
        self.t = t
        T._n += 1
        self.id = "t%d" % T._n
        self.name = name or self.id
        self.w = None
        self.r = {}
        self.dsem = None
        self.dval = 0
        self.psum = False

    def __getitem__(self, idx):
        return self.t[idx]


class Rot:
    def __init__(self, tiles):
        self.tiles = tiles
        self.i = 0

    def next(self):
        t = self.tiles[self.i % len(self.tiles)]
        self.i += 1
        return t


class K:
    def __init__(self, nc):
        self.nc = nc
        self.es = contextlib.ExitStack()
        self.eng = {"pe": nc.tensor, "act": nc.scalar, "dve": nc.vector,
                    "pool": nc.gpsimd, "sp": nc.sync}
        self.sem = {}
        self.cnt = {}
        self.seen = {}
        for e in self.eng:
            self.sem[e] = self.es.enter_context(nc.semaphore("s_" + e))
            self.cnt[e] = 0
            self.seen[e] = {}
        self.semof = dict(self.sem)
        self.dtiles = []
        self.dsem_pool = []
        self.ninstr = 0
        self.uid = 0

    def sb(self, shape, dtype, name, stack=None):
        st = stack if stack is not None else self.es
        self.uid += 1
        nb = int(np.prod(shape[1:])) * (2 if dtype == BF16 else 4)
        self.sb_bytes = getattr(self, "sb_bytes", 0) + nb
        self.sb_max = max(getattr(self, "sb_max", 0), self.sb_bytes)
        st.callback(self._free, nb)
        t = st.enter_context(self.nc.sbuf_tensor("%s_%d" % (name, self.uid), list(shape), dtype))
        return T(t, name)

    def ps(self, shape, dtype, name, stack=None):
        st = stack if stack is not None else self.es
        self.uid += 1
        t = st.enter_context(self.nc.psum_tensor("%s_%d" % (name, self.uid), list(shape), dtype))
        tt = T(t, name)
        tt.psum = True
        return tt

    def rot(self, n, shape, dtype, name, stack=None, psum=False):
        f = self.ps if psum else self.sb
        return Rot([f(shape, dtype, "%s%d" % (name, i), stack) for i in range(n)])

    def _free(self, nb):
        self.sb_bytes -= nb

    def _wait(self, e, deps):
        eng = self.eng[e]
        seen = self.seen[e]
        for key, val in sorted(deps, key=lambda d: str(d[0])):
            if key == e and (e in ("pe", "sp") or not SAME_ENG_SYNC):
                continue
            if seen.get(key, 0) >= val:
                continue
            eng.wait_ge(self.semof[key], val)
            seen[key] = val
            self.ninstr += 1

    def _deps(self, reads, writes):
        deps = set()
        for t in reads:
            if t.w:
                deps.add(t.w)
            if t.psum:
                for d in t.r.values():
                    deps.add(d)
        for t in writes:
            if t.w:
                deps.add(t.w)
            for d in t.r.values():
                deps.add(d)
        return deps

    def op(self, e, fn, reads=(), writes=()):
        self._wait(e, self._deps(reads, writes))
        ins = fn(self.eng[e])
        self.cnt[e] += 1
        ins.then_inc(self.sem[e], 1)
        self.ninstr += 1
        me = (e, self.cnt[e])
        for t in reads:
            t.r[e] = me
        for t in writes:
            t.w = me
            t.r = {}
        return ins

    def _dsem(self, tile):
        if tile.dsem is None:
            if self.dsem_pool:
                key, sem, val = self.dsem_pool.pop()
                tile.dkey, tile.dsem, tile.dval = key, sem, val
            else:
                tile.dkey = "d" + tile.id
                tile.dsem = self.es.enter_context(self.nc.semaphore(tile.dkey))
                self.semof[tile.dkey] = tile.dsem
            self.dtiles.append(tile)

    def load(self, q, tile, out_ap, in_ap, **kw):
        self._dsem(tile)
        self._wait(q, self._deps((), (tile,)))
        ins = self.eng[q].dma_start(out=out_ap, in_=in_ap, **kw)
        tile.dval += 16
        ins.then_inc(tile.dsem, 16)
        self.ninstr += 1
        tile.w = (tile.dkey, tile.dval)
        tile.r = {}

    def store(self, q, tile, out_ap, in_ap, **kw):
        self._dsem(tile)
        self._wait(q, self._deps((tile,), ()))
        ins = self.eng[q].dma_start(out=out_ap, in_=in_ap, **kw)
        tile.dval += 16
        ins.then_inc(tile.dsem, 16)
        self.ninstr += 1
        tile.r["dma"] = (tile.dkey, tile.dval)

    def barrier(self, release=True):
        deps = set()
        for e in self.eng:
            if self.cnt[e]:
                deps.add((e, self.cnt[e]))
        for t in self.dtiles:
            if t.dval:
                deps.add((t.dkey, t.dval))
        for e in self.eng:
            self._wait(e, deps)

    def release(self, tiles):
        for t in tiles:
            if t.dsem is not None:
                self.dsem_pool.append((t.dkey, t.dsem, t.dval))
                self.dtiles.remove(t)
                t.dsem = None

    def close(self):
        self.es.close()


class Phase:
    def __init__(self, k):
        self.k = k
        self.st = contextlib.ExitStack()
        self.tiles = []

    def sb(self, shape, dtype, name):
        t = self.k.sb(shape, dtype, name, self.st)
        self.tiles.append(t)
        return t

    def ps(self, shape, dtype, name):
        t = self.k.ps(shape, dtype, name, self.st)
        self.tiles.append(t)
        return t

    def rot(self, n, shape, dtype, name, psum=False):
        f = self.ps if psum else self.sb
        return Rot([f(shape, dtype, "%s%d" % (name, i)) for i in range(n)])

    def end(self):
        self.k.barrier()
        self.k.release(self.tiles)
        self.st.close()


def bc(ap, axis, shape):
    return ap.unsqueeze(axis).broadcast_to(list(shape))


def build(nL, nS, S, debug=False):
    nc = bass.Bass("TRN2", target_bir_lowering=False)
    NT = S // 128
    NQG = S // 512

    def din(name, shape, dt=F32):
        return nc.dram_tensor(name, list(shape), dt, kind="ExternalInput").ap()

    def dscr(name, shape, dt):
        if debug:
            return nc.dram_tensor(name, list(shape), dt, kind="ExternalOutput").ap()
        return nc.dram_tensor(name, list(shape), dt).ap()

    x_in = din("x", [nS, S, D])
    w_in = din("w_in", [nL, D, INC])
    w_qb = din("w_q_b", [nL, 256, 768])
    w_kvb = din("w_kv_b", [nL, 128, 1024])
    w_br = din("w_branch", [nL, 1280, D])
    w_out = din("w_out", [nL, D, D])
    w_ff1 = din("w_ff1", [nL, D, DFF])
    w_ff2 = din("w_ff2", [nL, DFF, D])
    g_attn = din("g_attn", [nL, 128, 8])
    g_mlp = din("g_mlp", [nL, 128, 8])
    g_qa = din("g_qa", [nL, 128, 2])
    g_kva = din("g_kva", [nL, 128, 1])
    g_aqk = din("g_aqk", [nL, 128, 128])
    g_bqk = din("g_bqk", [nL, 128, 192])
    ropeA = din("ropeA", [128, NT, 16])
    ropeM = din("ropeM", [128, NT, 32])
    consts = din("consts", [128, 5, 128])
    out = nc.dram_tensor("out", [nS, S, D], F32, kind="ExternalOutput").ap()

    Wb_in = dscr("Wb_in", [nL, D, INC], BF16)
    Wb_qb = dscr("Wb_qb", [nL, 256, 768], BF16)
    Wb_kvb = dscr("Wb_kvb", [nL, 128, 1024], BF16)
    Wb_br = dscr("Wb_br", [nL, 1280, D], BF16)
    Wb_out = dscr("Wb_out", [nL, D, D], BF16)
    Wb_ff1 = dscr("Wb_ff1", [nL, D, DFF], BF16)
    Wb_ff2 = dscr("Wb_ff2", [nL, DFF, D], BF16)
    xres = dscr("xres", [nS, S, D], F32)
    xmid = dscr("xmid", [nS, S, D], F32)
    AqT = dscr("AqT", [768, S], BF16)
    AkT = dscr("AkT", [768, S], BF16)
    Av = dscr("Av", [S, 780], BF16)
    BqT = dscr("BqT", [768, S], BF16)
    BkT = dscr("BkT", [768, S], BF16)
    Bv = dscr("Bv", [S, 520], BF16)
    CqT = dscr("CqT", [512, S], BF16)
    CkT = dscr("CkT", [512, S], BF16)
    Cv = dscr("Cv", [S, 512], BF16)
    Gt = dscr("Gt", [S, 3072], BF16)
    accA = dscr("accA", [3, S, 260], F32)
    oB = dscr("oB", [S, 512], BF16)
    oC = dscr("oC", [S, 512], BF16)

    k = K(nc)
    cf = k.sb([128, 5, 128], F32, "cf")
    cb = k.sb([128, 5, 128], BF16, "cb")
    k.load("sp", cf, cf[:], consts)
    k.op("dve", lambda e: e.tensor_copy(out=cb[:], in_=cf[:]), reads=[cf], writes=[cb])
    identb = cb[:, 0, :]
    identf = cf[:, 0, :]
    m_ge = cb[:, 1, :]
    m_gt = cb[:, 2, :]
    m_le = cb[:, 3, :]
    negU = cb[:, 4, :]
    onesb = k.sb([128, 1], BF16, "onesb")
    k.op("dve", lambda e: e.memset(onesb[:], 1.0), writes=[onesb])
    rA = k.sb([128, NT, 16], F32, "rA")
    rM = k.sb([128, NT, 32], F32, "rM")
    k.load("sp", rA, rA[:], ropeA)
    k.load("sp", rM, rM[:], ropeM)

    neghalf = k.sb([128, 24], F32, "neghalf")
    k.op("dve", lambda e: e.memset(neghalf[:], -0.5), writes=[neghalf])

    def rsq(rs_ap, ss_ap, n, inv_n, rs_t, ss_t):
        k.op("dve", lambda e: e.tensor_scalar(out=rs_ap, in0=ss_ap, scalar1=inv_n, scalar2=EPS, op0=ALU.mult, op1=ALU.add),
             reads=[ss_t], writes=[rs_t])
        k.op("pool", lambda e: e.tensor_tensor(out=rs_ap, in0=rs_ap, in1=neghalf[:, 0:n], op=ALU.pow),
             reads=[rs_t, neghalf], writes=[rs_t])

    ceng = Rot(["dve", "act", "pool"])

    def cast(e, out_ap, in_ap, reads, writes, scale=None):
        if e == "act":
            if scale is None:
                k.op("act", lambda g: g.activation(out=out_ap, in_=in_ap, func=AF.Copy), reads=reads, writes=writes)
            else:
                k.op("act", lambda g: g.activation(out=out_ap, in_=in_ap, func=AF.Copy, scale=scale),
                     reads=reads, writes=writes)
        else:
            if scale is None:
                k.op(e, lambda g: g.tensor_copy(out=out_ap, in_=in_ap), reads=reads, writes=writes)
            else:
                k.op(e, lambda g: g.tensor_scalar(out=out_ap, in0=in_ap, scalar1=scale, scalar2=None, op0=ALU.mult),
                     reads=reads, writes=writes)

    def prep_weights():
        ph = Phase(k)
        gcol = ph.sb([128, nL, 19], F32, "gcol")
        for l in range(nL):
            k.load("sp", gcol, gcol[:, l, 0:8], g_attn[l])
            k.load("sp", gcol, gcol[:, l, 8:16], g_mlp[l])
            k.load("sp", gcol, gcol[:, l, 16:18], g_qa[l])
            k.load("sp", gcol, gcol[:, l, 18:19], g_kva[l])
        CB = 2048
        wf = ph.rot(4, [128, CB], F32, "wf")
        wb = ph.rot(3, [128, CB], BF16, "wb")
        items = []
        for l in range(nL):
            jobs = [(w_in[l], Wb_in[l], D, INC, 0), (w_qb[l], Wb_qb[l], 256, 768, 16),
                    (w_kvb[l], Wb_kvb[l], 128, 1024, 18), (w_br[l], Wb_br[l], 1280, D, None),
                    (w_out[l], Wb_out[l], D, D, None), (w_ff1[l], Wb_ff1[l], D, DFF, 8),
                    (w_ff2[l], Wb_ff2[l], DFF, D, None)]
            for src, dst, R, C, gc in jobs:
                for c in range(R // 128):
                    for c0 in range(0, C, CB):
                        items.append((l, src, dst, c, c0, min(CB, C - c0), gc))
        loaded = {}

        def ld(i):
            l, src, dst, c, c0, n, gc = items[i]
            f = wf.next()
            k.load("sp", f, f[:, 0:n], src[c * 128:(c + 1) * 128, c0:c0 + n])
            loaded[i] = f

        for i in range(min(2, len(items))):
            ld(i)
        for i in range(len(items)):
            if i + 2 < len(items):
                ld(i + 2)
            l, src, dst, c, c0, n, gc = items[i]
            f = loaded.pop(i)
            b = wb.next()
            sc = None if gc is None else gcol[:, l, gc + c:gc + c + 1]
            e = ceng.next()
            rd = [f] if gc is None else [f, gcol]
            cast(e, b[:, 0:n], f[:, 0:n], rd, [b], scale=sc)
            k.store("sp", b, dst[c * 128:(c + 1) * 128, c0:c0 + n], b[:, 0:n])
        ph.end()

    def norm_transpose(ph, r, xsrc_rows, xt, hT_out, defer=False):
        ss = r["ss"].next()
        rstd = r["rstd"].next()
        junk = r["junk"]
        xn = r["xn"].next()
        k.op("act", lambda e: e.activation(out=junk[:], in_=xt[:], func=AF.Square, accum_out=ss[:]),
             reads=[xt], writes=[junk, ss])
        rsq(rstd[:], ss[:], 1, 1.0 / D, rstd, ss)
        k.op("act", lambda e: e.activation(out=xn[:], in_=xt[:], func=AF.Copy, scale=rstd[:]),
             reads=[xt, rstd], writes=[xn])
        hTt = r["hTtile"]

        def partB():
            pT = r["pT"].next()
            for c in range(8):
                k.op("pe", lambda e, c=c: e.transpose(out=pT[:, c, :], in_=xn[:, c * 128:(c + 1) * 128], identity=identb),
                     reads=[xn, cb], writes=[pT])
            k.op("dve", lambda e: e.tensor_copy(out=hT_out, in_=pT[:]), reads=[pT], writes=[hTt])
        if defer:
            return partB
        partB()

    def norm_res(ph, n=2):
        return {"ss": ph.rot(n, [128, 1], F32, "ss"), "rstd": ph.rot(n, [128, 1], F32, "rstd"),
                "junk": ph.sb([128, D], BF16, "junk"), "xn": ph.rot(2, [128, D], BF16, "xn"),
                "pT": ph.rot(1, [128, 8, 128], BF16, "pT", psum=True)}

    def rms_small(src_ap, nh, hd, reads_t, sq_t, ss_t, rs_t, extra_add=None):
        k.op("act", lambda e: e.activation(out=sq_t[:, 0:nh * hd].rearrange("p (h d) -> p h d", d=hd),
                                           in_=src_ap, func=AF.Square), reads=reads_t, writes=[sq_t])
        k.op("dve", lambda e: e.tensor_reduce(out=ss_t[:, 0:nh], in_=sq_t[:, 0:nh * hd].rearrange("p (h d) -> p h d", d=hd),
                                              axis=AX.X, op=ALU.add), reads=[sq_t], writes=[ss_t])

    def interleave(gens):
        gens = [g for g in gens if g is not None]
        while gens:
            for g in list(gens):
                try:
                    next(g)
                except StopIteration:
                    gens.remove(g)

    def phase1(l, s, xsrc, pas):
        ph = Phase(k)
        if pas == 0:
            c_lo, c_hi = 0, 3744
        else:
            c_lo, c_hi = 3744, INC
        NW = c_hi - c_lo
        W = ph.sb([128, 8, NW], BF16, "W")
        for c in range(8):
            k.load("sp", W, W[:, c, :], Wb_in[l][c * 128:(c + 1) * 128, c_lo:c_hi])
        r = norm_res(ph, 3)
        xts = ph.rot(3, [128, D], F32, "xt")
        hTs = ph.rot(2, [128, 8, 512], BF16, "hT4") if pas == 0 else ph.rot(3, [128, 8, 128], BF16, "hT")
        pj = ph.rot(2 if pas == 0 else 6, [128, 512], F32, "pj", psum=True)
        hT_of = {}
        x_of = {}
        if pas == 0:
            gaq = ph.sb([128, 128], F32, "gaq")
            gbq = ph.sb([128, 192], F32, "gbq")
            k.load("sp", gaq, gaq[:], g_aqk[l])
            k.load("sp", gbq, gbq[:], g_bqk[l])
            wqb = ph.sb([128, 2, 768], BF16, "wqb")
            wkvb = ph.sb([128, 1024], BF16, "wkvb")
            k.load("sp", wqb, wqb[:], Wb_qb[l].rearrange("(c p) n -> p c n", p=128))
            k.load("sp", wkvb, wkvb[:], Wb_kvb[l])
            qks = ph.rot(2, [128, 1536], F32, "qk")
            lats = ph.rot(2, [128, 416], F32, "lat")
            sqA = ph.sb([128, 1536], BF16, "sqA")
            ss24 = ph.sb([128, 24], F32, "ss24")
            rs24 = ph.sb([128, 24], F32, "rs24")
            qkb = ph.rot(2, [128, 1536], BF16, "qkb")
            rtA = [ph.sb([128, 24, 8], F32, "rtA%d" % i) for i in range(4)]
            x16 = ph.sb([128, 24, 16], F32, "x16")
            x32 = ph.sb([128, 8, 32], F32, "x32")
            rtK = [ph.sb([128, 1, 16], F32, "rtK%d" % i) for i in range(4)]
            sqK = ph.sb([128, 512], BF16, "sqK")
            ss8k = ph.sb([128, 8], F32, "ss8k")
            rs8k = ph.sb([128, 8], F32, "rs8k")
            rtM = [ph.sb([128, 8, 16], F32, "rtM%d" % i) for i in range(4)]
            vAs = ph.rot(2, [128, 12, 65], BF16, "vA")
            for t in vAs.tiles:
                k.op("dve", lambda e, t=t: e.memset(t[:], 1.0), writes=[t])
            pTA = ph.ps([128, 16, 128], BF16, "pTA")
            ATs = ph.rot(2, [128, 12, 128], BF16, "AT")
            sqM = ph.sb([128, 768], BF16, "sqM")
            ssl = ph.sb([128, 3], F32, "ssl")
            rsl = ph.sb([128, 3], F32, "rsl")
            latn = ph.sb([128, 384], BF16, "latn")
            pT2 = ph.ps([128, 8, 128], BF16, "pT2")
            latT = ph.sb([128, 3, 128], BF16, "latT")
            pM = ph.ps([128, 1024], F32, "pM")
            bq = ph.sb([128, 768], F32, "bq")
            bk = ph.sb([128, 8, 96], F32, "bk")
            ss8 = ph.sb([128, 8], F32, "ss8")
            rs8 = ph.sb([128, 8], F32, "rs8")
            bqbs = ph.rot(2, [128, 8, 96], BF16, "bqb")
            bkbs = ph.rot(2, [128, 8, 96], BF16, "bkb")
            tails = {}
            krg = ph.sb([128, 32], F32, "krg")
            krr = ph.sb([128, 32], F32, "krr")
            vBs = ph.rot(2, [128, 8, 65], BF16, "vB")
            for t in vBs.tiles:
                k.op("dve", lambda e, t=t: e.memset(t[:], 1.0), writes=[t])
            BTq = ph.rot(2, [96, 8, 128], BF16, "BTq")
            BTk = ph.rot(2, [96, 8, 128], BF16, "BTk")
            CTs = ph.rot(1, [128, 8, 512], BF16, "CT")
            qk_of, lat_of = {}, {}
        else:
            cvs = ph.rot(2, [128, 512], BF16, "cv")
            gts = ph.rot(2, [128, 3072], BF16, "gt")

        def proj_chunk(hTo, col0, n):
            hT, off = hTo
            p = pj.next()
            for c in range(8):
                k.op("pe", lambda e, c=c: e.matmul(out=p[:, 0:n], lhsT=hT[:, c, off:off + 128], rhs=W[:, c, col0:col0 + n],
                                                   start=(c == 0), stop=(c == 7)), reads=[hT, W], writes=[p])
            return p

        def rope(src, dst, nh, off, half, cs, sn, reads_t, dst_t, tmp):
            x1 = src[:, :, off:off + half]
            x2 = src[:, :, off + half:off + 2 * half]
            cB = bc(cs, 1, [128, nh, half])
            sB = bc(sn, 1, [128, nh, half])
            t1, t2, t3, t4 = [t[:, 0:nh, 0:half] for t in tmp]
            k.op("pool", lambda e: e.tensor_tensor(out=t1, in0=x1, in1=cB, op=ALU.mult), reads=reads_t, writes=[tmp[0]])
            k.op("pool", lambda e: e.tensor_tensor(out=t2, in0=x2, in1=sB, op=ALU.mult), reads=reads_t, writes=[tmp[1]])
            yield
            k.op("pool", lambda e: e.tensor_tensor(out=t3, in0=x2, in1=cB, op=ALU.mult), reads=reads_t, writes=[tmp[2]])
            k.op("pool", lambda e: e.tensor_tensor(out=t4, in0=x1, in1=sB, op=ALU.mult), reads=reads_t, writes=[tmp[3]])
            yield
            k.op("dve", lambda e: e.tensor_tensor(out=dst[:, :, off:off + half], in0=t1, in1=t2, op=ALU.subtract),
                 reads=[tmp[0], tmp[1]], writes=[dst_t])
            k.op("dve", lambda e: e.tensor_tensor(out=dst[:, :, off + half:off + 2 * half], in0=t3, in1=t4, op=ALU.add),
                 reads=[tmp[2], tmp[3]], writes=[dst_t])
            yield

        def Lx(tt):
            t0 = tt * 128
            xt = xts.next()
            k.load("sp", xt, xt[:], xsrc[t0:t0 + 128, :])
            x_of[tt] = xt

        def Nn(tt):
            xt = x_of.pop(tt)
            if pas == 0:
                if tt % 4 == 0:
                    hT_of["cur"] = hTs.next()
                hT = hT_of["cur"]
                off = (tt % 4) * 128
            else:
                hT = hTs.next()
                off = 0
            hT_of[tt] = (hT, off)
            r["hTtile"] = hT
            return norm_transpose(ph, r, None, xt, hT[:, :, off:off + 128], defer=True)

        def P0(tt):
            t0 = tt * 128
            hT = hT_of[tt]
            qk = qks.next()
            lat = lats.next()
            qk_of[tt] = qk
            lat_of[tt] = lat
            for j in range(3):
                p = proj_chunk(hT, j * 512, 512)
                cast("act", qk[:, j * 512:(j + 1) * 512], p[:, :], [p], [qk])
                yield
            p = proj_chunk(hT, 2304, 416)
            cast("dve", lat[:, :], p[:, 0:416], [p], [lat])
            yield
            vA = vAs.next()
            p = proj_chunk(hT, 1536, 512)
            cast("act", vA[:, 0:8, 0:64], p[:, :].rearrange("p (h d) -> p h d", d=64), [p], [vA])
            yield
            p = proj_chunk(hT, 2048, 256)
            cast("dve", vA[:, 8:12, 0:64], p[:, 0:256].rearrange("p (h d) -> p h d", d=64), [p], [vA])
            k.store("sp", vA, Av[t0:t0 + 128, :], vA[:].rearrange("p h d -> p (h d)"))
            yield

        def Cg(g):
            hT = hT_of[4 * g][0]
            CT = CTs.next()
            for j in range(8):
                col0 = 2720 + j * 128
                p = pj.next()
                for c in range(8):
                    k.op("pe", lambda e, c=c: e.matmul(out=p[:, 0:512], lhsT=W[:, c, col0:col0 + 128], rhs=hT[:, c, 0:512],
                                                       start=(c == 0), stop=(c == 7)), reads=[hT, W], writes=[p])
                if j < 4:
                    k.op("act", lambda e, p=p, j=j: e.activation(out=CT[:, j, :], in_=p[:, :], func=AF.Copy, scale=0.125),
                         reads=[p], writes=[CT])
                else:
                    cast("dve", CT[:, j, :], p[:, :], [p], [CT])
                yield
            k.store("sp", CT, CqT.rearrange("(j p) s -> p j s", p=128)[:, :, g * 512:(g + 1) * 512], CT[:, 0:4, :])
            k.store("sp", CT, CkT.rearrange("(j p) s -> p j s", p=128)[:, :, g * 512:(g + 1) * 512], CT[:, 4:8, :])
            yield

        def P1(tt):
            t0 = tt * 128
            hT = hT_of[tt]
            cv = cvs.next()
            p = proj_chunk(hT, 0, 512)
            cast("dve", cv[:], p[:], [p], [cv])
            k.store("sp", cv, Cv[t0:t0 + 128, :], cv[:])
            gt = gts.next()
            for j in range(6):
                p = proj_chunk(hT, 512 + j * 512, 512)
                k.op("act", lambda e, p=p, j=j: e.activation(out=gt[:, j * 512:(j + 1) * 512], in_=p[:], func=AF.Sigmoid),
                     reads=[p], writes=[gt])
            k.store("sp", gt, Gt[t0:t0 + 128, :], gt[:])

        def YA(tt):
            t0 = tt * 128
            qk = qk_of[tt]
            qk3 = qk[:, :].rearrange("p (h d) -> p h d", d=64)
            rms_small(qk3, 24, 64, [qk], sqA, ss24, rs24)
            yield
            rsq(rs24[:], ss24[:], 24, 1.0 / 64, rs24, ss24)
            yield
            k.op("dve", lambda e: e.tensor_tensor(out=qk3, in0=qk3, in1=bc(rs24[:, :], 2, [128, 24, 64]), op=ALU.mult),
                 reads=[qk, rs24], writes=[qk])
            yield
            qk4 = qk[:, :].rearrange("p (a h d) -> p a h d", a=2, d=64)
            g4 = bc(gaq[:, :].rearrange("p (a d) -> p a d", a=2), 2, [128, 2, 12, 64])
            qb = qkb.next()
            qb4 = qb[:, :].rearrange("p (a h d) -> p a h d", a=2, d=64)
            k.op("dve", lambda e: e.tensor_tensor(out=qb4, in0=qk4, in1=g4, op=ALU.mult), reads=[qk, gaq], writes=[qb])
            x16v = x16[:, :, :].rearrange("p (a h) d -> p a h d", a=2)
            k.op("pool", lambda e: e.tensor_tensor(out=x16v, in0=qk4[:, :, :, 0:16], in1=g4[:, :, :, 0:16], op=ALU.mult),
                 reads=[qk, gaq], writes=[x16])
            yield
            qb3 = qb[:, :].rearrange("p (h d) -> p h d", d=64)
            yield from rope(x16[:, :, :], qb3, 24, 0, 8, rA[:, tt, 0:8], rA[:, tt, 8:16], [x16, rA], qb, rtA)

            def tailA():
                for j in range(12):
                    k.op("pe", lambda e, j=j: e.transpose(out=pTA[:, j, :], in_=qb[:, j * 128:(j + 1) * 128], identity=identb),
                         reads=[qb, cb], writes=[pTA])
                AT = ATs.next()
                cast("act", AT[:], pTA[:, 0:12, :], [pTA], [AT])
                k.store("sp", AT, AqT.rearrange("(j p) s -> p j s", p=128)[:, :, t0:t0 + 128], AT[:, 0:6, :])
                k.store("sp", AT, AkT.rearrange("(j p) s -> p j s", p=128)[:, :, t0:t0 + 128], AT[:, 6:12, :])
            tails.setdefault(tt, []).append(tailA)

        def YM(tt):
            t0 = tt * 128
            lat = lat_of[tt]
            k.op("act", lambda e: e.activation(out=sqM[:, 0:256], in_=lat[:, 0:256], func=AF.Square, accum_out=ssl[:, 0:1]),
                 reads=[lat], writes=[sqM, ssl])
            k.op("act", lambda e: e.activation(out=sqM[:, 256:384], in_=lat[:, 256:384], func=AF.Square, accum_out=ssl[:, 1:2]),
                 reads=[lat], writes=[sqM, ssl])
            k.op("act", lambda e: e.activation(out=sqM[:, 384:416], in_=lat[:, 384:416], func=AF.Square, accum_out=ssl[:, 2:3]),
                 reads=[lat], writes=[sqM, ssl])
            yield
            k.op("dve", lambda e: e.tensor_scalar(out=rsl[:, 0:1], in0=ssl[:, 0:1], scalar1=1.0 / 256, scalar2=EPS,
                                                  op0=ALU.mult, op1=ALU.add), reads=[ssl], writes=[rsl])
            k.op("dve", lambda e: e.tensor_scalar(out=rsl[:, 1:2], in0=ssl[:, 1:2], scalar1=1.0 / 128, scalar2=EPS,
                                                  op0=ALU.mult, op1=ALU.add), reads=[ssl], writes=[rsl])
            k.op("pool", lambda e: e.tensor_tensor(out=rsl[:, 0:2], in0=rsl[:, 0:2], in1=neghalf[:, 0:2], op=ALU.pow),
                 reads=[rsl, neghalf], writes=[rsl])
            yield
            cast("dve", latn[:, 0:256], lat[:, 0:256], [lat, rsl], [latn], scale=rsl[:, 0:1])
            cast("dve", latn[:, 256:384], lat[:, 256:384], [lat, rsl], [latn], scale=rsl[:, 1:2])
            yield
            for j in range(3):
                k.op("pe", lambda e, j=j: e.transpose(out=pT2[:, j, :], in_=latn[:, j * 128:(j + 1) * 128], identity=identb),
                     reads=[latn, cb], writes=[pT2])
            cast("dve", latT[:], pT2[:, 0:3, :], [pT2], [latT])
            yield
            for (c0, n) in ((0, 512), (512, 256)):
                for c in range(2):
                    k.op("pe", lambda e, c=c, c0=c0, n=n: e.matmul(out=pM[:, c0:c0 + n], lhsT=latT[:, c, :],
                                                                  rhs=wqb[:, c, c0:c0 + n], start=(c == 0), stop=(c == 1)),
                         reads=[latT, wqb], writes=[pM])
            cast("act", bq[:, :], pM[:, 0:768], [pM], [bq])
            yield
            for c0 in (0, 512):
                k.op("pe", lambda e, c0=c0: e.matmul(out=pM[:, c0:c0 + 512], lhsT=latT[:, 2, :], rhs=wkvb[:, c0:c0 + 512],
                                                    start=True, stop=True), reads=[latT, wkvb], writes=[pM])
            vB = vBs.next()
            pM3 = pM[:, :].rearrange("p (h d) -> p h d", d=128)
            cast("act", vB[:, :, 0:64], pM3[:, :, 64:128], [pM], [vB])
            cast("dve", bk[:, :, 0:64], pM3[:, :, 0:64], [pM], [bk])
            k.store("sp", vB, Bv[t0:t0 + 128, :], vB[:].rearrange("p h d -> p (h d)"))
            yield
            yield from rr2(Qc(tt), Kc(tt, lat))

        def rr2(g1, g2):
            gens = [g1, g2]
            while gens:
                for g in list(gens):
                    try:
                        next(g)
                        yield
                    except StopIteration:
                        gens.remove(g)

        def Qc(tt):
            t0 = tt * 128
            bqb = bqbs.next()
            bq3 = bq[:, :].rearrange("p (h d) -> p h d", d=96)
            rms_small(bq3, 8, 96, [bq], sqM, ss8, rs8)
            yield
            rsq(rs8[:], ss8[:], 8, 1.0 / 96, rs8, ss8)
            yield
            k.op("dve", lambda e: e.tensor_tensor(out=bq3, in0=bq3, in1=bc(rs8[:, :], 2, [128, 8, 96]), op=ALU.mult),
                 reads=[bq, rs8], writes=[bq])
            yield
            k.op("dve", lambda e: e.tensor_tensor(out=bqb[:, :, 0:64], in0=bq3[:, :, 0:64],
                                                  in1=bc(gbq[:, 0:64], 1, [128, 8, 64]), op=ALU.mult),
                 reads=[bq, gbq], writes=[bqb])
            k.op("pool", lambda e: e.tensor_tensor(out=x32[:, :, :], in0=bq3[:, :, 64:96],
                                                   in1=bc(gbq[:, 64:96], 1, [128, 8, 32]), op=ALU.mult),
                 reads=[bq, gbq], writes=[x32])
            yield
            yield from rope(x32[:, :, :], bqb[:, :, 64:96], 8, 0, 16, rM[:, tt, 0:16], rM[:, tt, 16:32], [x32, rM], bqb, rtM)

            def tailQ():
                for h in range(8):
                    k.op("pe", lambda e, h=h: e.transpose(out=pT2[0:96, h, :], in_=bqb[:, h, :], identity=identb),
                         reads=[bqb, cb], writes=[pT2])
                Bq = BTq.next()
                cast("act", Bq[:], pT2[0:96, :, :], [pT2], [Bq])
                k.store("sp", Bq, BqT.rearrange("(h f) s -> f h s", f=96)[:, :, t0:t0 + 128], Bq[:])
            tails.setdefault(tt, []).append(tailQ)

        def Kc(tt, lat):
            t0 = tt * 128
            bkb = bkbs.next()
            rms_small(bk[:, :, 0:64], 8, 64, [bk], sqK, ss8k, rs8k)
            yield
            k.op("dve", lambda e: e.tensor_scalar(out=ss8k[:], in0=ss8k[:], scalar1=ssl[:, 2:3], scalar2=None, op0=ALU.add),
                 reads=[ss8k, ssl], writes=[ss8k])
            rsq(rs8k[:], ss8k[:], 8, 1.0 / 96, rs8k, ss8k)
            yield
            k.op("dve", lambda e: e.tensor_tensor(out=krg[:, :], in0=lat[:, 384:416], in1=gbq[:, 160:192], op=ALU.mult),
                 reads=[lat, gbq], writes=[krg])
            yield from rope(krg[:, :].unsqueeze(1), krr[:, :].unsqueeze(1), 1, 0, 16, rM[:, tt, 0:16], rM[:, tt, 16:32],
                            [krg, rM], krr, rtK)
            k.op("dve", lambda e: e.tensor_tensor(out=bk[:, :, 0:64], in0=bk[:, :, 0:64],
                                                  in1=bc(gbq[:, 96:160], 1, [128, 8, 64]), op=ALU.mult),
                 reads=[bk, gbq], writes=[bk])
            yield
            k.op("dve", lambda e: e.tensor_tensor(out=bkb[:, :, 0:64], in0=bk[:, :, 0:64],
                                                  in1=bc(rs8k[:, :], 2, [128, 8, 64]), op=ALU.mult),
                 reads=[bk, rs8k], writes=[bkb])
            k.op("dve", lambda e: e.tensor_tensor(out=bkb[:, :, 64:96], in0=bc(krr[:, :], 1, [128, 8, 32]),
                                                  in1=bc(rs8k[:, :], 2, [128, 8, 32]), op=ALU.mult),
                 reads=[krr, rs8k], writes=[bkb])
            yield

            def tailK():
                for h in range(8):
                    k.op("pe", lambda e, h=h: e.transpose(out=pT2[0:96, h, :], in_=bkb[:, h, :], identity=identb),
                         reads=[bkb, cb], writes=[pT2])
                Bk = BTk.next()
                cast("act", Bk[:], pT2[0:96, :, :], [pT2], [Bk])
                k.store("sp", Bk, BkT.rearrange("(h f) s -> f h s", f=96)[:, :, t0:t0 + 128], Bk[:])
            tails.setdefault(tt, []).append(tailK)

        nt = min(NT, TLIM)
        for tt in range(min(2, nt)):
            Lx(tt)
        Nn(0)()
        if nt > 1:
            if nt > 2:
                Lx(2)
            Nn(1)()
        if pas == 0:
            interleave([P0(0)])
        for tt in range(nt):
            if tt + 3 < nt:
                Lx(tt + 3)
            nB = None
            if tt + 2 < nt:
                nB = Nn(tt + 2)
            if pas == 0:
                def P0n(tt=tt, nB=nB):
                    if tt + 1 < nt:
                        yield from P0(tt + 1)
                    if nB:
                        nB()
                    yield
                interleave([P0n(), YA(tt), YM(tt), Cg(tt // 4) if tt % 4 == 3 else None])
                for f in tails.pop(tt - 1, []):
                    f()
            else:
                P1(tt)
                if nB:
                    nB()
        if pas == 0:
            for f in tails.pop(nt - 1, []):
                f()
        ph.end()

    def phase2():
        ph = Phase(k)
        qn = [ph.sb([64, S], BF16, "qn%d" % i) for i in range(4)]
        kn = [ph.sb([64, S], BF16, "kn%d" % i) for i in range(4)]
        qd = [ph.sb([64, S], BF16, "qd%d" % i) for i in range(4)]
        kd = [ph.sb([64, S], BF16, "kd%d" % i) for i in range(4)]
        vA = ph.sb([128, NT, 4, 65], BF16, "vAall")
        mb = ph.sb([128, 2, 256], BF16, "mband")
        for hi in range(2):
            k.op("dve", lambda e, hi=hi: e.tensor_copy(out=mb[:, hi, 0:128], in_=m_ge), reads=[cb], writes=[mb])
            k.op("dve", lambda e, hi=hi: e.tensor_copy(out=mb[:, hi, 128:256], in_=m_le), reads=[cb], writes=[mb])
        S2 = ph.rot(4, [128, 2, 256], F32, "S2", psum=True)
        Pt = [ph.rot(4, [128, 2, 256], BF16, "Pt%d_" % hp) for hp in range(2)]
        O4 = ph.rot(2, [128, 512], F32, "O4", psum=True)
        osb = ph.rot(2, [128, 260], F32, "osb")
        for g, d in enumerate((1, 4, 16)):
            L = S // d
            nb = L // 128
            for hs in range(4):
                h = g * 4 + hs
                k.load("sp", qn[hs], qn[hs][:], AqT[h * 64:(h + 1) * 64, :])
                k.load("sp", kn[hs], kn[hs][:], AkT[h * 64:(h + 1) * 64, :])
                if d > 1:
                    k.op("pool", lambda e, hs=hs: e.tensor_copy(out=qd[hs][:, :].rearrange("p (r u) -> p r u", r=d),
                                                                in_=qn[hs][:, :].rearrange("p (u r) -> p r u", r=d)),
                         reads=[qn[hs]], writes=[qd[hs]])
                    k.op("dve", lambda e, hs=hs: e.tensor_copy(out=kd[hs][:, :].rearrange("p (r u) -> p r u", r=d),
                                                               in_=kn[hs][:, :].rearrange("p (u r) -> p r u", r=d)),
                         reads=[kn[hs]], writes=[kd[hs]])
            Q = qd if d > 1 else qn
            Kk = kd if d > 1 else kn
            for r in range(d):
                src = Av.rearrange("(n p r) c -> r p n c", p=128, r=d)[r][:, :, g * 260:(g + 1) * 260]
                k.load("sp", vA, vA[:, r * nb:(r + 1) * nb, :, :].rearrange("p n h d -> p n (h d)"), src)
            blocks = [(r, n) for r in range(d) for n in range(nb)]
            Pof = {}

            def QE(i):
                r, n = blocks[i]
                col0 = r * L + n * 128
                nq = 256 if n < nb - 1 else 128
                cur = []
                for hp in range(2):
                    s2 = S2.next()
                    for hi in range(2):
                        hs = hp * 2 + hi
                        k.op("pe", lambda e, hs=hs, hi=hi, s2=s2: e.matmul(
                            out=s2[:, hi, 0:nq], lhsT=Kk[hs][:, col0:col0 + 128], rhs=Q[hs][:, col0:col0 + nq],
                            start=True, stop=True), reads=[Kk[hs], Q[hs]], writes=[s2])
                    pt = Pt[hp].next()
                    k.op("act", lambda e, s2=s2, pt=pt: e.activation(out=pt[:, :, 0:nq], in_=s2[:, :, 0:nq], func=AF.Exp,
                                                                     scale=0.125), reads=[s2], writes=[pt])
                    k.op("pool", lambda e, pt=pt: e.tensor_tensor(out=pt[:, :, 0:nq], in0=pt[:, :, 0:nq],
                                                                  in1=mb[:, :, 0:nq], op=ALU.mult),
                         reads=[pt, mb], writes=[pt])
                    cur.append(pt)
                Pof[i] = cur

            def PVs(i):
                r, n = blocks[i]
                b = r * nb + n
                curP = Pof[i]
                prevP = Pof.get(i - 1) if n > 0 else None
                o4t = O4.next()
                o4 = o4t[:, 0:260].rearrange("p (h d) -> p h d", d=65)
                for hs in range(4):
                    hp, hi = hs // 2, hs % 2
                    if n > 0:
                        k.op("pe", lambda e, hs=hs, hp=hp, hi=hi: e.matmul(
                            out=o4[:, hs, :], lhsT=prevP[hp][:, hi, 128:256], rhs=vA[:, b - 1, hs, :],
                            start=True, stop=False), reads=[prevP[hp], vA], writes=[o4t])
                    k.op("pe", lambda e, hs=hs, hp=hp, hi=hi: e.matmul(
                        out=o4[:, hs, :], lhsT=curP[hp][:, hi, 0:128], rhs=vA[:, b, hs, :],
                        start=(n == 0), stop=True), reads=[curP[hp], vA], writes=[o4t])
                ob = osb.next()
                cast("act", ob[:, :], o4t[:, 0:260], [o4t], [ob])
                dst = accA[g].rearrange("(n p r) c -> r n p c", p=128, r=d)[r, n]
                k.store("sp", ob, dst, ob[:, :])
                Pof.pop(i - 1, None)

            QE(0)
            for i in range(len(blocks)):
                if i + 1 < len(blocks):
                    QE(i + 1)
                PVs(i)
        ph.end()

    def phase3():
        ph = Phase(k)
        qTs = ph.rot(2, [96, S], BF16, "bqT")
        kTs = ph.rot(2, [96, S], BF16, "bkT")
        vB = ph.sb([128, NT, 520], BF16, "vBall")
        k.load("sp", vB, vB[:], Bv.rearrange("(t p) c -> p t c", p=128))
        oB_sb = ph.sb([128, NT, 512], BF16, "oB_sb")
        Sb = ph.rot(4, [128, 512], F32, "Sb", psum=True)
        Pts = ph.rot(5, [128, 512], BF16, "Ptb")
        Ob = ph.rot(2, [65, 512], F32, "Ob", psum=True)
        Osb = ph.rot(2, [65, 512], F32, "Osb")
        pTo = ph.rot(1, [128, 512], F32, "pTo", psum=True)
        rden = ph.rot(2, [128, 4], F32, "rden")
        sc = 96 ** -0.5
        heads = {}

        def load_head(h):
            qT = qTs.next()
            kT = kTs.next()
            k.load("sp", qT, qT[:], BqT[h * 96:(h + 1) * 96, :])
            k.load("sp", kT, kT[:], BkT[h * 96:(h + 1) * 96, :])
            heads[h] = (qT, kT)

        units = []
        for h in range(8):
            for qg in range(NQG):
                for kb in range(4 * qg + 4):
                    units.append({"h": h, "qg": qg, "kb": kb, "c0": max(0, kb - 4 * qg) * 128, "last": kb == 4 * qg + 3})
        grp = {}

        def A(u):
            h, qg, kb, c0 = u["h"], u["qg"], u["kb"], u["c0"]
            if h not in heads:
                load_head(h)
            if kb == 0 and qg == 0 and h + 1 < 8 and (h + 1) not in heads:
                load_head(h + 1)
            qT, kT = heads[h]
            sb_ = Sb.next()
            u["sb"] = sb_
            k.op("pe", lambda e: e.matmul(out=sb_[:, c0:512], lhsT=kT[:, kb * 128:(kb + 1) * 128],
                                          rhs=qT[:, qg * 512 + c0:(qg + 1) * 512], start=True, stop=True),
                 reads=[kT, qT], writes=[sb_])

        def B(u):
            c0, sb_ = u["c0"], u["sb"]
            pt = Pts.next()
            u["pt"] = pt
            k.op("act", lambda e: e.activation(out=pt[:, c0:512], in_=sb_[:, c0:512], func=AF.Exp, scale=sc),
                 reads=[sb_], writes=[pt])
            if u["kb"] >= 4 * u["qg"]:
                k.op("pool", lambda e: e.tensor_tensor(out=pt[:, c0:c0 + 128], in0=pt[:, c0:c0 + 128], in1=m_ge,
                                                       op=ALU.mult), reads=[pt, cb], writes=[pt])

        def F(u):
            h, qg, kb, c0, pt = u["h"], u["qg"], u["kb"], u["c0"], u["pt"]
            if kb == 0:
                grp[(h, qg)] = Ob.next()
            ob = grp[(h, qg)]
            k.op("pe", lambda e: e.matmul(out=ob[:, c0:512], lhsT=vB[:, kb, h * 65:(h + 1) * 65], rhs=pt[:, c0:512],
                                          start=(kb == 0), stop=u["last"]), reads=[vB, pt], writes=[ob])
            if u["last"]:
                osb_ = Osb.next()
                cast("act", osb_[:, :], ob[:, :], [ob], [osb_])
                ptot = pTo.next()
                pto = ptot[:, 0:260].rearrange("p (j d) -> p j d", d=65)
                for j in range(4):
                    k.op("pe", lambda e, j=j: e.transpose(out=pto[:, j, :], in_=osb_[:, j * 128:(j + 1) * 128],
                                                          identity=identf[0:65, 0:65]), reads=[osb_, cf], writes=[ptot])
                rd = rden.next()
                k.op("dve", lambda e: e.reciprocal(out=rd[:, :], in_=pto[:, :, 64]), reads=[ptot], writes=[rd])
                k.op("dve", lambda e: e.tensor_tensor(out=oB_sb[:, 4 * qg:4 * qg + 4, h * 64:(h + 1) * 64], in0=pto[:, :, 0:64],
                                                      in1=bc(rd[:, :], 2, [128, 4, 64]), op=ALU.mult),
                     reads=[ptot, rd], writes=[oB_sb])

        N = len(units)
        A(units[0])
        for t in range(N + 1):
            if t + 1 < N:
                A(units[t + 1])
            if t >= 1:
                F(units[t - 1])
            if t < N:
                B(units[t])
        k.store("sp", oB_sb, oB.rearrange("(t p) c -> p t c", p=128), oB_sb[:])
        ph.end()

    def phase4():
        ph = Phase(k)
        qTs = ph.rot(2, [64, S], BF16, "cqT")
        kTs = ph.rot(2, [64, S], BF16, "ckT")
        vC = ph.sb([128, NT, 512], BF16, "vCall")
        k.load("sp", vC, vC[:], Cv.rearrange("(t p) c -> p t c", p=128))
        oC_sb = ph.sb([128, NT, 512], BF16, "oC_sb")
        Zb = ph.rot(4, [128, 512], F32, "Zb", psum=True)
        Nb = ph.rot(2, [128, 512], F32, "Nb", psum=True)
        es = ph.rot(2, [128, 512], F32, "e_")
        sps = ph.rot(4, [128, 512], BF16, "sp_")
        Es = ph.rot(3, [128, 512], BF16, "E_")
        gts = ph.rot(2, [128, 4], F32, "g_")
        accs = ph.rot(2, [128, 4, 64], F32, "acc")
        heads = {}

        def load_head(h):
            qT = qTs.next()
            kT = kTs.next()
            k.load("sp", qT, qT[:], CqT[h * 64:(h + 1) * 64, :])
            k.load("sp", kT, kT[:], CkT[h * 64:(h + 1) * 64, :])
            heads[h] = (qT, kT)

        units = []
        for h in range(8):
            for qg in range(NQG):
                for kb in range(4 * qg + 4):
                    j0 = max(0, kb - 4 * qg)
                    units.append({"h": h, "qg": qg, "kb": kb, "j0": j0, "c0": j0 * 128, "diag": kb >= 4 * qg,
                                  "last": kb == 4 * qg + 3})
        grp = {}

        def A(u):
            h, qg, kb, c0 = u["h"], u["qg"], u["kb"], u["c0"]
            if h not in heads:
                load_head(h)
            if kb == 0 and qg == 0 and h + 1 < 8 and (h + 1) not in heads:
                load_head(h + 1)
            qT, kT = heads[h]
            zb = Zb.next()
            u["zb"] = zb
            k.op("pe", lambda e: e.matmul(out=zb[:, c0:512], lhsT=kT[:, kb * 128:(kb + 1) * 128],
                                          rhs=qT[:, qg * 512 + c0:(qg + 1) * 512], start=True, stop=True),
                 reads=[kT, qT], writes=[zb])

        def B(u):
            c0, zb = u["c0"], u["zb"]
            ee = es.next()
            k.op("act", lambda e: e.activation(out=ee[:, c0:512], in_=zb[:, c0:512], func=AF.Exp), reads=[zb], writes=[ee])
            sp = sps.next()
            u["sp"] = sp
            k.op("act", lambda e: e.activation(out=sp[:, c0:512], in_=ee[:, c0:512], func=AF.Ln, bias=1.0),
                 reads=[ee], writes=[sp])
            if u["diag"]:
                k.op("pool", lambda e: e.tensor_tensor(out=sp[:, c0:c0 + 128], in0=sp[:, c0:c0 + 128], in1=m_gt,
                                                       op=ALU.mult), reads=[sp, cb], writes=[sp])

        def C(u):
            c0, zb, sp = u["c0"], u["zb"], u["sp"]
            k.op("pe", lambda e: e.matmul(out=zb[:, c0:512], lhsT=negU, rhs=sp[:, c0:512], start=False, stop=True,
                                          skip_group_check=True), reads=[cb, sp], writes=[zb])

        def Dd(u):
            c0, zb = u["c0"], u["zb"]
            E = Es.next()
            u["E"] = E
            k.op("act", lambda e: e.activation(out=E[:, c0:512], in_=zb[:, c0:512], func=AF.Exp), reads=[zb], writes=[E])
            if u["diag"]:
                k.op("pool", lambda e: e.tensor_tensor(out=E[:, c0:c0 + 128], in0=E[:, c0:c0 + 128], in1=m_gt,
                                                       op=ALU.mult), reads=[E, cb], writes=[E])

        def F(u):
            h, kb, j0, E, sp = u["h"], u["kb"], u["j0"], u["E"], u["sp"]
            nbt = Nb.next()
            u["nbt"] = nbt
            nbk = nbt[:, 0:260].rearrange("p (j d) -> p j d", d=65)
            for j in range(j0, 4):
                k.op("pe", lambda e, j=j: e.matmul(out=nbk[:, j, 0:64], lhsT=E[:, j * 128:(j + 1) * 128],
                                                   rhs=vC[:, kb, h * 64:(h + 1) * 64], start=True, stop=True,
                                                   skip_group_check=True), reads=[E, vC], writes=[nbt])
                k.op("pe", lambda e, j=j: e.matmul(out=nbk[:, j, 64:65], lhsT=sp[:, j * 128:(j + 1) * 128],
                                                   rhs=onesb[:, 0:1], start=True, stop=True,
                                                   skip_group_check=True), reads=[sp, onesb], writes=[nbt])

        def G(u):
            h, qg, kb, j0, nbt = u["h"], u["qg"], u["kb"], u["j0"], u["nbt"]
            nbk = nbt[:, 0:260].rearrange("p (j d) -> p j d", d=65)
            if kb == 0:
                acc = accs.next()
                grp[(h, qg)] = acc
                k.op("dve", lambda e: e.tensor_copy(out=acc[:], in_=nbk[:, :, 0:64]), reads=[nbt], writes=[acc])
            else:
                acc = grp[(h, qg)]
                gt = gts.next()
                k.op("act", lambda e: e.activation(out=gt[:, j0:4], in_=nbk[:, j0:4, 64], func=AF.Exp, scale=-1.0),
                     reads=[nbt], writes=[gt])
                for j in range(j0, 4):
                    k.op("dve", lambda e, j=j: e.scalar_tensor_tensor(out=acc[:, j, :], in0=acc[:, j, :], scalar=gt[:, j:j + 1],
                                                                      in1=nbk[:, j, 0:64], op0=ALU.mult, op1=ALU.add),
                         reads=[acc, gt, nbt], writes=[acc])
            if u["last"]:
                k.op("pool", lambda e: e.tensor_copy(out=oC_sb[:, 4 * qg:4 * qg + 4, h * 64:(h + 1) * 64], in_=acc[:]),
                     reads=[acc], writes=[oC_sb])

        N = len(units)
        for t in range(N + 2):
            if t < N:
                A(units[t])
                B(units[t])
            if 1 <= t <= N:
                C(units[t - 1])
                Dd(units[t - 1])
            if t >= 2:
                F(units[t - 2])
                G(units[t - 2])
        k.store("sp", oC_sb, oC.rearrange("(t p) c -> p t c", p=128), oC_sb[:])
        ph.end()

    def phase5(l, s, xsrc):
        ph = Phase(k)
        Wbr = ph.sb([128, 10, D], BF16, "Wbr")
        Wo = ph.sb([128, 8, D], BF16, "Wo")
        k.load("sp", Wbr, Wbr[:], Wb_br[l].rearrange("(c p) n -> p c n", p=128))
        k.load("sp", Wo, Wo[:], Wb_out[l].rearrange("(c p) n -> p c n", p=128))
        xts = ph.rot(3, [128, D], F32, "xt")
        a3s = ph.rot(3, [128, 3, 260], F32, "a3")
        ocs = ph.rot(3, [128, 1280], BF16, "ocat")
        gts = ph.rot(3, [128, 3072], BF16, "gt")
        rds = ph.rot(2, [128, 4], F32, "rd")
        pT = ph.ps([128, 16, 128], BF16, "pT")
        pTm = ph.ps([128, 8, 128], BF16, "pTm")
        oTs = ph.rot(2, [128, 10, 128], BF16, "oT")
        PP = ph.rot(2, [128, D], F32, "PP", psum=True)
        tf = ph.sb([128, D], F32, "tf")
        uf = ph.sb([128, D], F32, "uf")
        u2 = ph.sb([128, D], F32, "u2")
        mbfs = ph.rot(2, [128, D], BF16, "mbf")
        mTs = ph.rot(2, [128, 8, 128], BF16, "mT")
        xo = ph.rot(2, [128, D], F32, "xo")
        st = {}

        def L(tt):
            t0 = tt * 128
            xt, a3, oc, gt = xts.next(), a3s.next(), ocs.next(), gts.next()
            k.load("sp", xt, xt[:], xsrc[t0:t0 + 128, :])
            k.load("sp", a3, a3[:], accA[:, t0:t0 + 128, :].rearrange("g p c -> p g c"))
            k.load("sp", oc, oc[:, 256:768], oB[t0:t0 + 128, :])
            k.load("sp", oc, oc[:, 768:1280], oC[t0:t0 + 128, :])
            k.load("sp", gt, gt[:], Gt[t0:t0 + 128, :])
            st[tt] = {"xt": xt, "a3": a3, "oc": oc, "gt": gt}

        def branch(P, oT, ca, cbn):
            for half in range(2):
                for c in range(ca, cbn):
                    k.op("pe", lambda e, c=c, half=half: e.matmul(
                        out=P[:, half * 512:(half + 1) * 512], lhsT=oT[:, c, :], rhs=Wbr[:, c, half * 512:(half + 1) * 512],
                        start=(c == ca), stop=(c == cbn - 1)), reads=[oT, Wbr], writes=[P])

        def S1(tt):
            d = st[tt]
            a3, oc, gt = d["a3"], d["oc"], d["gt"]
            k.op("pool", lambda e: e.tensor_tensor(out=a3[:, 0, :], in0=a3[:, 0, :], in1=a3[:, 1, :], op=ALU.add),
                 reads=[a3], writes=[a3])
            k.op("pool", lambda e: e.tensor_tensor(out=a3[:, 0, :], in0=a3[:, 0, :], in1=a3[:, 2, :], op=ALU.add),
                 reads=[a3], writes=[a3])
            n4 = a3[:, 0, :].rearrange("p (h d) -> p h d", d=65)
            rd = rds.next()
            k.op("dve", lambda e: e.reciprocal(out=rd[:, :], in_=n4[:, :, 64]), reads=[a3], writes=[rd])
            k.op("dve", lambda e: e.tensor_tensor(out=oc[:, 0:256].rearrange("p (h d) -> p h d", d=64), in0=n4[:, :, 0:64],
                                                  in1=bc(rd[:, :], 2, [128, 4, 64]), op=ALU.mult), reads=[a3, rd], writes=[oc])
            for j in range(10):
                k.op("pe", lambda e, j=j: e.transpose(out=pT[:, j, :], in_=oc[:, j * 128:(j + 1) * 128], identity=identb),
                     reads=[oc, cb], writes=[pT])
            oT = oTs.next()
            cast("act", oT[:], pT[:, 0:10, :], [pT], [oT])
            Pa = PP.next()
            branch(Pa, oT, 0, 2)
            k.op("dve", lambda e: e.tensor_tensor(out=tf[:], in0=Pa[:], in1=gt[:, 0:1024], op=ALU.mult),
                 reads=[Pa, gt], writes=[tf])
            Pb = PP.next()
            branch(Pb, oT, 2, 6)
            k.op("dve", lambda e: e.tensor_tensor(out=uf[:], in0=Pb[:], in1=gt[:, 1024:2048], op=ALU.mult),
                 reads=[Pb, gt], writes=[uf])
            k.op("pool", lambda e: e.tensor_tensor(out=tf[:], in0=tf[:], in1=uf[:], op=ALU.add), reads=[tf, uf], writes=[tf])
            Pc = PP.next()
            branch(Pc, oT, 6, 10)
            k.op("dve", lambda e: e.tensor_tensor(out=u2[:], in0=Pc[:], in1=gt[:, 2048:3072], op=ALU.mult),
                 reads=[Pc, gt], writes=[u2])
            mbf = mbfs.next()
            k.op("pool", lambda e: e.tensor_tensor(out=mbf[:], in0=tf[:], in1=u2[:], op=ALU.add), reads=[tf, u2], writes=[mbf])
            d["mbf"] = mbf

        def S2(tt):
            t0 = tt * 128
            d = st.pop(tt)
            mbf, xt = d["mbf"], d["xt"]
            for j in range(8):
                k.op("pe", lambda e, j=j: e.transpose(out=pTm[:, j, :], in_=mbf[:, j * 128:(j + 1) * 128], identity=identb),
                     reads=[mbf, cb], writes=[pTm])
            mT = mTs.next()
            cast("act", mT[:], pTm[:], [pTm], [mT])
            P = PP.next()
            for half in range(2):
                for c in range(8):
                    k.op("pe", lambda e, c=c, half=half: e.matmul(
                        out=P[:, half * 512:(half + 1) * 512], lhsT=mT[:, c, :], rhs=Wo[:, c, half * 512:(half + 1) * 512],
                        start=(c == 0), stop=(c == 7)), reads=[mT, Wo], writes=[P])
            o = xo.next()
            k.op("dve", lambda e: e.tensor_tensor(out=o[:], in0=P[:], in1=xt[:], op=ALU.add), reads=[P, xt], writes=[o])
            k.store("sp", o, xmid[s][t0:t0 + 128, :], o[:])

        L(0)
        L(1)
        S1(0)
        for tt in range(NT):
            if tt + 2 < NT:
                L(tt + 2)
            if tt + 1 < NT:
                S1(tt + 1)
            S2(tt)
        ph.end()

    def phase6(l, dsts):
        ph = Phase(k)
        W1 = ph.sb([128, 8, DFF], BF16, "W1")
        W2 = ph.sb([128, 32, D], BF16, "W2")
        for c in range(8):
            k.load("sp", W1, W1[:, c, :], Wb_ff1[l][c * 128:(c + 1) * 128, :])
        for c in range(4):
            k.load("sp", W2, W2[:, c * 8:(c + 1) * 8, :],
                   Wb_ff2[l][c * 1024:(c + 1) * 1024, :].rearrange("(c p) n -> p c n", p=128))
        r = norm_res(ph)
        xts = ph.rot(4, [128, D], F32, "xt")
        h2T = ph.rot(2, [128, 8, 256], BF16, "h2T")
        h1T = ph.sb([128, 32, 256], BF16, "h1T")
        rl = ph.rot(2, [128, 2, 256], F32, "rl")
        pF = ph.rot(2, [128, 2, 256], F32, "pF", psum=True)
        pY = ph.rot(3, [128, 512], F32, "pY", psum=True)
        xo = ph.rot(1, [128, D], F32, "xo")
        groups = [(s, tg) for s in range(nS) for tg in range(NT // 2)]
        gstate = {}

        def Ng(gi):
            s, tg = groups[gi]
            hT = h2T.next()
            xt2 = []
            for i in range(2):
                t0 = (tg * 2 + i) * 128
                xt = xts.next()
                k.load("sp", xt, xt[:], xmid[s][t0:t0 + 128, :])
                r["hTtile"] = hT
                pend.append(norm_transpose(ph, r, None, xt, hT[:, :, i * 128:(i + 1) * 128], defer=True))
                xt2.append(xt)
            gstate[gi] = (hT, xt2)

        pend = []
        Ng(0)
        for f in pend:
            f()
        pend.clear()
        for gi, (s, tg) in enumerate(groups):
            if True:
                if gi + 1 < len(groups):
                    Ng(gi + 1)
                hT, xt2 = gstate.pop(gi)
                for f2 in range(16):
                    p = pF.next()
                    for fi in range(2):
                        f = f2 * 2 + fi
                        for c in range(8):
                            k.op("pe", lambda e, c=c, f=f, fi=fi: e.matmul(
                                out=p[:, fi, :], lhsT=W1[:, c, f * 128:(f + 1) * 128], rhs=hT[:, c, :],
                                start=(c == 0), stop=(c == 7)), reads=[W1, hT], writes=[p])
                    rr = rl.next()
                    k.op("act", lambda e: e.activation(out=rr[:], in_=p[:], func=AF.Relu), reads=[p], writes=[rr])
                    k.op("pool", lambda e: e.tensor_tensor(out=h1T[:, f2 * 2:f2 * 2 + 2, :], in0=rr[:], in1=rr[:], op=ALU.mult),
                         reads=[rr], writes=[h1T])
                for f in pend:
                    f()
                pend.clear()
                for i in range(2):
                    t0 = (tg * 2 + i) * 128
                    o = xo.next()
                    for half in range(2):
                        p = pY.next()
                        for f in range(32):
                            k.op("pe", lambda e, f=f: e.matmul(out=p[:], lhsT=h1T[:, f, i * 128:(i + 1) * 128],
                                                               rhs=W2[:, f, half * 512:(half + 1) * 512],
                                                               start=(f == 0), stop=(f == 31)), reads=[h1T, W2], writes=[p])
                        k.op("dve", lambda e: e.tensor_tensor(out=o[:, half * 512:(half + 1) * 512], in0=p[:],
                                                              in1=xt2[i][:, half * 512:(half + 1) * 512], op=ALU.add),
                             reads=[p, xt2[i]], writes=[o])
                    k.store("sp", o, dsts[s][t0:t0 + 128, :], o[:])
        ph.end()

    U = UPTO
    prep_weights()
    for l in range(nL):
        for s in range(nS):
            xsrc = x_in[s] if l == 0 else xres[s]
            if U >= 1:
                phase1(l, s, xsrc, 0)
            if U >= 2:
                phase1(l, s, xsrc, 1)
            if U >= 3:
                phase2()
            if U >= 4:
                phase3()
            if U >= 5:
                phase4()
            if U >= 6:
                phase5(l, s, xsrc)
        dsts = [out[s] if l == nL - 1 else xres[s] for s in range(nS)]
        if U >= 7:
            phase6(l, dsts)
    k.barrier()
    print("sbuf max bytes/partition:", k.sb_max, "instructions:", k.ninstr, {e: k.cnt[e] for e in k.cnt})
    k.close()
    return nc


def host_consts(S):
    NT = S // 128
    p = np.arange(128)[:, None]
    c = np.arange(128)[None, :]
    consts = np.zeros((128, 5, 128), np.float32)
    consts[:, 0, :] = (p == c)
    consts[:, 1, :] = (c >= p)
    consts[:, 2, :] = (c > p)
    consts[:, 3, :] = (c <= p)
    consts[:, 4, :] = -1.0 * (p >= c)

    def tables(dim):
        pos = np.arange(S, dtype=np.float32)
        inv = (np.float32(500000.0) ** (-np.arange(0, dim, 2, dtype=np.float32) / np.float32(dim))).astype(np.float32)
        ang = (pos[:, None] * inv[None, :]).astype(np.float32)
        t = np.concatenate([np.cos(ang), np.sin(ang)], axis=1).astype(np.float32)
        return np.ascontiguousarray(t.reshape(NT, 128, dim).transpose(1, 0, 2))
    return consts, tables(16), tables(32)


def make_inputs(x_sh, l0, l1, S, attn_norm, w_in, a_q_norm, a_k_norm, b_q_a_norm, w_q_b, b_kv_a_norm,
                w_kv_b, b_q_norm, b_k_norm, w_branch, w_out, mlp_norm, w_ff1, w_ff2):
    nL = l1 - l0
    sl = slice(l0, l1)
    f = lambda a: np.ascontiguousarray(np.asarray(a, dtype=np.float32))
    col = lambda g, c: np.ascontiguousarray(np.asarray(g, np.float32)[sl].reshape(nL, c, 128).transpose(0, 2, 1))
    rep = lambda a, b: np.ascontiguousarray(np.broadcast_to(
        np.concatenate([np.asarray(a, np.float32)[sl], np.asarray(b, np.float32)[sl]], axis=1)[:, None, :],
        (nL, 128, a.shape[1] + b.shape[1])))
    consts, rA, rM = host_consts(S)
    return {
        "x": f(x_sh), "w_in": f(w_in[sl]), "w_q_b": f(np.asarray(w_q_b)[sl].reshape(nL, 256, 768)),
        "w_kv_b": f(np.asarray(w_kv_b)[sl].reshape(nL, 128, 1024)), "w_branch": f(w_branch[sl]),
        "w_out": f(w_out[sl]), "w_ff1": f(w_ff1[sl]), "w_ff2": f(w_ff2[sl]),
        "g_attn": col(attn_norm, 8), "g_mlp": col(mlp_norm, 8), "g_qa": col(b_q_a_norm, 2),
        "g_kva": col(b_kv_a_norm, 1), "g_aqk": rep(a_q_norm, a_k_norm), "g_bqk": rep(b_q_norm, b_k_norm),
        "ropeA": rA, "ropeM": rM, "consts": consts,
    }


def kernel(x, attn_norm, w_in, a_q_norm, a_k_norm, b_q_a_norm, w_q_b, b_kv_a_norm,
           w_kv_b, b_q_norm, b_k_norm, w_branch, w_out, mlp_norm, w_ff1, w_ff2):
    x = np.asarray(x, dtype=np.float32)
    B, S, _ = x.shape
    nL = np.asarray(attn_norm).shape[0]
    ncores = 8
    nS = B // ncores
    nc = build(nL, nS, S)
    in_maps = []
    for c in range(ncores):
        in_maps.append(make_inputs(x[c * nS:(c + 1) * nS], 0, nL, S, attn_norm, w_in, a_q_norm, a_k_norm, b_q_a_norm,
                                   w_q_b, b_kv_a_norm, w_kv_b, b_q_norm, b_k_norm, w_branch, w_out, mlp_norm,
                                   w_ff1, w_ff2))
    res = run_bass_kernel_spmd(nc, in_maps, core_ids=list(range(ncores)))
    return np.concatenate([np.asarray(r["out"], dtype=np.float32) for r in res.results], axis=0)
```

```python
import contextlib
import numpy as np
import concourse.bass as bass
import concourse.mybir as mybir
from concourse.alu_op_type import AluOpType as ALU
from concourse.bass_utils import run_bass_kernel_spmd

AF = mybir.ActivationFunctionType
AX = mybir.AxisListType
F32 = mybir.dt.float32
BF16 = mybir.dt.bfloat16

SAME_ENG_SYNC = True
UPTO = 99
STAGE1 = 99
TLIM = 9999
EPS = 1e-6
D = 1024
INC = 7328
DFF = 4096


class T:
    _n = 0

    def __init__(self, t, name=None):
        self.t = t
        T._n += 1
        self.id = "t%d" % T._n
        self.name = name or self.id
        self.w = None
        self.r = {}
        self.dsem = None
        self.dval = 0
        self.psum = False

    def __getitem__(self, idx):
        return self.t[idx]


class Rot:
    def __init__(self, tiles):
        self.tiles = tiles
        self.i = 0

    def next(self):
        t = self.tiles[self.i % len(self.tiles)]
        self.i += 1
        return t


class K:
    def __init__(self, nc):
        self.nc = nc
        self.es = contextlib.ExitStack()
        self.eng = {"pe": nc.tensor, "act": nc.scalar, "dve": nc.vector,
                    "pool": nc.gpsimd, "sp": nc.sync}
        self.sem = {}
        self.cnt = {}
        self.seen = {}
        for e in self.eng:
            self.sem[e] = self.es.enter_context(nc.semaphore("s_" + e))
            self.cnt[e] = 0
            self.seen[e] = {}
        self.semof = dict(self.sem)
        self.dtiles = []
        self.dsem_pool = []
        self.ninstr = 0
        self.uid = 0

    def sb(self, shape, dtype, name, stack=None):
        st = stack if stack is not None else self.es
        self.uid += 1
        nb = int(np.prod(shape[1:])) * (2 if dtype == BF16 else 4)
        self.sb_bytes = getattr(self, "sb_bytes", 0) + nb
        self.sb_max = max(getattr(self, "sb_max", 0), self.sb_bytes)
        st.callback(self._free, nb)
        t = st.enter_context(self.nc.sbuf_tensor("%s_%d" % (name, self.uid), list(shape), dtype))
        return T(t, name)

    def ps(self, shape, dtype, name, stack=None):
        st = stack if stack is not None else self.es
        self.uid += 1
        t = st.enter_context(self.nc.psum_tensor("%s_%d" % (name, self.uid), list(shape), dtype))
        tt = T(t, name)
        tt.psum = True
        return tt

    def rot(self, n, shape, dtype, name, stack=None, psum=False):
        f = self.ps if psum else self.sb
        return Rot([f(shape, dtype, "%s%d" % (name, i), stack) for i in range(n)])

    def _free(self, nb):
        self.sb_bytes -= nb

    def _wait(self, e, deps):
        eng = self.eng[e]
        seen = self.seen[e]
        for key, val in sorted(deps, key=lambda d: str(d[0])):
            if key == e and (e in ("pe", "sp") or not SAME_ENG_SYNC):
                continue
            if seen.get(key, 0) >= val:
                continue
            eng.wait_ge(self.semof[key], val)
            seen[key] = val
            self.ninstr += 1

    def _deps(self, reads, writes):
        deps = set()
        for t in reads:
            if t.w:
                deps.add(t.w)
            if t.psum:
                for d in t.r.values():
                    deps.add(d)
        for t in writes:
            if t.w:
                deps.add(t.w)
            for d in t.r.values():
                deps.add(d)
        return deps

    def op(self, e, fn, reads=(), writes=()):
        self._wait(e, self._deps(reads, writes))
        ins = fn(self.eng[e])
        self.cnt[e] += 1
        ins.then_inc(self.sem[e], 1)
        self.ninstr += 1
        me = (e, self.cnt[e])
        for t in reads:
            t.r[e] = me
        for t in writes:
            t.w = me
            t.r = {}
        return ins

    def _dsem(self, tile):
        if tile.dsem is None:
            if self.dsem_pool:
                key, sem, val = self.dsem_pool.pop()
                tile.dkey, tile.dsem, tile.dval = key, sem, val
            else:
                tile.dkey = "d" + tile.id
                tile.dsem = self.es.enter_context(self.nc.semaphore(tile.dkey))
                self.semof[tile.dkey] = tile.dsem
            self.dtiles.append(tile)

    def load(self, q, tile, out_ap, in_ap, **kw):
        self._dsem(tile)
        self._wait(q, self._deps((), (tile,)))
        ins = self.eng[q].dma_start(out=out_ap, in_=in_ap, **kw)
        tile.dval += 16
        ins.then_inc(tile.dsem, 16)
        self.ninstr += 1
        tile.w = (tile.dkey, tile.dval)
        tile.r = {}

    def store(self, q, tile, out_ap, in_ap, **kw):
        self._dsem(tile)
        self._wait(q, self._deps((tile,), ()))
        ins = self.eng[q].dma_start(out=out_ap, in_=in_ap, **kw)
        tile.dval += 16
        ins.then_inc(tile.dsem, 16)
        self.ninstr += 1
        tile.r["dma"] = (tile.dkey, tile.dval)

    def barrier(self, release=True):
        deps = set()
        for e in self.eng:
            if self.cnt[e]:
                deps.add((e, self.cnt[e]))
        for t in self.dtiles:
            if t.dval:
                deps.add((t.dkey, t.dval))
        for e in self.eng:
            self._wait(e, deps)

    def release(self, tiles):
        for t in tiles:
            if t.dsem is not None:
                self.dsem_pool.append((t.dkey, t.dsem, t.dval))
                self.dtiles.remove(t)
                t.dsem = None

    def close(self):
        self.es.close()


class Phase:
    def __init__(self, k):
        self.k = k
        self.st = contextlib.ExitStack()
        self.tiles = []

    def sb(self, shape, dtype, name):
        t = self.k.sb(shape, dtype, name, self.st)
        self.tiles.append(t)
        return t

    def ps(self, shape, dtype, name):
        t = self.k.ps(shape, dtype, name, self.st)
        self.tiles.append(t)
        return t

    def rot(self, n, shape, dtype, name, psum=False):
        f = self.ps if psum else self.sb
        return Rot([f(shape, dtype, "%s%d" % (name, i)) for i in range(n)])

    def end(self):
        self.k.barrier()
        self.k.release(self.tiles)
        self.st.close()


def bc(ap, axis, shape):
    return ap.unsqueeze(axis).broadcast_to(list(shape))


def build(nL, nS, S, debug=False):
    nc = bass.Bass("TRN2", target_bir_lowering=False)
    NT = S // 128
    NQG = S // 512

    def din(name, shape, dt=F32):
        return nc.dram_tensor(name, list(shape), dt, kind="ExternalInput").ap()

    def dscr(name, shape, dt):
        if debug:
            return nc.dram_tensor(name, list(shape), dt, kind="ExternalOutput").ap()
        return nc.dram_tensor(name, list(shape), dt).ap()

    x_in = din("x", [nS, S, D])
    w_in = din("w_in", [nL, D, INC])
    w_qb = din("w_q_b", [nL, 256, 768])
    w_kvb = din("w_kv_b", [nL, 128, 1024])
    w_br = din("w_branch", [nL, 1280, D])
    w_out = din("w_out", [nL, D, D])
    w_ff1 = din("w_ff1", [nL, D, DFF])
    w_ff2 = din("w_ff2", [nL, DFF, D])
    g_attn = din("g_attn", [nL, 128, 8])
    g_mlp = din("g_mlp", [nL, 128, 8])
    g_qa = din("g_qa", [nL, 128, 2])
    g_kva = din("g_kva", [nL, 128, 1])
    g_aqk = din("g_aqk", [nL, 128, 128])
    g_bqk = din("g_bqk", [nL, 128, 192])
    ropeA = din("ropeA", [128, NT, 16])
    ropeM = din("ropeM", [128, NT, 32])
    consts = din("consts", [128, 5, 128])
    out = nc.dram_tensor("out", [nS, S, D], F32, kind="ExternalOutput").ap()

    Wb_in = dscr("Wb_in", [nL, D, INC], BF16)
    Wb_qb = dscr("Wb_qb", [nL, 256, 768], BF16)
    Wb_kvb = dscr("Wb_kvb", [nL, 128, 1024], BF16)
    Wb_br = dscr("Wb_br", [nL, 1280, D], BF16)
    Wb_out = dscr("Wb_out", [nL, D, D], BF16)
    Wb_ff1 = dscr("Wb_ff1", [nL, D, DFF], BF16)
    Wb_ff2 = dscr("Wb_ff2", [nL, DFF, D], BF16)
    xres = dscr("xres", [nS, S, D], F32)
    xmid = dscr("xmid", [nS, S, D], F32)
    AqT = dscr("AqT", [768, S], BF16)
    AkT = dscr("AkT", [768, S], BF16)
    Av = dscr("Av", [S, 780], BF16)
    BqT = dscr("BqT", [768, S], BF16)
    BkT = dscr("BkT", [768, S], BF16)
    Bv = dscr("Bv", [S, 520], BF16)
    CqT = dscr("CqT", [512, S], BF16)
    CkT = dscr("CkT", [512, S], BF16)
    Cv = dscr("Cv", [S, 512], BF16)
    Gt = dscr("Gt", [S, 3072], BF16)
    accA = dscr("accA", [3, S, 260], F32)
    oB = dscr("oB", [S, 512], BF16)
    oC = dscr("oC", [S, 512], BF16)

    k = K(nc)
    cf = k.sb([128, 5, 128], F32, "cf")
    cb = k.sb([128, 5, 128], BF16, "cb")
    k.load("sp", cf, cf[:], consts)
    k.op("dve", lambda e: e.tensor_copy(out=cb[:], in_=cf[:]), reads=[cf], writes=[cb])
    identb = cb[:, 0, :]
    identf = cf[:, 0, :]
    m_ge = cb[:, 1, :]
    m_gt = cb[:, 2, :]
    m_le = cb[:, 3, :]
    negU = cb[:, 4, :]
    onesb = k.sb([128, 1], BF16, "onesb")
    k.op("dve", lambda e: e.memset(onesb[:], 1.0), writes=[onesb])
    rA = k.sb([128, NT, 16], F32, "rA")
    rM = k.sb([128, NT, 32], F32, "rM")
    k.load("sp", rA, rA[:], ropeA)
    k.load("sp", rM, rM[:], ropeM)

    neghalf = k.sb([128, 24], F32, "neghalf")
    k.op("dve", lambda e: e.memset(neghalf[:], -0.5), writes=[neghalf])

    def rsq(rs_ap, ss_ap, n, inv_n, rs_t, ss_t):
        k.op("dve", lambda e: e.tensor_scalar(out=rs_ap, in0=ss_ap, scalar1=inv_n, scalar2=EPS, op0=ALU.mult, op1=ALU.add),
             reads=[ss_t], writes=[rs_t])
        k.op("pool", lambda e: e.tensor_tensor(out=rs_ap, in0=rs_ap, in1=neghalf[:, 0:n], op=ALU.pow),
             reads=[rs_t, neghalf], writes=[rs_t])

    ceng = Rot(["dve", "act", "pool"])

    def cast(e, out_ap, in_ap, reads, writes, scale=None):
        if e == "act":
            if scale is None:
                k.op("act", lambda g: g.activation(out=out_ap, in_=in_ap, func=AF.Copy), reads=reads, writes=writes)
            else:
                k.op("act", lambda g: g.activation(out=out_ap, in_=in_ap, func=AF.Copy, scale=scale),
                     reads=reads, writes=writes)
        else:
            if scale is None:
                k.op(e, lambda g: g.tensor_copy(out=out_ap, in_=in_ap), reads=reads, writes=writes)
            else:
                k.op(e, lambda g: g.tensor_scalar(out=out_ap, in0=in_ap, scalar1=scale, scalar2=None, op0=ALU.mult),
                     reads=reads, writes=writes)

    def prep_weights():
        ph = Phase(k)
        gcol = ph.sb([128, nL, 19], F32, "gcol")
        for l in range(nL):
            k.load("sp", gcol, gcol[:, l, 0:8], g_attn[l])
            k.load("sp", gcol, gcol[:, l, 8:16], g_mlp[l])
            k.load("sp", gcol, gcol[:, l, 16:18], g_qa[l])
            k.load("sp", gcol, gcol[:, l, 18:19], g_kva[l])
        CB = 2048
        wf = ph.rot(4, [128, CB], F32, "wf")
        wb = ph.rot(3, [128, CB], BF16, "wb")
        items = []
        for l in range(nL):
            jobs = [(w_in[l], Wb_in[l], D, INC, 0), (w_qb[l], Wb_qb[l], 256, 768, 16),
                    (w_kvb[l], Wb_kvb[l], 128, 1024, 18), (w_br[l], Wb_br[l], 1280, D, None),
                    (w_out[l], Wb_out[l], D, D, None), (w_ff1[l], Wb_ff1[l], D, DFF, 8),
                    (w_ff2[l], Wb_ff2[l], DFF, D, None)]
            for src, dst, R, C, gc in jobs:
                for c in range(R // 128):
                    for c0 in range(0, C, CB):
                        items.append((l, src, dst, c, c0, min(CB, C - c0), gc))
        loaded = {}

        def ld(i):
            l, src, dst, c, c0, n, gc = items[i]
            f = wf.next()
            k.load("sp", f, f[:, 0:n], src[c * 128:(c + 1) * 128, c0:c0 + n])
            loaded[i] = f

        for i in range(min(2, len(items))):
            ld(i)
        for i in range(len(items)):
            if i + 2 < len(items):
                ld(i + 2)
            l, src, dst, c, c0, n, gc = items[i]
            f = loaded.pop(i)
            b = wb.next()
            sc = None if gc is None else gcol[:, l, gc + c:gc + c + 1]
            e = ceng.next()
            rd = [f] if gc is None else [f, gcol]
            cast(e, b[:, 0:n], f[:, 0:n], rd, [b], scale=sc)
            k.store("sp", b, dst[c * 128:(c + 1) * 128, c0:c0 + n], b[:, 0:n])
        ph.end()

    def norm_transpose(ph, r, xsrc_rows, xt, hT_out, defer=False):
        ss = r["ss"].next()
        rstd = r["rstd"].next()
        junk = r["junk"]
        xn = r["xn"].next()
        k.op("act", lambda e: e.activation(out=junk[:], in_=xt[:], func=AF.Square, accum_out=ss[:]),
             reads=[xt], writes=[junk, ss])
        rsq(rstd[:], ss[:], 1, 1.0 / D, rstd, ss)
        k.op("act", lambda e: e.activation(out=xn[:], in_=xt[:], func=AF.Copy, scale=rstd[:]),
             reads=[xt, rstd], writes=[xn])
        hTt = r["hTtile"]

        def partB():
            pT = r["pT"].next()
            for c in range(8):
                k.op("pe", lambda e, c=c: e.transpose(out=pT[:, c, :], in_=xn[:, c * 128:(c + 1) * 128], identity=identb),
                     reads=[xn, cb], writes=[pT])
            k.op("dve", lambda e: e.tensor_copy(out=hT_out, in_=pT[:]), reads=[pT], writes=[hTt])
        if defer:
            return partB
        partB()

    def norm_res(ph, n=2):
        return {"ss": ph.rot(n, [128, 1], F32, "ss"), "rstd": ph.rot(n, [128, 1], F32, "rstd"),
                "junk": ph.sb([128, D], BF16, "junk"), "xn": ph.rot(2, [128, D], BF16, "xn"),
                "pT": ph.rot(1, [128, 8, 128], BF16, "pT", psum=True)}

    def rms_small(src_ap, nh, hd, reads_t, sq_t, ss_t, rs_t, extra_add=None):
        k.op("act", lambda e: e.activation(out=sq_t[:, 0:nh * hd].rearrange("p (h d) -> p h d", d=hd),
                                           in_=src_ap, func=AF.Square), reads=reads_t, writes=[sq_t])
        k.op("dve", lambda e: e.tensor_reduce(out=ss_t[:, 0:nh], in_=sq_t[:, 0:nh * hd].rearrange("p (h d) -> p h d", d=hd),
                                              axis=AX.X, op=ALU.add), reads=[sq_t], writes=[ss_t])

    def interleave(gens):
        gens = [g for g in gens if g is not None]
        while gens:
            for g in list(gens):
                try:
                    next(g)
                except StopIteration:
                    gens.remove(g)

    def phase1(l, s, xsrc, pas):
        ph = Phase(k)
        if pas == 0:
            c_lo, c_hi = 0, 3744
        else:
            c_lo, c_hi = 3744, INC
        NW = c_hi - c_lo
        W = ph.sb([128, 8, NW], BF16, "W")
        for c in range(8):
            k.load("sp", W, W[:, c, :], Wb_in[l][c * 128:(c + 1) * 128, c_lo:c_hi])
        r = norm_res(ph, 3)
        xts = ph.rot(3, [128, D], F32, "xt")
        hTs = ph.rot(2, [128, 8, 512], BF16, "hT4") if pas == 0 else ph.rot(3, [128, 8, 128], BF16, "hT")
        pj = ph.rot(2 if pas == 0 else 6, [128, 512], F32, "pj", psum=True)
        hT_of = {}
        x_of = {}
        if pas == 0:
            gaq = ph.sb([128, 128], F32, "gaq")
            gbq = ph.sb([128, 192], F32, "gbq")
            k.load("sp", gaq, gaq[:], g_aqk[l])
            k.load("sp", gbq, gbq[:], g_bqk[l])
            wqb = ph.sb([128, 2, 768], BF16, "wqb")
            wkvb = ph.sb([128, 1024], BF16, "wkvb")
            k.load("sp", wqb, wqb[:], Wb_qb[l].rearrange("(c p) n -> p c n", p=128))
            k.load("sp", wkvb, wkvb[:], Wb_kvb[l])
            qks = ph.rot(2, [128, 1536], F32, "qk")
            lats = ph.rot(2, [128, 416], F32, "lat")
            sqA = ph.sb([128, 1536], BF16, "sqA")
            ss24 = ph.sb([128, 24], F32, "ss24")
            rs24 = ph.sb([128, 24], F32, "rs24")
            qkb = ph.rot(2, [128, 1536], BF16, "qkb")
            rtA = [ph.sb([128, 24, 8], F32, "rtA%d" % i) for i in range(4)]
            x16 = ph.sb([128, 24, 16], F32, "x16")
            x32 = ph.sb([128, 8, 32], F32, "x32")
            rtK = [ph.sb([128, 1, 16], F32, "rtK%d" % i) for i in range(4)]
            sqK = ph.sb([128, 512], BF16, "sqK")
            ss8k = ph.sb([128, 8], F32, "ss8k")
            rs8k = ph.sb([128, 8], F32, "rs8k")
            rtM = [ph.sb([128, 8, 16], F32, "rtM%d" % i) for i in range(4)]
            vAs = ph.rot(2, [128, 12, 65], BF16, "vA")
            for t in vAs.tiles:
                k.op("dve", lambda e, t=t: e.memset(t[:], 1.0), writes=[t])
            pTA = ph.ps([128, 16, 128], BF16, "pTA")
            ATs = ph.rot(2, [128, 12, 128], BF16, "AT")
            sqM = ph.sb([128, 768], BF16, "sqM")
            ssl = ph.sb([128, 3], F32, "ssl")
            rsl = ph.sb([128, 3], F32, "rsl")
            latn = ph.sb([128, 384], BF16, "latn")
            pT2 = ph.ps([128, 8, 128], BF16, "pT2")
            latT = ph.sb([128, 3, 128], BF16, "latT")
            pM = ph.ps([128, 1024], F32, "pM")
            bq = ph.sb([128, 768], F32, "bq")
            bk = ph.sb([128, 8, 96], F32, "bk")
            ss8 = ph.sb([128, 8], F32, "ss8")
            rs8 = ph.sb([128, 8], F32, "rs8")
            bqbs = ph.rot(2, [128, 8, 96], BF16, "bqb")
            bkbs = ph.rot(2, [128, 8, 96], BF16, "bkb")
            tails = {}
            krg = ph.sb([128, 32], F32, "krg")
            krr = ph.sb([128, 32], F32, "krr")
            vBs = ph.rot(2, [128, 8, 65], BF16, "vB")
            for t in vBs.tiles:
                k.op("dve", lambda e, t=t: e.memset(t[:], 1.0), writes=[t])
            BTq = ph.rot(2, [96, 8, 128], BF16, "BTq")
            BTk = ph.rot(2, [96, 8, 128], BF16, "BTk")
            CTs = ph.rot(1, [128, 8, 512], BF16, "CT")
            qk_of, lat_of = {}, {}
        else:
            cvs = ph.rot(2, [128, 512], BF16, "cv")
            gts = ph.rot(2, [128, 3072], BF16, "gt")

        def proj_chunk(hTo, col0, n):
            hT, off = hTo
            p = pj.next()
            for c in range(8):
                k.op("pe", lambda e, c=c: e.matmul(out=p[:, 0:n], lhsT=hT[:, c, off:off + 128], rhs=W[:, c, col0:col0 + n],
                                                   start=(c == 0), stop=(c == 7)), reads=[hT, W], writes=[p])
            return p

        def rope(src, dst, nh, off, half, cs, sn, reads_t, dst_t, tmp):
            x1 = src[:, :, off:off + half]
            x2 = src[:, :, off + half:off + 2 * half]
            cB = bc(cs, 1, [128, nh, half])
            sB = bc(sn, 1, [128, nh, half])
            t1, t2, t3, t4 = [t[:, 0:nh, 0:half] for t in tmp]
            k.op("pool", lambda e: e.tensor_tensor(out=t1, in0=x1, in1=cB, op=ALU.mult), reads=reads_t, writes=[tmp[0]])
            k.op("pool", lambda e: e.tensor_tensor(out=t2, in0=x2, in1=sB, op=ALU.mult), reads=reads_t, writes=[tmp[1]])
            yield
            k.op("pool", lambda e: e.tensor_tensor(out=t3, in0=x2, in1=cB, op=ALU.mult), reads=reads_t, writes=[tmp[2]])
            k.op("pool", lambda e: e.tensor_tensor(out=t4, in0=x1, in1=sB, op=ALU.mult), reads=reads_t, writes=[tmp[3]])
            yield
            k.op("dve", lambda e: e.tensor_tensor(out=dst[:, :, off:off + half], in0=t1, in1=t2, op=ALU.subtract),
                 reads=[tmp[0], tmp[1]], writes=[dst_t])
            k.op("dve", lambda e: e.tensor_tensor(out=dst[:, :, off + half:off + 2 * half], in0=t3, in1=t4, op=ALU.add),
                 reads=[tmp[2], tmp[3]], writes=[dst_t])
            yield

        def Lx(tt):
            t0 = tt * 128
            xt = xts.next()
            k.load("sp", xt, xt[:], xsrc[t0:t0 + 128, :])
            x_of[tt] = xt

        def Nn(tt):
            xt = x_of.pop(tt)
            if pas == 0:
                if tt % 4 == 0:
                    hT_of["cur"] = hTs.next()
                hT = hT_of["cur"]
                off = (tt % 4) * 128
            else:
                hT = hTs.next()
                off = 0
            hT_of[tt] = (hT, off)
            r["hTtile"] = hT
            return norm_transpose(ph, r, None, xt, hT[:, :, off:off + 128], defer=True)

        def P0(tt):
            t0 = tt * 128
            hT = hT_of[tt]
            qk = qks.next()
            lat = lats.next()
            qk_of[tt] = qk
            lat_of[tt] = lat
            for j in range(3):
                p = proj_chunk(hT, j * 512, 512)
                cast("act", qk[:, j * 512:(j + 1) * 512], p[:, :], [p], [qk])
                yield
            p = proj_chunk(hT, 2304, 416)
            cast("dve", lat[:, :], p[:, 0:416], [p], [lat])
            yield
            vA = vAs.next()
            p = proj_chunk(hT, 1536, 512)
            cast("act", vA[:, 0:8, 0:64], p[:, :].rearrange("p (h d) -> p h d", d=64), [p], [vA])
            yield
            p = proj_chunk(hT, 2048, 256)
            cast("dve", vA[:, 8:12, 0:64], p[:, 0:256].rearrange("p (h d) -> p h d", d=64), [p], [vA])
            k.store("sp", vA, Av[t0:t0 + 128, :], vA[:].rearrange("p h d -> p (h d)"))
            yield

        def Cg(g):
            hT = hT_of[4 * g][0]
            CT = CTs.next()
            for j in range(8):
                col0 = 2720 + j * 128
                p = pj.next()
                for c in range(8):
                    k.op("pe", lambda e, c=c: e.matmul(out=p[:, 0:512], lhsT=W[:, c, col0:col0 + 128], rhs=hT[:, c, 0:512],
                                                       start=(c == 0), stop=(c == 7)), reads=[hT, W], writes=[p])
                if j < 4:
                    k.op("act", lambda e, p=p, j=j: e.activation(out=CT[:, j, :], in_=p[:, :], func=AF.Copy, scale=0.125),
                         reads=[p], writes=[CT])
                else:
                    cast("dve", CT[:, j, :], p[:, :], [p], [CT])
                yield
            k.store("sp", CT, CqT.rearrange("(j p) s -> p j s", p=128)[:, :, g * 512:(g + 1) * 512], CT[:, 0:4, :])
            k.store("sp", CT, CkT.rearrange("(j p) s -> p j s", p=128)[:, :, g * 512:(g + 1) * 512], CT[:, 4:8, :])
            yield

        def P1(tt):
            t0 = tt * 128
            hT = hT_of[tt]
            cv = cvs.next()
            p = proj_chunk(hT, 0, 512)
            cast("dve", cv[:], p[:], [p], [cv])
            k.store("sp", cv, Cv[t0:t0 + 128, :], cv[:])
            gt = gts.next()
            for j in range(6):
                p = proj_chunk(hT, 512 + j * 512, 512)
                k.op("act", lambda e, p=p, j=j: e.activation(out=gt[:, j * 512:(j + 1) * 512], in_=p[:], func=AF.Sigmoid),
                     reads=[p], writes=[gt])
            k.store("sp", gt, Gt[t0:t0 + 128, :], gt[:])

        def YA(tt):
            t0 = tt * 128
            qk = qk_of[tt]
            qk3 = qk[:, :].rearrange("p (h d) -> p h d", d=64)
            rms_small(qk3, 24, 64, [qk], sqA, ss24, rs24)
            yield
            rsq(rs24[:], ss24[:], 24, 1.0 / 64, rs24, ss24)
            yield
            k.op("dve", lambda e: e.tensor_tensor(out=qk3, in0=qk3, in1=bc(rs24[:, :], 2, [128, 24, 64]), op=ALU.mult),
                 reads=[qk, rs24], writes=[qk])
            yield
            qk4 = qk[:, :].rearrange("p (a h d) -> p a h d", a=2, d=64)
            g4 = bc(gaq[:, :].rearrange("p (a d) -> p a d", a=2), 2, [128, 2, 12, 64])
            qb = qkb.next()
            qb4 = qb[:, :].rearrange("p (a h d) -> p a h d", a=2, d=64)
            k.op("dve", lambda e: e.tensor_tensor(out=qb4, in0=qk4, in1=g4, op=ALU.mult), reads=[qk, gaq], writes=[qb])
            x16v = x16[:, :, :].rearrange("p (a h) d -> p a h d", a=2)
            k.op("pool", lambda e: e.tensor_tensor(out=x16v, in0=qk4[:, :, :, 0:16], in1=g4[:, :, :, 0:16], op=ALU.mult),
                 reads=[qk, gaq], writes=[x16])
            yield
            qb3 = qb[:, :].rearrange("p (h d) -> p h d", d=64)
            yield from rope(x16[:, :, :], qb3, 24, 0, 8, rA[:, tt, 0:8], rA[:, tt, 8:16], [x16, rA], qb, rtA)

            def tailA():
                for j in range(12):
                    k.op("pe", lambda e, j=j: e.transpose(out=pTA[:, j, :], in_=qb[:, j * 128:(j + 1) * 128], identity=identb),
                         reads=[qb, cb], writes=[pTA])
                AT = ATs.next()
                cast("act", AT[:], pTA[:, 0:12, :], [pTA], [AT])
                k.store("sp", AT, AqT.rearrange("(j p) s -> p j s", p=128)[:, :, t0:t0 + 128], AT[:, 0:6, :])
                k.store("sp", AT, AkT.rearrange("(j p) s -> p j s", p=128)[:, :, t0:t0 + 128], AT[:, 6:12, :])
            tails.setdefault(tt, []).append(tailA)

        def YM(tt):
            t0 = tt * 128
            lat = lat_of[tt]
            k.op("act", lambda e: e.activation(out=sqM[:, 0:256], in_=lat[:, 0:256], func=AF.Square, accum_out=ssl[:, 0:1]),
                 reads=[lat], writes=[sqM, ssl])
            k.op("act", lambda e: e.activation(out=sqM[:, 256:384], in_=lat[:, 256:384], func=AF.Square, accum_out=ssl[:, 1:2]),
                 reads=[lat], writes=[sqM, ssl])
            k.op("act", lambda e: e.activation(out=sqM[:, 384:416], in_=lat[:, 384:416], func=AF.Square, accum_out=ssl[:, 2:3]),
                 reads=[lat], writes=[sqM, ssl])
            yield
            k.op("dve", lambda e: e.tensor_scalar(out=rsl[:, 0:1], in0=ssl[:, 0:1], scalar1=1.0 / 256, scalar2=EPS,
                                                  op0=ALU.mult, op1=ALU.add), reads=[ssl], writes=[rsl])
            k.op("dve", lambda e: e.tensor_scalar(out=rsl[:, 1:2], in0=ssl[:, 1:2], scalar1=1.0 / 128, scalar2=EPS,
                                                  op0=ALU.mult, op1=ALU.add), reads=[ssl], writes=[rsl])
            k.op("pool", lambda e: e.tensor_tensor(out=rsl[:, 0:2], in0=rsl[:, 0:2], in1=neghalf[:, 0:2], op=ALU.pow),
                 reads=[rsl, neghalf], writes=[rsl])
            yield
            cast("dve", latn[:, 0:256], lat[:, 0:256], [lat, rsl], [latn], scale=rsl[:, 0:1])
            cast("dve", latn[:, 256:384], lat[:, 256:384], [lat, rsl], [latn], scale=rsl[:, 1:2])
            yield
            for j in range(3):
                k.op("pe", lambda e, j=j: e.transpose(out=pT2[:, j, :], in_=latn[:, j * 128:(j + 1) * 128], identity=identb),
                     reads=[latn, cb], writes=[pT2])
            cast("dve", latT[:], pT2[:, 0:3, :], [pT2], [latT])
            yield
            for (c0, n) in ((0, 512), (512, 256)):
                for c in range(2):
                    k.op("pe", lambda e, c=c, c0=c0, n=n: e.matmul(out=pM[:, c0:c0 + n], lhsT=latT[:, c, :],
                                                                  rhs=wqb[:, c, c0:c0 + n], start=(c == 0), stop=(c == 1)),
                         reads=[latT, wqb], writes=[pM])
            cast("act", bq[:, :], pM[:, 0:768], [pM], [bq])
            yield
            for c0 in (0, 512):
                k.op("pe", lambda e, c0=c0: e.matmul(out=pM[:, c0:c0 + 512], lhsT=latT[:, 2, :], rhs=wkvb[:, c0:c0 + 512],
                                                    start=True, stop=True), reads=[latT, wkvb], writes=[pM])
            vB = vBs.next()
            pM3 = pM[:, :].rearrange("p (h d) -> p h d", d=128)
            cast("act", vB[:, :, 0:64], pM3[:, :, 64:128], [pM], [vB])
            cast("dve", bk[:, :, 0:64], pM3[:, :, 0:64], [pM], [bk])
            k.store("sp", vB, Bv[t0:t0 + 128, :], vB[:].rearrange("p h d -> p (h d)"))
            yield
            yield from rr2(Qc(tt), Kc(tt, lat))

        def rr2(g1, g2):
            gens = [g1, g2]
            while gens:
                for g in list(gens):
                    try:
                        next(g)
                        yield
                    except StopIteration:
                        gens.remove(g)

        def Qc(tt):
            t0 = tt * 128
            bqb = bqbs.next()
            bq3 = bq[:, :].rearrange("p (h d) -> p h d", d=96)
            rms_small(bq3, 8, 96, [bq], sqM, ss8, rs8)
            yield
            rsq(rs8[:], ss8[:], 8, 1.0 / 96, rs8, ss8)
            yield
            k.op("dve", lambda e: e.tensor_tensor(out=bq3, in0=bq3, in1=bc(rs8[:, :], 2, [128, 8, 96]), op=ALU.mult),
                 reads=[bq, rs8], writes=[bq])
            yield
            k.op("dve", lambda e: e.tensor_tensor(out=bqb[:, :, 0:64], in0=bq3[:, :, 0:64],
                                                  in1=bc(gbq[:, 0:64], 1, [128, 8, 64]), op=ALU.mult),
                 reads=[bq, gbq], writes=[bqb])
            k.op("pool", lambda e: e.tensor_tensor(out=x32[:, :, :], in0=bq3[:, :, 64:96],
                                                   in1=bc(gbq[:, 64:96], 1, [128, 8, 32]), op=ALU.mult),
                 reads=[bq, gbq], writes=[x32])
            yield
            yield from rope(x32[:, :, :], bqb[:, :, 64:96], 8, 0, 16, rM[:, tt, 0:16], rM[:, tt, 16:32], [x32, rM], bqb, rtM)

            def tailQ():
                for h in range(8):
                    k.op("pe", lambda e, h=h: e.transpose(out=pT2[0:96, h, :], in_=bqb[:, h, :], identity=identb),
                         reads=[bqb, cb], writes=[pT2])
                Bq = BTq.next()
                cast("act", Bq[:], pT2[0:96, :, :], [pT2], [Bq])
                k.store("sp", Bq, BqT.rearrange("(h f) s -> f h s", f=96)[:, :, t0:t0 + 128], Bq[:])
            tails.setdefault(tt, []).append(tailQ)

        def Kc(tt, lat):
            t0 = tt * 128
            bkb = bkbs.next()
            rms_small(bk[:, :, 0:64], 8, 64, [bk], sqK, ss8k, rs8k)
            yield
            k.op("dve", lambda e: e.tensor_scalar(out=ss8k[:], in0=ss8k[:], scalar1=ssl[:, 2:3], scalar2=None, op0=ALU.add),
                 reads=[ss8k, ssl], writes=[ss8k])
            rsq(rs8k[:], ss8k[:], 8, 1.0 / 96, rs8k, ss8k)
            yield
            k.op("dve", lambda e: e.tensor_tensor(out=krg[:, :], in0=lat[:, 384:416], in1=gbq[:, 160:192], op=ALU.mult),
                 reads=[lat, gbq], writes=[krg])
            yield from rope(krg[:, :].unsqueeze(1), krr[:, :].unsqueeze(1), 1, 0, 16, rM[:, tt, 0:16], rM[:, tt, 16:32],
                            [krg, rM], krr, rtK)
            k.op("dve", lambda e: e.tensor_tensor(out=bk[:, :, 0:64], in0=bk[:, :, 0:64],
                                                  in1=bc(gbq[:, 96:160], 1, [128, 8, 64]), op=ALU.mult),
                 reads=[bk, gbq], writes=[bk])
            yield
            k.op("dve", lambda e: e.tensor_tensor(out=bkb[:, :, 0:64], in0=bk[:, :, 0:64],
                                                  in1=bc(rs8k[:, :], 2, [128, 8, 64]), op=ALU.mult),
                 reads=[bk, rs8k], writes=[bkb])
            k.op("dve", lambda e: e.tensor_tensor(out=bkb[:, :, 64:96], in0=bc(krr[:, :], 1, [128, 8, 32]),
                                                  in1=bc(rs8k[:, :], 2, [128, 8, 32]), op=ALU.mult),
                 reads=[krr, rs8k], writes=[bkb])
            yield

            def tailK():
                for h in range(8):
                    k.op("pe", lambda e, h=h: e.transpose(out=pT2[0:96, h, :], in_=bkb[:, h, :], identity=identb),
                         reads=[bkb, cb], writes=[pT2])
                Bk = BTk.next()
                cast("act", Bk[:], pT2[0:96, :, :], [pT2], [Bk])
                k.store("sp", Bk, BkT.rearrange("(h f) s -> f h s", f=96)[:, :, t0:t0 + 128], Bk[:])
            tails.setdefault(tt, []).append(tailK)

        nt = min(NT, TLIM)
        for tt in range(min(2, nt)):
            Lx(tt)
        Nn(0)()
        if nt > 1:
            if nt > 2:
                Lx(2)
            Nn(1)()
        if pas == 0:
            interleave([P0(0)])
        for tt in range(nt):
            if tt + 3 < nt:
                Lx(tt + 3)
            nB = None
            if tt + 2 < nt:
                nB = Nn(tt + 2)
            if pas == 0:
                def P0n(tt=tt, nB=nB):
                    if tt + 1 < nt:
                        yield from P0(tt + 1)
                    if nB:
                        nB()
                    yield
                interleave([P0n(), YA(tt), YM(tt), Cg(tt // 4) if tt % 4 == 3 else None])
                for f in tails.pop(tt - 1, []):
                    f()
            else:
                P1(tt)
                if nB:
                    nB()
        if pas == 0:
            for f in tails.pop(nt - 1, []):
                f()
        ph.end()

    def phase2():
        ph = Phase(k)
        qn = [ph.sb([64, S], BF16, "qn%d" % i) for i in range(4)]
        kn = [ph.sb([64, S], BF16, "kn%d" % i) for i in range(4)]
        qd = [ph.sb([64, S], BF16, "qd%d" % i) for i in range(4)]
        kd = [ph.sb([64, S], BF16, "kd%d" % i) for i in range(4)]
        vA = ph.sb([128, NT, 4, 65], BF16, "vAall")
        mb = ph.sb([128, 2, 256], BF16, "mband")
        for hi in range(2):
            k.op("dve", lambda e, hi=hi: e.tensor_copy(out=mb[:, hi, 0:128], in_=m_ge), reads=[cb], writes=[mb])
            k.op("dve", lambda e, hi=hi: e.tensor_copy(out=mb[:, hi, 128:256], in_=m_le), reads=[cb], writes=[mb])
        S2 = ph.rot(6, [128, 2, 256], F32, "S2", psum=True)
        Pt = [ph.rot(5, [128, 2, 256], BF16, "Pt%d_" % hp) for hp in range(2)]
        O4 = ph.rot(2, [128, 512], F32, "O4", psum=True)
        osb = ph.rot(2, [128, 260], F32, "osb")
        for g, d in enumerate((1, 4, 16)):
            L = S // d
            nb = L // 128
            for hs in range(4):
                h = g * 4 + hs
                k.load("sp", qn[hs], qn[hs][:], AqT[h * 64:(h + 1) * 64, :])
                k.load("sp", kn[hs], kn[hs][:], AkT[h * 64:(h + 1) * 64, :])
                if d > 1:
                    k.op("pool", lambda e, hs=hs: e.tensor_copy(out=qd[hs][:, :].rearrange("p (r u) -> p r u", r=d),
                                                                in_=qn[hs][:, :].rearrange("p (u r) -> p r u", r=d)),
                         reads=[qn[hs]], writes=[qd[hs]])
                    k.op("dve", lambda e, hs=hs: e.tensor_copy(out=kd[hs][:, :].rearrange("p (r u) -> p r u", r=d),
                                                               in_=kn[hs][:, :].rearrange("p (u r) -> p r u", r=d)),
                         reads=[kn[hs]], writes=[kd[hs]])
            Q = qd if d > 1 else qn
            Kk = kd if d > 1 else kn
            for r in range(d):
                src = Av.rearrange("(n p r) c -> r p n c", p=128, r=d)[r][:, :, g * 260:(g + 1) * 260]
                k.load("sp", vA, vA[:, r * nb:(r + 1) * nb, :, :].rearrange("p n h d -> p n (h d)"), src)
            blocks = [(r, n) for r in range(d) for n in range(nb)]
            Pof = {}

            def QE(i):
                r, n = blocks[i]
                col0 = r * L + n * 128
                nq = 256 if n < nb - 1 else 128
                cur = []
                for hp in range(2):
                    s2 = S2.next()
                    for hi in range(2):
                        hs = hp * 2 + hi
                        k.op("pe", lambda e, hs=hs, hi=hi, s2=s2: e.matmul(
                            out=s2[:, hi, 0:nq], lhsT=Kk[hs][:, col0:col0 + 128], rhs=Q[hs][:, col0:col0 + nq],
                            start=True, stop=True), reads=[Kk[hs], Q[hs]], writes=[s2])
                    pt = Pt[hp].next()
                    k.op("act", lambda e, s2=s2, pt=pt: e.activation(out=pt[:, :, 0:nq], in_=s2[:, :, 0:nq], func=AF.Exp,
                                                                     scale=0.125), reads=[s2], writes=[pt])
                    k.op("pool", lambda e, pt=pt: e.tensor_tensor(out=pt[:, :, 0:nq], in0=pt[:, :, 0:nq],
                                                                  in1=mb[:, :, 0:nq], op=ALU.mult),
                         reads=[pt, mb], writes=[pt])
                    cur.append(pt)
                Pof[i] = cur

            def PVs(i):
                r, n = blocks[i]
                b = r * nb + n
                curP = Pof[i]
                prevP = Pof.get(i - 1) if n > 0 else None
                o4t = O4.next()
                o4 = o4t[:, 0:260].rearrange("p (h d) -> p h d", d=65)
                for hs in range(4):
                    hp, hi = hs // 2, hs % 2
                    if n > 0:
                        k.op("pe", lambda e, hs=hs, hp=hp, hi=hi: e.matmul(
                            out=o4[:, hs, :], lhsT=prevP[hp][:, hi, 128:256], rhs=vA[:, b - 1, hs, :],
                            start=True, stop=False), reads=[prevP[hp], vA], writes=[o4t])
                    k.op("pe", lambda e, hs=hs, hp=hp, hi=hi: e.matmul(
                        out=o4[:, hs, :], lhsT=curP[hp][:, hi, 0:128], rhs=vA[:, b, hs, :],
                        start=(n == 0), stop=True), reads=[curP[hp], vA], writes=[o4t])
                ob = osb.next()
                cast("act", ob[:, :], o4t[:, 0:260], [o4t], [ob])
                dst = accA[g].rearrange("(n p r) c -> r n p c", p=128, r=d)[r, n]
                k.store("sp", ob, dst, ob[:, :])
                Pof.pop(i - 1, None)

            QE(0)
            if len(blocks) > 1:
                QE(1)
            for i in range(len(blocks)):
                if i + 2 < len(blocks):
                    QE(i + 2)
                PVs(i)
        ph.end()

    def phase3():
        ph = Phase(k)
        qTs = ph.rot(2, [96, S], BF16, "bqT")
        kTs = ph.rot(2, [96, S], BF16, "bkT")
        vB = ph.sb([128, NT, 520], BF16, "vBall")
        k.load("sp", vB, vB[:], Bv.rearrange("(t p) c -> p t c", p=128))
        oB_sb = ph.sb([128, NT, 512], BF16, "oB_sb")
        Sb = ph.rot(4, [128, 512], F32, "Sb", psum=True)
        Pts = ph.rot(5, [128, 512], BF16, "Ptb")
        Ob = ph.rot(2, [65, 512], F32, "Ob", psum=True)
        Osb = ph.rot(2, [65, 512], F32, "Osb")
        pTo = ph.rot(1, [128, 512], F32, "pTo", psum=True)
        rden = ph.rot(2, [128, 4], F32, "rden")
        sc = 96 ** -0.5
        heads = {}

        def load_head(h):
            qT = qTs.next()
            kT = kTs.next()
            k.load("sp", qT, qT[:], BqT[h * 96:(h + 1) * 96, :])
            k.load("sp", kT, kT[:], BkT[h * 96:(h + 1) * 96, :])
            heads[h] = (qT, kT)

        units = []
        for h in range(8):
            for qg in range(NQG):
                for kb in range(4 * qg + 4):
                    units.append({"h": h, "qg": qg, "kb": kb, "c0": max(0, kb - 4 * qg) * 128, "last": kb == 4 * qg + 3})
        grp = {}

        def A(u):
            h, qg, kb, c0 = u["h"], u["qg"], u["kb"], u["c0"]
            if h not in heads:
                load_head(h)
            if kb == 0 and qg == 0 and h + 1 < 8 and (h + 1) not in heads:
                load_head(h + 1)
            qT, kT = heads[h]
            sb_ = Sb.next()
            u["sb"] = sb_
            k.op("pe", lambda e: e.matmul(out=sb_[:, c0:512], lhsT=kT[:, kb * 128:(kb + 1) * 128],
                                          rhs=qT[:, qg * 512 + c0:(qg + 1) * 512], start=True, stop=True),
                 reads=[kT, qT], writes=[sb_])

        def B(u):
            c0, sb_ = u["c0"], u["sb"]
            pt = Pts.next()
            u["pt"] = pt
            k.op("act", lambda e: e.activation(out=pt[:, c0:512], in_=sb_[:, c0:512], func=AF.Exp, scale=sc),
                 reads=[sb_], writes=[pt])
            if u["kb"] >= 4 * u["qg"]:
                k.op("pool", lambda e: e.tensor_tensor(out=pt[:, c0:c0 + 128], in0=pt[:, c0:c0 + 128], in1=m_ge,
                                                       op=ALU.mult), reads=[pt, cb], writes=[pt])

        def F(u):
            h, qg, kb, c0, pt = u["h"], u["qg"], u["kb"], u["c0"], u["pt"]
            if kb == 0:
                grp[(h, qg)] = Ob.next()
            ob = grp[(h, qg)]
            k.op("pe", lambda e: e.matmul(out=ob[:, c0:512], lhsT=vB[:, kb, h * 65:(h + 1) * 65], rhs=pt[:, c0:512],
                                          start=(kb == 0), stop=u["last"]), reads=[vB, pt], writes=[ob])
            if u["last"]:
                osb_ = Osb.next()
                cast("act", osb_[:, :], ob[:, :], [ob], [osb_])
                ptot = pTo.next()
                pto = ptot[:, 0:260].rearrange("p (j d) -> p j d", d=65)
                for j in range(4):
                    k.op("pe", lambda e, j=j: e.transpose(out=pto[:, j, :], in_=osb_[:, j * 128:(j + 1) * 128],
                                                          identity=identf[0:65, 0:65]), reads=[osb_, cf], writes=[ptot])
                rd = rden.next()
                k.op("dve", lambda e: e.reciprocal(out=rd[:, :], in_=pto[:, :, 64]), reads=[ptot], writes=[rd])
                k.op("dve", lambda e: e.tensor_tensor(out=oB_sb[:, 4 * qg:4 * qg + 4, h * 64:(h + 1) * 64], in0=pto[:, :, 0:64],
                                                      in1=bc(rd[:, :], 2, [128, 4, 64]), op=ALU.mult),
                     reads=[ptot, rd], writes=[oB_sb])

        N = len(units)
        A(units[0])
        for t in range(N + 1):
            if t + 1 < N:
                A(units[t + 1])
            if t >= 1:
                F(units[t - 1])
            if t < N:
                B(units[t])
        k.store("sp", oB_sb, oB.rearrange("(t p) c -> p t c", p=128), oB_sb[:])
        ph.end()

    def phase4():
        ph = Phase(k)
        qTs = ph.rot(2, [64, S], BF16, "cqT")
        kTs = ph.rot(2, [64, S], BF16, "ckT")
        vC = ph.sb([128, NT, 512], BF16, "vCall")
        k.load("sp", vC, vC[:], Cv.rearrange("(t p) c -> p t c", p=128))
        oC_sb = ph.sb([128, NT, 512], BF16, "oC_sb")
        Zb = ph.rot(5, [128, 512], F32, "Zb", psum=True)
        Nb = ph.rot(2, [128, 512], F32, "Nb", psum=True)
        es = ph.rot(2, [128, 512], F32, "e_")
        sps = ph.rot(4, [128, 512], BF16, "sp_")
        Es = ph.rot(3, [128, 512], BF16, "E_")
        gts = ph.rot(2, [128, 4], F32, "g_")
        accs = ph.rot(2, [128, 4, 64], F32, "acc")
        heads = {}

        def load_head(h):
            qT = qTs.next()
            kT = kTs.next()
            k.load("sp", qT, qT[:], CqT[h * 64:(h + 1) * 64, :])
            k.load("sp", kT, kT[:], CkT[h * 64:(h + 1) * 64, :])
            heads[h] = (qT, kT)

        units = []
        for h in range(8):
            for qg in range(NQG):
                for kb in range(4 * qg + 4):
                    j0 = max(0, kb - 4 * qg)
                    units.append({"h": h, "qg": qg, "kb": kb, "j0": j0, "c0": j0 * 128, "diag": kb >= 4 * qg,
                                  "last": kb == 4 * qg + 3})
        grp = {}

        def A(u):
            h, qg, kb, c0 = u["h"], u["qg"], u["kb"], u["c0"]
            if h not in heads:
                load_head(h)
            if kb == 0 and qg == 0 and h + 1 < 8 and (h + 1) not in heads:
                load_head(h + 1)
            qT, kT = heads[h]
            zb = Zb.next()
            u["zb"] = zb
            k.op("pe", lambda e: e.matmul(out=zb[:, c0:512], lhsT=kT[:, kb * 128:(kb + 1) * 128],
                                          rhs=qT[:, qg * 512 + c0:(qg + 1) * 512], start=True, stop=True),
                 reads=[kT, qT], writes=[zb])

        def B(u):
            c0, zb = u["c0"], u["zb"]
            ee = es.next()
            k.op("act", lambda e: e.activation(out=ee[:, c0:512], in_=zb[:, c0:512], func=AF.Exp), reads=[zb], writes=[ee])
            sp = sps.next()
            u["sp"] = sp
            k.op("act", lambda e: e.activation(out=sp[:, c0:512], in_=ee[:, c0:512], func=AF.Ln, bias=1.0),
                 reads=[ee], writes=[sp])
            if u["diag"]:
                k.op("pool", lambda e: e.tensor_tensor(out=sp[:, c0:c0 + 128], in0=sp[:, c0:c0 + 128], in1=m_gt,
                                                       op=ALU.mult), reads=[sp, cb], writes=[sp])

        def C(u):
            c0, zb, sp = u["c0"], u["zb"], u["sp"]
            k.op("pe", lambda e: e.matmul(out=zb[:, c0:512], lhsT=negU, rhs=sp[:, c0:512], start=False, stop=True,
                                          skip_group_check=True), reads=[cb, sp], writes=[zb])

        def Dd(u):
            c0, zb = u["c0"], u["zb"]
            E = Es.next()
            u["E"] = E
            k.op("act", lambda e: e.activation(out=E[:, c0:512], in_=zb[:, c0:512], func=AF.Exp), reads=[zb], writes=[E])
            if u["diag"]:
                k.op("pool", lambda e: e.tensor_tensor(out=E[:, c0:c0 + 128], in0=E[:, c0:c0 + 128], in1=m_gt,
                                                       op=ALU.mult), reads=[E, cb], writes=[E])

        def F(u):
            h, kb, j0, E, sp = u["h"], u["kb"], u["j0"], u["E"], u["sp"]
            nbt = Nb.next()
            u["nbt"] = nbt
            nbk = nbt[:, 0:260].rearrange("p (j d) -> p j d", d=65)
            for j in range(j0, 4):
                k.op("pe", lambda e, j=j: e.matmul(out=nbk[:, j, 0:64], lhsT=E[:, j * 128:(j + 1) * 128],
                                                   rhs=vC[:, kb, h * 64:(h + 1) * 64], start=True, stop=True,
                                                   skip_group_check=True), reads=[E, vC], writes=[nbt])
                k.op("pe", lambda e, j=j: e.matmul(out=nbk[:, j, 64:65], lhsT=sp[:, j * 128:(j + 1) * 128],
                                                   rhs=onesb[:, 0:1], start=True, stop=True,
                                                   skip_group_check=True), reads=[sp, onesb], writes=[nbt])

        def G(u):
            h, qg, kb, j0, nbt = u["h"], u["qg"], u["kb"], u["j0"], u["nbt"]
            nbk = nbt[:, 0:260].rearrange("p (j d) -> p j d", d=65)
            if kb == 0:
                acc = accs.next()
                grp[(h, qg)] = acc
                k.op("dve", lambda e: e.tensor_copy(out=acc[:], in_=nbk[:, :, 0:64]), reads=[nbt], writes=[acc])
            else:
                acc = grp[(h, qg)]
                gt = gts.next()
                k.op("act", lambda e: e.activation(out=gt[:, j0:4], in_=nbk[:, j0:4, 64], func=AF.Exp, scale=-1.0),
                     reads=[nbt], writes=[gt])
                for j in range(j0, 4):
                    k.op("dve", lambda e, j=j: e.scalar_tensor_tensor(out=acc[:, j, :], in0=acc[:, j, :], scalar=gt[:, j:j + 1],
                                                                      in1=nbk[:, j, 0:64], op0=ALU.mult, op1=ALU.add),
                         reads=[acc, gt, nbt], writes=[acc])
            if u["last"]:
                k.op("pool", lambda e: e.tensor_copy(out=oC_sb[:, 4 * qg:4 * qg + 4, h * 64:(h + 1) * 64], in_=acc[:]),
                     reads=[acc], writes=[oC_sb])

        N = len(units)
        A(units[0])
        for t in range(N + 2):
            if t + 1 < N:
                A(units[t + 1])
            if t < N:
                B(units[t])
            if 1 <= t <= N:
                C(units[t - 1])
                Dd(units[t - 1])
            if t >= 2:
                F(units[t - 2])
                G(units[t - 2])
        k.store("sp", oC_sb, oC.rearrange("(t p) c -> p t c", p=128), oC_sb[:])
        ph.end()

    def phase5(l, s, xsrc):
        ph = Phase(k)
        Wbr = ph.sb([128, 10, D], BF16, "Wbr")
        Wo = ph.sb([128, 8, D], BF16, "Wo")
        k.load("sp", Wbr, Wbr[:], Wb_br[l].rearrange("(c p) n -> p c n", p=128))
        k.load("sp", Wo, Wo[:], Wb_out[l].rearrange("(c p) n -> p c n", p=128))
        xts = ph.rot(3, [128, D], F32, "xt")
        a3s = ph.rot(3, [128, 3, 260], F32, "a3")
        ocs = ph.rot(3, [128, 1280], BF16, "ocat")
        gts = ph.rot(3, [128, 3072], BF16, "gt")
        rds = ph.rot(2, [128, 4], F32, "rd")
        pT = ph.ps([128, 16, 128], BF16, "pT")
        pTm = ph.ps([128, 8, 128], BF16, "pTm")
        oTs = ph.rot(2, [128, 10, 128], BF16, "oT")
        PP = ph.rot(2, [128, D], F32, "PP", psum=True)
        tf = ph.sb([128, D], F32, "tf")
        uf = ph.sb([128, D], F32, "uf")
        u2 = ph.sb([128, D], F32, "u2")
        mbfs = ph.rot(2, [128, D], BF16, "mbf")
        mTs = ph.rot(2, [128, 8, 128], BF16, "mT")
        xo = ph.rot(2, [128, D], F32, "xo")
        st = {}

        def L(tt):
            t0 = tt * 128
            xt, a3, oc, gt = xts.next(), a3s.next(), ocs.next(), gts.next()
            k.load("sp", xt, xt[:], xsrc[t0:t0 + 128, :])
            k.load("sp", a3, a3[:], accA[:, t0:t0 + 128, :].rearrange("g p c -> p g c"))
            k.load("sp", oc, oc[:, 256:768], oB[t0:t0 + 128, :])
            k.load("sp", oc, oc[:, 768:1280], oC[t0:t0 + 128, :])
            k.load("sp", gt, gt[:], Gt[t0:t0 + 128, :])
            st[tt] = {"xt": xt, "a3": a3, "oc": oc, "gt": gt}

        def branch(P, oT, ca, cbn):
            for half in range(2):
                for c in range(ca, cbn):
                    k.op("pe", lambda e, c=c, half=half: e.matmul(
                        out=P[:, half * 512:(half + 1) * 512], lhsT=oT[:, c, :], rhs=Wbr[:, c, half * 512:(half + 1) * 512],
                        start=(c == ca), stop=(c == cbn - 1)), reads=[oT, Wbr], writes=[P])

        def S1(tt):
            d = st[tt]
            a3, oc, gt = d["a3"], d["oc"], d["gt"]
            k.op("pool", lambda e: e.tensor_tensor(out=a3[:, 0, :], in0=a3[:, 0, :], in1=a3[:, 1, :], op=ALU.add),
                 reads=[a3], writes=[a3])
            k.op("pool", lambda e: e.tensor_tensor(out=a3[:, 0, :], in0=a3[:, 0, :], in1=a3[:, 2, :], op=ALU.add),
                 reads=[a3], writes=[a3])
            n4 = a3[:, 0, :].rearrange("p (h d) -> p h d", d=65)
            rd = rds.next()
            k.op("dve", lambda e: e.reciprocal(out=rd[:, :], in_=n4[:, :, 64]), reads=[a3], writes=[rd])
            k.op("dve", lambda e: e.tensor_tensor(out=oc[:, 0:256].rearrange("p (h d) -> p h d", d=64), in0=n4[:, :, 0:64],
                                                  in1=bc(rd[:, :], 2, [128, 4, 64]), op=ALU.mult), reads=[a3, rd], writes=[oc])
            for j in range(10):
                k.op("pe", lambda e, j=j: e.transpose(out=pT[:, j, :], in_=oc[:, j * 128:(j + 1) * 128], identity=identb),
                     reads=[oc, cb], writes=[pT])
            oT = oTs.next()
            cast("act", oT[:], pT[:, 0:10, :], [pT], [oT])
            Pa = PP.next()
            branch(Pa, oT, 0, 2)
            k.op("dve", lambda e: e.tensor_tensor(out=tf[:], in0=Pa[:], in1=gt[:, 0:1024], op=ALU.mult),
                 reads=[Pa, gt], writes=[tf])
            Pb = PP.next()
            branch(Pb, oT, 2, 6)
            k.op("dve", lambda e: e.tensor_tensor(out=uf[:], in0=Pb[:], in1=gt[:, 1024:2048], op=ALU.mult),
                 reads=[Pb, gt], writes=[uf])
            k.op("pool", lambda e: e.tensor_tensor(out=tf[:], in0=tf[:], in1=uf[:], op=ALU.add), reads=[tf, uf], writes=[tf])
            Pc = PP.next()
            branch(Pc, oT, 6, 10)
            k.op("dve", lambda e: e.tensor_tensor(out=u2[:], in0=Pc[:], in1=gt[:, 2048:3072], op=ALU.mult),
                 reads=[Pc, gt], writes=[u2])
            mbf = mbfs.next()
            k.op("pool", lambda e: e.tensor_tensor(out=mbf[:], in0=tf[:], in1=u2[:], op=ALU.add), reads=[tf, u2], writes=[mbf])
            d["mbf"] = mbf

        def S2(tt):
            t0 = tt * 128
            d = st.pop(tt)
            mbf, xt = d["mbf"], d["xt"]
            for j in range(8):
                k.op("pe", lambda e, j=j: e.transpose(out=pTm[:, j, :], in_=mbf[:, j * 128:(j + 1) * 128], identity=identb),
                     reads=[mbf, cb], writes=[pTm])
            mT = mTs.next()
            cast("act", mT[:], pTm[:], [pTm], [mT])
            P = PP.next()
            for half in range(2):
                for c in range(8):
                    k.op("pe", lambda e, c=c, half=half: e.matmul(
                        out=P[:, half * 512:(half + 1) * 512], lhsT=mT[:, c, :], rhs=Wo[:, c, half * 512:(half + 1) * 512],
                        start=(c == 0), stop=(c == 7)), reads=[mT, Wo], writes=[P])
            o = xo.next()
            k.op("dve", lambda e: e.tensor_tensor(out=o[:], in0=P[:], in1=xt[:], op=ALU.add), reads=[P, xt], writes=[o])
            k.store("sp", o, xmid[s][t0:t0 + 128, :], o[:])

        L(0)
        L(1)
        S1(0)
        for tt in range(NT):
            if tt + 2 < NT:
                L(tt + 2)
            if tt + 1 < NT:
                S1(tt + 1)
            S2(tt)
        ph.end()

    def phase6(l, dsts):
        ph = Phase(k)
        W1 = ph.sb([128, 8, DFF], BF16, "W1")
        W2 = ph.sb([128, 32, D], BF16, "W2")
        for c in range(8):
            k.load("sp", W1, W1[:, c, :], Wb_ff1[l][c * 128:(c + 1) * 128, :])
        for c in range(4):
            k.load("sp", W2, W2[:, c * 8:(c + 1) * 8, :],
                   Wb_ff2[l][c * 1024:(c + 1) * 1024, :].rearrange("(c p) n -> p c n", p=128))
        r = norm_res(ph)
        xts = ph.rot(4, [128, D], F32, "xt")
        h2T = ph.rot(2, [128, 8, 256], BF16, "h2T")
        h1T = ph.sb([128, 32, 256], BF16, "h1T")
        rl = ph.rot(2, [128, 2, 256], F32, "rl")
        pF = ph.rot(2, [128, 2, 256], F32, "pF", psum=True)
        pY = ph.rot(3, [128, 512], F32, "pY", psum=True)
        xo = ph.rot(1, [128, D], F32, "xo")
        groups = [(s, tg) for s in range(nS) for tg in range(NT // 2)]
        gstate = {}

        def Ng(gi):
            s, tg = groups[gi]
            hT = h2T.next()
            xt2 = []
            for i in range(2):
                t0 = (tg * 2 + i) * 128
                xt = xts.next()
                k.load("sp", xt, xt[:], xmid[s][t0:t0 + 128, :])
                r["hTtile"] = hT
                pend.append(norm_transpose(ph, r, None, xt, hT[:, :, i * 128:(i + 1) * 128], defer=True))
                xt2.append(xt)
            gstate[gi] = (hT, xt2)

        pend = []
        Ng(0)
        for f in pend:
            f()
        pend.clear()
        for gi, (s, tg) in enumerate(groups):
            if True:
                if gi + 1 < len(groups):
                    Ng(gi + 1)
                hT, xt2 = gstate.pop(gi)
                for f2 in range(16):
                    p = pF.next()
                    for fi in range(2):
                        f = f2 * 2 + fi
                        for c in range(8):
                            k.op("pe", lambda e, c=c, f=f, fi=fi: e.matmul(
                                out=p[:, fi, :], lhsT=W1[:, c, f * 128:(f + 1) * 128], rhs=hT[:, c, :],
                                start=(c == 0), stop=(c == 7)), reads=[W1, hT], writes=[p])
                    rr = rl.next()
                    k.op("act", lambda e: e.activation(out=rr[:], in_=p[:], func=AF.Relu), reads=[p], writes=[rr])
                    k.op("pool", lambda e: e.tensor_tensor(out=h1T[:, f2 * 2:f2 * 2 + 2, :], in0=rr[:], in1=rr[:], op=ALU.mult),
                         reads=[rr], writes=[h1T])
                for f in pend:
                    f()
                pend.clear()
                for i in range(2):
                    t0 = (tg * 2 + i) * 128
                    o = xo.next()
                    for half in range(2):
                        p = pY.next()
                        for f in range(32):
                            k.op("pe", lambda e, f=f: e.matmul(out=p[:], lhsT=h1T[:, f, i * 128:(i + 1) * 128],
                                                               rhs=W2[:, f, half * 512:(half + 1) * 512],
                                                               start=(f == 0), stop=(f == 31)), reads=[h1T, W2], writes=[p])
                        k.op("dve", lambda e: e.tensor_tensor(out=o[:, half * 512:(half + 1) * 512], in0=p[:],
                                                              in1=xt2[i][:, half * 512:(half + 1) * 512], op=ALU.add),
                             reads=[p, xt2[i]], writes=[o])
                    k.store("sp", o, dsts[s][t0:t0 + 128, :], o[:])
        ph.end()

    U = UPTO
    prep_weights()
    for l in range(nL):
        for s in range(nS):
            xsrc = x_in[s] if l == 0 else xres[s]
            if U >= 1:
                phase1(l, s, xsrc, 0)
            if U >= 2:
                phase1(l, s, xsrc, 1)
            if U >= 3:
                phase2()
            if U >= 4:
                phase3()
            if U >= 5:
                phase4()
            if U >= 6:
                phase5(l, s, xsrc)
        dsts = [out[s] if l == nL - 1 else xres[s] for s in range(nS)]
        if U >= 7:
            phase6(l, dsts)
    k.barrier()
    print("sbuf max bytes/partition:", k.sb_max, "instructions:", k.ninstr, {e: k.cnt[e] for e in k.cnt})
    k.close()
    return nc


def host_consts(S):
    NT = S // 128
    p = np.arange(128)[:, None]
    c = np.arange(128)[None, :]
    consts = np.zeros((128, 5, 128), np.float32)
    consts[:, 0, :] = (p == c)
    consts[:, 1, :] = (c >= p)
    consts[:, 2, :] = (c > p)
    consts[:, 3, :] = (c <= p)
    consts[:, 4, :] = -1.0 * (p >= c)

    def tables(dim):
        pos = np.arange(S, dtype=np.float32)
        inv = (np.float32(500000.0) ** (-np.arange(0, dim, 2, dtype=np.float32) / np.float32(dim))).astype(np.float32)
        ang = (pos[:, None] * inv[None, :]).astype(np.float32)
        t = np.concatenate([np.cos(ang), np.sin(ang)], axis=1).astype(np.float32)
        return np.ascontiguousarray(t.reshape(NT, 128, dim).transpose(1, 0, 2))
    return consts, tables(16), tables(32)


def make_inputs(x_sh, l0, l1, S, attn_norm, w_in, a_q_norm, a_k_norm, b_q_a_norm, w_q_b, b_kv_a_norm,
                w_kv_b, b_q_norm, b_k_norm, w_branch, w_out, mlp_norm, w_ff1, w_ff2):
    nL = l1 - l0
    sl = slice(l0, l1)
    f = lambda a: np.ascontiguousarray(np.asarray(a, dtype=np.float32))
    col = lambda g, c: np.ascontiguousarray(np.asarray(g, np.float32)[sl].reshape(nL, c, 128).transpose(0, 2, 1))
    rep = lambda a, b: np.ascontiguousarray(np.broadcast_to(
        np.concatenate([np.asarray(a, np.float32)[sl], np.asarray(b, np.float32)[sl]], axis=1)[:, None, :],
        (nL, 128, a.shape[1] + b.shape[1])))
    consts, rA, rM = host_consts(S)
    return {
        "x": f(x_sh), "w_in": f(w_in[sl]), "w_q_b": f(np.asarray(w_q_b)[sl].reshape(nL, 256, 768)),
        "w_kv_b": f(np.asarray(w_kv_b)[sl].reshape(nL, 128, 1024)), "w_branch": f(w_branch[sl]),
        "w_out": f(w_out[sl]), "w_ff1": f(w_ff1[sl]), "w_ff2": f(w_ff2[sl]),
        "g_attn": col(attn_norm, 8), "g_mlp": col(mlp_norm, 8), "g_qa": col(b_q_a_norm, 2),
        "g_kva": col(b_kv_a_norm, 1), "g_aqk": rep(a_q_norm, a_k_norm), "g_bqk": rep(b_q_norm, b_k_norm),
        "ropeA": rA, "ropeM": rM, "consts": consts,
    }


def kernel(x, attn_norm, w_in, a_q_norm, a_k_norm, b_q_a_norm, w_q_b, b_kv_a_norm,
           w_kv_b, b_q_norm, b_k_norm, w_branch, w_out, mlp_norm, w_ff1, w_ff2):
    x = np.asarray(x, dtype=np.float32)
    B, S, _ = x.shape
    nL = np.asarray(attn_norm).shape[0]
    ncores = 8
    nS = B // ncores
    nc = build(nL, nS, S)
    in_maps = []
    for c in range(ncores):
        in_maps.append(make_inputs(x[c * nS:(c + 1) * nS], 0, nL, S, attn_norm, w_in, a_q_norm, a_k_norm, b_q_a_norm,
                                   w_q_b, b_kv_a_norm, w_kv_b, b_q_norm, b_k_norm, w_branch, w_out, mlp_norm,
                                   w_ff1, w_ff2))
    res = run_bass_kernel_spmd(nc, in_maps, core_ids=list(range(ncores)))
    return np.concatenate([np.asarray(r["out"], dtype=np.float32) for r in res.results], axis=0)
```

```python
import contextlib
import numpy as np
import concourse.bass as bass
import concourse.mybir as mybir
from concourse.alu_op_type import AluOpType as ALU
from concourse.bass_utils import run_bass_kernel_spmd

AF = mybir.ActivationFunctionType
AX = mybir.AxisListType
F32 = mybir.dt.float32
BF16 = mybir.dt.bfloat16

SAME_ENG_SYNC = True
UPTO = 99
STAGE1 = 99
TLIM = 9999
EPS = 1e-6
D = 1024
INC = 7328
DFF = 4096


class T:
    _n = 0

    def __init__(self, t, name=None):
        self.t = t
        T._n += 1
        self.id = "t%d" % T._n
        self.name = name or self.id
        self.w = None
        self.r = {}
        self.dsem = None
        self.dval = 0
        self.psum = False

    def __getitem__(self, idx):
        return self.t[idx]


class Rot:
    def __init__(self, tiles):
        self.tiles = tiles
        self.i = 0

    def next(self):
        t = self.tiles[self.i % len(self.tiles)]
        self.i += 1
        return t


class K:
    def __init__(self, nc):
        self.nc = nc
        self.es = contextlib.ExitStack()
        self.eng = {"pe": nc.tensor, "act": nc.scalar, "dve": nc.vector,
                    "pool": nc.gpsimd, "sp": nc.sync}
        self.sem = {}
        self.cnt = {}
        self.seen = {}
        for e in self.eng:
            self.sem[e] = self.es.enter_context(nc.semaphore("s_" + e))
            self.cnt[e] = 0
            self.seen[e] = {}
        self.semof = dict(self.sem)
        self.dtiles = []
        self.dsem_pool = []
        self.ninstr = 0
        self.uid = 0

    def sb(self, shape, dtype, name, stack=None):
        st = stack if stack is not None else self.es
        self.uid += 1
        nb = int(np.prod(shape[1:])) * (2 if dtype == BF16 else 4)
        self.sb_bytes = getattr(self, "sb_bytes", 0) + nb
        self.sb_max = max(getattr(self, "sb_max", 0), self.sb_bytes)
        st.callback(self._free, nb)
        t = st.enter_context(self.nc.sbuf_tensor("%s_%d" % (name, self.uid), list(shape), dtype))
        return T(t, name)

    def ps(self, shape, dtype, name, stack=None):
        st = stack if stack is not None else self.es
        self.uid += 1
        t = st.enter_context(self.nc.psum_tensor("%s_%d" % (name, self.uid), list(shape), dtype))
        tt = T(t, name)
        tt.psum = True
        return tt

    def rot(self, n, shape, dtype, name, stack=None, psum=False):
        f = self.ps if psum else self.sb
        return Rot([f(shape, dtype, "%s%d" % (name, i), stack) for i in range(n)])

    def _free(self, nb):
        self.sb_bytes -= nb

    def _wait(self, e, deps):
        eng = self.eng[e]
        seen = self.seen[e]
        for key, val in sorted(deps, key=lambda d: str(d[0])):
            if key == e and (e in ("pe", "sp") or not SAME_ENG_SYNC):
                continue
            if seen.get(key, 0) >= val:
                continue
            eng.wait_ge(self.semof[key], val)
            seen[key] = val
            self.ninstr += 1

    def _deps(self, reads, writes):
        deps = set()
        for t in reads:
            if t.w:
                deps.add(t.w)
            if t.psum:
                for d in t.r.values():
                    deps.add(d)
        for t in writes:
            if t.w:
                deps.add(t.w)
            for d in t.r.values():
                deps.add(d)
        return deps

    def op(self, e, fn, reads=(), writes=()):
        self._wait(e, self._deps(reads, writes))
        ins = fn(self.eng[e])
        self.cnt[e] += 1
        ins.then_inc(self.sem[e], 1)
        self.ninstr += 1
        me = (e, self.cnt[e])
        for t in reads:
            t.r[e] = me
        for t in writes:
            t.w = me
            t.r = {}
        return ins

    def _dsem(self, tile):
        if tile.dsem is None:
            if self.dsem_pool:
                key, sem, val = self.dsem_pool.pop()
                tile.dkey, tile.dsem, tile.dval = key, sem, val
            else:
                tile.dkey = "d" + tile.id
                tile.dsem = self.es.enter_context(self.nc.semaphore(tile.dkey))
                self.semof[tile.dkey] = tile.dsem
            self.dtiles.append(tile)

    def load(self, q, tile, out_ap, in_ap, **kw):
        self._dsem(tile)
        self._wait(q, self._deps((), (tile,)))
        ins = self.eng[q].dma_start(out=out_ap, in_=in_ap, **kw)
        tile.dval += 16
        ins.then_inc(tile.dsem, 16)
        self.ninstr += 1
        tile.w = (tile.dkey, tile.dval)
        tile.r = {}

    def store(self, q, tile, out_ap, in_ap, **kw):
        self._dsem(tile)
        self._wait(q, self._deps((tile,), ()))
        ins = self.eng[q].dma_start(out=out_ap, in_=in_ap, **kw)
        tile.dval += 16
        ins.then_inc(tile.dsem, 16)
        self.ninstr += 1
        tile.r["dma"] = (tile.dkey, tile.dval)

    def barrier(self, release=True):
        deps = set()
        for e in self.eng:
            if self.cnt[e]:
                deps.add((e, self.cnt[e]))
        for t in self.dtiles:
            if t.dval:
                deps.add((t.dkey, t.dval))
        for e in self.eng:
            self._wait(e, deps)

    def release(self, tiles):
        for t in tiles:
            if t.dsem is not None:
                self.dsem_pool.append((t.dkey, t.dsem, t.dval))
                self.dtiles.remove(t)
                t.dsem = None

    def close(self):
        self.es.close()


class Phase:
    def __init__(self, k):
        self.k = k
        self.st = contextlib.ExitStack()
        self.tiles = []

    def sb(self, shape, dtype, name):
        t = self.k.sb(shape, dtype, name, self.st)
        self.tiles.append(t)
        return t

    def ps(self, shape, dtype, name):
        t = self.k.ps(shape, dtype, name, self.st)
        self.tiles.append(t)
        return t

    def rot(self, n, shape, dtype, name, psum=False):
        f = self.ps if psum else self.sb
        return Rot([f(shape, dtype, "%s%d" % (name, i)) for i in range(n)])

    def end(self):
        self.k.barrier()
        self.k.release(self.tiles)
        self.st.close()


def bc(ap, axis, shape):
    return ap.unsqueeze(axis).broadcast_to(list(shape))


def build(nL, nS, S, debug=False):
    nc = bass.Bass("TRN2", target_bir_lowering=False)
    NT = S // 128
    NQG = S // 512

    def din(name, shape, dt=F32):
        return nc.dram_tensor(name, list(shape), dt, kind="ExternalInput").ap()

    def dscr(name, shape, dt):
        if debug:
            return nc.dram_tensor(name, list(shape), dt, kind="ExternalOutput").ap()
        return nc.dram_tensor(name, list(shape), dt).ap()

    x_in = din("x", [nS, S, D])
    w_in = din("w_in", [nL, D, INC])
    w_qb = din("w_q_b", [nL, 256, 768])
    w_kvb = din("w_kv_b", [nL, 128, 1024])
    w_br = din("w_branch", [nL, 1280, D])
    w_out = din("w_out", [nL, D, D])
    w_ff1 = din("w_ff1", [nL, D, DFF])
    w_ff2 = din("w_ff2", [nL, DFF, D])
    g_attn = din("g_attn", [nL, 128, 8])
    g_mlp = din("g_mlp", [nL, 128, 8])
    g_qa = din("g_qa", [nL, 128, 2])
    g_kva = din("g_kva", [nL, 128, 1])
    g_aqk = din("g_aqk", [nL, 128, 128])
    g_bqk = din("g_bqk", [nL, 128, 192])
    ropeA = din("ropeA", [128, NT, 16])
    ropeM = din("ropeM", [128, NT, 32])
    consts = din("consts", [128, 5, 128])
    out = nc.dram_tensor("out", [nS, S, D], F32, kind="ExternalOutput").ap()

    Wb_in = dscr("Wb_in", [nL, D, INC], BF16)
    Wb_qb = dscr("Wb_qb", [nL, 256, 768], BF16)
    Wb_kvb = dscr("Wb_kvb", [nL, 128, 1024], BF16)
    Wb_br = dscr("Wb_br", [nL, 1280, D], BF16)
    Wb_out = dscr("Wb_out", [nL, D, D], BF16)
    Wb_ff1 = dscr("Wb_ff1", [nL, D, DFF], BF16)
    Wb_ff2 = dscr("Wb_ff2", [nL, DFF, D], BF16)
    xres = dscr("xres", [nS, S, D], F32)
    xmid = dscr("xmid", [nS, S, D], F32)
    AqT = dscr("AqT", [768, S], BF16)
    AkT = dscr("AkT", [768, S], BF16)
    Av = dscr("Av", [S, 780], BF16)
    BqT = dscr("BqT", [768, S], BF16)
    BkT = dscr("BkT", [768, S], BF16)
    Bv = dscr("Bv", [S, 520], BF16)
    CqT = dscr("CqT", [512, S], BF16)
    CkT = dscr("CkT", [512, S], BF16)
    Cv = dscr("Cv", [S, 512], BF16)
    Gt = dscr("Gt", [S, 3072], BF16)
    accA = dscr("accA", [3, S, 260], F32)
    oB = dscr("oB", [S, 512], BF16)
    oC = dscr("oC", [S, 512], BF16)

    k = K(nc)
    cf = k.sb([128, 5, 128], F32, "cf")
    cb = k.sb([128, 5, 128], BF16, "cb")
    k.load("sp", cf, cf[:], consts)
    k.op("dve", lambda e: e.tensor_copy(out=cb[:], in_=cf[:]), reads=[cf], writes=[cb])
    identb = cb[:, 0, :]
    identf = cf[:, 0, :]
    m_ge = cb[:, 1, :]
    m_gt = cb[:, 2, :]
    m_le = cb[:, 3, :]
    negU = cb[:, 4, :]
    onesb = k.sb([128, 1], BF16, "onesb")
    k.op("dve", lambda e: e.memset(onesb[:], 1.0), writes=[onesb])
    rA = k.sb([128, NT, 16], F32, "rA")
    rM = k.sb([128, NT, 32], F32, "rM")
    k.load("sp", rA, rA[:], ropeA)
    k.load("sp", rM, rM[:], ropeM)

    neghalf = k.sb([128, 24], F32, "neghalf")
    k.op("dve", lambda e: e.memset(neghalf[:], -0.5), writes=[neghalf])

    def rsq(rs_ap, ss_ap, n, inv_n, rs_t, ss_t):
        k.op("dve", lambda e: e.tensor_scalar(out=rs_ap, in0=ss_ap, scalar1=inv_n, scalar2=EPS, op0=ALU.mult, op1=ALU.add),
             reads=[ss_t], writes=[rs_t])
        k.op("pool", lambda e: e.tensor_tensor(out=rs_ap, in0=rs_ap, in1=neghalf[:, 0:n], op=ALU.pow),
             reads=[rs_t, neghalf], writes=[rs_t])

    ceng = Rot(["dve", "act", "pool"])

    def cast(e, out_ap, in_ap, reads, writes, scale=None):
        if e == "act":
            if scale is None:
                k.op("act", lambda g: g.activation(out=out_ap, in_=in_ap, func=AF.Copy), reads=reads, writes=writes)
            else:
                k.op("act", lambda g: g.activation(out=out_ap, in_=in_ap, func=AF.Copy, scale=scale),
                     reads=reads, writes=writes)
        else:
            if scale is None:
                k.op(e, lambda g: g.tensor_copy(out=out_ap, in_=in_ap), reads=reads, writes=writes)
            else:
                k.op(e, lambda g: g.tensor_scalar(out=out_ap, in0=in_ap, scalar1=scale, scalar2=None, op0=ALU.mult),
                     reads=reads, writes=writes)

    gcol = k.sb([128, nL, 19], F32, "gcol")
    for l in range(nL):
        k.load("sp", gcol, gcol[:, l, 0:8], g_attn[l])
        k.load("sp", gcol, gcol[:, l, 8:16], g_mlp[l])
        k.load("sp", gcol, gcol[:, l, 16:18], g_qa[l])
        k.load("sp", gcol, gcol[:, l, 18:19], g_kva[l])
    CB = 2048

    def prep_gen(ph, l, engines):
        wf = ph.rot(4, [128, CB], F32, "wf")
        wb = ph.rot(3, [128, CB], BF16, "wb")
        items = []
        jobs = [(w_in[l], Wb_in[l], D, INC, 0), (w_qb[l], Wb_qb[l], 256, 768, 16),
                (w_kvb[l], Wb_kvb[l], 128, 1024, 18), (w_br[l], Wb_br[l], 1280, D, None),
                (w_out[l], Wb_out[l], D, D, None), (w_ff1[l], Wb_ff1[l], D, DFF, 8),
                (w_ff2[l], Wb_ff2[l], DFF, D, None)]
        for src, dst, R, C, gc in jobs:
            for c in range(R // 128):
                for c0 in range(0, C, CB):
                    items.append((src, dst, c, c0, min(CB, C - c0), gc))
        loaded = {}

        def ld(i):
            src, dst, c, c0, n, gc = items[i]
            f = wf.next()
            k.load("sp", f, f[:, 0:n], src[c * 128:(c + 1) * 128, c0:c0 + n])
            loaded[i] = f

        for i in range(min(2, len(items))):
            ld(i)
        for i in range(len(items)):
            if i + 2 < len(items):
                ld(i + 2)
            src, dst, c, c0, n, gc = items[i]
            f = loaded.pop(i)
            b = wb.next()
            sc = None if gc is None else gcol[:, l, gc + c:gc + c + 1]
            e = engines.next()
            rd = [f] if gc is None else [f, gcol]
            cast(e, b[:, 0:n], f[:, 0:n], rd, [b], scale=sc)
            k.store("sp", b, dst[c * 128:(c + 1) * 128, c0:c0 + n], b[:, 0:n])
            yield

    def prep_weights(l):
        ph = Phase(k)
        for _ in prep_gen(ph, l, ceng):
            pass
        ph.end()

    def norm_transpose(ph, r, xsrc_rows, xt, hT_out, defer=False):
        ss = r["ss"].next()
        rstd = r["rstd"].next()
        junk = r["junk"]
        xn = r["xn"].next()
        k.op("act", lambda e: e.activation(out=junk[:], in_=xt[:], func=AF.Square, accum_out=ss[:]),
             reads=[xt], writes=[junk, ss])
        rsq(rstd[:], ss[:], 1, 1.0 / D, rstd, ss)
        k.op("act", lambda e: e.activation(out=xn[:], in_=xt[:], func=AF.Copy, scale=rstd[:]),
             reads=[xt, rstd], writes=[xn])
        hTt = r["hTtile"]

        def partB():
            pT = r["pT"].next()
            for c in range(8):
                k.op("pe", lambda e, c=c: e.transpose(out=pT[:, c, :], in_=xn[:, c * 128:(c + 1) * 128], identity=identb),
                     reads=[xn, cb], writes=[pT])
            k.op("dve", lambda e: e.tensor_copy(out=hT_out, in_=pT[:]), reads=[pT], writes=[hTt])
        if defer:
            return partB
        partB()

    def norm_res(ph, n=2):
        return {"ss": ph.rot(n, [128, 1], F32, "ss"), "rstd": ph.rot(n, [128, 1], F32, "rstd"),
                "junk": ph.sb([128, D], BF16, "junk"), "xn": ph.rot(2, [128, D], BF16, "xn"),
                "pT": ph.rot(1, [128, 8, 128], BF16, "pT", psum=True)}

    def rms_small(src_ap, nh, hd, reads_t, sq_t, ss_t, rs_t, extra_add=None):
        k.op("act", lambda e: e.activation(out=sq_t[:, 0:nh * hd].rearrange("p (h d) -> p h d", d=hd),
                                           in_=src_ap, func=AF.Square), reads=reads_t, writes=[sq_t])
        k.op("dve", lambda e: e.tensor_reduce(out=ss_t[:, 0:nh], in_=sq_t[:, 0:nh * hd].rearrange("p (h d) -> p h d", d=hd),
                                              axis=AX.X, op=ALU.add), reads=[sq_t], writes=[ss_t])

    def interleave(gens):
        gens = [g for g in gens if g is not None]
        while gens:
            for g in list(gens):
                try:
                    next(g)
                except StopIteration:
                    gens.remove(g)

    def phase1(l, s, xsrc, pas):
        ph = Phase(k)
        if pas == 0:
            c_lo, c_hi = 0, 3744
        else:
            c_lo, c_hi = 3744, INC
        NW = c_hi - c_lo
        W = ph.sb([128, 8, NW], BF16, "W")
        for c in range(8):
            k.load("sp", W, W[:, c, :], Wb_in[l][c * 128:(c + 1) * 128, c_lo:c_hi])
        r = norm_res(ph, 3)
        xts = ph.rot(3, [128, D], F32, "xt")
        hTs = ph.rot(2, [128, 8, 512], BF16, "hT4") if pas == 0 else ph.rot(3, [128, 8, 128], BF16, "hT")
        pj = ph.rot(2 if pas == 0 else 6, [128, 512], F32, "pj", psum=True)
        hT_of = {}
        x_of = {}
        if pas == 0:
            gaq = ph.sb([128, 128], F32, "gaq")
            gbq = ph.sb([128, 192], F32, "gbq")
            k.load("sp", gaq, gaq[:], g_aqk[l])
            k.load("sp", gbq, gbq[:], g_bqk[l])
            wqb = ph.sb([128, 2, 768], BF16, "wqb")
            wkvb = ph.sb([128, 1024], BF16, "wkvb")
            k.load("sp", wqb, wqb[:], Wb_qb[l].rearrange("(c p) n -> p c n", p=128))
            k.load("sp", wkvb, wkvb[:], Wb_kvb[l])
            qks = ph.rot(2, [128, 1536], F32, "qk")
            lats = ph.rot(2, [128, 416], F32, "lat")
            sqA = ph.sb([128, 1536], BF16, "sqA")
            ss24 = ph.sb([128, 24], F32, "ss24")
            rs24 = ph.sb([128, 24], F32, "rs24")
            qkb = ph.rot(2, [128, 1536], BF16, "qkb")
            rtA = [ph.sb([128, 24, 8], F32, "rtA%d" % i) for i in range(4)]
            x16 = ph.sb([128, 24, 16], F32, "x16")
            x32 = ph.sb([128, 8, 32], F32, "x32")
            rtK = [ph.sb([128, 1, 16], F32, "rtK%d" % i) for i in range(4)]
            sqK = ph.sb([128, 512], BF16, "sqK")
            ss8k = ph.sb([128, 8], F32, "ss8k")
            rs8k = ph.sb([128, 8], F32, "rs8k")
            rtM = [ph.sb([128, 8, 16], F32, "rtM%d" % i) for i in range(4)]
            vAs = ph.rot(2, [128, 12, 65], BF16, "vA")
            for t in vAs.tiles:
                k.op("dve", lambda e, t=t: e.memset(t[:], 1.0), writes=[t])
            pTA = ph.ps([128, 16, 128], BF16, "pTA")
            ATs = ph.rot(2, [128, 12, 128], BF16, "AT")
            sqM = ph.sb([128, 768], BF16, "sqM")
            ssl = ph.sb([128, 3], F32, "ssl")
            rsl = ph.sb([128, 3], F32, "rsl")
            latn = ph.sb([128, 384], BF16, "latn")
            pT2 = ph.ps([128, 8, 128], BF16, "pT2")
            latT = ph.sb([128, 3, 128], BF16, "latT")
            pM = ph.ps([128, 1024], F32, "pM")
            bq = ph.sb([128, 768], F32, "bq")
            bk = ph.sb([128, 8, 96], F32, "bk")
            ss8 = ph.sb([128, 8], F32, "ss8")
            rs8 = ph.sb([128, 8], F32, "rs8")
            bqbs = ph.rot(2, [128, 8, 96], BF16, "bqb")
            bkbs = ph.rot(2, [128, 8, 96], BF16, "bkb")
            tails = {}
            krg = ph.sb([128, 32], F32, "krg")
            krr = ph.sb([128, 32], F32, "krr")
            vBs = ph.rot(2, [128, 8, 65], BF16, "vB")
            for t in vBs.tiles:
                k.op("dve", lambda e, t=t: e.memset(t[:], 1.0), writes=[t])
            BTq = ph.rot(2, [96, 8, 128], BF16, "BTq")
            BTk = ph.rot(2, [96, 8, 128], BF16, "BTk")
            CTs = ph.rot(1, [128, 8, 512], BF16, "CT")
            qk_of, lat_of = {}, {}
        else:
            cvs = ph.rot(2, [128, 512], BF16, "cv")
            gts = ph.rot(2, [128, 3072], BF16, "gt")

        def proj_chunk(hTo, col0, n):
            hT, off = hTo
            p = pj.next()
            for c in range(8):
                k.op("pe", lambda e, c=c: e.matmul(out=p[:, 0:n], lhsT=hT[:, c, off:off + 128], rhs=W[:, c, col0:col0 + n],
                                                   start=(c == 0), stop=(c == 7)), reads=[hT, W], writes=[p])
            return p

        def rope(src, dst, nh, off, half, cs, sn, reads_t, dst_t, tmp):
            x1 = src[:, :, off:off + half]
            x2 = src[:, :, off + half:off + 2 * half]
            cB = bc(cs, 1, [128, nh, half])
            sB = bc(sn, 1, [128, nh, half])
            t1, t2, t3, t4 = [t[:, 0:nh, 0:half] for t in tmp]
            k.op("pool", lambda e: e.tensor_tensor(out=t1, in0=x1, in1=cB, op=ALU.mult), reads=reads_t, writes=[tmp[0]])
            k.op("pool", lambda e: e.tensor_tensor(out=t2, in0=x2, in1=sB, op=ALU.mult), reads=reads_t, writes=[tmp[1]])
            yield
            k.op("pool", lambda e: e.tensor_tensor(out=t3, in0=x2, in1=cB, op=ALU.mult), reads=reads_t, writes=[tmp[2]])
            k.op("pool", lambda e: e.tensor_tensor(out=t4, in0=x1, in1=sB, op=ALU.mult), reads=reads_t, writes=[tmp[3]])
            yield
            k.op("dve", lambda e: e.tensor_tensor(out=dst[:, :, off:off + half], in0=t1, in1=t2, op=ALU.subtract),
                 reads=[tmp[0], tmp[1]], writes=[dst_t])
            k.op("dve", lambda e: e.tensor_tensor(out=dst[:, :, off + half:off + 2 * half], in0=t3, in1=t4, op=ALU.add),
                 reads=[tmp[2], tmp[3]], writes=[dst_t])
            yield

        def Lx(tt):
            t0 = tt * 128
            xt = xts.next()
            k.load("sp", xt, xt[:], xsrc[t0:t0 + 128, :])
            x_of[tt] = xt

        def Nn(tt):
            xt = x_of.pop(tt)
            if pas == 0:
                if tt % 4 == 0:
                    hT_of["cur"] = hTs.next()
                hT = hT_of["cur"]
                off = (tt % 4) * 128
            else:
                hT = hTs.next()
                off = 0
            hT_of[tt] = (hT, off)
            r["hTtile"] = hT
            return norm_transpose(ph, r, None, xt, hT[:, :, off:off + 128], defer=True)

        def P0(tt):
            t0 = tt * 128
            hT = hT_of[tt]
            qk = qks.next()
            lat = lats.next()
            qk_of[tt] = qk
            lat_of[tt] = lat
            for j in range(3):
                p = proj_chunk(hT, j * 512, 512)
                cast("act", qk[:, j * 512:(j + 1) * 512], p[:, :], [p], [qk])
                yield
            p = proj_chunk(hT, 2304, 416)
            cast("dve", lat[:, :], p[:, 0:416], [p], [lat])
            yield
            vA = vAs.next()
            p = proj_chunk(hT, 1536, 512)
            cast("act", vA[:, 0:8, 0:64], p[:, :].rearrange("p (h d) -> p h d", d=64), [p], [vA])
            yield
            p = proj_chunk(hT, 2048, 256)
            cast("dve", vA[:, 8:12, 0:64], p[:, 0:256].rearrange("p (h d) -> p h d", d=64), [p], [vA])
            k.store("sp", vA, Av[t0:t0 + 128, :], vA[:].rearrange("p h d -> p (h d)"))
            yield

        def Cg(g):
            hT = hT_of[4 * g][0]
            CT = CTs.next()
            for j in range(8):
                col0 = 2720 + j * 128
                p = pj.next()
                for c in range(8):
                    k.op("pe", lambda e, c=c: e.matmul(out=p[:, 0:512], lhsT=W[:, c, col0:col0 + 128], rhs=hT[:, c, 0:512],
                                                       start=(c == 0), stop=(c == 7)), reads=[hT, W], writes=[p])
                if j < 4:
                    k.op("act", lambda e, p=p, j=j: e.activation(out=CT[:, j, :], in_=p[:, :], func=AF.Copy, scale=0.125),
                         reads=[p], writes=[CT])
                else:
                    cast("dve", CT[:, j, :], p[:, :], [p], [CT])
                yield
            k.store("sp", CT, CqT.rearrange("(j p) s -> p j s", p=128)[:, :, g * 512:(g + 1) * 512], CT[:, 0:4, :])
            k.store("sp", CT, CkT.rearrange("(j p) s -> p j s", p=128)[:, :, g * 512:(g + 1) * 512], CT[:, 4:8, :])
            yield

        def P1(tt):
            t0 = tt * 128
            hT = hT_of[tt]
            cv = cvs.next()
            p = proj_chunk(hT, 0, 512)
            cast("dve", cv[:], p[:], [p], [cv])
            k.store("sp", cv, Cv[t0:t0 + 128, :], cv[:])
            gt = gts.next()
            for j in range(6):
                p = proj_chunk(hT, 512 + j * 512, 512)
                k.op("act", lambda e, p=p, j=j: e.activation(out=gt[:, j * 512:(j + 1) * 512], in_=p[:], func=AF.Sigmoid),
                     reads=[p], writes=[gt])
            k.store("sp", gt, Gt[t0:t0 + 128, :], gt[:])

        def YA(tt):
            t0 = tt * 128
            qk = qk_of[tt]
            qk3 = qk[:, :].rearrange("p (h d) -> p h d", d=64)
            rms_small(qk3, 24, 64, [qk], sqA, ss24, rs24)
            yield
            rsq(rs24[:], ss24[:], 24, 1.0 / 64, rs24, ss24)
            yield
            k.op("dve", lambda e: e.tensor_tensor(out=qk3, in0=qk3, in1=bc(rs24[:, :], 2, [128, 24, 64]), op=ALU.mult),
                 reads=[qk, rs24], writes=[qk])
            yield
            qk4 = qk[:, :].rearrange("p (a h d) -> p a h d", a=2, d=64)
            g4 = bc(gaq[:, :].rearrange("p (a d) -> p a d", a=2), 2, [128, 2, 12, 64])
            qb = qkb.next()
            qb4 = qb[:, :].rearrange("p (a h d) -> p a h d", a=2, d=64)
            k.op("dve", lambda e: e.tensor_tensor(out=qb4, in0=qk4, in1=g4, op=ALU.mult), reads=[qk, gaq], writes=[qb])
            x16v = x16[:, :, :].rearrange("p (a h) d -> p a h d", a=2)
            k.op("pool", lambda e: e.tensor_tensor(out=x16v, in0=qk4[:, :, :, 0:16], in1=g4[:, :, :, 0:16], op=ALU.mult),
                 reads=[qk, gaq], writes=[x16])
            yield
            qb3 = qb[:, :].rearrange("p (h d) -> p h d", d=64)
            yield from rope(x16[:, :, :], qb3, 24, 0, 8, rA[:, tt, 0:8], rA[:, tt, 8:16], [x16, rA], qb, rtA)

            def tailA():
                for j in range(12):
                    k.op("pe", lambda e, j=j: e.transpose(out=pTA[:, j, :], in_=qb[:, j * 128:(j + 1) * 128], identity=identb),
                         reads=[qb, cb], writes=[pTA])
                AT = ATs.next()
                cast("act", AT[:], pTA[:, 0:12, :], [pTA], [AT])
                k.store("sp", AT, AqT.rearrange("(j p) s -> p j s", p=128)[:, :, t0:t0 + 128], AT[:, 0:6, :])
                k.store("sp", AT, AkT.rearrange("(j p) s -> p j s", p=128)[:, :, t0:t0 + 128], AT[:, 6:12, :])
            tails.setdefault(tt, []).append(tailA)

        def YM(tt):
            t0 = tt * 128
            lat = lat_of[tt]
            k.op("act", lambda e: e.activation(out=sqM[:, 0:256], in_=lat[:, 0:256], func=AF.Square, accum_out=ssl[:, 0:1]),
                 reads=[lat], writes=[sqM, ssl])
            k.op("act", lambda e: e.activation(out=sqM[:, 256:384], in_=lat[:, 256:384], func=AF.Square, accum_out=ssl[:, 1:2]),
                 reads=[lat], writes=[sqM, ssl])
            k.op("act", lambda e: e.activation(out=sqM[:, 384:416], in_=lat[:, 384:416], func=AF.Square, accum_out=ssl[:, 2:3]),
                 reads=[lat], writes=[sqM, ssl])
            yield
            k.op("dve", lambda e: e.tensor_scalar(out=rsl[:, 0:1], in0=ssl[:, 0:1], scalar1=1.0 / 256, scalar2=EPS,
                                                  op0=ALU.mult, op1=ALU.add), reads=[ssl], writes=[rsl])
            k.op("dve", lambda e: e.tensor_scalar(out=rsl[:, 1:2], in0=ssl[:, 1:2], scalar1=1.0 / 128, scalar2=EPS,
                                                  op0=ALU.mult, op1=ALU.add), reads=[ssl], writes=[rsl])
            k.op("pool", lambda e: e.tensor_tensor(out=rsl[:, 0:2], in0=rsl[:, 0:2], in1=neghalf[:, 0:2], op=ALU.pow),
                 reads=[rsl, neghalf], writes=[rsl])
            yield
            cast("dve", latn[:, 0:256], lat[:, 0:256], [lat, rsl], [latn], scale=rsl[:, 0:1])
            cast("dve", latn[:, 256:384], lat[:, 256:384], [lat, rsl], [latn], scale=rsl[:, 1:2])
            yield
            for j in range(3):
                k.op("pe", lambda e, j=j: e.transpose(out=pT2[:, j, :], in_=latn[:, j * 128:(j + 1) * 128], identity=identb),
                     reads=[latn, cb], writes=[pT2])
            cast("dve", latT[:], pT2[:, 0:3, :], [pT2], [latT])
            yield
            for (c0, n) in ((0, 512), (512, 256)):
                for c in range(2):
                    k.op("pe", lambda e, c=c, c0=c0, n=n: e.matmul(out=pM[:, c0:c0 + n], lhsT=latT[:, c, :],
                                                                  rhs=wqb[:, c, c0:c0 + n], start=(c == 0), stop=(c == 1)),
                         reads=[latT, wqb], writes=[pM])
            cast("act", bq[:, :], pM[:, 0:768], [pM], [bq])
            yield
            for c0 in (0, 512):
                k.op("pe", lambda e, c0=c0: e.matmul(out=pM[:, c0:c0 + 512], lhsT=latT[:, 2, :], rhs=wkvb[:, c0:c0 + 512],
                                                    start=True, stop=True), reads=[latT, wkvb], writes=[pM])
            vB = vBs.next()
            pM3 = pM[:, :].rearrange("p (h d) -> p h d", d=128)
            cast("act", vB[:, :, 0:64], pM3[:, :, 64:128], [pM], [vB])
            cast("dve", bk[:, :, 0:64], pM3[:, :, 0:64], [pM], [bk])
            k.store("sp", vB, Bv[t0:t0 + 128, :], vB[:].rearrange("p h d -> p (h d)"))
            yield
            yield from rr2(Qc(tt), Kc(tt, lat))

        def rr2(g1, g2):
            gens = [g1, g2]
            while gens:
                for g in list(gens):
                    try:
                        next(g)
                        yield
                    except StopIteration:
                        gens.remove(g)

        def Qc(tt):
            t0 = tt * 128
            bqb = bqbs.next()
            bq3 = bq[:, :].rearrange("p (h d) -> p h d", d=96)
            rms_small(bq3, 8, 96, [bq], sqM, ss8, rs8)
            yield
            rsq(rs8[:], ss8[:], 8, 1.0 / 96, rs8, ss8)
            yield
            k.op("dve", lambda e: e.tensor_tensor(out=bq3, in0=bq3, in1=bc(rs8[:, :], 2, [128, 8, 96]), op=ALU.mult),
                 reads=[bq, rs8], writes=[bq])
            yield
            k.op("dve", lambda e: e.tensor_tensor(out=bqb[:, :, 0:64], in0=bq3[:, :, 0:64],
                                                  in1=bc(gbq[:, 0:64], 1, [128, 8, 64]), op=ALU.mult),
                 reads=[bq, gbq], writes=[bqb])
            k.op("pool", lambda e: e.tensor_tensor(out=x32[:, :, :], in0=bq3[:, :, 64:96],
                                                   in1=bc(gbq[:, 64:96], 1, [128, 8, 32]), op=ALU.mult),
                 reads=[bq, gbq], writes=[x32])
            yield
            yield from rope(x32[:, :, :], bqb[:, :, 64:96], 8, 0, 16, rM[:, tt, 0:16], rM[:, tt, 16:32], [x32, rM], bqb, rtM)

            def tailQ():
                for h in range(8):
                    k.op("pe", lambda e, h=h: e.transpose(out=pT2[0:96, h, :], in_=bqb[:, h, :], identity=identb),
                         reads=[bqb, cb], writes=[pT2])
                Bq = BTq.next()
                cast("act", Bq[:], pT2[0:96, :, :], [pT2], [Bq])
                k.store("sp", Bq, BqT.rearrange("(h f) s -> f h s", f=96)[:, :, t0:t0 + 128], Bq[:])
            tails.setdefault(tt, []).append(tailQ)

        def Kc(tt, lat):
            t0 = tt * 128
            bkb = bkbs.next()
            rms_small(bk[:, :, 0:64], 8, 64, [bk], sqK, ss8k, rs8k)
            yield
            k.op("dve", lambda e: e.tensor_scalar(out=ss8k[:], in0=ss8k[:], scalar1=ssl[:, 2:3], scalar2=None, op0=ALU.add),
                 reads=[ss8k, ssl], writes=[ss8k])
            rsq(rs8k[:], ss8k[:], 8, 1.0 / 96, rs8k, ss8k)
            yield
            k.op("dve", lambda e: e.tensor_tensor(out=krg[:, :], in0=lat[:, 384:416], in1=gbq[:, 160:192], op=ALU.mult),
                 reads=[lat, gbq], writes=[krg])
            yield from rope(krg[:, :].unsqueeze(1), krr[:, :].unsqueeze(1), 1, 0, 16, rM[:, tt, 0:16], rM[:, tt, 16:32],
                            [krg, rM], krr, rtK)
            k.op("dve", lambda e: e.tensor_tensor(out=bk[:, :, 0:64], in0=bk[:, :, 0:64],
                                                  in1=bc(gbq[:, 96:160], 1, [128, 8, 64]), op=ALU.mult),
                 reads=[bk, gbq], writes=[bk])
            yield
            k.op("dve", lambda e: e.tensor_tensor(out=bkb[:, :, 0:64], in0=bk[:, :, 0:64],
                                                  in1=bc(rs8k[:, :], 2, [128, 8, 64]), op=ALU.mult),
                 reads=[bk, rs8k], writes=[bkb])
            k.op("dve", lambda e: e.tensor_tensor(out=bkb[:, :, 64:96], in0=bc(krr[:, :], 1, [128, 8, 32]),
                                                  in1=bc(rs8k[:, :], 2, [128, 8, 32]), op=ALU.mult),
                 reads=[krr, rs8k], writes=[bkb])
            yield

            def tailK():
                for h in range(8):
                    k.op("pe", lambda e, h=h: e.transpose(out=pT2[0:96, h, :], in_=bkb[:, h, :], identity=identb),
                         reads=[bkb, cb], writes=[pT2])
                Bk = BTk.next()
                cast("act", Bk[:], pT2[0:96, :, :], [pT2], [Bk])
                k.store("sp", Bk, BkT.rearrange("(h f) s -> f h s", f=96)[:, :, t0:t0 + 128], Bk[:])
            tails.setdefault(tt, []).append(tailK)

        nt = min(NT, TLIM)
        for tt in range(min(2, nt)):
            Lx(tt)
        Nn(0)()
        if nt > 1:
            if nt > 2:
                Lx(2)
            Nn(1)()
        if pas == 0:
            interleave([P0(0)])
        for tt in range(nt):
            if tt + 3 < nt:
                Lx(tt + 3)
            nB = None
            if tt + 2 < nt:
                nB = Nn(tt + 2)
            if pas == 0:
                def P0n(tt=tt, nB=nB):
                    if tt + 1 < nt:
                        yield from P0(tt + 1)
                    if nB:
                        nB()
                    yield
                interleave([P0n(), YA(tt), YM(tt), Cg(tt // 4) if tt % 4 == 3 else None])
                for f in tails.pop(tt - 1, []):
                    f()
            else:
                P1(tt)
                if nB:
                    nB()
        if pas == 0:
            for f in tails.pop(nt - 1, []):
                f()
        ph.end()

    def phase2():
        ph = Phase(k)
        qn = [ph.sb([64, S], BF16, "qn%d" % i) for i in range(4)]
        kn = [ph.sb([64, S], BF16, "kn%d" % i) for i in range(4)]
        qd = [ph.sb([64, S], BF16, "qd%d" % i) for i in range(4)]
        kd = [ph.sb([64, S], BF16, "kd%d" % i) for i in range(4)]
        vA = ph.sb([128, NT, 4, 65], BF16, "vAall")
        mb = ph.sb([128, 2, 256], BF16, "mband")
        for hi in range(2):
            k.op("dve", lambda e, hi=hi: e.tensor_copy(out=mb[:, hi, 0:128], in_=m_ge), reads=[cb], writes=[mb])
            k.op("dve", lambda e, hi=hi: e.tensor_copy(out=mb[:, hi, 128:256], in_=m_le), reads=[cb], writes=[mb])
        S2 = ph.rot(6, [128, 2, 256], F32, "S2", psum=True)
        Pt = [ph.rot(5, [128, 2, 256], BF16, "Pt%d_" % hp) for hp in range(2)]
        O4 = ph.rot(2, [128, 512], F32, "O4", psum=True)
        osb = ph.rot(2, [128, 260], F32, "osb")
        for g, d in enumerate((1, 4, 16)):
            L = S // d
            nb = L // 128
            for hs in range(4):
                h = g * 4 + hs
                k.load("sp", qn[hs], qn[hs][:], AqT[h * 64:(h + 1) * 64, :])
                k.load("sp", kn[hs], kn[hs][:], AkT[h * 64:(h + 1) * 64, :])
                if d > 1:
                    k.op("pool", lambda e, hs=hs: e.tensor_copy(out=qd[hs][:, :].rearrange("p (r u) -> p r u", r=d),
                                                                in_=qn[hs][:, :].rearrange("p (u r) -> p r u", r=d)),
                         reads=[qn[hs]], writes=[qd[hs]])
                    k.op("dve", lambda e, hs=hs: e.tensor_copy(out=kd[hs][:, :].rearrange("p (r u) -> p r u", r=d),
                                                               in_=kn[hs][:, :].rearrange("p (u r) -> p r u", r=d)),
                         reads=[kn[hs]], writes=[kd[hs]])
            Q = qd if d > 1 else qn
            Kk = kd if d > 1 else kn
            for r in range(d):
                src = Av.rearrange("(n p r) c -> r p n c", p=128, r=d)[r][:, :, g * 260:(g + 1) * 260]
                k.load("sp", vA, vA[:, r * nb:(r + 1) * nb, :, :].rearrange("p n h d -> p n (h d)"), src)
            blocks = [(r, n) for r in range(d) for n in range(nb)]
            Pof = {}

            def QE(i):
                r, n = blocks[i]
                col0 = r * L + n * 128
                nq = 256 if n < nb - 1 else 128
                cur = []
                for hp in range(2):
                    s2 = S2.next()
                    for hi in range(2):
                        hs = hp * 2 + hi
                        k.op("pe", lambda e, hs=hs, hi=hi, s2=s2: e.matmul(
                            out=s2[:, hi, 0:nq], lhsT=Kk[hs][:, col0:col0 + 128], rhs=Q[hs][:, col0:col0 + nq],
                            start=True, stop=True), reads=[Kk[hs], Q[hs]], writes=[s2])
                    pt = Pt[hp].next()
                    k.op("act", lambda e, s2=s2, pt=pt: e.activation(out=pt[:, :, 0:nq], in_=s2[:, :, 0:nq], func=AF.Exp,
                                                                     scale=0.125), reads=[s2], writes=[pt])
                    k.op("pool", lambda e, pt=pt: e.tensor_tensor(out=pt[:, :, 0:nq], in0=pt[:, :, 0:nq],
                                                                  in1=mb[:, :, 0:nq], op=ALU.mult),
                         reads=[pt, mb], writes=[pt])
                    cur.append(pt)
                Pof[i] = cur

            def PVs(i):
                r, n = blocks[i]
                b = r * nb + n
                curP = Pof[i]
                prevP = Pof.get(i - 1) if n > 0 else None
                o4t = O4.next()
                o4 = o4t[:, 0:260].rearrange("p (h d) -> p h d", d=65)
                for hs in range(4):
                    hp, hi = hs // 2, hs % 2
                    if n > 0:
                        k.op("pe", lambda e, hs=hs, hp=hp, hi=hi: e.matmul(
                            out=o4[:, hs, :], lhsT=prevP[hp][:, hi, 128:256], rhs=vA[:, b - 1, hs, :],
                            start=True, stop=False), reads=[prevP[hp], vA], writes=[o4t])
                    k.op("pe", lambda e, hs=hs, hp=hp, hi=hi: e.matmul(
                        out=o4[:, hs, :], lhsT=curP[hp][:, hi, 0:128], rhs=vA[:, b, hs, :],
                        start=(n == 0), stop=True), reads=[curP[hp], vA], writes=[o4t])
                ob = osb.next()
                cast("act", ob[:, :], o4t[:, 0:260], [o4t], [ob])
                dst = accA[g].rearrange("(n p r) c -> r n p c", p=128, r=d)[r, n]
                k.store("sp", ob, dst, ob[:, :])
                Pof.pop(i - 1, None)

            QE(0)
            if len(blocks) > 1:
                QE(1)
            for i in range(len(blocks)):
                if i + 2 < len(blocks):
                    QE(i + 2)
                PVs(i)
        ph.end()

    def phase3(prep_l=None):
        ph = Phase(k)
        pg = prep_gen(ph, prep_l, Rot(["dve"])) if prep_l is not None else None
        qTs = ph.rot(2, [96, S], BF16, "bqT")
        kTs = ph.rot(2, [96, S], BF16, "bkT")
        vB = ph.sb([128, NT, 520], BF16, "vBall")
        k.load("sp", vB, vB[:], Bv.rearrange("(t p) c -> p t c", p=128))
        oB_sb = ph.sb([128, NT, 512], BF16, "oB_sb")
        Sb = ph.rot(4, [128, 512], F32, "Sb", psum=True)
        Pts = ph.rot(5, [128, 512], BF16, "Ptb")
        Ob = ph.rot(2, [65, 512], F32, "Ob", psum=True)
        Osb = ph.rot(2, [65, 512], F32, "Osb")
        pTo = ph.rot(1, [128, 512], F32, "pTo", psum=True)
        rden = ph.rot(2, [128, 4], F32, "rden")
        sc = 96 ** -0.5
        heads = {}

        def load_head(h):
            qT = qTs.next()
            kT = kTs.next()
            k.load("sp", qT, qT[:], BqT[h * 96:(h + 1) * 96, :])
            k.load("sp", kT, kT[:], BkT[h * 96:(h + 1) * 96, :])
            heads[h] = (qT, kT)

        units = []
        for h in range(8):
            for qg in range(NQG):
                for kb in range(4 * qg + 4):
                    units.append({"h": h, "qg": qg, "kb": kb, "c0": max(0, kb - 4 * qg) * 128, "last": kb == 4 * qg + 3})
        grp = {}

        def A(u):
            h, qg, kb, c0 = u["h"], u["qg"], u["kb"], u["c0"]
            if h not in heads:
                load_head(h)
            if kb == 0 and qg == 0 and h + 1 < 8 and (h + 1) not in heads:
                load_head(h + 1)
            qT, kT = heads[h]
            sb_ = Sb.next()
            u["sb"] = sb_
            k.op("pe", lambda e: e.matmul(out=sb_[:, c0:512], lhsT=kT[:, kb * 128:(kb + 1) * 128],
                                          rhs=qT[:, qg * 512 + c0:(qg + 1) * 512], start=True, stop=True),
                 reads=[kT, qT], writes=[sb_])

        def B(u):
            c0, sb_ = u["c0"], u["sb"]
            pt = Pts.next()
            u["pt"] = pt
            k.op("act", lambda e: e.activation(out=pt[:, c0:512], in_=sb_[:, c0:512], func=AF.Exp, scale=sc),
                 reads=[sb_], writes=[pt])
            if u["kb"] >= 4 * u["qg"]:
                k.op("pool", lambda e: e.tensor_tensor(out=pt[:, c0:c0 + 128], in0=pt[:, c0:c0 + 128], in1=m_ge,
                                                       op=ALU.mult), reads=[pt, cb], writes=[pt])

        def F(u):
            h, qg, kb, c0, pt = u["h"], u["qg"], u["kb"], u["c0"], u["pt"]
            if kb == 0:
                grp[(h, qg)] = Ob.next()
            ob = grp[(h, qg)]
            k.op("pe", lambda e: e.matmul(out=ob[:, c0:512], lhsT=vB[:, kb, h * 65:(h + 1) * 65], rhs=pt[:, c0:512],
                                          start=(kb == 0), stop=u["last"]), reads=[vB, pt], writes=[ob])
            if u["last"]:
                osb_ = Osb.next()
                cast("act", osb_[:, :], ob[:, :], [ob], [osb_])
                ptot = pTo.next()
                pto = ptot[:, 0:260].rearrange("p (j d) -> p j d", d=65)
                for j in range(4):
                    k.op("pe", lambda e, j=j: e.transpose(out=pto[:, j, :], in_=osb_[:, j * 128:(j + 1) * 128],
                                                          identity=identf[0:65, 0:65]), reads=[osb_, cf], writes=[ptot])
                rd = rden.next()
                k.op("dve", lambda e: e.reciprocal(out=rd[:, :], in_=pto[:, :, 64]), reads=[ptot], writes=[rd])
                k.op("dve", lambda e: e.tensor_tensor(out=oB_sb[:, 4 * qg:4 * qg + 4, h * 64:(h + 1) * 64], in0=pto[:, :, 0:64],
                                                      in1=bc(rd[:, :], 2, [128, 4, 64]), op=ALU.mult),
                     reads=[ptot, rd], writes=[oB_sb])

        N = len(units)
        A(units[0])
        for t in range(N + 1):
            if t + 1 < N:
                A(units[t + 1])
            if t >= 1:
                F(units[t - 1])
            if t < N:
                B(units[t])
            if pg is not None and t % 8 == 4:
                try:
                    next(pg)
                except StopIteration:
                    pg = None
        if pg is not None:
            for _ in pg:
                pass
        k.store("sp", oB_sb, oB.rearrange("(t p) c -> p t c", p=128), oB_sb[:])
        ph.end()

    def phase4():
        ph = Phase(k)
        qTs = ph.rot(2, [64, S], BF16, "cqT")
        kTs = ph.rot(2, [64, S], BF16, "ckT")
        vC = ph.sb([128, NT, 512], BF16, "vCall")
        k.load("sp", vC, vC[:], Cv.rearrange("(t p) c -> p t c", p=128))
        oC_sb = ph.sb([128, NT, 512], BF16, "oC_sb")
        Zb = ph.rot(5, [128, 512], F32, "Zb", psum=True)
        Nb = ph.rot(2, [128, 512], F32, "Nb", psum=True)
        es = ph.rot(2, [128, 512], F32, "e_")
        sps = ph.rot(4, [128, 512], BF16, "sp_")
        Es = ph.rot(3, [128, 512], BF16, "E_")
        gts = ph.rot(2, [128, 4], F32, "g_")
        accs = ph.rot(2, [128, 4, 64], F32, "acc")
        heads = {}

        def load_head(h):
            qT = qTs.next()
            kT = kTs.next()
            k.load("sp", qT, qT[:], CqT[h * 64:(h + 1) * 64, :])
            k.load("sp", kT, kT[:], CkT[h * 64:(h + 1) * 64, :])
            heads[h] = (qT, kT)

        units = []
        for h in range(8):
            for qg in range(NQG):
                for kb in range(4 * qg + 4):
                    j0 = max(0, kb - 4 * qg)
                    units.append({"h": h, "qg": qg, "kb": kb, "j0": j0, "c0": j0 * 128, "diag": kb >= 4 * qg,
                                  "last": kb == 4 * qg + 3})
        grp = {}

        def A(u):
            h, qg, kb, c0 = u["h"], u["qg"], u["kb"], u["c0"]
            if h not in heads:
                load_head(h)
            if kb == 0 and qg == 0 and h + 1 < 8 and (h + 1) not in heads:
                load_head(h + 1)
            qT, kT = heads[h]
            zb = Zb.next()
            u["zb"] = zb
            k.op("pe", lambda e: e.matmul(out=zb[:, c0:512], lhsT=kT[:, kb * 128:(kb + 1) * 128],
                                          rhs=qT[:, qg * 512 + c0:(qg + 1) * 512], start=True, stop=True),
                 reads=[kT, qT], writes=[zb])

        def B(u):
            c0, zb = u["c0"], u["zb"]
            ee = es.next()
            k.op("act", lambda e: e.activation(out=ee[:, c0:512], in_=zb[:, c0:512], func=AF.Exp), reads=[zb], writes=[ee])
            sp = sps.next()
            u["sp"] = sp
            k.op("act", lambda e: e.activation(out=sp[:, c0:512], in_=ee[:, c0:512], func=AF.Ln, bias=1.0),
                 reads=[ee], writes=[sp])
            if u["diag"]:
                k.op("pool", lambda e: e.tensor_tensor(out=sp[:, c0:c0 + 128], in0=sp[:, c0:c0 + 128], in1=m_gt,
                                                       op=ALU.mult), reads=[sp, cb], writes=[sp])

        def C(u):
            c0, zb, sp = u["c0"], u["zb"], u["sp"]
            k.op("pe", lambda e: e.matmul(out=zb[:, c0:512], lhsT=negU, rhs=sp[:, c0:512], start=False, stop=True,
                                          skip_group_check=True), reads=[cb, sp], writes=[zb])

        def Dd(u):
            c0, zb = u["c0"], u["zb"]
            E = Es.next()
            u["E"] = E
            k.op("act", lambda e: e.activation(out=E[:, c0:512], in_=zb[:, c0:512], func=AF.Exp), reads=[zb], writes=[E])
            if u["diag"]:
                k.op("pool", lambda e: e.tensor_tensor(out=E[:, c0:c0 + 128], in0=E[:, c0:c0 + 128], in1=m_gt,
                                                       op=ALU.mult), reads=[E, cb], writes=[E])

        def F(u):
            h, kb, j0, E, sp = u["h"], u["kb"], u["j0"], u["E"], u["sp"]
            nbt = Nb.next()
            u["nbt"] = nbt
            nbk = nbt[:, 0:260].rearrange("p (j d) -> p j d", d=65)
            for j in range(j0, 4):
                k.op("pe", lambda e, j=j: e.matmul(out=nbk[:, j, 0:64], lhsT=E[:, j * 128:(j + 1) * 128],
                                                   rhs=vC[:, kb, h * 64:(h + 1) * 64], start=True, stop=True,
                                                   skip_group_check=True), reads=[E, vC], writes=[nbt])
                k.op("pe", lambda e, j=j: e.matmul(out=nbk[:, j, 64:65], lhsT=sp[:, j * 128:(j + 1) * 128],
                                                   rhs=onesb[:, 0:1], start=True, stop=True,
                                                   skip_group_check=True), reads=[sp, onesb], writes=[nbt])

        def G(u):
            h, qg, kb, j0, nbt = u["h"], u["qg"], u["kb"], u["j0"], u["nbt"]
            nbk = nbt[:, 0:260].rearrange("p (j d) -> p j d", d=65)
            if kb == 0:
                acc = accs.next()
                grp[(h, qg)] = acc
                k.op("dve", lambda e: e.tensor_copy(out=acc[:], in_=nbk[:, :, 0:64]), reads=[nbt], writes=[acc])
            else:
                acc = grp[(h, qg)]
                gt = gts.next()
                k.op("act", lambda e: e.activation(out=gt[:, j0:4], in_=nbk[:, j0:4, 64], func=AF.Exp, scale=-1.0),
                     reads=[nbt], writes=[gt])
                for j in range(j0, 4):
                    k.op("dve", lambda e, j=j: e.scalar_tensor_tensor(out=acc[:, j, :], in0=acc[:, j, :], scalar=gt[:, j:j + 1],
                                                                      in1=nbk[:, j, 0:64], op0=ALU.mult, op1=ALU.add),
                         reads=[acc, gt, nbt], writes=[acc])
            if u["last"]:
                k.op("pool", lambda e: e.tensor_copy(out=oC_sb[:, 4 * qg:4 * qg + 4, h * 64:(h + 1) * 64], in_=acc[:]),
                     reads=[acc], writes=[oC_sb])

        N = len(units)
        A(units[0])
        for t in range(N + 2):
            if t + 1 < N:
                A(units[t + 1])
            if t < N:
                B(units[t])
            if 1 <= t <= N:
                C(units[t - 1])
                Dd(units[t - 1])
            if t >= 2:
                F(units[t - 2])
                G(units[t - 2])
        k.store("sp", oC_sb, oC.rearrange("(t p) c -> p t c", p=128), oC_sb[:])
        ph.end()

    def phase5(l, s, xsrc):
        ph = Phase(k)
        Wbr = ph.sb([128, 10, D], BF16, "Wbr")
        Wo = ph.sb([128, 8, D], BF16, "Wo")
        k.load("sp", Wbr, Wbr[:], Wb_br[l].rearrange("(c p) n -> p c n", p=128))
        k.load("sp", Wo, Wo[:], Wb_out[l].rearrange("(c p) n -> p c n", p=128))
        xts = ph.rot(3, [128, D], F32, "xt")
        a3s = ph.rot(3, [128, 3, 260], F32, "a3")
        ocs = ph.rot(3, [128, 1280], BF16, "ocat")
        gts = ph.rot(3, [128, 3072], BF16, "gt")
        rds = ph.rot(2, [128, 4], F32, "rd")
        pT = ph.ps([128, 16, 128], BF16, "pT")
        pTm = ph.ps([128, 8, 128], BF16, "pTm")
        oTs = ph.rot(2, [128, 10, 128], BF16, "oT")
        PP = ph.rot(2, [128, D], F32, "PP", psum=True)
        tf = ph.sb([128, D], F32, "tf")
        uf = ph.sb([128, D], F32, "uf")
        u2 = ph.sb([128, D], F32, "u2")
        mbfs = ph.rot(2, [128, D], BF16, "mbf")
        mTs = ph.rot(2, [128, 8, 128], BF16, "mT")
        xo = ph.rot(2, [128, D], F32, "xo")
        st = {}

        def L(tt):
            t0 = tt * 128
            xt, a3, oc, gt = xts.next(), a3s.next(), ocs.next(), gts.next()
            k.load("sp", xt, xt[:], xsrc[t0:t0 + 128, :])
            k.load("sp", a3, a3[:], accA[:, t0:t0 + 128, :].rearrange("g p c -> p g c"))
            k.load("sp", oc, oc[:, 256:768], oB[t0:t0 + 128, :])
            k.load("sp", oc, oc[:, 768:1280], oC[t0:t0 + 128, :])
            k.load("sp", gt, gt[:], Gt[t0:t0 + 128, :])
            st[tt] = {"xt": xt, "a3": a3, "oc": oc, "gt": gt}

        def branch(P, oT, ca, cbn):
            for half in range(2):
                for c in range(ca, cbn):
                    k.op("pe", lambda e, c=c, half=half: e.matmul(
                        out=P[:, half * 512:(half + 1) * 512], lhsT=oT[:, c, :], rhs=Wbr[:, c, half * 512:(half + 1) * 512],
                        start=(c == ca), stop=(c == cbn - 1)), reads=[oT, Wbr], writes=[P])

        def S1(tt):
            d = st[tt]
            a3, oc, gt = d["a3"], d["oc"], d["gt"]
            k.op("pool", lambda e: e.tensor_tensor(out=a3[:, 0, :], in0=a3[:, 0, :], in1=a3[:, 1, :], op=ALU.add),
                 reads=[a3], writes=[a3])
            k.op("pool", lambda e: e.tensor_tensor(out=a3[:, 0, :], in0=a3[:, 0, :], in1=a3[:, 2, :], op=ALU.add),
                 reads=[a3], writes=[a3])
            n4 = a3[:, 0, :].rearrange("p (h d) -> p h d", d=65)
            rd = rds.next()
            k.op("dve", lambda e: e.reciprocal(out=rd[:, :], in_=n4[:, :, 64]), reads=[a3], writes=[rd])
            k.op("dve", lambda e: e.tensor_tensor(out=oc[:, 0:256].rearrange("p (h d) -> p h d", d=64), in0=n4[:, :, 0:64],
                                                  in1=bc(rd[:, :], 2, [128, 4, 64]), op=ALU.mult), reads=[a3, rd], writes=[oc])
            for j in range(10):
                k.op("pe", lambda e, j=j: e.transpose(out=pT[:, j, :], in_=oc[:, j * 128:(j + 1) * 128], identity=identb),
                     reads=[oc, cb], writes=[pT])
            oT = oTs.next()
            cast("act", oT[:], pT[:, 0:10, :], [pT], [oT])
            Pa = PP.next()
            branch(Pa, oT, 0, 2)
            k.op("dve", lambda e: e.tensor_tensor(out=tf[:], in0=Pa[:], in1=gt[:, 0:1024], op=ALU.mult),
                 reads=[Pa, gt], writes=[tf])
            Pb = PP.next()
            branch(Pb, oT, 2, 6)
            k.op("dve", lambda e: e.tensor_tensor(out=uf[:], in0=Pb[:], in1=gt[:, 1024:2048], op=ALU.mult),
                 reads=[Pb, gt], writes=[uf])
            k.op("pool", lambda e: e.tensor_tensor(out=tf[:], in0=tf[:], in1=uf[:], op=ALU.add), reads=[tf, uf], writes=[tf])
            Pc = PP.next()
            branch(Pc, oT, 6, 10)
            k.op("dve", lambda e: e.tensor_tensor(out=u2[:], in0=Pc[:], in1=gt[:, 2048:3072], op=ALU.mult),
                 reads=[Pc, gt], writes=[u2])
            mbf = mbfs.next()
            k.op("pool", lambda e: e.tensor_tensor(out=mbf[:], in0=tf[:], in1=u2[:], op=ALU.add), reads=[tf, u2], writes=[mbf])
            d["mbf"] = mbf

        def S2(tt):
            t0 = tt * 128
            d = st.pop(tt)
            mbf, xt = d["mbf"], d["xt"]
            for j in range(8):
                k.op("pe", lambda e, j=j: e.transpose(out=pTm[:, j, :], in_=mbf[:, j * 128:(j + 1) * 128], identity=identb),
                     reads=[mbf, cb], writes=[pTm])
            mT = mTs.next()
            cast("act", mT[:], pTm[:], [pTm], [mT])
            P = PP.next()
            for half in range(2):
                for c in range(8):
                    k.op("pe", lambda e, c=c, half=half: e.matmul(
                        out=P[:, half * 512:(half + 1) * 512], lhsT=mT[:, c, :], rhs=Wo[:, c, half * 512:(half + 1) * 512],
                        start=(c == 0), stop=(c == 7)), reads=[mT, Wo], writes=[P])
            o = xo.next()
            k.op("dve", lambda e: e.tensor_tensor(out=o[:], in0=P[:], in1=xt[:], op=ALU.add), reads=[P, xt], writes=[o])
            k.store("sp", o, xmid[s][t0:t0 + 128, :], o[:])

        L(0)
        L(1)
        S1(0)
        for tt in range(NT):
            if tt + 2 < NT:
                L(tt + 2)
            if tt + 1 < NT:
                S1(tt + 1)
            S2(tt)
        ph.end()

    def phase6(l, dsts):
        ph = Phase(k)
        W1 = ph.sb([128, 8, DFF], BF16, "W1")
        W2 = ph.sb([128, 32, D], BF16, "W2")
        for c in range(8):
            k.load("sp", W1, W1[:, c, :], Wb_ff1[l][c * 128:(c + 1) * 128, :])
        for c in range(4):
            k.load("sp", W2, W2[:, c * 8:(c + 1) * 8, :],
                   Wb_ff2[l][c * 1024:(c + 1) * 1024, :].rearrange("(c p) n -> p c n", p=128))
        r = norm_res(ph)
        xts = ph.rot(4, [128, D], F32, "xt")
        h2T = ph.rot(2, [128, 8, 256], BF16, "h2T")
        h1T = ph.sb([128, 32, 256], BF16, "h1T")
        rl = ph.rot(2, [128, 2, 256], F32, "rl")
        pF = ph.rot(2, [128, 2, 256], F32, "pF", psum=True)
        pY = ph.rot(3, [128, 512], F32, "pY", psum=True)
        xo = ph.rot(1, [128, D], F32, "xo")
        groups = [(s, tg) for s in range(nS) for tg in range(NT // 2)]
        gstate = {}

        def Ng(gi):
            s, tg = groups[gi]
            hT = h2T.next()
            xt2 = []
            for i in range(2):
                t0 = (tg * 2 + i) * 128
                xt = xts.next()
                k.load("sp", xt, xt[:], xmid[s][t0:t0 + 128, :])
                r["hTtile"] = hT
                pend.append(norm_transpose(ph, r, None, xt, hT[:, :, i * 128:(i + 1) * 128], defer=True))
                xt2.append(xt)
            gstate[gi] = (hT, xt2)

        pend = []
        Ng(0)
        for f in pend:
            f()
        pend.clear()
        for gi, (s, tg) in enumerate(groups):
            if True:
                if gi + 1 < len(groups):
                    Ng(gi + 1)
                hT, xt2 = gstate.pop(gi)
                for f2 in range(16):
                    p = pF.next()
                    for fi in range(2):
                        f = f2 * 2 + fi
                        for c in range(8):
                            k.op("pe", lambda e, c=c, f=f, fi=fi: e.matmul(
                                out=p[:, fi, :], lhsT=W1[:, c, f * 128:(f + 1) * 128], rhs=hT[:, c, :],
                                start=(c == 0), stop=(c == 7)), reads=[W1, hT], writes=[p])
                    rr = rl.next()
                    k.op("act", lambda e: e.activation(out=rr[:], in_=p[:], func=AF.Relu), reads=[p], writes=[rr])
                    k.op("pool", lambda e: e.tensor_tensor(out=h1T[:, f2 * 2:f2 * 2 + 2, :], in0=rr[:], in1=rr[:], op=ALU.mult),
                         reads=[rr], writes=[h1T])
                for f in pend:
                    f()
                pend.clear()
                for i in range(2):
                    t0 = (tg * 2 + i) * 128
                    o = xo.next()
                    for half in range(2):
                        p = pY.next()
                        for f in range(32):
                            k.op("pe", lambda e, f=f: e.matmul(out=p[:], lhsT=h1T[:, f, i * 128:(i + 1) * 128],
                                                               rhs=W2[:, f, half * 512:(half + 1) * 512],
                                                               start=(f == 0), stop=(f == 31)), reads=[h1T, W2], writes=[p])
                        k.op("dve", lambda e: e.tensor_tensor(out=o[:, half * 512:(half + 1) * 512], in0=p[:],
                                                              in1=xt2[i][:, half * 512:(half + 1) * 512], op=ALU.add),
                             reads=[p, xt2[i]], writes=[o])
                    k.store("sp", o, dsts[s][t0:t0 + 128, :], o[:])
        ph.end()

    U = UPTO
    prep_weights(0)
    if U < 4:
        for l in range(1, nL):
            prep_weights(l)
    for l in range(nL):
        for s in range(nS):
            xsrc = x_in[s] if l == 0 else xres[s]
            if U >= 1:
                phase1(l, s, xsrc, 0)
            if U >= 2:
                phase1(l, s, xsrc, 1)
            if U >= 3:
                phase2()
            if U >= 4:
                phase3(prep_l=(l + 1) if (s == 0 and l + 1 < nL) else None)
            if U >= 5:
                phase4()
            if U >= 6:
                phase5(l, s, xsrc)
        dsts = [out[s] if l == nL - 1 else xres[s] for s in range(nS)]
        if U >= 7:
            phase6(l, dsts)
    k.barrier()
    print("sbuf max bytes/partition:", k.sb_max, "instructions:", k.ninstr, {e: k.cnt[e] for e in k.cnt})
    k.close()
    return nc


def host_consts(S):
    NT = S // 128
    p = np.arange(128)[:, None]
    c = np.arange(128)[None, :]
    consts = np.zeros((128, 5, 128), np.float32)
    consts[:, 0, :] = (p == c)
    consts[:, 1, :] = (c >= p)
    consts[:, 2, :] = (c > p)
    consts[:, 3, :] = (c <= p)
    consts[:, 4, :] = -1.0 * (p >= c)

    def tables(dim):
        pos = np.arange(S, dtype=np.float32)
        inv = (np.float32(500000.0) ** (-np.arange(0, dim, 2, dtype=np.float32) / np.float32(dim))).astype(np.float32)
        ang = (pos[:, None] * inv[None, :]).astype(np.float32)
        t = np.concatenate([np.cos(ang), np.sin(ang)], axis=1).astype(np.float32)
        return np.ascontiguousarray(t.reshape(NT, 128, dim).transpose(1, 0, 2))
    return consts, tables(16), tables(32)


def make_inputs(x_sh, l0, l1, S, attn_norm, w_in, a_q_norm, a_k_norm, b_q_a_norm, w_q_b, b_kv_a_norm,
                w_kv_b, b_q_norm, b_k_norm, w_branch, w_out, mlp_norm, w_ff1, w_ff2):
    nL = l1 - l0
    sl = slice(l0, l1)
    f = lambda a: np.ascontiguousarray(np.asarray(a, dtype=np.float32))
    col = lambda g, c: np.ascontiguousarray(np.asarray(g, np.float32)[sl].reshape(nL, c, 128).transpose(0, 2, 1))
    rep = lambda a, b: np.ascontiguousarray(np.broadcast_to(
        np.concatenate([np.asarray(a, np.float32)[sl], np.asarray(b, np.float32)[sl]], axis=1)[:, None, :],
        (nL, 128, a.shape[1] + b.shape[1])))
    consts, rA, rM = host_consts(S)
    return {
        "x": f(x_sh), "w_in": f(w_in[sl]), "w_q_b": f(np.asarray(w_q_b)[sl].reshape(nL, 256, 768)),
        "w_kv_b": f(np.asarray(w_kv_b)[sl].reshape(nL, 128, 1024)), "w_branch": f(w_branch[sl]),
        "w_out": f(w_out[sl]), "w_ff1": f(w_ff1[sl]), "w_ff2": f(w_ff2[sl]),
        "g_attn": col(attn_norm, 8), "g_mlp": col(mlp_norm, 8), "g_qa": col(b_q_a_norm, 2),
        "g_kva": col(b_kv_a_norm, 1), "g_aqk": rep(a_q_norm, a_k_norm), "g_bqk": rep(b_q_norm, b_k_norm),
        "ropeA": rA, "ropeM": rM, "consts": consts,
    }


def kernel(x, attn_norm, w_in, a_q_norm, a_k_norm, b_q_a_norm, w_q_b, b_kv_a_norm,
           w_kv_b, b_q_norm, b_k_norm, w_branch, w_out, mlp_norm, w_ff1, w_ff2):
    x = np.asarray(x, dtype=np.float32)
    B, S, _ = x.shape
    nL = np.asarray(attn_norm).shape[0]
    ncores = 8
    nS = B // ncores
    nc = build(nL, nS, S)
    in_maps = []
    for c in range(ncores):
        in_maps.append(make_inputs(x[c * nS:(c + 1) * nS], 0, nL, S, attn_norm, w_in, a_q_norm, a_k_norm, b_q_a_norm,
                                   w_q_b, b_kv_a_norm, w_kv_b, b_q_norm, b_k_norm, w_branch, w_out, mlp_norm,
                                   w_ff1, w_ff2))
    res = run_bass_kernel_spmd(nc, in_maps, core_ids=list(range(ncores)))
    return np.concatenate([np.asarray(r["out"], dtype=np.float32) for r in res.results], axis=0)
```

```python
import contextlib
import numpy as np
import concourse.bass as bass
import concourse.mybir as mybir
from concourse.alu_op_type import AluOpType as ALU
from concourse.bass_utils import run_bass_kernel_spmd

AF = mybir.ActivationFunctionType
AX = mybir.AxisListType
F32 = mybir.dt.float32
BF16 = mybir.dt.bfloat16

SAME_ENG_SYNC = True
UPTO = 99
STAGE1 = 99
TLIM = 9999
EPS = 1e-6
D = 1024
INC = 7328
DFF = 4096


class T:
    _n = 0

    def __init__(self, t, name=None):
        self.t = t
        T._n += 1
        self.id = "t%d" % T._n
        self.name = name or self.id
        self.w = None
        self.r = {}
        self.dsem = None
        self.dval = 0
        self.psum = False

    def __getitem__(self, idx):
        return self.t[idx]


class Rot:
    def __init__(self, tiles):
        self.tiles = tiles
        self.i = 0

    def next(self):
        t = self.tiles[self.i % len(self.tiles)]
        self.i += 1
        return t


class K:
    def __init__(self, nc):
        self.nc = nc
        self.es = contextlib.ExitStack()
        self.eng = {"pe": nc.tensor, "act": nc.scalar, "dve": nc.vector,
                    "pool": nc.gpsimd, "sp": nc.sync}
        self.sem = {}
        self.cnt = {}
        self.seen = {}
        for e in self.eng:
            self.sem[e] = self.es.enter_context(nc.semaphore("s_" + e))
            self.cnt[e] = 0
            self.seen[e] = {}
        self.semof = dict(self.sem)
        self.dtiles = []
        self.dsem_pool = []
        self.ninstr = 0
        self.uid = 0

    def sb(self, shape, dtype, name, stack=None):
        st = stack if stack is not None else self.es
        self.uid += 1
        nb = int(np.prod(shape[1:])) * (2 if dtype == BF16 else 4)
        self.sb_bytes = getattr(self, "sb_bytes", 0) + nb
        self.sb_max = max(getattr(self, "sb_max", 0), self.sb_bytes)
        st.callback(self._free, nb)
        t = st.enter_context(self.nc.sbuf_tensor("%s_%d" % (name, self.uid), list(shape), dtype))
        return T(t, name)

    def ps(self, shape, dtype, name, stack=None):
        st = stack if stack is not None else self.es
        self.uid += 1
        t = st.enter_context(self.nc.psum_tensor("%s_%d" % (name, self.uid), list(shape), dtype))
        tt = T(t, name)
        tt.psum = True
        return tt

    def rot(self, n, shape, dtype, name, stack=None, psum=False):
        f = self.ps if psum else self.sb
        return Rot([f(shape, dtype, "%s%d" % (name, i), stack) for i in range(n)])

    def _free(self, nb):
        self.sb_bytes -= nb

    def _wait(self, e, deps):
        eng = self.eng[e]
        seen = self.seen[e]
        for key, val in sorted(deps, key=lambda d: str(d[0])):
            if key == e and (e in ("pe", "sp") or not SAME_ENG_SYNC):
                continue
            if seen.get(key, 0) >= val:
                continue
            eng.wait_ge(self.semof[key], val)
            seen[key] = val
            self.ninstr += 1

    def _deps(self, reads, writes):
        deps = set()
        for t in reads:
            if t.w:
                deps.add(t.w)
            if t.psum:
                for d in t.r.values():
                    deps.add(d)
        for t in writes:
            if t.w:
                deps.add(t.w)
            for d in t.r.values():
                deps.add(d)
        return deps

    def op(self, e, fn, reads=(), writes=()):
        self._wait(e, self._deps(reads, writes))
        ins = fn(self.eng[e])
        self.cnt[e] += 1
        ins.then_inc(self.sem[e], 1)
        self.ninstr += 1
        me = (e, self.cnt[e])
        for t in reads:
            t.r[e] = me
        for t in writes:
            t.w = me
            t.r = {}
        return ins

    def _dsem(self, tile):
        if tile.dsem is None:
            if self.dsem_pool:
                key, sem, val = self.dsem_pool.pop()
                tile.dkey, tile.dsem, tile.dval = key, sem, val
            else:
                tile.dkey = "d" + tile.id
                tile.dsem = self.es.enter_context(self.nc.semaphore(tile.dkey))
                self.semof[tile.dkey] = tile.dsem
            self.dtiles.append(tile)

    def load(self, q, tile, out_ap, in_ap, **kw):
        self._dsem(tile)
        self._wait(q, self._deps((), (tile,)))
        ins = self.eng[q].dma_start(out=out_ap, in_=in_ap, **kw)
        tile.dval += 16
        ins.then_inc(tile.dsem, 16)
        self.ninstr += 1
        tile.w = (tile.dkey, tile.dval)
        tile.r = {}

    def store(self, q, tile, out_ap, in_ap, **kw):
        self._dsem(tile)
        self._wait(q, self._deps((tile,), ()))
        ins = self.eng[q].dma_start(out=out_ap, in_=in_ap, **kw)
        tile.dval += 16
        ins.then_inc(tile.dsem, 16)
        self.ninstr += 1
        tile.r["dma"] = (tile.dkey, tile.dval)

    def barrier(self, release=True):
        deps = set()
        for e in self.eng:
            if self.cnt[e]:
                deps.add((e, self.cnt[e]))
        for t in self.dtiles:
            if t.dval:
                deps.add((t.dkey, t.dval))
        for e in self.eng:
            self._wait(e, deps)

    def release(self, tiles):
        for t in tiles:
            if t.dsem is not None:
                self.dsem_pool.append((t.dkey, t.dsem, t.dval))
                self.dtiles.remove(t)
                t.dsem = None

    def close(self):
        self.es.close()


class Phase:
    def __init__(self, k):
        self.k = k
        self.st = contextlib.ExitStack()
        self.tiles = []

    def sb(self, shape, dtype, name):
        t = self.k.sb(shape, dtype, name, self.st)
        self.tiles.append(t)
        return t

    def ps(self, shape, dtype, name):
        t = self.k.ps(shape, dtype, name, self.st)
        self.tiles.append(t)
        return t

    def rot(self, n, shape, dtype, name, psum=False):
        f = self.ps if psum else self.sb
        return Rot([f(shape, dtype, "%s%d" % (name, i)) for i in range(n)])

    def end(self):
        self.k.barrier()
        self.k.release(self.tiles)
        self.st.close()


def bc(ap, axis, shape):
    return ap.unsqueeze(axis).broadcast_to(list(shape))


def build(nL, nS, S, debug=False):
    nc = bass.Bass("TRN2", target_bir_lowering=False)
    NT = S // 128
    NQG = S // 512

    def din(name, shape, dt=F32):
        return nc.dram_tensor(name, list(shape), dt, kind="ExternalInput").ap()

    def dscr(name, shape, dt):
        if debug:
            return nc.dram_tensor(name, list(shape), dt, kind="ExternalOutput").ap()
        return nc.dram_tensor(name, list(shape), dt).ap()

    x_in = din("x", [nS, S, D])
    w_in = din("w_in", [nL, D, INC])
    w_qb = din("w_q_b", [nL, 256, 768])
    w_kvb = din("w_kv_b", [nL, 128, 1024])
    w_br = din("w_branch", [nL, 1280, D])
    w_out = din("w_out", [nL, D, D])
    w_ff1 = din("w_ff1", [nL, D, DFF])
    w_ff2 = din("w_ff2", [nL, DFF, D])
    g_attn = din("g_attn", [nL, 128, 8])
    g_mlp = din("g_mlp", [nL, 128, 8])
    g_qa = din("g_qa", [nL, 128, 2])
    g_kva = din("g_kva", [nL, 128, 1])
    g_aqk = din("g_aqk", [nL, 128, 128])
    g_bqk = din("g_bqk", [nL, 128, 192])
    ropeA = din("ropeA", [128, NT, 16])
    ropeM = din("ropeM", [128, NT, 32])
    consts = din("consts", [128, 5, 128])
    out = nc.dram_tensor("out", [nS, S, D], F32, kind="ExternalOutput").ap()

    Wb_in = dscr("Wb_in", [nL, D, INC], BF16)
    Wb_qb = dscr("Wb_qb", [nL, 256, 768], BF16)
    Wb_kvb = dscr("Wb_kvb", [nL, 128, 1024], BF16)
    Wb_br = dscr("Wb_br", [nL, 1280, D], BF16)
    Wb_out = dscr("Wb_out", [nL, D, D], BF16)
    Wb_ff1 = dscr("Wb_ff1", [nL, D, DFF], BF16)
    Wb_ff2 = dscr("Wb_ff2", [nL, DFF, D], BF16)
    xres = dscr("xres", [nS, S, D], F32)
    xmid = dscr("xmid", [nS, S, D], F32)
    AqT = dscr("AqT", [768, S], BF16)
    AkT = dscr("AkT", [768, S], BF16)
    Av = dscr("Av", [S, 780], BF16)
    BqT = dscr("BqT", [768, S], BF16)
    BkT = dscr("BkT", [768, S], BF16)
    Bv = dscr("Bv", [S, 520], BF16)
    CqT = dscr("CqT", [512, S], BF16)
    CkT = dscr("CkT", [512, S], BF16)
    Cv = dscr("Cv", [S, 512], BF16)
    Gt = dscr("Gt", [S, 3072], BF16)
    accA = dscr("accA", [3, S, 260], F32)
    oB = dscr("oB", [S, 512], BF16)
    oC = dscr("oC", [S, 512], BF16)

    k = K(nc)
    cf = k.sb([128, 5, 128], F32, "cf")
    cb = k.sb([128, 5, 128], BF16, "cb")
    k.load("sp", cf, cf[:], consts)
    k.op("dve", lambda e: e.tensor_copy(out=cb[:], in_=cf[:]), reads=[cf], writes=[cb])
    identb = cb[:, 0, :]
    identf = cf[:, 0, :]
    m_ge = cb[:, 1, :]
    m_gt = cb[:, 2, :]
    m_le = cb[:, 3, :]
    negU = cb[:, 4, :]
    onesb = k.sb([128, 1], BF16, "onesb")
    k.op("dve", lambda e: e.memset(onesb[:], 1.0), writes=[onesb])
    rA = k.sb([128, NT, 16], F32, "rA")
    rM = k.sb([128, NT, 32], F32, "rM")
    k.load("sp", rA, rA[:], ropeA)
    k.load("sp", rM, rM[:], ropeM)

    neghalf = k.sb([128, 24], F32, "neghalf")
    k.op("dve", lambda e: e.memset(neghalf[:], -0.5), writes=[neghalf])

    def rsq(rs_ap, ss_ap, n, inv_n, rs_t, ss_t):
        k.op("dve", lambda e: e.tensor_scalar(out=rs_ap, in0=ss_ap, scalar1=inv_n, scalar2=EPS, op0=ALU.mult, op1=ALU.add),
             reads=[ss_t], writes=[rs_t])
        k.op("pool", lambda e: e.tensor_tensor(out=rs_ap, in0=rs_ap, in1=neghalf[:, 0:n], op=ALU.pow),
             reads=[rs_t, neghalf], writes=[rs_t])

    ceng = Rot(["dve", "act", "pool"])

    def cast(e, out_ap, in_ap, reads, writes, scale=None):
        if e == "act":
            if scale is None:
                k.op("act", lambda g: g.activation(out=out_ap, in_=in_ap, func=AF.Copy), reads=reads, writes=writes)
            else:
                k.op("act", lambda g: g.activation(out=out_ap, in_=in_ap, func=AF.Copy, scale=scale),
                     reads=reads, writes=writes)
        else:
            if scale is None:
                k.op(e, lambda g: g.tensor_copy(out=out_ap, in_=in_ap), reads=reads, writes=writes)
            else:
                k.op(e, lambda g: g.tensor_scalar(out=out_ap, in0=in_ap, scalar1=scale, scalar2=None, op0=ALU.mult),
                     reads=reads, writes=writes)

    gcol = k.sb([128, nL, 19], F32, "gcol")
    for l in range(nL):
        k.load("sp", gcol, gcol[:, l, 0:8], g_attn[l])
        k.load("sp", gcol, gcol[:, l, 8:16], g_mlp[l])
        k.load("sp", gcol, gcol[:, l, 16:18], g_qa[l])
        k.load("sp", gcol, gcol[:, l, 18:19], g_kva[l])
    CB = 2048

    def prep_gen(ph, plist, engines):
        wf = ph.rot(4, [128, CB], F32, "wf")
        wb = ph.rot(3, [128, CB], BF16, "wb")
        items = []
        for (l, part) in plist:
            first = [(w_in[l], Wb_in[l], D, INC, 0), (w_qb[l], Wb_qb[l], 256, 768, 16),
                     (w_kvb[l], Wb_kvb[l], 128, 1024, 18)]
            rest = [(w_br[l], Wb_br[l], 1280, D, None),
                    (w_out[l], Wb_out[l], D, D, None), (w_ff1[l], Wb_ff1[l], D, DFF, 8),
                    (w_ff2[l], Wb_ff2[l], DFF, D, None)]
            jobs = first if part == "first" else rest if part == "rest" else first + rest
            for src, dst, R, C, gc in jobs:
                for c in range(R // 128):
                    for c0 in range(0, C, CB):
                        items.append((l, src, dst, c, c0, min(CB, C - c0), gc))
        loaded = {}

        def ld(i):
            l, src, dst, c, c0, n, gc = items[i]
            f = wf.next()
            k.load("sp", f, f[:, 0:n], src[c * 128:(c + 1) * 128, c0:c0 + n])
            loaded[i] = f

        for i in range(min(2, len(items))):
            ld(i)
        for i in range(len(items)):
            if i + 2 < len(items):
                ld(i + 2)
            l, src, dst, c, c0, n, gc = items[i]
            f = loaded.pop(i)
            b = wb.next()
            sc = None if gc is None else gcol[:, l, gc + c:gc + c + 1]
            e = engines.next()
            rd = [f] if gc is None else [f, gcol]
            cast(e, b[:, 0:n], f[:, 0:n], rd, [b], scale=sc)
            k.store("sp", b, dst[c * 128:(c + 1) * 128, c0:c0 + n], b[:, 0:n])
            yield

    def prep_weights(plist):
        ph = Phase(k)
        for _ in prep_gen(ph, plist, ceng):
            pass
        ph.end()

    def norm_transpose(ph, r, xsrc_rows, xt, hT_out, defer=False):
        ss = r["ss"].next()
        rstd = r["rstd"].next()
        junk = r["junk"]
        xn = r["xn"].next()
        k.op("act", lambda e: e.activation(out=junk[:], in_=xt[:], func=AF.Square, accum_out=ss[:]),
             reads=[xt], writes=[junk, ss])
        rsq(rstd[:], ss[:], 1, 1.0 / D, rstd, ss)
        k.op("act", lambda e: e.activation(out=xn[:], in_=xt[:], func=AF.Copy, scale=rstd[:]),
             reads=[xt, rstd], writes=[xn])
        hTt = r["hTtile"]

        def partB():
            pT = r["pT"].next()
            for c in range(8):
                k.op("pe", lambda e, c=c: e.transpose(out=pT[:, c, :], in_=xn[:, c * 128:(c + 1) * 128], identity=identb),
                     reads=[xn, cb], writes=[pT])
            k.op("dve", lambda e: e.tensor_copy(out=hT_out, in_=pT[:]), reads=[pT], writes=[hTt])
        if defer:
            return partB
        partB()

    def norm_res(ph, n=2):
        return {"ss": ph.rot(n, [128, 1], F32, "ss"), "rstd": ph.rot(n, [128, 1], F32, "rstd"),
                "junk": ph.sb([128, D], BF16, "junk"), "xn": ph.rot(2, [128, D], BF16, "xn"),
                "pT": ph.rot(1, [128, 8, 128], BF16, "pT", psum=True)}

    def rms_small(src_ap, nh, hd, reads_t, sq_t, ss_t, rs_t, extra_add=None):
        k.op("act", lambda e: e.activation(out=sq_t[:, 0:nh * hd].rearrange("p (h d) -> p h d", d=hd),
                                           in_=src_ap, func=AF.Square), reads=reads_t, writes=[sq_t])
        k.op("dve", lambda e: e.tensor_reduce(out=ss_t[:, 0:nh], in_=sq_t[:, 0:nh * hd].rearrange("p (h d) -> p h d", d=hd),
                                              axis=AX.X, op=ALU.add), reads=[sq_t], writes=[ss_t])

    def interleave(gens):
        gens = [g for g in gens if g is not None]
        while gens:
            for g in list(gens):
                try:
                    next(g)
                except StopIteration:
                    gens.remove(g)

    def phase1(l, s, xsrc, pas):
        ph = Phase(k)
        if pas == 0:
            c_lo, c_hi = 0, 3744
        else:
            c_lo, c_hi = 3744, INC
        NW = c_hi - c_lo
        segb = [0, 1536, 2720, 3744] if pas == 0 else [0, 1024, 2048, NW]
        Wsegs = []
        for si in range(3):
            a, b = segb[si], segb[si + 1]
            Wt = ph.sb([128, 8, b - a], BF16, "W%d" % si)
            k.load("sp", Wt, Wt[:], Wb_in[l][:, c_lo + a:c_lo + b].rearrange("(c p) n -> p c n", p=128))
            Wsegs.append((a, b, Wt))

        def wsl(c, col0, n):
            for a, b, Wt in Wsegs:
                if a <= col0 and col0 + n <= b:
                    return Wt[:, c, col0 - a:col0 - a + n], Wt
            raise AssertionError("chunk crosses W segment")
        r = norm_res(ph, 3)
        xts = ph.rot(3, [128, D], F32, "xt")
        hTs = ph.rot(2, [128, 8, 512], BF16, "hT4") if pas == 0 else ph.rot(3, [128, 8, 128], BF16, "hT")
        pj = ph.rot(2 if pas == 0 else 6, [128, 512], F32, "pj", psum=True)
        hT_of = {}
        x_of = {}
        if pas == 0:
            gaq = ph.sb([128, 128], F32, "gaq")
            gbq = ph.sb([128, 192], F32, "gbq")
            k.load("sp", gaq, gaq[:], g_aqk[l])
            k.load("sp", gbq, gbq[:], g_bqk[l])
            wqb = ph.sb([128, 2, 768], BF16, "wqb")
            wkvb = ph.sb([128, 1024], BF16, "wkvb")
            k.load("sp", wqb, wqb[:], Wb_qb[l].rearrange("(c p) n -> p c n", p=128))
            k.load("sp", wkvb, wkvb[:], Wb_kvb[l])
            qks = ph.rot(2, [128, 1536], F32, "qk")
            lats = ph.rot(2, [128, 416], F32, "lat")
            sqA = ph.sb([128, 1536], BF16, "sqA")
            ss24 = ph.sb([128, 24], F32, "ss24")
            rs24 = ph.sb([128, 24], F32, "rs24")
            qkb = ph.rot(2, [128, 1536], BF16, "qkb")
            rtA = [ph.sb([128, 24, 8], F32, "rtA%d" % i) for i in range(4)]
            x16 = ph.sb([128, 24, 16], F32, "x16")
            x32 = ph.sb([128, 8, 32], F32, "x32")
            rtK = [ph.sb([128, 1, 16], F32, "rtK%d" % i) for i in range(4)]
            sqK = ph.sb([128, 512], BF16, "sqK")
            ss8k = ph.sb([128, 8], F32, "ss8k")
            rs8k = ph.sb([128, 8], F32, "rs8k")
            rtM = [ph.sb([128, 8, 16], F32, "rtM%d" % i) for i in range(4)]
            vAs = ph.rot(2, [128, 12, 65], BF16, "vA")
            for t in vAs.tiles:
                k.op("dve", lambda e, t=t: e.memset(t[:], 1.0), writes=[t])
            pTA = ph.ps([128, 16, 128], BF16, "pTA")
            ATs = ph.rot(2, [128, 12, 128], BF16, "AT")
            sqM = ph.sb([128, 768], BF16, "sqM")
            ssl = ph.sb([128, 3], F32, "ssl")
            rsl = ph.sb([128, 3], F32, "rsl")
            latn = ph.sb([128, 384], BF16, "latn")
            pT2 = ph.ps([128, 8, 128], BF16, "pT2")
            latT = ph.sb([128, 3, 128], BF16, "latT")
            pM = ph.ps([128, 1024], F32, "pM")
            bq = ph.sb([128, 768], F32, "bq")
            bk = ph.sb([128, 8, 96], F32, "bk")
            ss8 = ph.sb([128, 8], F32, "ss8")
            rs8 = ph.sb([128, 8], F32, "rs8")
            bqbs = ph.rot(2, [128, 8, 96], BF16, "bqb")
            bkbs = ph.rot(2, [128, 8, 96], BF16, "bkb")
            tails = {}
            krg = ph.sb([128, 32], F32, "krg")
            krr = ph.sb([128, 32], F32, "krr")
            vBs = ph.rot(2, [128, 8, 65], BF16, "vB")
            for t in vBs.tiles:
                k.op("dve", lambda e, t=t: e.memset(t[:], 1.0), writes=[t])
            BTq = ph.rot(2, [96, 8, 128], BF16, "BTq")
            BTk = ph.rot(2, [96, 8, 128], BF16, "BTk")
            CTs = ph.rot(1, [128, 8, 512], BF16, "CT")
            qk_of, lat_of = {}, {}
        else:
            cvs = ph.rot(2, [128, 512], BF16, "cv")
            gts = ph.rot(2, [128, 3072], BF16, "gt")

        def proj_chunk(hTo, col0, n):
            hT, off = hTo
            p = pj.next()
            for c in range(8):
                wap, Wt = wsl(c, col0, n)
                k.op("pe", lambda e, c=c, wap=wap: e.matmul(out=p[:, 0:n], lhsT=hT[:, c, off:off + 128], rhs=wap,
                                                            start=(c == 0), stop=(c == 7)), reads=[hT, Wt], writes=[p])
            return p

        def rope(src, dst, nh, off, half, cs, sn, reads_t, dst_t, tmp):
            x1 = src[:, :, off:off + half]
            x2 = src[:, :, off + half:off + 2 * half]
            cB = bc(cs, 1, [128, nh, half])
            sB = bc(sn, 1, [128, nh, half])
            t1, t2, t3, t4 = [t[:, 0:nh, 0:half] for t in tmp]
            k.op("pool", lambda e: e.tensor_tensor(out=t1, in0=x1, in1=cB, op=ALU.mult), reads=reads_t, writes=[tmp[0]])
            k.op("pool", lambda e: e.tensor_tensor(out=t2, in0=x2, in1=sB, op=ALU.mult), reads=reads_t, writes=[tmp[1]])
            yield
            k.op("pool", lambda e: e.tensor_tensor(out=t3, in0=x2, in1=cB, op=ALU.mult), reads=reads_t, writes=[tmp[2]])
            k.op("pool", lambda e: e.tensor_tensor(out=t4, in0=x1, in1=sB, op=ALU.mult), reads=reads_t, writes=[tmp[3]])
            yield
            k.op("dve", lambda e: e.tensor_tensor(out=dst[:, :, off:off + half], in0=t1, in1=t2, op=ALU.subtract),
                 reads=[tmp[0], tmp[1]], writes=[dst_t])
            k.op("dve", lambda e: e.tensor_tensor(out=dst[:, :, off + half:off + 2 * half], in0=t3, in1=t4, op=ALU.add),
                 reads=[tmp[2], tmp[3]], writes=[dst_t])
            yield

        def Lx(tt):
            t0 = tt * 128
            xt = xts.next()
            k.load("sp", xt, xt[:], xsrc[t0:t0 + 128, :])
            x_of[tt] = xt

        def Nn(tt):
            xt = x_of.pop(tt)
            if pas == 0:
                if tt % 4 == 0:
                    hT_of["cur"] = hTs.next()
                hT = hT_of["cur"]
                off = (tt % 4) * 128
            else:
                hT = hTs.next()
                off = 0
            hT_of[tt] = (hT, off)
            r["hTtile"] = hT
            return norm_transpose(ph, r, None, xt, hT[:, :, off:off + 128], defer=True)

        def P0(tt):
            t0 = tt * 128
            hT = hT_of[tt]
            qk = qks.next()
            lat = lats.next()
            qk_of[tt] = qk
            lat_of[tt] = lat
            for j in range(3):
                p = proj_chunk(hT, j * 512, 512)
                cast("act", qk[:, j * 512:(j + 1) * 512], p[:, :], [p], [qk])
                yield
            p = proj_chunk(hT, 2304, 416)
            cast("dve", lat[:, :], p[:, 0:416], [p], [lat])
            yield
            vA = vAs.next()
            p = proj_chunk(hT, 1536, 512)
            cast("act", vA[:, 0:8, 0:64], p[:, :].rearrange("p (h d) -> p h d", d=64), [p], [vA])
            yield
            p = proj_chunk(hT, 2048, 256)
            cast("dve", vA[:, 8:12, 0:64], p[:, 0:256].rearrange("p (h d) -> p h d", d=64), [p], [vA])
            k.store("sp", vA, Av[t0:t0 + 128, :], vA[:].rearrange("p h d -> p (h d)"))
            yield

        def Cg(g):
            hT = hT_of[4 * g][0]
            CT = CTs.next()
            for j in range(8):
                col0 = 2720 + j * 128
                p = pj.next()
                for c in range(8):
                    wap, Wt = wsl(c, col0, 128)
                    k.op("pe", lambda e, c=c, wap=wap: e.matmul(out=p[:, 0:512], lhsT=wap, rhs=hT[:, c, 0:512],
                                                                start=(c == 0), stop=(c == 7)), reads=[hT, Wt], writes=[p])
                if j < 4:
                    k.op("act", lambda e, p=p, j=j: e.activation(out=CT[:, j, :], in_=p[:, :], func=AF.Copy, scale=0.125),
                         reads=[p], writes=[CT])
                else:
                    cast("dve", CT[:, j, :], p[:, :], [p], [CT])
                yield
            k.store("sp", CT, CqT.rearrange("(j p) s -> p j s", p=128)[:, :, g * 512:(g + 1) * 512], CT[:, 0:4, :])
            k.store("sp", CT, CkT.rearrange("(j p) s -> p j s", p=128)[:, :, g * 512:(g + 1) * 512], CT[:, 4:8, :])
            yield

        def P1(tt):
            t0 = tt * 128
            hT = hT_of[tt]
            cv = cvs.next()
            p = proj_chunk(hT, 0, 512)
            cast("dve", cv[:], p[:], [p], [cv])
            k.store("sp", cv, Cv[t0:t0 + 128, :], cv[:])
            gt = gts.next()
            for j in range(6):
                p = proj_chunk(hT, 512 + j * 512, 512)
                k.op("act", lambda e, p=p, j=j: e.activation(out=gt[:, j * 512:(j + 1) * 512], in_=p[:], func=AF.Sigmoid),
                     reads=[p], writes=[gt])
            k.store("sp", gt, Gt[t0:t0 + 128, :], gt[:])

        def YA(tt):
            t0 = tt * 128
            qk = qk_of[tt]
            qk3 = qk[:, :].rearrange("p (h d) -> p h d", d=64)
            rms_small(qk3, 24, 64, [qk], sqA, ss24, rs24)
            yield
            rsq(rs24[:], ss24[:], 24, 1.0 / 64, rs24, ss24)
            yield
            k.op("dve", lambda e: e.tensor_tensor(out=qk3, in0=qk3, in1=bc(rs24[:, :], 2, [128, 24, 64]), op=ALU.mult),
                 reads=[qk, rs24], writes=[qk])
            yield
            qk4 = qk[:, :].rearrange("p (a h d) -> p a h d", a=2, d=64)
            g4 = bc(gaq[:, :].rearrange("p (a d) -> p a d", a=2), 2, [128, 2, 12, 64])
            qb = qkb.next()
            qb4 = qb[:, :].rearrange("p (a h d) -> p a h d", a=2, d=64)
            k.op("dve", lambda e: e.tensor_tensor(out=qb4, in0=qk4, in1=g4, op=ALU.mult), reads=[qk, gaq], writes=[qb])
            x16v = x16[:, :, :].rearrange("p (a h) d -> p a h d", a=2)
            k.op("pool", lambda e: e.tensor_tensor(out=x16v, in0=qk4[:, :, :, 0:16], in1=g4[:, :, :, 0:16], op=ALU.mult),
                 reads=[qk, gaq], writes=[x16])
            yield
            qb3 = qb[:, :].rearrange("p (h d) -> p h d", d=64)
            yield from rope(x16[:, :, :], qb3, 24, 0, 8, rA[:, tt, 0:8], rA[:, tt, 8:16], [x16, rA], qb, rtA)

            def tailA():
                for j in range(12):
                    k.op("pe", lambda e, j=j: e.transpose(out=pTA[:, j, :], in_=qb[:, j * 128:(j + 1) * 128], identity=identb),
                         reads=[qb, cb], writes=[pTA])
                AT = ATs.next()
                cast("act", AT[:], pTA[:, 0:12, :], [pTA], [AT])
                k.store("sp", AT, AqT.rearrange("(j p) s -> p j s", p=128)[:, :, t0:t0 + 128], AT[:, 0:6, :])
                k.store("sp", AT, AkT.rearrange("(j p) s -> p j s", p=128)[:, :, t0:t0 + 128], AT[:, 6:12, :])
            tails.setdefault(tt, []).append(tailA)

        def YM(tt):
            t0 = tt * 128
            lat = lat_of[tt]
            k.op("act", lambda e: e.activation(out=sqM[:, 0:256], in_=lat[:, 0:256], func=AF.Square, accum_out=ssl[:, 0:1]),
                 reads=[lat], writes=[sqM, ssl])
            k.op("act", lambda e: e.activation(out=sqM[:, 256:384], in_=lat[:, 256:384], func=AF.Square, accum_out=ssl[:, 1:2]),
                 reads=[lat], writes=[sqM, ssl])
            k.op("act", lambda e: e.activation(out=sqM[:, 384:416], in_=lat[:, 384:416], func=AF.Square, accum_out=ssl[:, 2:3]),
                 reads=[lat], writes=[sqM, ssl])
            yield
            k.op("dve", lambda e: e.tensor_scalar(out=rsl[:, 0:1], in0=ssl[:, 0:1], scalar1=1.0 / 256, scalar2=EPS,
                                                  op0=ALU.mult, op1=ALU.add), reads=[ssl], writes=[rsl])
            k.op("dve", lambda e: e.tensor_scalar(out=rsl[:, 1:2], in0=ssl[:, 1:2], scalar1=1.0 / 128, scalar2=EPS,
                                                  op0=ALU.mult, op1=ALU.add), reads=[ssl], writes=[rsl])
            k.op("pool", lambda e: e.tensor_tensor(out=rsl[:, 0:2], in0=rsl[:, 0:2], in1=neghalf[:, 0:2], op=ALU.pow),
                 reads=[rsl, neghalf], writes=[rsl])
            yield
            cast("dve", latn[:, 0:256], lat[:, 0:256], [lat, rsl], [latn], scale=rsl[:, 0:1])
            cast("dve", latn[:, 256:384], lat[:, 256:384], [lat, rsl], [latn], scale=rsl[:, 1:2])
            yield
            for j in range(3):
                k.op("pe", lambda e, j=j: e.transpose(out=pT2[:, j, :], in_=latn[:, j * 128:(j + 1) * 128], identity=identb),
                     reads=[latn, cb], writes=[pT2])
            cast("dve", latT[:], pT2[:, 0:3, :], [pT2], [latT])
            yield
            for (c0, n) in ((0, 512), (512, 256)):
                for c in range(2):
                    k.op("pe", lambda e, c=c, c0=c0, n=n: e.matmul(out=pM[:, c0:c0 + n], lhsT=latT[:, c, :],
                                                                  rhs=wqb[:, c, c0:c0 + n], start=(c == 0), stop=(c == 1)),
                         reads=[latT, wqb], writes=[pM])
            cast("act", bq[:, :], pM[:, 0:768], [pM], [bq])
            yield
            for c0 in (0, 512):
                k.op("pe", lambda e, c0=c0: e.matmul(out=pM[:, c0:c0 + 512], lhsT=latT[:, 2, :], rhs=wkvb[:, c0:c0 + 512],
                                                    start=True, stop=True), reads=[latT, wkvb], writes=[pM])
            vB = vBs.next()
            pM3 = pM[:, :].rearrange("p (h d) -> p h d", d=128)
            cast("act", vB[:, :, 0:64], pM3[:, :, 64:128], [pM], [vB])
            cast("dve", bk[:, :, 0:64], pM3[:, :, 0:64], [pM], [bk])
            k.store("sp", vB, Bv[t0:t0 + 128, :], vB[:].rearrange("p h d -> p (h d)"))
            yield
            yield from rr2(Qc(tt), Kc(tt, lat))

        def rr2(g1, g2):
            gens = [g1, g2]
            while gens:
                for g in list(gens):
                    try:
                        next(g)
                        yield
                    except StopIteration:
                        gens.remove(g)

        def Qc(tt):
            t0 = tt * 128
            bqb = bqbs.next()
            bq3 = bq[:, :].rearrange("p (h d) -> p h d", d=96)
            rms_small(bq3, 8, 96, [bq], sqM, ss8, rs8)
            yield
            rsq(rs8[:], ss8[:], 8, 1.0 / 96, rs8, ss8)
            yield
            k.op("dve", lambda e: e.tensor_tensor(out=bq3, in0=bq3, in1=bc(rs8[:, :], 2, [128, 8, 96]), op=ALU.mult),
                 reads=[bq, rs8], writes=[bq])
            yield
            k.op("dve", lambda e: e.tensor_tensor(out=bqb[:, :, 0:64], in0=bq3[:, :, 0:64],
                                                  in1=bc(gbq[:, 0:64], 1, [128, 8, 64]), op=ALU.mult),
                 reads=[bq, gbq], writes=[bqb])
            k.op("pool", lambda e: e.tensor_tensor(out=x32[:, :, :], in0=bq3[:, :, 64:96],
                                                   in1=bc(gbq[:, 64:96], 1, [128, 8, 32]), op=ALU.mult),
                 reads=[bq, gbq], writes=[x32])
            yield
            yield from rope(x32[:, :, :], bqb[:, :, 64:96], 8, 0, 16, rM[:, tt, 0:16], rM[:, tt, 16:32], [x32, rM], bqb, rtM)

            def tailQ():
                for h in range(8):
                    k.op("pe", lambda e, h=h: e.transpose(out=pT2[0:96, h, :], in_=bqb[:, h, :], identity=identb),
                         reads=[bqb, cb], writes=[pT2])
                Bq = BTq.next()
                cast("act", Bq[:], pT2[0:96, :, :], [pT2], [Bq])
                k.store("sp", Bq, BqT.rearrange("(h f) s -> f h s", f=96)[:, :, t0:t0 + 128], Bq[:])
            tails.setdefault(tt, []).append(tailQ)

        def Kc(tt, lat):
            t0 = tt * 128
            bkb = bkbs.next()
            rms_small(bk[:, :, 0:64], 8, 64, [bk], sqK, ss8k, rs8k)
            yield
            k.op("dve", lambda e: e.tensor_scalar(out=ss8k[:], in0=ss8k[:], scalar1=ssl[:, 2:3], scalar2=None, op0=ALU.add),
                 reads=[ss8k, ssl], writes=[ss8k])
            rsq(rs8k[:], ss8k[:], 8, 1.0 / 96, rs8k, ss8k)
            yield
            k.op("dve", lambda e: e.tensor_tensor(out=krg[:, :], in0=lat[:, 384:416], in1=gbq[:, 160:192], op=ALU.mult),
                 reads=[lat, gbq], writes=[krg])
            yield from rope(krg[:, :].unsqueeze(1), krr[:, :].unsqueeze(1), 1, 0, 16, rM[:, tt, 0:16], rM[:, tt, 16:32],
                            [krg, rM], krr, rtK)
            k.op("dve", lambda e: e.tensor_tensor(out=bk[:, :, 0:64], in0=bk[:, :, 0:64],
                                                  in1=bc(gbq[:, 96:160], 1, [128, 8, 64]), op=ALU.mult),
                 reads=[bk, gbq], writes=[bk])
            yield
            k.op("dve", lambda e: e.tensor_tensor(out=bkb[:, :, 0:64], in0=bk[:, :, 0:64],
                                                  in1=bc(rs8k[:, :], 2, [128, 8, 64]), op=ALU.mult),
                 reads=[bk, rs8k], writes=[bkb])
            k.op("dve", lambda e: e.tensor_tensor(out=bkb[:, :, 64:96], in0=bc(krr[:, :], 1, [128, 8, 32]),
                                                  in1=bc(rs8k[:, :], 2, [128, 8, 32]), op=ALU.mult),
                 reads=[krr, rs8k], writes=[bkb])
            yield

            def tailK():
                for h in range(8):
                    k.op("pe", lambda e, h=h: e.transpose(out=pT2[0:96, h, :], in_=bkb[:, h, :], identity=identb),
                         reads=[bkb, cb], writes=[pT2])
                Bk = BTk.next()
                cast("act", Bk[:], pT2[0:96, :, :], [pT2], [Bk])
                k.store("sp", Bk, BkT.rearrange("(h f) s -> f h s", f=96)[:, :, t0:t0 + 128], Bk[:])
            tails.setdefault(tt, []).append(tailK)

        nt = min(NT, TLIM)
        for tt in range(min(2, nt)):
            Lx(tt)
        Nn(0)()
        if nt > 1:
            if nt > 2:
                Lx(2)
            Nn(1)()
        if pas == 0:
            interleave([P0(0)])
        for tt in range(nt):
            if tt + 3 < nt:
                Lx(tt + 3)
            nB = None
            if tt + 2 < nt:
                nB = Nn(tt + 2)
            if pas == 0:
                def P0n(tt=tt, nB=nB):
                    if tt + 1 < nt:
                        yield from P0(tt + 1)
                    if nB:
                        nB()
                    yield
                interleave([P0n(), YA(tt), YM(tt), Cg(tt // 4) if tt % 4 == 3 else None])
                for f in tails.pop(tt - 1, []):
                    f()
            else:
                P1(tt)
                if nB:
                    nB()
        if pas == 0:
            for f in tails.pop(nt - 1, []):
                f()
        ph.end()

    def phase2():
        ph = Phase(k)
        qn = [ph.sb([64, S], BF16, "qn%d" % i) for i in range(4)]
        kn = [ph.sb([64, S], BF16, "kn%d" % i) for i in range(4)]
        qd = [ph.sb([64, S], BF16, "qd%d" % i) for i in range(4)]
        kd = [ph.sb([64, S], BF16, "kd%d" % i) for i in range(4)]
        vA = ph.sb([128, NT, 4, 65], BF16, "vAall")
        mb = ph.sb([128, 2, 256], BF16, "mband")
        for hi in range(2):
            k.op("dve", lambda e, hi=hi: e.tensor_copy(out=mb[:, hi, 0:128], in_=m_ge), reads=[cb], writes=[mb])
            k.op("dve", lambda e, hi=hi: e.tensor_copy(out=mb[:, hi, 128:256], in_=m_le), reads=[cb], writes=[mb])
        S2 = ph.rot(6, [128, 2, 256], F32, "S2", psum=True)
        Pt = [ph.rot(5, [128, 2, 256], BF16, "Pt%d_" % hp) for hp in range(2)]
        O4 = ph.rot(2, [128, 512], F32, "O4", psum=True)
        osb = ph.rot(2, [128, 260], F32, "osb")
        for g, d in enumerate((1, 4, 16)):
            L = S // d
            nb = L // 128
            for hs in range(4):
                h = g * 4 + hs
                k.load("sp", qn[hs], qn[hs][:], AqT[h * 64:(h + 1) * 64, :])
                k.load("sp", kn[hs], kn[hs][:], AkT[h * 64:(h + 1) * 64, :])
                if d > 1:
                    k.op("pool", lambda e, hs=hs: e.tensor_copy(out=qd[hs][:, :].rearrange("p (r u) -> p r u", r=d),
                                                                in_=qn[hs][:, :].rearrange("p (u r) -> p r u", r=d)),
                         reads=[qn[hs]], writes=[qd[hs]])
                    k.op("dve", lambda e, hs=hs: e.tensor_copy(out=kd[hs][:, :].rearrange("p (r u) -> p r u", r=d),
                                                               in_=kn[hs][:, :].rearrange("p (u r) -> p r u", r=d)),
                         reads=[kn[hs]], writes=[kd[hs]])
            Q = qd if d > 1 else qn
            Kk = kd if d > 1 else kn
            for r in range(d):
                src = Av.rearrange("(n p r) c -> r p n c", p=128, r=d)[r][:, :, g * 260:(g + 1) * 260]
                k.load("sp", vA, vA[:, r * nb:(r + 1) * nb, :, :].rearrange("p n h d -> p n (h d)"), src)
            blocks = [(r, n) for r in range(d) for n in range(nb)]
            Pof = {}

            def QE(i):
                r, n = blocks[i]
                col0 = r * L + n * 128
                nq = 256 if n < nb - 1 else 128
                cur = []
                for hp in range(2):
                    s2 = S2.next()
                    for hi in range(2):
                        hs = hp * 2 + hi
                        k.op("pe", lambda e, hs=hs, hi=hi, s2=s2: e.matmul(
                            out=s2[:, hi, 0:nq], lhsT=Kk[hs][:, col0:col0 + 128], rhs=Q[hs][:, col0:col0 + nq],
                            start=True, stop=True), reads=[Kk[hs], Q[hs]], writes=[s2])
                    pt = Pt[hp].next()
                    k.op("act", lambda e, s2=s2, pt=pt: e.activation(out=pt[:, :, 0:nq], in_=s2[:, :, 0:nq], func=AF.Exp,
                                                                     scale=0.125), reads=[s2], writes=[pt])
                    k.op("pool", lambda e, pt=pt: e.tensor_tensor(out=pt[:, :, 0:nq], in0=pt[:, :, 0:nq],
                                                                  in1=mb[:, :, 0:nq], op=ALU.mult),
                         reads=[pt, mb], writes=[pt])
                    cur.append(pt)
                Pof[i] = cur

            def PVs(i):
                r, n = blocks[i]
                b = r * nb + n
                curP = Pof[i]
                prevP = Pof.get(i - 1) if n > 0 else None
                o4t = O4.next()
                o4 = o4t[:, 0:260].rearrange("p (h d) -> p h d", d=65)
                for hs in range(4):
                    hp, hi = hs // 2, hs % 2
                    if n > 0:
                        k.op("pe", lambda e, hs=hs, hp=hp, hi=hi: e.matmul(
                            out=o4[:, hs, :], lhsT=prevP[hp][:, hi, 128:256], rhs=vA[:, b - 1, hs, :],
                            start=True, stop=False), reads=[prevP[hp], vA], writes=[o4t])
                    k.op("pe", lambda e, hs=hs, hp=hp, hi=hi: e.matmul(
                        out=o4[:, hs, :], lhsT=curP[hp][:, hi, 0:128], rhs=vA[:, b, hs, :],
                        start=(n == 0), stop=True), reads=[curP[hp], vA], writes=[o4t])
                ob = osb.next()
                cast("act", ob[:, :], o4t[:, 0:260], [o4t], [ob])
                dst = accA[g].rearrange("(n p r) c -> r n p c", p=128, r=d)[r, n]
                k.store("sp", ob, dst, ob[:, :])
                Pof.pop(i - 1, None)

            QE(0)
            if len(blocks) > 1:
                QE(1)
            for i in range(len(blocks)):
                if i + 2 < len(blocks):
                    QE(i + 2)
                PVs(i)
        ph.end()

    def phase3(prep_l=None):
        ph = Phase(k)
        pg = prep_gen(ph, prep_l, Rot(["dve"])) if prep_l else None
        qTs = ph.rot(2, [96, S], BF16, "bqT")
        kTs = ph.rot(2, [96, S], BF16, "bkT")
        vB = ph.sb([128, NT, 520], BF16, "vBall")
        k.load("sp", vB, vB[:], Bv.rearrange("(t p) c -> p t c", p=128))
        oB_sb = ph.sb([128, NT, 512], BF16, "oB_sb")
        Sb = ph.rot(4, [128, 512], F32, "Sb", psum=True)
        Pts = ph.rot(5, [128, 512], BF16, "Ptb")
        Ob = ph.rot(2, [65, 512], F32, "Ob", psum=True)
        Osb = ph.rot(2, [65, 512], F32, "Osb")
        pTo = ph.rot(1, [128, 512], F32, "pTo", psum=True)
        rden = ph.rot(2, [128, 4], F32, "rden")
        sc = 96 ** -0.5
        heads = {}

        def load_head(h):
            qT = qTs.next()
            kT = kTs.next()
            k.load("sp", qT, qT[:], BqT[h * 96:(h + 1) * 96, :])
            k.load("sp", kT, kT[:], BkT[h * 96:(h + 1) * 96, :])
            heads[h] = (qT, kT)

        units = []
        for h in range(8):
            for qg in range(NQG):
                for kb in range(4 * qg + 4):
                    units.append({"h": h, "qg": qg, "kb": kb, "c0": max(0, kb - 4 * qg) * 128, "last": kb == 4 * qg + 3})
        grp = {}

        def A(u):
            h, qg, kb, c0 = u["h"], u["qg"], u["kb"], u["c0"]
            if h not in heads:
                load_head(h)
            if kb == 0 and qg == 0 and h + 1 < 8 and (h + 1) not in heads:
                load_head(h + 1)
            qT, kT = heads[h]
            sb_ = Sb.next()
            u["sb"] = sb_
            k.op("pe", lambda e: e.matmul(out=sb_[:, c0:512], lhsT=kT[:, kb * 128:(kb + 1) * 128],
                                          rhs=qT[:, qg * 512 + c0:(qg + 1) * 512], start=True, stop=True),
                 reads=[kT, qT], writes=[sb_])

        def B(u):
            c0, sb_ = u["c0"], u["sb"]
            pt = Pts.next()
            u["pt"] = pt
            k.op("act", lambda e: e.activation(out=pt[:, c0:512], in_=sb_[:, c0:512], func=AF.Exp, scale=sc),
                 reads=[sb_], writes=[pt])
            if u["kb"] >= 4 * u["qg"]:
                k.op("pool", lambda e: e.tensor_tensor(out=pt[:, c0:c0 + 128], in0=pt[:, c0:c0 + 128], in1=m_ge,
                                                       op=ALU.mult), reads=[pt, cb], writes=[pt])

        def F(u):
            h, qg, kb, c0, pt = u["h"], u["qg"], u["kb"], u["c0"], u["pt"]
            if kb == 0:
                grp[(h, qg)] = Ob.next()
            ob = grp[(h, qg)]
            k.op("pe", lambda e: e.matmul(out=ob[:, c0:512], lhsT=vB[:, kb, h * 65:(h + 1) * 65], rhs=pt[:, c0:512],
                                          start=(kb == 0), stop=u["last"]), reads=[vB, pt], writes=[ob])
            if u["last"]:
                osb_ = Osb.next()
                cast("act", osb_[:, :], ob[:, :], [ob], [osb_])
                ptot = pTo.next()
                pto = ptot[:, 0:260].rearrange("p (j d) -> p j d", d=65)
                for j in range(4):
                    k.op("pe", lambda e, j=j: e.transpose(out=pto[:, j, :], in_=osb_[:, j * 128:(j + 1) * 128],
                                                          identity=identf[0:65, 0:65]), reads=[osb_, cf], writes=[ptot])
                rd = rden.next()
                k.op("dve", lambda e: e.reciprocal(out=rd[:, :], in_=pto[:, :, 64]), reads=[ptot], writes=[rd])
                k.op("dve", lambda e: e.tensor_tensor(out=oB_sb[:, 4 * qg:4 * qg + 4, h * 64:(h + 1) * 64], in0=pto[:, :, 0:64],
                                                      in1=bc(rd[:, :], 2, [128, 4, 64]), op=ALU.mult),
                     reads=[ptot, rd], writes=[oB_sb])

        N = len(units)
        A(units[0])
        for t in range(N + 1):
            if t + 1 < N:
                A(units[t + 1])
            if t >= 1:
                F(units[t - 1])
            if t < N:
                B(units[t])
            if pg is not None and t % 6 == 3:
                try:
                    next(pg)
                except StopIteration:
                    pg = None
        if pg is not None:
            for _ in pg:
                pass
        k.store("sp", oB_sb, oB.rearrange("(t p) c -> p t c", p=128), oB_sb[:])
        ph.end()

    def phase4():
        ph = Phase(k)
        qTs = ph.rot(2, [64, S], BF16, "cqT")
        kTs = ph.rot(2, [64, S], BF16, "ckT")
        vC = ph.sb([128, NT, 512], BF16, "vCall")
        k.load("sp", vC, vC[:], Cv.rearrange("(t p) c -> p t c", p=128))
        oC_sb = ph.sb([128, NT, 512], BF16, "oC_sb")
        Zb = ph.rot(5, [128, 512], F32, "Zb", psum=True)
        Nb = ph.rot(2, [128, 512], F32, "Nb", psum=True)
        es = ph.rot(2, [128, 512], F32, "e_")
        sps = ph.rot(4, [128, 512], BF16, "sp_")
        Es = ph.rot(3, [128, 512], BF16, "E_")
        gts = ph.rot(2, [128, 4], F32, "g_")
        accs = ph.rot(2, [128, 4, 64], F32, "acc")
        heads = {}

        def load_head(h):
            qT = qTs.next()
            kT = kTs.next()
            k.load("sp", qT, qT[:], CqT[h * 64:(h + 1) * 64, :])
            k.load("sp", kT, kT[:], CkT[h * 64:(h + 1) * 64, :])
            heads[h] = (qT, kT)

        units = []
        for h in range(8):
            for qg in range(NQG):
                for kb in range(4 * qg + 4):
                    j0 = max(0, kb - 4 * qg)
                    units.append({"h": h, "qg": qg, "kb": kb, "j0": j0, "c0": j0 * 128, "diag": kb >= 4 * qg,
                                  "last": kb == 4 * qg + 3})
        grp = {}

        def A(u):
            h, qg, kb, c0 = u["h"], u["qg"], u["kb"], u["c0"]
            if h not in heads:
                load_head(h)
            if kb == 0 and qg == 0 and h + 1 < 8 and (h + 1) not in heads:
                load_head(h + 1)
            qT, kT = heads[h]
            zb = Zb.next()
            u["zb"] = zb
            k.op("pe", lambda e: e.matmul(out=zb[:, c0:512], lhsT=kT[:, kb * 128:(kb + 1) * 128],
                                          rhs=qT[:, qg * 512 + c0:(qg + 1) * 512], start=True, stop=True),
                 reads=[kT, qT], writes=[zb])

        def B(u):
            c0, zb = u["c0"], u["zb"]
            ee = es.next()
            k.op("act", lambda e: e.activation(out=ee[:, c0:512], in_=zb[:, c0:512], func=AF.Exp), reads=[zb], writes=[ee])
            sp = sps.next()
            u["sp"] = sp
            k.op("act", lambda e: e.activation(out=sp[:, c0:512], in_=ee[:, c0:512], func=AF.Ln, bias=1.0),
                 reads=[ee], writes=[sp])
            if u["diag"]:
                k.op("pool", lambda e: e.tensor_tensor(out=sp[:, c0:c0 + 128], in0=sp[:, c0:c0 + 128], in1=m_gt,
                                                       op=ALU.mult), reads=[sp, cb], writes=[sp])

        def C(u):
            c0, zb, sp = u["c0"], u["zb"], u["sp"]
            k.op("pe", lambda e: e.matmul(out=zb[:, c0:512], lhsT=negU, rhs=sp[:, c0:512], start=False, stop=True,
                                          skip_group_check=True), reads=[cb, sp], writes=[zb])

        def Dd(u):
            c0, zb = u["c0"], u["zb"]
            E = Es.next()
            u["E"] = E
            k.op("act", lambda e: e.activation(out=E[:, c0:512], in_=zb[:, c0:512], func=AF.Exp), reads=[zb], writes=[E])
            if u["diag"]:
                k.op("pool", lambda e: e.tensor_tensor(out=E[:, c0:c0 + 128], in0=E[:, c0:c0 + 128], in1=m_gt,
                                                       op=ALU.mult), reads=[E, cb], writes=[E])

        def F(u):
            h, kb, j0, E, sp = u["h"], u["kb"], u["j0"], u["E"], u["sp"]
            nbt = Nb.next()
            u["nbt"] = nbt
            nbk = nbt[:, 0:260].rearrange("p (j d) -> p j d", d=65)
            for j in range(j0, 4):
                k.op("pe", lambda e, j=j: e.matmul(out=nbk[:, j, 0:64], lhsT=E[:, j * 128:(j + 1) * 128],
                                                   rhs=vC[:, kb, h * 64:(h + 1) * 64], start=True, stop=True,
                                                   skip_group_check=True), reads=[E, vC], writes=[nbt])
                k.op("pe", lambda e, j=j: e.matmul(out=nbk[:, j, 64:65], lhsT=sp[:, j * 128:(j + 1) * 128],
                                                   rhs=onesb[:, 0:1], start=True, stop=True,
                                                   skip_group_check=True), reads=[sp, onesb], writes=[nbt])

        def G(u):
            h, qg, kb, j0, nbt = u["h"], u["qg"], u["kb"], u["j0"], u["nbt"]
            nbk = nbt[:, 0:260].rearrange("p (j d) -> p j d", d=65)
            if kb == 0:
                acc = accs.next()
                grp[(h, qg)] = acc
                k.op("dve", lambda e: e.tensor_copy(out=acc[:], in_=nbk[:, :, 0:64]), reads=[nbt], writes=[acc])
            else:
                acc = grp[(h, qg)]
                gt = gts.next()
                k.op("act", lambda e: e.activation(out=gt[:, j0:4], in_=nbk[:, j0:4, 64], func=AF.Exp, scale=-1.0),
                     reads=[nbt], writes=[gt])
                for j in range(j0, 4):
                    k.op("dve", lambda e, j=j: e.scalar_tensor_tensor(out=acc[:, j, :], in0=acc[:, j, :], scalar=gt[:, j:j + 1],
                                                                      in1=nbk[:, j, 0:64], op0=ALU.mult, op1=ALU.add),
                         reads=[acc, gt, nbt], writes=[acc])
            if u["last"]:
                k.op("pool", lambda e: e.tensor_copy(out=oC_sb[:, 4 * qg:4 * qg + 4, h * 64:(h + 1) * 64], in_=acc[:]),
                     reads=[acc], writes=[oC_sb])

        N = len(units)
        A(units[0])
        for t in range(N + 2):
            if t + 1 < N:
                A(units[t + 1])
            if t < N:
                B(units[t])
            if 1 <= t <= N:
                C(units[t - 1])
                Dd(units[t - 1])
            if t >= 2:
                F(units[t - 2])
                G(units[t - 2])
        k.store("sp", oC_sb, oC.rearrange("(t p) c -> p t c", p=128), oC_sb[:])
        ph.end()

    def phase5(l, s, xsrc):
        ph = Phase(k)
        Wbr = ph.sb([128, 10, D], BF16, "Wbr")
        Wo = ph.sb([128, 8, D], BF16, "Wo")
        k.load("sp", Wbr, Wbr[:], Wb_br[l].rearrange("(c p) n -> p c n", p=128))
        k.load("sp", Wo, Wo[:], Wb_out[l].rearrange("(c p) n -> p c n", p=128))
        xts = ph.rot(3, [128, D], F32, "xt")
        a3s = ph.rot(3, [128, 3, 260], F32, "a3")
        ocs = ph.rot(3, [128, 1280], BF16, "ocat")
        gts = ph.rot(3, [128, 3072], BF16, "gt")
        rds = ph.rot(2, [128, 4], F32, "rd")
        pT = ph.ps([128, 16, 128], BF16, "pT")
        pTm = ph.ps([128, 8, 128], BF16, "pTm")
        oTs = ph.rot(2, [128, 10, 128], BF16, "oT")
        PP = ph.rot(2, [128, D], F32, "PP", psum=True)
        tf = ph.sb([128, D], F32, "tf")
        uf = ph.sb([128, D], F32, "uf")
        u2 = ph.sb([128, D], F32, "u2")
        mbfs = ph.rot(2, [128, D], BF16, "mbf")
        mTs = ph.rot(2, [128, 8, 128], BF16, "mT")
        xo = ph.rot(2, [128, D], F32, "xo")
        st = {}

        def L(tt):
            t0 = tt * 128
            xt, a3, oc, gt = xts.next(), a3s.next(), ocs.next(), gts.next()
            k.load("sp", xt, xt[:], xsrc[t0:t0 + 128, :])
            k.load("sp", a3, a3[:], accA[:, t0:t0 + 128, :].rearrange("g p c -> p g c"))
            k.load("sp", oc, oc[:, 256:768], oB[t0:t0 + 128, :])
            k.load("sp", oc, oc[:, 768:1280], oC[t0:t0 + 128, :])
            k.load("sp", gt, gt[:], Gt[t0:t0 + 128, :])
            st[tt] = {"xt": xt, "a3": a3, "oc": oc, "gt": gt}

        def branch(P, oT, ca, cbn):
            for half in range(2):
                for c in range(ca, cbn):
                    k.op("pe", lambda e, c=c, half=half: e.matmul(
                        out=P[:, half * 512:(half + 1) * 512], lhsT=oT[:, c, :], rhs=Wbr[:, c, half * 512:(half + 1) * 512],
                        start=(c == ca), stop=(c == cbn - 1)), reads=[oT, Wbr], writes=[P])

        def S1(tt):
            d = st[tt]
            a3, oc, gt = d["a3"], d["oc"], d["gt"]
            k.op("pool", lambda e: e.tensor_tensor(out=a3[:, 0, :], in0=a3[:, 0, :], in1=a3[:, 1, :], op=ALU.add),
                 reads=[a3], writes=[a3])
            k.op("pool", lambda e: e.tensor_tensor(out=a3[:, 0, :], in0=a3[:, 0, :], in1=a3[:, 2, :], op=ALU.add),
                 reads=[a3], writes=[a3])
            n4 = a3[:, 0, :].rearrange("p (h d) -> p h d", d=65)
            rd = rds.next()
            k.op("dve", lambda e: e.reciprocal(out=rd[:, :], in_=n4[:, :, 64]), reads=[a3], writes=[rd])
            k.op("dve", lambda e: e.tensor_tensor(out=oc[:, 0:256].rearrange("p (h d) -> p h d", d=64), in0=n4[:, :, 0:64],
                                                  in1=bc(rd[:, :], 2, [128, 4, 64]), op=ALU.mult), reads=[a3, rd], writes=[oc])
            for j in range(10):
                k.op("pe", lambda e, j=j: e.transpose(out=pT[:, j, :], in_=oc[:, j * 128:(j + 1) * 128], identity=identb),
                     reads=[oc, cb], writes=[pT])
            oT = oTs.next()
            cast("act", oT[:], pT[:, 0:10, :], [pT], [oT])
            Pa = PP.next()
            branch(Pa, oT, 0, 2)
            k.op("dve", lambda e: e.tensor_tensor(out=tf[:], in0=Pa[:], in1=gt[:, 0:1024], op=ALU.mult),
                 reads=[Pa, gt], writes=[tf])
            Pb = PP.next()
            branch(Pb, oT, 2, 6)
            k.op("dve", lambda e: e.tensor_tensor(out=uf[:], in0=Pb[:], in1=gt[:, 1024:2048], op=ALU.mult),
                 reads=[Pb, gt], writes=[uf])
            k.op("pool", lambda e: e.tensor_tensor(out=tf[:], in0=tf[:], in1=uf[:], op=ALU.add), reads=[tf, uf], writes=[tf])
            Pc = PP.next()
            branch(Pc, oT, 6, 10)
            k.op("dve", lambda e: e.tensor_tensor(out=u2[:], in0=Pc[:], in1=gt[:, 2048:3072], op=ALU.mult),
                 reads=[Pc, gt], writes=[u2])
            mbf = mbfs.next()
            k.op("pool", lambda e: e.tensor_tensor(out=mbf[:], in0=tf[:], in1=u2[:], op=ALU.add), reads=[tf, u2], writes=[mbf])
            d["mbf"] = mbf

        def S2(tt):
            t0 = tt * 128
            d = st.pop(tt)
            mbf, xt = d["mbf"], d["xt"]
            for j in range(8):
                k.op("pe", lambda e, j=j: e.transpose(out=pTm[:, j, :], in_=mbf[:, j * 128:(j + 1) * 128], identity=identb),
                     reads=[mbf, cb], writes=[pTm])
            mT = mTs.next()
            cast("act", mT[:], pTm[:], [pTm], [mT])
            P = PP.next()
            for half in range(2):
                for c in range(8):
                    k.op("pe", lambda e, c=c, half=half: e.matmul(
                        out=P[:, half * 512:(half + 1) * 512], lhsT=mT[:, c, :], rhs=Wo[:, c, half * 512:(half + 1) * 512],
                        start=(c == 0), stop=(c == 7)), reads=[mT, Wo], writes=[P])
            o = xo.next()
            k.op("dve", lambda e: e.tensor_tensor(out=o[:], in0=P[:], in1=xt[:], op=ALU.add), reads=[P, xt], writes=[o])
            k.store("sp", o, xmid[s][t0:t0 + 128, :], o[:])

        L(0)
        L(1)
        S1(0)
        for tt in range(NT):
            if tt + 2 < NT:
                L(tt + 2)
            if tt + 1 < NT:
                S1(tt + 1)
            S2(tt)
        ph.end()

    def phase6(l, dsts):
        ph = Phase(k)
        W1 = ph.sb([128, 8, DFF], BF16, "W1")
        W2 = ph.sb([128, 32, D], BF16, "W2")
        for c in range(8):
            k.load("sp", W1, W1[:, c, :], Wb_ff1[l][c * 128:(c + 1) * 128, :])
        for c in range(4):
            k.load("sp", W2, W2[:, c * 8:(c + 1) * 8, :],
                   Wb_ff2[l][c * 1024:(c + 1) * 1024, :].rearrange("(c p) n -> p c n", p=128))
        r = norm_res(ph)
        xts = ph.rot(4, [128, D], F32, "xt")
        h2T = ph.rot(2, [128, 8, 256], BF16, "h2T")
        h1T = ph.sb([128, 32, 256], BF16, "h1T")
        rl = ph.rot(2, [128, 2, 256], F32, "rl")
        pF = ph.rot(2, [128, 2, 256], F32, "pF", psum=True)
        pY = ph.rot(3, [128, 512], F32, "pY", psum=True)
        xo = ph.rot(1, [128, D], F32, "xo")
        groups = [(s, tg) for s in range(nS) for tg in range(NT // 2)]
        gstate = {}

        def Ng(gi):
            s, tg = groups[gi]
            hT = h2T.next()
            xt2 = []
            for i in range(2):
                t0 = (tg * 2 + i) * 128
                xt = xts.next()
                k.load("sp", xt, xt[:], xmid[s][t0:t0 + 128, :])
                r["hTtile"] = hT
                pend.append(norm_transpose(ph, r, None, xt, hT[:, :, i * 128:(i + 1) * 128], defer=True))
                xt2.append(xt)
            gstate[gi] = (hT, xt2)

        pend = []
        Ng(0)
        for f in pend:
            f()
        pend.clear()
        for gi, (s, tg) in enumerate(groups):
            if True:
                if gi + 1 < len(groups):
                    Ng(gi + 1)
                hT, xt2 = gstate.pop(gi)
                for f2 in range(16):
                    p = pF.next()
                    for fi in range(2):
                        f = f2 * 2 + fi
                        for c in range(8):
                            k.op("pe", lambda e, c=c, f=f, fi=fi: e.matmul(
                                out=p[:, fi, :], lhsT=W1[:, c, f * 128:(f + 1) * 128], rhs=hT[:, c, :],
                                start=(c == 0), stop=(c == 7)), reads=[W1, hT], writes=[p])
                    rr = rl.next()
                    k.op("act", lambda e: e.activation(out=rr[:], in_=p[:], func=AF.Relu), reads=[p], writes=[rr])
                    k.op("pool", lambda e: e.tensor_tensor(out=h1T[:, f2 * 2:f2 * 2 + 2, :], in0=rr[:], in1=rr[:], op=ALU.mult),
                         reads=[rr], writes=[h1T])
                for f in pend:
                    f()
                pend.clear()
                for i in range(2):
                    t0 = (tg * 2 + i) * 128
                    o = xo.next()
                    for half in range(2):
                        p = pY.next()
                        for f in range(32):
                            k.op("pe", lambda e, f=f: e.matmul(out=p[:], lhsT=h1T[:, f, i * 128:(i + 1) * 128],
                                                               rhs=W2[:, f, half * 512:(half + 1) * 512],
                                                               start=(f == 0), stop=(f == 31)), reads=[h1T, W2], writes=[p])
                        k.op("dve", lambda e: e.tensor_tensor(out=o[:, half * 512:(half + 1) * 512], in0=p[:],
                                                              in1=xt2[i][:, half * 512:(half + 1) * 512], op=ALU.add),
                             reads=[p, xt2[i]], writes=[o])
                    k.store("sp", o, dsts[s][t0:t0 + 128, :], o[:])
        ph.end()

    U = UPTO
    if U < 4:
        prep_weights([(l, "all") for l in range(nL)])
    else:
        prep_weights([(0, "first")])
    for l in range(nL):
        for s in range(nS):
            xsrc = x_in[s] if l == 0 else xres[s]
            if U >= 1:
                phase1(l, s, xsrc, 0)
            if U >= 2:
                phase1(l, s, xsrc, 1)
            if U >= 3:
                phase2()
            if U >= 4:
                pl = []
                if l == 0 and s == 0:
                    pl.append((0, "rest"))
                if s == nS - 1 and l + 1 < nL:
                    pl.append((l + 1, "all"))
                phase3(prep_l=pl)
            if U >= 5:
                phase4()
            if U >= 6:
                phase5(l, s, xsrc)
        dsts = [out[s] if l == nL - 1 else xres[s] for s in range(nS)]
        if U >= 7:
            phase6(l, dsts)
    k.barrier()
    print("sbuf max bytes/partition:", k.sb_max, "instructions:", k.ninstr, {e: k.cnt[e] for e in k.cnt})
    k.close()
    return nc


def host_consts(S):
    NT = S // 128
    p = np.arange(128)[:, None]
    c = np.arange(128)[None, :]
    consts = np.zeros((128, 5, 128), np.float32)
    consts[:, 0, :] = (p == c)
    consts[:, 1, :] = (c >= p)
    consts[:, 2, :] = (c > p)
    consts[:, 3, :] = (c <= p)
    consts[:, 4, :] = -1.0 * (p >= c)

    def tables(dim):
        pos = np.arange(S, dtype=np.float32)
        inv = (np.float32(500000.0) ** (-np.arange(0, dim, 2, dtype=np.float32) / np.float32(dim))).astype(np.float32)
        ang = (pos[:, None] * inv[None, :]).astype(np.float32)
        t = np.concatenate([np.cos(ang), np.sin(ang)], axis=1).astype(np.float32)
        return np.ascontiguousarray(t.reshape(NT, 128, dim).transpose(1, 0, 2))
    return consts, tables(16), tables(32)


def make_inputs(x_sh, l0, l1, S, attn_norm, w_in, a_q_norm, a_k_norm, b_q_a_norm, w_q_b, b_kv_a_norm,
                w_kv_b, b_q_norm, b_k_norm, w_branch, w_out, mlp_norm, w_ff1, w_ff2):
    nL = l1 - l0
    sl = slice(l0, l1)
    f = lambda a: np.ascontiguousarray(np.asarray(a, dtype=np.float32))
    col = lambda g, c: np.ascontiguousarray(np.asarray(g, np.float32)[sl].reshape(nL, c, 128).transpose(0, 2, 1))
    rep = lambda a, b: np.ascontiguousarray(np.broadcast_to(
        np.concatenate([np.asarray(a, np.float32)[sl], np.asarray(b, np.float32)[sl]], axis=1)[:, None, :],
        (nL, 128, a.shape[1] + b.shape[1])))
    consts, rA, rM = host_consts(S)
    return {
        "x": f(x_sh), "w_in": f(w_in[sl]), "w_q_b": f(np.asarray(w_q_b)[sl].reshape(nL, 256, 768)),
        "w_kv_b": f(np.asarray(w_kv_b)[sl].reshape(nL, 128, 1024)), "w_branch": f(w_branch[sl]),
        "w_out": f(w_out[sl]), "w_ff1": f(w_ff1[sl]), "w_ff2": f(w_ff2[sl]),
        "g_attn": col(attn_norm, 8), "g_mlp": col(mlp_norm, 8), "g_qa": col(b_q_a_norm, 2),
        "g_kva": col(b_kv_a_norm, 1), "g_aqk": rep(a_q_norm, a_k_norm), "g_bqk": rep(b_q_norm, b_k_norm),
        "ropeA": rA, "ropeM": rM, "consts": consts,
    }


def kernel(x, attn_norm, w_in, a_q_norm, a_k_norm, b_q_a_norm, w_q_b, b_kv_a_norm,
           w_kv_b, b_q_norm, b_k_norm, w_branch, w_out, mlp_norm, w_ff1, w_ff2):
    x = np.asarray(x, dtype=np.float32)
    B, S, _ = x.shape
    nL = np.asarray(attn_norm).shape[0]
    ncores = 8
    nS = B // ncores
    nc = build(nL, nS, S)
    in_maps = []
    for c in range(ncores):
        in_maps.append(make_inputs(x[c * nS:(c + 1) * nS], 0, nL, S, attn_norm, w_in, a_q_norm, a_k_norm, b_q_a_norm,
                                   w_q_b, b_kv_a_norm, w_kv_b, b_q_norm, b_k_norm, w_branch, w_out, mlp_norm,
                                   w_ff1, w_ff2))
    res = run_bass_kernel_spmd(nc, in_maps, core_ids=list(range(ncores)))
    return np.concatenate([np.asarray(r["out"], dtype=np.float32) for r in res.results], axis=0)
```

```python
import contextlib
import numpy as np
import concourse.bass as bass
import concourse.mybir as mybir
from concourse.alu_op_type import AluOpType as ALU
from concourse.bass_utils import run_bass_kernel_spmd

AF = mybir.ActivationFunctionType
AX = mybir.AxisListType
F32 = mybir.dt.float32
BF16 = mybir.dt.bfloat16

SAME_ENG_SYNC = True
UPTO = 99
STAGE1 = 99
TLIM = 9999
EPS = 1e-6
D = 1024
INC = 7328
DFF = 4096


class T:
    _n = 0

    def __init__(self, t, name=None):
        self.t = t
        T._n += 1
        self.id = "t%d" % T._n
        self.name = name or self.id
        self.w = None
        self.r = {}
        self.dsem = None
        self.dval = 0
        self.psum = False

    def __getitem__(self, idx):
        return self.t[idx]


class Rot:
    def __init__(self, tiles):
        self.tiles = tiles
        self.i = 0

    def next(self):
        t = self.tiles[self.i % len(self.tiles)]
        self.i += 1
        return t


class K:
    def __init__(self, nc):
        self.nc = nc
        self.es = contextlib.ExitStack()
        self.eng = {"pe": nc.tensor, "act": nc.scalar, "dve": nc.vector,
                    "pool": nc.gpsimd, "sp": nc.sync}
        self.sem = {}
        self.cnt = {}
        self.seen = {}
        for e in self.eng:
            self.sem[e] = self.es.enter_context(nc.semaphore("s_" + e))
            self.cnt[e] = 0
            self.seen[e] = {}
        self.semof = dict(self.sem)
        self.dtiles = []
        self.dsem_pool = []
        self.ninstr = 0
        self.uid = 0

    def sb(self, shape, dtype, name, stack=None):
        st = stack if stack is not None else self.es
        self.uid += 1
        nb = int(np.prod(shape[1:])) * (2 if dtype == BF16 else 4)
        self.sb_bytes = getattr(self, "sb_bytes", 0) + nb
        self.sb_max = max(getattr(self, "sb_max", 0), self.sb_bytes)
        st.callback(self._free, nb)
        t = st.enter_context(self.nc.sbuf_tensor("%s_%d" % (name, self.uid), list(shape), dtype))
        return T(t, name)

    def ps(self, shape, dtype, name, stack=None):
        st = stack if stack is not None else self.es
        self.uid += 1
        t = st.enter_context(self.nc.psum_tensor("%s_%d" % (name, self.uid), list(shape), dtype))
        tt = T(t, name)
        tt.psum = True
        return tt

    def rot(self, n, shape, dtype, name, stack=None, psum=False):
        f = self.ps if psum else self.sb
        return Rot([f(shape, dtype, "%s%d" % (name, i), stack) for i in range(n)])

    def _free(self, nb):
        self.sb_bytes -= nb

    def _wait(self, e, deps):
        eng = self.eng[e]
        seen = self.seen[e]
        for key, val in sorted(deps, key=lambda d: str(d[0])):
            if key == e and (e in ("pe", "sp") or not SAME_ENG_SYNC):
                continue
            if seen.get(key, 0) >= val:
                continue
            eng.wait_ge(self.semof[key], val)
            seen[key] = val
            self.ninstr += 1

    def _deps(self, reads, writes):
        deps = set()
        for t in reads:
            if t.w:
                deps.add(t.w)
            if t.psum:
                for d in t.r.values():
                    deps.add(d)
        for t in writes:
            if t.w:
                deps.add(t.w)
            for d in t.r.values():
                deps.add(d)
        return deps

    def op(self, e, fn, reads=(), writes=()):
        self._wait(e, self._deps(reads, writes))
        ins = fn(self.eng[e])
        self.cnt[e] += 1
        ins.then_inc(self.sem[e], 1)
        self.ninstr += 1
        me = (e, self.cnt[e])
        for t in reads:
            t.r[e] = me
        for t in writes:
            t.w = me
            t.r = {}
        return ins

    def _dsem(self, tile):
        if tile.dsem is None:
            if self.dsem_pool:
                key, sem, val = self.dsem_pool.pop()
                tile.dkey, tile.dsem, tile.dval = key, sem, val
            else:
                tile.dkey = "d" + tile.id
                tile.dsem = self.es.enter_context(self.nc.semaphore(tile.dkey))
                self.semof[tile.dkey] = tile.dsem
            self.dtiles.append(tile)

    def load(self, q, tile, out_ap, in_ap, **kw):
        self._dsem(tile)
        self._wait(q, self._deps((), (tile,)))
        ins = self.eng[q].dma_start(out=out_ap, in_=in_ap, **kw)
        tile.dval += 16
        ins.then_inc(tile.dsem, 16)
        self.ninstr += 1
        tile.w = (tile.dkey, tile.dval)
        tile.r = {}

    def store(self, q, tile, out_ap, in_ap, **kw):
        self._dsem(tile)
        self._wait(q, self._deps((tile,), ()))
        ins = self.eng[q].dma_start(out=out_ap, in_=in_ap, **kw)
        tile.dval += 16
        ins.then_inc(tile.dsem, 16)
        self.ninstr += 1
        tile.r["dma"] = (tile.dkey, tile.dval)

    def barrier(self, release=True):
        deps = set()
        for e in self.eng:
            if self.cnt[e]:
                deps.add((e, self.cnt[e]))
        for t in self.dtiles:
            if t.dval:
                deps.add((t.dkey, t.dval))
        for e in self.eng:
            self._wait(e, deps)

    def release(self, tiles):
        for t in tiles:
            if t.dsem is not None:
                self.dsem_pool.append((t.dkey, t.dsem, t.dval))
                self.dtiles.remove(t)
                t.dsem = None

    def close(self):
        self.es.close()


class Phase:
    def __init__(self, k):
        self.k = k
        self.st = contextlib.ExitStack()
        self.tiles = []

    def sb(self, shape, dtype, name):
        t = self.k.sb(shape, dtype, name, self.st)
        self.tiles.append(t)
        return t

    def ps(self, shape, dtype, name):
        t = self.k.ps(shape, dtype, name, self.st)
        self.tiles.append(t)
        return t

    def rot(self, n, shape, dtype, name, psum=False):
        f = self.ps if psum else self.sb
        return Rot([f(shape, dtype, "%s%d" % (name, i)) for i in range(n)])

    def end(self):
        self.k.barrier()
        self.k.release(self.tiles)
        self.st.close()


def bc(ap, axis, shape):
    return ap.unsqueeze(axis).broadcast_to(list(shape))


def build(nL, nS, S, debug=False):
    nc = bass.Bass("TRN2", target_bir_lowering=False)
    NT = S // 128
    NQG = S // 512

    def din(name, shape, dt=F32):
        return nc.dram_tensor(name, list(shape), dt, kind="ExternalInput").ap()

    def dscr(name, shape, dt):
        if debug:
            return nc.dram_tensor(name, list(shape), dt, kind="ExternalOutput").ap()
        return nc.dram_tensor(name, list(shape), dt).ap()

    x_in = din("x", [nS, S, D])
    w_in = din("w_in", [nL, D, INC])
    w_qb = din("w_q_b", [nL, 256, 768])
    w_kvb = din("w_kv_b", [nL, 128, 1024])
    w_br = din("w_branch", [nL, 1280, D])
    w_out = din("w_out", [nL, D, D])
    w_ff1 = din("w_ff1", [nL, D, DFF])
    w_ff2 = din("w_ff2", [nL, DFF, D])
    g_attn = din("g_attn", [nL, 128, 8])
    g_mlp = din("g_mlp", [nL, 128, 8])
    g_qa = din("g_qa", [nL, 128, 2])
    g_kva = din("g_kva", [nL, 128, 1])
    g_aqk = din("g_aqk", [nL, 128, 128])
    g_bqk = din("g_bqk", [nL, 128, 192])
    ropeA = din("ropeA", [128, NT, 16])
    ropeM = din("ropeM", [128, NT, 32])
    consts = din("consts", [128, 5, 128])
    out = nc.dram_tensor("out", [nS, S, D], F32, kind="ExternalOutput").ap()

    Wb_in = dscr("Wb_in", [nL, D, INC], BF16)
    Wb_qb = dscr("Wb_qb", [nL, 256, 768], BF16)
    Wb_kvb = dscr("Wb_kvb", [nL, 128, 1024], BF16)
    Wb_br = dscr("Wb_br", [nL, 1280, D], BF16)
    Wb_out = dscr("Wb_out", [nL, D, D], BF16)
    Wb_ff1 = dscr("Wb_ff1", [nL, D, DFF], BF16)
    Wb_ff2 = dscr("Wb_ff2", [nL, DFF, D], BF16)
    xres = dscr("xres", [nS, S, D], F32)
    xmid = dscr("xmid", [nS, S, D], F32)
    AqT = dscr("AqT", [768, S], BF16)
    AkT = dscr("AkT", [768, S], BF16)
    Av = dscr("Av", [S, 780], BF16)
    BqT = dscr("BqT", [768, S], BF16)
    BkT = dscr("BkT", [768, S], BF16)
    Bv = dscr("Bv", [S, 520], BF16)
    CqT = dscr("CqT", [512, S], BF16)
    CkT = dscr("CkT", [512, S], BF16)
    Cv = dscr("Cv", [S, 512], BF16)
    Gt = dscr("Gt", [S, 3072], BF16)
    accA = dscr("accA", [3, S, 260], F32)
    oB = dscr("oB", [S, 512], BF16)
    oC = dscr("oC", [S, 512], BF16)

    k = K(nc)
    cf = k.sb([128, 5, 128], F32, "cf")
    cb = k.sb([128, 5, 128], BF16, "cb")
    k.load("sp", cf, cf[:], consts)
    k.op("dve", lambda e: e.tensor_copy(out=cb[:], in_=cf[:]), reads=[cf], writes=[cb])
    identb = cb[:, 0, :]
    identf = cf[:, 0, :]
    m_ge = cb[:, 1, :]
    m_gt = cb[:, 2, :]
    m_le = cb[:, 3, :]
    negU = cb[:, 4, :]
    onesb = k.sb([128, 1], BF16, "onesb")
    k.op("dve", lambda e: e.memset(onesb[:], 1.0), writes=[onesb])
    rA = k.sb([128, NT, 16], F32, "rA")
    rM = k.sb([128, NT, 32], F32, "rM")
    k.load("sp", rA, rA[:], ropeA)
    k.load("sp", rM, rM[:], ropeM)

    neghalf = k.sb([128, 24], F32, "neghalf")
    k.op("dve", lambda e: e.memset(neghalf[:], -0.5), writes=[neghalf])

    def rsq(rs_ap, ss_ap, n, inv_n, rs_t, ss_t):
        k.op("dve", lambda e: e.tensor_scalar(out=rs_ap, in0=ss_ap, scalar1=inv_n, scalar2=EPS, op0=ALU.mult, op1=ALU.add),
             reads=[ss_t], writes=[rs_t])
        k.op("pool", lambda e: e.tensor_tensor(out=rs_ap, in0=rs_ap, in1=neghalf[:, 0:n], op=ALU.pow),
             reads=[rs_t, neghalf], writes=[rs_t])

    ceng = Rot(["dve", "act", "pool"])

    def cast(e, out_ap, in_ap, reads, writes, scale=None):
        if e == "act":
            if scale is None:
                k.op("act", lambda g: g.activation(out=out_ap, in_=in_ap, func=AF.Copy), reads=reads, writes=writes)
            else:
                k.op("act", lambda g: g.activation(out=out_ap, in_=in_ap, func=AF.Copy, scale=scale),
                     reads=reads, writes=writes)
        else:
            if scale is None:
                k.op(e, lambda g: g.tensor_copy(out=out_ap, in_=in_ap), reads=reads, writes=writes)
            else:
                k.op(e, lambda g: g.tensor_scalar(out=out_ap, in0=in_ap, scalar1=scale, scalar2=None, op0=ALU.mult),
                     reads=reads, writes=writes)

    gcol = k.sb([128, nL, 19], F32, "gcol")
    for l in range(nL):
        k.load("sp", gcol, gcol[:, l, 0:8], g_attn[l])
        k.load("sp", gcol, gcol[:, l, 8:16], g_mlp[l])
        k.load("sp", gcol, gcol[:, l, 16:18], g_qa[l])
        k.load("sp", gcol, gcol[:, l, 18:19], g_kva[l])
    CB = 2048

    def prep_gen(ph, plist, engines):
        wf = ph.rot(4, [128, CB], F32, "wf")
        wb = ph.rot(3, [128, CB], BF16, "wb")
        items = []
        for (l, part) in plist:
            first = [(w_in[l], Wb_in[l], D, INC, 0), (w_qb[l], Wb_qb[l], 256, 768, 16),
                     (w_kvb[l], Wb_kvb[l], 128, 1024, 18)]
            rest = [(w_br[l], Wb_br[l], 1280, D, None),
                    (w_out[l], Wb_out[l], D, D, None), (w_ff1[l], Wb_ff1[l], D, DFF, 8),
                    (w_ff2[l], Wb_ff2[l], DFF, D, None)]
            jobs = first if part == "first" else rest if part == "rest" else first + rest
            for src, dst, R, C, gc in jobs:
                for c in range(R // 128):
                    for c0 in range(0, C, CB):
                        items.append((l, src, dst, c, c0, min(CB, C - c0), gc))
        loaded = {}

        def ld(i):
            l, src, dst, c, c0, n, gc = items[i]
            f = wf.next()
            k.load("sp", f, f[:, 0:n], src[c * 128:(c + 1) * 128, c0:c0 + n])
            loaded[i] = f

        for i in range(min(2, len(items))):
            ld(i)
        for i in range(len(items)):
            if i + 2 < len(items):
                ld(i + 2)
            l, src, dst, c, c0, n, gc = items[i]
            f = loaded.pop(i)
            b = wb.next()
            sc = None if gc is None else gcol[:, l, gc + c:gc + c + 1]
            e = engines.next()
            rd = [f] if gc is None else [f, gcol]
            cast(e, b[:, 0:n], f[:, 0:n], rd, [b], scale=sc)
            k.store("sp", b, dst[c * 128:(c + 1) * 128, c0:c0 + n], b[:, 0:n])
            yield

    def prep_weights(plist):
        ph = Phase(k)
        for _ in prep_gen(ph, plist, ceng):
            pass
        ph.end()

    def norm_transpose(ph, r, xsrc_rows, xt, hT_out, defer=False):
        ss = r["ss"].next()
        rstd = r["rstd"].next()
        junk = r["junk"]
        xn = r["xn"].next()
        k.op("act", lambda e: e.activation(out=junk[:], in_=xt[:], func=AF.Square, accum_out=ss[:]),
             reads=[xt], writes=[junk, ss])
        rsq(rstd[:], ss[:], 1, 1.0 / D, rstd, ss)
        k.op("act", lambda e: e.activation(out=xn[:], in_=xt[:], func=AF.Copy, scale=rstd[:]),
             reads=[xt, rstd], writes=[xn])
        hTt = r["hTtile"]

        def partB():
            pT = r["pT"].next()
            for c in range(8):
                k.op("pe", lambda e, c=c: e.transpose(out=pT[:, c, :], in_=xn[:, c * 128:(c + 1) * 128], identity=identb),
                     reads=[xn, cb], writes=[pT])
            k.op("dve", lambda e: e.tensor_copy(out=hT_out, in_=pT[:]), reads=[pT], writes=[hTt])
        if defer:
            return partB
        partB()

    def norm_res(ph, n=2):
        return {"ss": ph.rot(n, [128, 1], F32, "ss"), "rstd": ph.rot(n, [128, 1], F32, "rstd"),
                "junk": ph.sb([128, D], BF16, "junk"), "xn": ph.rot(2, [128, D], BF16, "xn"),
                "pT": ph.rot(1, [128, 8, 128], BF16, "pT", psum=True)}

    def rms_small(src_ap, nh, hd, reads_t, sq_t, ss_t, rs_t, extra_add=None):
        k.op("act", lambda e: e.activation(out=sq_t[:, 0:nh * hd].rearrange("p (h d) -> p h d", d=hd),
                                           in_=src_ap, func=AF.Square), reads=reads_t, writes=[sq_t])
        k.op("dve", lambda e: e.tensor_reduce(out=ss_t[:, 0:nh], in_=sq_t[:, 0:nh * hd].rearrange("p (h d) -> p h d", d=hd),
                                              axis=AX.X, op=ALU.add), reads=[sq_t], writes=[ss_t])

    def interleave(gens):
        gens = [g for g in gens if g is not None]
        while gens:
            for g in list(gens):
                try:
                    next(g)
                except StopIteration:
                    gens.remove(g)

    def phase1(l, s, xsrc, pas):
        ph = Phase(k)
        if pas == 0:
            c_lo, c_hi = 0, 3744
        else:
            c_lo, c_hi = 3744, INC
        NW = c_hi - c_lo
        segb = [0, 1536, 2720, 3744] if pas == 0 else [0, 1024, 2048, NW]
        Wsegs = []
        for si in range(3):
            a, b = segb[si], segb[si + 1]
            Wt = ph.sb([128, 8, b - a], BF16, "W%d" % si)
            k.load("sp", Wt, Wt[:], Wb_in[l][:, c_lo + a:c_lo + b].rearrange("(c p) n -> p c n", p=128))
            Wsegs.append((a, b, Wt))

        def wsl(c, col0, n):
            for a, b, Wt in Wsegs:
                if a <= col0 and col0 + n <= b:
                    return Wt[:, c, col0 - a:col0 - a + n], Wt
            raise AssertionError("chunk crosses W segment")
        r = norm_res(ph, 3)
        xts = ph.rot(3, [128, D], F32, "xt")
        hTs = ph.rot(2, [128, 8, 512], BF16, "hT4") if pas == 0 else ph.rot(3, [128, 8, 128], BF16, "hT")
        pj = ph.rot(2 if pas == 0 else 6, [128, 512], F32, "pj", psum=True)
        hT_of = {}
        x_of = {}
        if pas == 0:
            gaq = ph.sb([128, 128], F32, "gaq")
            gbq = ph.sb([128, 192], F32, "gbq")
            k.load("sp", gaq, gaq[:], g_aqk[l])
            k.load("sp", gbq, gbq[:], g_bqk[l])
            wqb = ph.sb([128, 2, 768], BF16, "wqb")
            wkvb = ph.sb([128, 1024], BF16, "wkvb")
            k.load("sp", wqb, wqb[:], Wb_qb[l].rearrange("(c p) n -> p c n", p=128))
            k.load("sp", wkvb, wkvb[:], Wb_kvb[l])
            qks = ph.rot(2, [128, 1536], F32, "qk")
            lats = ph.rot(2, [128, 416], F32, "lat")
            sqA = ph.sb([128, 1536], BF16, "sqA")
            ss24 = ph.sb([128, 24], F32, "ss24")
            rs24 = ph.sb([128, 24], F32, "rs24")
            qkb = ph.rot(2, [128, 1536], BF16, "qkb")
            rtA = [ph.sb([128, 24, 8], F32, "rtA%d" % i) for i in range(4)]
            x16 = ph.sb([128, 24, 16], F32, "x16")
            x32 = ph.sb([128, 8, 32], F32, "x32")
            rtK = [ph.sb([128, 1, 16], F32, "rtK%d" % i) for i in range(4)]
            sqK = ph.sb([128, 512], BF16, "sqK")
            ss8k = ph.sb([128, 8], F32, "ss8k")
            rs8k = ph.sb([128, 8], F32, "rs8k")
            rtM = [ph.sb([128, 8, 16], F32, "rtM%d" % i) for i in range(4)]
            vAs = ph.rot(2, [128, 12, 65], BF16, "vA")
            for t in vAs.tiles:
                k.op("dve", lambda e, t=t: e.memset(t[:], 1.0), writes=[t])
            pTA = ph.ps([128, 16, 128], BF16, "pTA")
            ATs = ph.rot(2, [128, 12, 128], BF16, "AT")
            sqM = ph.sb([128, 768], BF16, "sqM")
            ssl = ph.sb([128, 3], F32, "ssl")
            rsl = ph.sb([128, 3], F32, "rsl")
            latn = ph.sb([128, 384], BF16, "latn")
            pT2 = ph.ps([128, 8, 128], BF16, "pT2")
            latT = ph.sb([128, 3, 128], BF16, "latT")
            pM = ph.ps([128, 1024], F32, "pM")
            bq = ph.sb([128, 768], F32, "bq")
            bk = ph.sb([128, 8, 96], F32, "bk")
            ss8 = ph.sb([128, 8], F32, "ss8")
            rs8 = ph.sb([128, 8], F32, "rs8")
            bqbs = ph.rot(2, [128, 8, 96], BF16, "bqb")
            bkbs = ph.rot(2, [128, 8, 96], BF16, "bkb")
            tails = {}
            krg = ph.sb([128, 32], F32, "krg")
            krr = ph.sb([128, 32], F32, "krr")
            vBs = ph.rot(2, [128, 8, 65], BF16, "vB")
            for t in vBs.tiles:
                k.op("dve", lambda e, t=t: e.memset(t[:], 1.0), writes=[t])
            BTq = ph.rot(2, [96, 8, 128], BF16, "BTq")
            BTk = ph.rot(2, [96, 8, 128], BF16, "BTk")
            CTs = ph.rot(1, [128, 8, 512], BF16, "CT")
            qk_of, lat_of = {}, {}
        else:
            cvs = ph.rot(2, [128, 512], BF16, "cv")
            gts = ph.rot(2, [128, 3072], BF16, "gt")

        def proj_chunk(hTo, col0, n):
            hT, off = hTo
            p = pj.next()
            for c in range(8):
                wap, Wt = wsl(c, col0, n)
                k.op("pe", lambda e, c=c, wap=wap: e.matmul(out=p[:, 0:n], lhsT=hT[:, c, off:off + 128], rhs=wap,
                                                            start=(c == 0), stop=(c == 7)), reads=[hT, Wt], writes=[p])
            return p

        def rope(src, dst, nh, off, half, cs, sn, reads_t, dst_t, tmp):
            x1 = src[:, :, off:off + half]
            x2 = src[:, :, off + half:off + 2 * half]
            cB = bc(cs, 1, [128, nh, half])
            sB = bc(sn, 1, [128, nh, half])
            t1, t2, t3, t4 = [t[:, 0:nh, 0:half] for t in tmp]
            k.op("pool", lambda e: e.tensor_tensor(out=t1, in0=x1, in1=cB, op=ALU.mult), reads=reads_t, writes=[tmp[0]])
            k.op("pool", lambda e: e.tensor_tensor(out=t2, in0=x2, in1=sB, op=ALU.mult), reads=reads_t, writes=[tmp[1]])
            yield
            k.op("pool", lambda e: e.tensor_tensor(out=t3, in0=x2, in1=cB, op=ALU.mult), reads=reads_t, writes=[tmp[2]])
            k.op("pool", lambda e: e.tensor_tensor(out=t4, in0=x1, in1=sB, op=ALU.mult), reads=reads_t, writes=[tmp[3]])
            yield
            k.op("dve", lambda e: e.tensor_tensor(out=dst[:, :, off:off + half], in0=t1, in1=t2, op=ALU.subtract),
                 reads=[tmp[0], tmp[1]], writes=[dst_t])
            k.op("dve", lambda e: e.tensor_tensor(out=dst[:, :, off + half:off + 2 * half], in0=t3, in1=t4, op=ALU.add),
                 reads=[tmp[2], tmp[3]], writes=[dst_t])
            yield

        def Lx(tt):
            t0 = tt * 128
            xt = xts.next()
            k.load("sp", xt, xt[:], xsrc[t0:t0 + 128, :])
            x_of[tt] = xt

        def Nn(tt):
            xt = x_of.pop(tt)
            if pas == 0:
                if tt % 4 == 0:
                    hT_of["cur"] = hTs.next()
                hT = hT_of["cur"]
                off = (tt % 4) * 128
            else:
                hT = hTs.next()
                off = 0
            hT_of[tt] = (hT, off)
            r["hTtile"] = hT
            return norm_transpose(ph, r, None, xt, hT[:, :, off:off + 128], defer=True)

        def P0(tt):
            t0 = tt * 128
            hT = hT_of[tt]
            qk = qks.next()
            lat = lats.next()
            qk_of[tt] = qk
            lat_of[tt] = lat
            for j in range(3):
                p = proj_chunk(hT, j * 512, 512)
                cast("act", qk[:, j * 512:(j + 1) * 512], p[:, :], [p], [qk])
                yield
            p = proj_chunk(hT, 2304, 416)
            cast("dve", lat[:, :], p[:, 0:416], [p], [lat])
            yield
            vA = vAs.next()
            p = proj_chunk(hT, 1536, 512)
            cast("act", vA[:, 0:8, 0:64], p[:, :].rearrange("p (h d) -> p h d", d=64), [p], [vA])
            yield
            p = proj_chunk(hT, 2048, 256)
            cast("dve", vA[:, 8:12, 0:64], p[:, 0:256].rearrange("p (h d) -> p h d", d=64), [p], [vA])
            k.store("sp", vA, Av[t0:t0 + 128, :], vA[:].rearrange("p h d -> p (h d)"))
            yield

        def Cg(g):
            hT = hT_of[4 * g][0]
            CT = CTs.next()
            for j in range(8):
                col0 = 2720 + j * 128
                p = pj.next()
                for c in range(8):
                    wap, Wt = wsl(c, col0, 128)
                    k.op("pe", lambda e, c=c, wap=wap: e.matmul(out=p[:, 0:512], lhsT=wap, rhs=hT[:, c, 0:512],
                                                                start=(c == 0), stop=(c == 7)), reads=[hT, Wt], writes=[p])
                if j < 4:
                    k.op("act", lambda e, p=p, j=j: e.activation(out=CT[:, j, :], in_=p[:, :], func=AF.Copy, scale=0.125),
                         reads=[p], writes=[CT])
                else:
                    cast("dve", CT[:, j, :], p[:, :], [p], [CT])
                yield
            k.store("sp", CT, CqT.rearrange("(j p) s -> p j s", p=128)[:, :, g * 512:(g + 1) * 512], CT[:, 0:4, :])
            k.store("sp", CT, CkT.rearrange("(j p) s -> p j s", p=128)[:, :, g * 512:(g + 1) * 512], CT[:, 4:8, :])
            yield

        def P1(tt):
            t0 = tt * 128
            hT = hT_of[tt]
            cv = cvs.next()
            p = proj_chunk(hT, 0, 512)
            cast("dve", cv[:], p[:], [p], [cv])
            k.store("sp", cv, Cv[t0:t0 + 128, :], cv[:])
            gt = gts.next()
            for j in range(6):
                p = proj_chunk(hT, 512 + j * 512, 512)
                k.op("act", lambda e, p=p, j=j: e.activation(out=gt[:, j * 512:(j + 1) * 512], in_=p[:], func=AF.Sigmoid),
                     reads=[p], writes=[gt])
            k.store("sp", gt, Gt[t0:t0 + 128, :], gt[:])

        def YA(tt):
            t0 = tt * 128
            qk = qk_of[tt]
            qk3 = qk[:, :].rearrange("p (h d) -> p h d", d=64)
            rms_small(qk3, 24, 64, [qk], sqA, ss24, rs24)
            yield
            rsq(rs24[:], ss24[:], 24, 1.0 / 64, rs24, ss24)
            yield
            k.op("dve", lambda e: e.tensor_tensor(out=qk3, in0=qk3, in1=bc(rs24[:, :], 2, [128, 24, 64]), op=ALU.mult),
                 reads=[qk, rs24], writes=[qk])
            yield
            qk4 = qk[:, :].rearrange("p (a h d) -> p a h d", a=2, d=64)
            g4 = bc(gaq[:, :].rearrange("p (a d) -> p a d", a=2), 2, [128, 2, 12, 64])
            qb = qkb.next()
            qb4 = qb[:, :].rearrange("p (a h d) -> p a h d", a=2, d=64)
            k.op("dve", lambda e: e.tensor_tensor(out=qb4, in0=qk4, in1=g4, op=ALU.mult), reads=[qk, gaq], writes=[qb])
            x16v = x16[:, :, :].rearrange("p (a h) d -> p a h d", a=2)
            k.op("pool", lambda e: e.tensor_tensor(out=x16v, in0=qk4[:, :, :, 0:16], in1=g4[:, :, :, 0:16], op=ALU.mult),
                 reads=[qk, gaq], writes=[x16])
            yield
            qb3 = qb[:, :].rearrange("p (h d) -> p h d", d=64)
            yield from rope(x16[:, :, :], qb3, 24, 0, 8, rA[:, tt, 0:8], rA[:, tt, 8:16], [x16, rA], qb, rtA)

            def tailA():
                for j in range(12):
                    k.op("pe", lambda e, j=j: e.transpose(out=pTA[:, j, :], in_=qb[:, j * 128:(j + 1) * 128], identity=identb),
                         reads=[qb, cb], writes=[pTA])
                AT = ATs.next()
                cast("act", AT[:], pTA[:, 0:12, :], [pTA], [AT])
                k.store("sp", AT, AqT.rearrange("(j p) s -> p j s", p=128)[:, :, t0:t0 + 128], AT[:, 0:6, :])
                k.store("sp", AT, AkT.rearrange("(j p) s -> p j s", p=128)[:, :, t0:t0 + 128], AT[:, 6:12, :])
            tails.setdefault(tt, []).append(tailA)

        def YM(tt):
            t0 = tt * 128
            lat = lat_of[tt]
            k.op("act", lambda e: e.activation(out=sqM[:, 0:256], in_=lat[:, 0:256], func=AF.Square, accum_out=ssl[:, 0:1]),
                 reads=[lat], writes=[sqM, ssl])
            k.op("act", lambda e: e.activation(out=sqM[:, 256:384], in_=lat[:, 256:384], func=AF.Square, accum_out=ssl[:, 1:2]),
                 reads=[lat], writes=[sqM, ssl])
            k.op("act", lambda e: e.activation(out=sqM[:, 384:416], in_=lat[:, 384:416], func=AF.Square, accum_out=ssl[:, 2:3]),
                 reads=[lat], writes=[sqM, ssl])
            yield
            k.op("dve", lambda e: e.tensor_scalar(out=rsl[:, 0:1], in0=ssl[:, 0:1], scalar1=1.0 / 256, scalar2=EPS,
                                                  op0=ALU.mult, op1=ALU.add), reads=[ssl], writes=[rsl])
            k.op("dve", lambda e: e.tensor_scalar(out=rsl[:, 1:2], in0=ssl[:, 1:2], scalar1=1.0 / 128, scalar2=EPS,
                                                  op0=ALU.mult, op1=ALU.add), reads=[ssl], writes=[rsl])
            k.op("pool", lambda e: e.tensor_tensor(out=rsl[:, 0:2], in0=rsl[:, 0:2], in1=neghalf[:, 0:2], op=ALU.pow),
                 reads=[rsl, neghalf], writes=[rsl])
            yield
            cast("dve", latn[:, 0:256], lat[:, 0:256], [lat, rsl], [latn], scale=rsl[:, 0:1])
            cast("dve", latn[:, 256:384], lat[:, 256:384], [lat, rsl], [latn], scale=rsl[:, 1:2])
            yield
            for j in range(3):
                k.op("pe", lambda e, j=j: e.transpose(out=pT2[:, j, :], in_=latn[:, j * 128:(j + 1) * 128], identity=identb),
                     reads=[latn, cb], writes=[pT2])
            cast("dve", latT[:], pT2[:, 0:3, :], [pT2], [latT])
            yield
            for (c0, n) in ((0, 512), (512, 256)):
                for c in range(2):
                    k.op("pe", lambda e, c=c, c0=c0, n=n: e.matmul(out=pM[:, c0:c0 + n], lhsT=latT[:, c, :],
                                                                  rhs=wqb[:, c, c0:c0 + n], start=(c == 0), stop=(c == 1)),
                         reads=[latT, wqb], writes=[pM])
            cast("act", bq[:, :], pM[:, 0:768], [pM], [bq])
            yield
            for c0 in (0, 512):
                k.op("pe", lambda e, c0=c0: e.matmul(out=pM[:, c0:c0 + 512], lhsT=latT[:, 2, :], rhs=wkvb[:, c0:c0 + 512],
                                                    start=True, stop=True), reads=[latT, wkvb], writes=[pM])
            vB = vBs.next()
            pM3 = pM[:, :].rearrange("p (h d) -> p h d", d=128)
            cast("act", vB[:, :, 0:64], pM3[:, :, 64:128], [pM], [vB])
            cast("dve", bk[:, :, 0:64], pM3[:, :, 0:64], [pM], [bk])
            k.store("sp", vB, Bv[t0:t0 + 128, :], vB[:].rearrange("p h d -> p (h d)"))
            yield
            yield from rr2(Qc(tt), Kc(tt, lat))

        def rr2(g1, g2):
            gens = [g1, g2]
            while gens:
                for g in list(gens):
                    try:
                        next(g)
                        yield
                    except StopIteration:
                        gens.remove(g)

        def Qc(tt):
            t0 = tt * 128
            bqb = bqbs.next()
            bq3 = bq[:, :].rearrange("p (h d) -> p h d", d=96)
            rms_small(bq3, 8, 96, [bq], sqM, ss8, rs8)
            yield
            rsq(rs8[:], ss8[:], 8, 1.0 / 96, rs8, ss8)
            yield
            k.op("dve", lambda e: e.tensor_tensor(out=bq3, in0=bq3, in1=bc(rs8[:, :], 2, [128, 8, 96]), op=ALU.mult),
                 reads=[bq, rs8], writes=[bq])
            yield
            k.op("dve", lambda e: e.tensor_tensor(out=bqb[:, :, 0:64], in0=bq3[:, :, 0:64],
                                                  in1=bc(gbq[:, 0:64], 1, [128, 8, 64]), op=ALU.mult),
                 reads=[bq, gbq], writes=[bqb])
            k.op("pool", lambda e: e.tensor_tensor(out=x32[:, :, :], in0=bq3[:, :, 64:96],
                                                   in1=bc(gbq[:, 64:96], 1, [128, 8, 32]), op=ALU.mult),
                 reads=[bq, gbq], writes=[x32])
            yield
            yield from rope(x32[:, :, :], bqb[:, :, 64:96], 8, 0, 16, rM[:, tt, 0:16], rM[:, tt, 16:32], [x32, rM], bqb, rtM)

            def tailQ():
                for h in range(8):
                    k.op("pe", lambda e, h=h: e.transpose(out=pT2[0:96, h, :], in_=bqb[:, h, :], identity=identb),
                         reads=[bqb, cb], writes=[pT2])
                Bq = BTq.next()
                cast("act", Bq[:], pT2[0:96, :, :], [pT2], [Bq])
                k.store("sp", Bq, BqT.rearrange("(h f) s -> f h s", f=96)[:, :, t0:t0 + 128], Bq[:])
            tails.setdefault(tt, []).append(tailQ)

        def Kc(tt, lat):
            t0 = tt * 128
            bkb = bkbs.next()
            rms_small(bk[:, :, 0:64], 8, 64, [bk], sqK, ss8k, rs8k)
            yield
            k.op("dve", lambda e: e.tensor_scalar(out=ss8k[:], in0=ss8k[:], scalar1=ssl[:, 2:3], scalar2=None, op0=ALU.add),
                 reads=[ss8k, ssl], writes=[ss8k])
            rsq(rs8k[:], ss8k[:], 8, 1.0 / 96, rs8k, ss8k)
            yield
            k.op("dve", lambda e: e.tensor_tensor(out=krg[:, :], in0=lat[:, 384:416], in1=gbq[:, 160:192], op=ALU.mult),
                 reads=[lat, gbq], writes=[krg])
            yield from rope(krg[:, :].unsqueeze(1), krr[:, :].unsqueeze(1), 1, 0, 16, rM[:, tt, 0:16], rM[:, tt, 16:32],
                            [krg, rM], krr, rtK)
            k.op("dve", lambda e: e.tensor_tensor(out=bk[:, :, 0:64], in0=bk[:, :, 0:64],
                                                  in1=bc(gbq[:, 96:160], 1, [128, 8, 64]), op=ALU.mult),
                 reads=[bk, gbq], writes=[bk])
            yield
            k.op("dve", lambda e: e.tensor_tensor(out=bkb[:, :, 0:64], in0=bk[:, :, 0:64],
                                                  in1=bc(rs8k[:, :], 2, [128, 8, 64]), op=ALU.mult),
                 reads=[bk, rs8k], writes=[bkb])
            k.op("dve", lambda e: e.tensor_tensor(out=bkb[:, :, 64:96], in0=bc(krr[:, :], 1, [128, 8, 32]),
                                                  in1=bc(rs8k[:, :], 2, [128, 8, 32]), op=ALU.mult),
                 reads=[krr, rs8k], writes=[bkb])
            yield

            def tailK():
                for h in range(8):
                    k.op("pe", lambda e, h=h: e.transpose(out=pT2[0:96, h, :], in_=bkb[:, h, :], identity=identb),
                         reads=[bkb, cb], writes=[pT2])
                Bk = BTk.next()
                cast("act", Bk[:], pT2[0:96, :, :], [pT2], [Bk])
                k.store("sp", Bk, BkT.rearrange("(h f) s -> f h s", f=96)[:, :, t0:t0 + 128], Bk[:])
            tails.setdefault(tt, []).append(tailK)

        nt = min(NT, TLIM)
        for tt in range(min(2, nt)):
            Lx(tt)
        Nn(0)()
        if nt > 1:
            if nt > 2:
                Lx(2)
            Nn(1)()
        if pas == 0:
            interleave([P0(0)])
        for tt in range(nt):
            if tt + 3 < nt:
                Lx(tt + 3)
            nB = None
            if tt + 2 < nt:
                nB = Nn(tt + 2)
            if pas == 0:
                def P0n(tt=tt, nB=nB):
                    if tt + 1 < nt:
                        yield from P0(tt + 1)
                    if nB:
                        nB()
                    yield
                interleave([P0n(), YA(tt), YM(tt), Cg(tt // 4) if tt % 4 == 3 else None])
                for f in tails.pop(tt - 1, []):
                    f()
            else:
                P1(tt)
                if nB:
                    nB()
        if pas == 0:
            for f in tails.pop(nt - 1, []):
                f()
        ph.end()

    def phase2():
        ph = Phase(k)
        qn = [ph.sb([64, S], BF16, "qn%d" % i) for i in range(4)]
        kn = [ph.sb([64, S], BF16, "kn%d" % i) for i in range(4)]
        qd = [ph.sb([64, S], BF16, "qd%d" % i) for i in range(4)]
        kd = [ph.sb([64, S], BF16, "kd%d" % i) for i in range(4)]
        vA = ph.sb([128, NT, 4, 65], BF16, "vAall")
        mb = ph.sb([128, 2, 256], BF16, "mband")
        for hi in range(2):
            k.op("dve", lambda e, hi=hi: e.tensor_copy(out=mb[:, hi, 0:128], in_=m_ge), reads=[cb], writes=[mb])
            k.op("dve", lambda e, hi=hi: e.tensor_copy(out=mb[:, hi, 128:256], in_=m_le), reads=[cb], writes=[mb])
        S2 = ph.rot(6, [128, 2, 256], F32, "S2", psum=True)
        Pt = [ph.rot(5, [128, 2, 256], BF16, "Pt%d_" % hp) for hp in range(2)]
        O4 = ph.rot(2, [128, 512], F32, "O4", psum=True)
        osb = ph.rot(2, [128, 260], F32, "osb")
        for g, d in enumerate((1, 4, 16)):
            L = S // d
            nb = L // 128
            for hs in range(4):
                h = g * 4 + hs
                k.load("sp", qn[hs], qn[hs][:], AqT[h * 64:(h + 1) * 64, :])
                k.load("sp", kn[hs], kn[hs][:], AkT[h * 64:(h + 1) * 64, :])
                if d > 1:
                    k.op("pool", lambda e, hs=hs: e.tensor_copy(out=qd[hs][:, :].rearrange("p (r u) -> p r u", r=d),
                                                                in_=qn[hs][:, :].rearrange("p (u r) -> p r u", r=d)),
                         reads=[qn[hs]], writes=[qd[hs]])
                    k.op("dve", lambda e, hs=hs: e.tensor_copy(out=kd[hs][:, :].rearrange("p (r u) -> p r u", r=d),
                                                               in_=kn[hs][:, :].rearrange("p (u r) -> p r u", r=d)),
                         reads=[kn[hs]], writes=[kd[hs]])
            Q = qd if d > 1 else qn
            Kk = kd if d > 1 else kn
            for r in range(d):
                src = Av.rearrange("(n p r) c -> r p n c", p=128, r=d)[r][:, :, g * 260:(g + 1) * 260]
                k.load("sp", vA, vA[:, r * nb:(r + 1) * nb, :, :].rearrange("p n h d -> p n (h d)"), src)
            blocks = [(r, n) for r in range(d) for n in range(nb)]
            Pof = {}

            def QE(i):
                r, n = blocks[i]
                col0 = r * L + n * 128
                nq = 256 if n < nb - 1 else 128
                cur = []
                for hp in range(2):
                    s2 = S2.next()
                    for hi in range(2):
                        hs = hp * 2 + hi
                        k.op("pe", lambda e, hs=hs, hi=hi, s2=s2: e.matmul(
                            out=s2[:, hi, 0:nq], lhsT=Kk[hs][:, col0:col0 + 128], rhs=Q[hs][:, col0:col0 + nq],
                            start=True, stop=True), reads=[Kk[hs], Q[hs]], writes=[s2])
                    pt = Pt[hp].next()
                    k.op("act", lambda e, s2=s2, pt=pt: e.activation(out=pt[:, :, 0:nq], in_=s2[:, :, 0:nq], func=AF.Exp,
                                                                     scale=0.125), reads=[s2], writes=[pt])
                    k.op("pool", lambda e, pt=pt: e.tensor_tensor(out=pt[:, :, 0:nq], in0=pt[:, :, 0:nq],
                                                                  in1=mb[:, :, 0:nq], op=ALU.mult),
                         reads=[pt, mb], writes=[pt])
                    cur.append(pt)
                Pof[i] = cur

            def PVs(i):
                r, n = blocks[i]
                b = r * nb + n
                curP = Pof[i]
                prevP = Pof.get(i - 1) if n > 0 else None
                o4t = O4.next()
                o4 = o4t[:, 0:260].rearrange("p (h d) -> p h d", d=65)
                for hs in range(4):
                    hp, hi = hs // 2, hs % 2
                    if n > 0:
                        k.op("pe", lambda e, hs=hs, hp=hp, hi=hi: e.matmul(
                            out=o4[:, hs, :], lhsT=prevP[hp][:, hi, 128:256], rhs=vA[:, b - 1, hs, :],
                            start=True, stop=False), reads=[prevP[hp], vA], writes=[o4t])
                    k.op("pe", lambda e, hs=hs, hp=hp, hi=hi: e.matmul(
                        out=o4[:, hs, :], lhsT=curP[hp][:, hi, 0:128], rhs=vA[:, b, hs, :],
                        start=(n == 0), stop=True), reads=[curP[hp], vA], writes=[o4t])
                ob = osb.next()
                cast("act", ob[:, :], o4t[:, 0:260], [o4t], [ob])
                dst = accA[g].rearrange("(n p r) c -> r n p c", p=128, r=d)[r, n]
                k.store("sp", ob, dst, ob[:, :])
                Pof.pop(i - 1, None)

            QE(0)
            if len(blocks) > 1:
                QE(1)
            for i in range(len(blocks)):
                if i + 2 < len(blocks):
                    QE(i + 2)
                PVs(i)
        ph.end()

    def phase3(prep_l=None):
        ph = Phase(k)
        pg = prep_gen(ph, prep_l, Rot(["dve"])) if prep_l else None
        qTs = ph.rot(2, [96, S], BF16, "bqT")
        kTs = ph.rot(2, [96, S], BF16, "bkT")
        VCH = NT // 4
        vBs = [ph.sb([128, VCH, 520], BF16, "vB%d" % i) for i in range(4)]
        oB_sb = ph.sb([128, NT, 512], BF16, "oB_sb")
        Sb = ph.rot(4, [128, 512], F32, "Sb", psum=True)
        Pts = ph.rot(5, [128, 512], BF16, "Ptb")
        Ob = ph.rot(2, [65, 512], F32, "Ob", psum=True)
        Osb = ph.rot(2, [65, 512], F32, "Osb")
        pTo = ph.rot(1, [128, 512], F32, "pTo", psum=True)
        rden = ph.rot(2, [128, 4], F32, "rden")
        sc = 96 ** -0.5
        heads = {}

        def load_head(h):
            qT = qTs.next()
            kT = kTs.next()
            k.load("sp", qT, qT[:], BqT[h * 96:(h + 1) * 96, :])
            k.load("sp", kT, kT[:], BkT[h * 96:(h + 1) * 96, :])
            heads[h] = (qT, kT)

        load_head(0)
        for i in range(4):
            k.load("sp", vBs[i], vBs[i][:], Bv.rearrange("(t p) c -> p t c", p=128)[:, i * VCH:(i + 1) * VCH, :])
        units = []
        for h in range(8):
            for qg in range(NQG):
                for kb in range(4 * qg + 4):
                    units.append({"h": h, "qg": qg, "kb": kb, "c0": max(0, kb - 4 * qg) * 128, "last": kb == 4 * qg + 3})
        grp = {}

        def A(u):
            h, qg, kb, c0 = u["h"], u["qg"], u["kb"], u["c0"]
            if h not in heads:
                load_head(h)
            if kb == 0 and qg == 0 and h + 1 < 8 and (h + 1) not in heads:
                load_head(h + 1)
            qT, kT = heads[h]
            sb_ = Sb.next()
            u["sb"] = sb_
            k.op("pe", lambda e: e.matmul(out=sb_[:, c0:512], lhsT=kT[:, kb * 128:(kb + 1) * 128],
                                          rhs=qT[:, qg * 512 + c0:(qg + 1) * 512], start=True, stop=True),
                 reads=[kT, qT], writes=[sb_])

        def B(u):
            c0, sb_ = u["c0"], u["sb"]
            pt = Pts.next()
            u["pt"] = pt
            k.op("act", lambda e: e.activation(out=pt[:, c0:512], in_=sb_[:, c0:512], func=AF.Exp, scale=sc),
                 reads=[sb_], writes=[pt])
            if u["kb"] >= 4 * u["qg"]:
                k.op("pool", lambda e: e.tensor_tensor(out=pt[:, c0:c0 + 128], in0=pt[:, c0:c0 + 128], in1=m_ge,
                                                       op=ALU.mult), reads=[pt, cb], writes=[pt])

        def F(u):
            h, qg, kb, c0, pt = u["h"], u["qg"], u["kb"], u["c0"], u["pt"]
            if kb == 0:
                grp[(h, qg)] = Ob.next()
            ob = grp[(h, qg)]
            vBc = vBs[kb // VCH]
            k.op("pe", lambda e: e.matmul(out=ob[:, c0:512], lhsT=vBc[:, kb % VCH, h * 65:(h + 1) * 65], rhs=pt[:, c0:512],
                                          start=(kb == 0), stop=u["last"]), reads=[vBc, pt], writes=[ob])
            if u["last"]:
                osb_ = Osb.next()
                cast("act", osb_[:, :], ob[:, :], [ob], [osb_])
                ptot = pTo.next()
                pto = ptot[:, 0:260].rearrange("p (j d) -> p j d", d=65)
                for j in range(4):
                    k.op("pe", lambda e, j=j: e.transpose(out=pto[:, j, :], in_=osb_[:, j * 128:(j + 1) * 128],
                                                          identity=identf[0:65, 0:65]), reads=[osb_, cf], writes=[ptot])
                rd = rden.next()
                k.op("dve", lambda e: e.reciprocal(out=rd[:, :], in_=pto[:, :, 64]), reads=[ptot], writes=[rd])
                k.op("dve", lambda e: e.tensor_tensor(out=oB_sb[:, 4 * qg:4 * qg + 4, h * 64:(h + 1) * 64], in0=pto[:, :, 0:64],
                                                      in1=bc(rd[:, :], 2, [128, 4, 64]), op=ALU.mult),
                     reads=[ptot, rd], writes=[oB_sb])

        N = len(units)
        A(units[0])
        for t in range(N + 1):
            if t + 1 < N:
                A(units[t + 1])
            if t >= 1:
                F(units[t - 1])
            if t < N:
                B(units[t])
            if pg is not None and t % 6 == 3:
                try:
                    next(pg)
                except StopIteration:
                    pg = None
        if pg is not None:
            for _ in pg:
                pass
        k.store("sp", oB_sb, oB.rearrange("(t p) c -> p t c", p=128), oB_sb[:])
        ph.end()

    def phase4():
        ph = Phase(k)
        qTs = ph.rot(2, [64, S], BF16, "cqT")
        kTs = ph.rot(2, [64, S], BF16, "ckT")
        VCH = NT // 4
        vCs = [ph.sb([128, VCH, 512], BF16, "vC%d" % i) for i in range(4)]
        oC_sb = ph.sb([128, NT, 512], BF16, "oC_sb")
        Zb = ph.rot(5, [128, 512], F32, "Zb", psum=True)
        Nb = ph.rot(2, [128, 512], F32, "Nb", psum=True)
        es = ph.rot(2, [128, 512], F32, "e_")
        sps = ph.rot(4, [128, 512], BF16, "sp_")
        Es = ph.rot(3, [128, 512], BF16, "E_")
        gts = ph.rot(2, [128, 4], F32, "g_")
        accs = ph.rot(2, [128, 4, 64], F32, "acc")
        heads = {}

        def load_head(h):
            qT = qTs.next()
            kT = kTs.next()
            k.load("sp", qT, qT[:], CqT[h * 64:(h + 1) * 64, :])
            k.load("sp", kT, kT[:], CkT[h * 64:(h + 1) * 64, :])
            heads[h] = (qT, kT)

        load_head(0)
        for i in range(4):
            k.load("sp", vCs[i], vCs[i][:], Cv.rearrange("(t p) c -> p t c", p=128)[:, i * VCH:(i + 1) * VCH, :])
        units = []
        for h in range(8):
            for qg in range(NQG):
                for kb in range(4 * qg + 4):
                    j0 = max(0, kb - 4 * qg)
                    units.append({"h": h, "qg": qg, "kb": kb, "j0": j0, "c0": j0 * 128, "diag": kb >= 4 * qg,
                                  "last": kb == 4 * qg + 3})
        grp = {}

        def A(u):
            h, qg, kb, c0 = u["h"], u["qg"], u["kb"], u["c0"]
            if h not in heads:
                load_head(h)
            if kb == 0 and qg == 0 and h + 1 < 8 and (h + 1) not in heads:
                load_head(h + 1)
            qT, kT = heads[h]
            zb = Zb.next()
            u["zb"] = zb
            k.op("pe", lambda e: e.matmul(out=zb[:, c0:512], lhsT=kT[:, kb * 128:(kb + 1) * 128],
                                          rhs=qT[:, qg * 512 + c0:(qg + 1) * 512], start=True, stop=True),
                 reads=[kT, qT], writes=[zb])

        def B(u):
            c0, zb = u["c0"], u["zb"]
            ee = es.next()
            k.op("act", lambda e: e.activation(out=ee[:, c0:512], in_=zb[:, c0:512], func=AF.Exp), reads=[zb], writes=[ee])
            sp = sps.next()
            u["sp"] = sp
            k.op("act", lambda e: e.activation(out=sp[:, c0:512], in_=ee[:, c0:512], func=AF.Ln, bias=1.0),
                 reads=[ee], writes=[sp])
            if u["diag"]:
                k.op("pool", lambda e: e.tensor_tensor(out=sp[:, c0:c0 + 128], in0=sp[:, c0:c0 + 128], in1=m_gt,
                                                       op=ALU.mult), reads=[sp, cb], writes=[sp])

        def C(u):
            c0, zb, sp = u["c0"], u["zb"], u["sp"]
            k.op("pe", lambda e: e.matmul(out=zb[:, c0:512], lhsT=negU, rhs=sp[:, c0:512], start=False, stop=True,
                                          skip_group_check=True), reads=[cb, sp], writes=[zb])

        def Dd(u):
            c0, zb = u["c0"], u["zb"]
            E = Es.next()
            u["E"] = E
            k.op("act", lambda e: e.activation(out=E[:, c0:512], in_=zb[:, c0:512], func=AF.Exp), reads=[zb], writes=[E])
            if u["diag"]:
                k.op("pool", lambda e: e.tensor_tensor(out=E[:, c0:c0 + 128], in0=E[:, c0:c0 + 128], in1=m_gt,
                                                       op=ALU.mult), reads=[E, cb], writes=[E])

        def F(u):
            h, kb, j0, E, sp = u["h"], u["kb"], u["j0"], u["E"], u["sp"]
            nbt = Nb.next()
            u["nbt"] = nbt
            nbk = nbt[:, 0:260].rearrange("p (j d) -> p j d", d=65)
            vCc = vCs[kb // VCH]
            for j in range(j0, 4):
                k.op("pe", lambda e, j=j: e.matmul(out=nbk[:, j, 0:64], lhsT=E[:, j * 128:(j + 1) * 128],
                                                   rhs=vCc[:, kb % VCH, h * 64:(h + 1) * 64], start=True, stop=True,
                                                   skip_group_check=True), reads=[E, vCc], writes=[nbt])
                k.op("pe", lambda e, j=j: e.matmul(out=nbk[:, j, 64:65], lhsT=sp[:, j * 128:(j + 1) * 128],
                                                   rhs=onesb[:, 0:1], start=True, stop=True,
                                                   skip_group_check=True), reads=[sp, onesb], writes=[nbt])

        def G(u):
            h, qg, kb, j0, nbt = u["h"], u["qg"], u["kb"], u["j0"], u["nbt"]
            nbk = nbt[:, 0:260].rearrange("p (j d) -> p j d", d=65)
            if kb == 0:
                acc = accs.next()
                grp[(h, qg)] = acc
                k.op("dve", lambda e: e.tensor_copy(out=acc[:], in_=nbk[:, :, 0:64]), reads=[nbt], writes=[acc])
            else:
                acc = grp[(h, qg)]
                gt = gts.next()
                k.op("act", lambda e: e.activation(out=gt[:, j0:4], in_=nbk[:, j0:4, 64], func=AF.Exp, scale=-1.0),
                     reads=[nbt], writes=[gt])
                for j in range(j0, 4):
                    k.op("dve", lambda e, j=j: e.scalar_tensor_tensor(out=acc[:, j, :], in0=acc[:, j, :], scalar=gt[:, j:j + 1],
                                                                      in1=nbk[:, j, 0:64], op0=ALU.mult, op1=ALU.add),
                         reads=[acc, gt, nbt], writes=[acc])
            if u["last"]:
                k.op("pool", lambda e: e.tensor_copy(out=oC_sb[:, 4 * qg:4 * qg + 4, h * 64:(h + 1) * 64], in_=acc[:]),
                     reads=[acc], writes=[oC_sb])

        N = len(units)
        A(units[0])
        for t in range(N + 2):
            if t + 1 < N:
                A(units[t + 1])
            if t < N:
                B(units[t])
            if 1 <= t <= N:
                C(units[t - 1])
                Dd(units[t - 1])
            if t >= 2:
                F(units[t - 2])
                G(units[t - 2])
        k.store("sp", oC_sb, oC.rearrange("(t p) c -> p t c", p=128), oC_sb[:])
        ph.end()

    def phase5(l, s, xsrc):
        ph = Phase(k)
        Wbr = ph.sb([128, 10, D], BF16, "Wbr")
        Wo = ph.sb([128, 8, D], BF16, "Wo")
        k.load("sp", Wbr, Wbr[:], Wb_br[l].rearrange("(c p) n -> p c n", p=128))
        k.load("sp", Wo, Wo[:], Wb_out[l].rearrange("(c p) n -> p c n", p=128))
        xts = ph.rot(3, [128, D], F32, "xt")
        a3s = ph.rot(3, [128, 3, 260], F32, "a3")
        ocs = ph.rot(3, [128, 1280], BF16, "ocat")
        gts = ph.rot(3, [128, 3072], BF16, "gt")
        rds = ph.rot(2, [128, 4], F32, "rd")
        pT = ph.ps([128, 16, 128], BF16, "pT")
        pTm = ph.ps([128, 8, 128], BF16, "pTm")
        oTs = ph.rot(2, [128, 10, 128], BF16, "oT")
        PP = ph.rot(2, [128, D], F32, "PP", psum=True)
        tf = ph.sb([128, D], F32, "tf")
        uf = ph.sb([128, D], F32, "uf")
        u2 = ph.sb([128, D], F32, "u2")
        mbfs = ph.rot(2, [128, D], BF16, "mbf")
        mTs = ph.rot(2, [128, 8, 128], BF16, "mT")
        xo = ph.rot(2, [128, D], F32, "xo")
        st = {}

        def L(tt):
            t0 = tt * 128
            xt, a3, oc, gt = xts.next(), a3s.next(), ocs.next(), gts.next()
            k.load("sp", xt, xt[:], xsrc[t0:t0 + 128, :])
            k.load("sp", a3, a3[:], accA[:, t0:t0 + 128, :].rearrange("g p c -> p g c"))
            k.load("sp", oc, oc[:, 256:768], oB[t0:t0 + 128, :])
            k.load("sp", oc, oc[:, 768:1280], oC[t0:t0 + 128, :])
            k.load("sp", gt, gt[:], Gt[t0:t0 + 128, :])
            st[tt] = {"xt": xt, "a3": a3, "oc": oc, "gt": gt}

        def branch(P, oT, ca, cbn):
            for half in range(2):
                for c in range(ca, cbn):
                    k.op("pe", lambda e, c=c, half=half: e.matmul(
                        out=P[:, half * 512:(half + 1) * 512], lhsT=oT[:, c, :], rhs=Wbr[:, c, half * 512:(half + 1) * 512],
                        start=(c == ca), stop=(c == cbn - 1)), reads=[oT, Wbr], writes=[P])

        def S1(tt):
            d = st[tt]
            a3, oc, gt = d["a3"], d["oc"], d["gt"]
            k.op("pool", lambda e: e.tensor_tensor(out=a3[:, 0, :], in0=a3[:, 0, :], in1=a3[:, 1, :], op=ALU.add),
                 reads=[a3], writes=[a3])
            k.op("pool", lambda e: e.tensor_tensor(out=a3[:, 0, :], in0=a3[:, 0, :], in1=a3[:, 2, :], op=ALU.add),
                 reads=[a3], writes=[a3])
            n4 = a3[:, 0, :].rearrange("p (h d) -> p h d", d=65)
            rd = rds.next()
            k.op("dve", lambda e: e.reciprocal(out=rd[:, :], in_=n4[:, :, 64]), reads=[a3], writes=[rd])
            k.op("dve", lambda e: e.tensor_tensor(out=oc[:, 0:256].rearrange("p (h d) -> p h d", d=64), in0=n4[:, :, 0:64],
                                                  in1=bc(rd[:, :], 2, [128, 4, 64]), op=ALU.mult), reads=[a3, rd], writes=[oc])
            for j in range(10):
                k.op("pe", lambda e, j=j: e.transpose(out=pT[:, j, :], in_=oc[:, j * 128:(j + 1) * 128], identity=identb),
                     reads=[oc, cb], writes=[pT])
            oT = oTs.next()
            cast("act", oT[:], pT[:, 0:10, :], [pT], [oT])
            Pa = PP.next()
            branch(Pa, oT, 0, 2)
            k.op("dve", lambda e: e.tensor_tensor(out=tf[:], in0=Pa[:], in1=gt[:, 0:1024], op=ALU.mult),
                 reads=[Pa, gt], writes=[tf])
            Pb = PP.next()
            branch(Pb, oT, 2, 6)
            k.op("dve", lambda e: e.tensor_tensor(out=uf[:], in0=Pb[:], in1=gt[:, 1024:2048], op=ALU.mult),
                 reads=[Pb, gt], writes=[uf])
            k.op("pool", lambda e: e.tensor_tensor(out=tf[:], in0=tf[:], in1=uf[:], op=ALU.add), reads=[tf, uf], writes=[tf])
            Pc = PP.next()
            branch(Pc, oT, 6, 10)
            k.op("dve", lambda e: e.tensor_tensor(out=u2[:], in0=Pc[:], in1=gt[:, 2048:3072], op=ALU.mult),
                 reads=[Pc, gt], writes=[u2])
            mbf = mbfs.next()
            k.op("pool", lambda e: e.tensor_tensor(out=mbf[:], in0=tf[:], in1=u2[:], op=ALU.add), reads=[tf, u2], writes=[mbf])
            d["mbf"] = mbf

        def S2(tt):
            t0 = tt * 128
            d = st.pop(tt)
            mbf, xt = d["mbf"], d["xt"]
            for j in range(8):
                k.op("pe", lambda e, j=j: e.transpose(out=pTm[:, j, :], in_=mbf[:, j * 128:(j + 1) * 128], identity=identb),
                     reads=[mbf, cb], writes=[pTm])
            mT = mTs.next()
            cast("act", mT[:], pTm[:], [pTm], [mT])
            P = PP.next()
            for half in range(2):
                for c in range(8):
                    k.op("pe", lambda e, c=c, half=half: e.matmul(
                        out=P[:, half * 512:(half + 1) * 512], lhsT=mT[:, c, :], rhs=Wo[:, c, half * 512:(half + 1) * 512],
                        start=(c == 0), stop=(c == 7)), reads=[mT, Wo], writes=[P])
            o = xo.next()
            k.op("dve", lambda e: e.tensor_tensor(out=o[:], in0=P[:], in1=xt[:], op=ALU.add), reads=[P, xt], writes=[o])
            k.store("sp", o, xmid[s][t0:t0 + 128, :], o[:])

        L(0)
        L(1)
        S1(0)
        for tt in range(NT):
            if tt + 2 < NT:
                L(tt + 2)
            if tt + 1 < NT:
                S1(tt + 1)
            S2(tt)
        ph.end()

    def phase6(l, dsts):
        ph = Phase(k)
        W1q = [ph.sb([128, 8, 1024], BF16, "W1q%d" % i) for i in range(4)]
        W2 = ph.sb([128, 32, D], BF16, "W2")
        for i in range(4):
            k.load("sp", W1q[i], W1q[i][:], Wb_ff1[l][:, i * 1024:(i + 1) * 1024].rearrange("(c p) n -> p c n", p=128))
        for c in range(4):
            k.load("sp", W2, W2[:, c * 8:(c + 1) * 8, :],
                   Wb_ff2[l][c * 1024:(c + 1) * 1024, :].rearrange("(c p) n -> p c n", p=128))
        r = norm_res(ph)
        xts = ph.rot(4, [128, D], F32, "xt")
        h2T = ph.rot(2, [128, 8, 256], BF16, "h2T")
        h1T = ph.sb([128, 32, 256], BF16, "h1T")
        rl = ph.rot(2, [128, 2, 256], F32, "rl")
        pF = ph.rot(2, [128, 2, 256], F32, "pF", psum=True)
        pY = ph.rot(3, [128, 512], F32, "pY", psum=True)
        xo = ph.rot(1, [128, D], F32, "xo")
        groups = [(s, tg) for s in range(nS) for tg in range(NT // 2)]
        gstate = {}

        def Ng(gi):
            s, tg = groups[gi]
            hT = h2T.next()
            xt2 = []
            for i in range(2):
                t0 = (tg * 2 + i) * 128
                xt = xts.next()
                k.load("sp", xt, xt[:], xmid[s][t0:t0 + 128, :])
                r["hTtile"] = hT
                pend.append(norm_transpose(ph, r, None, xt, hT[:, :, i * 128:(i + 1) * 128], defer=True))
                xt2.append(xt)
            gstate[gi] = (hT, xt2)

        pend = []
        Ng(0)
        for f in pend:
            f()
        pend.clear()
        for gi, (s, tg) in enumerate(groups):
            if True:
                if gi + 1 < len(groups):
                    Ng(gi + 1)
                hT, xt2 = gstate.pop(gi)
                for f2 in range(16):
                    p = pF.next()
                    for fi in range(2):
                        f = f2 * 2 + fi
                        W1t = W1q[f // 8]
                        fo = (f % 8) * 128
                        for c in range(8):
                            k.op("pe", lambda e, c=c, fi=fi: e.matmul(
                                out=p[:, fi, :], lhsT=W1t[:, c, fo:fo + 128], rhs=hT[:, c, :],
                                start=(c == 0), stop=(c == 7)), reads=[W1t, hT], writes=[p])
                    rr = rl.next()
                    k.op("act", lambda e: e.activation(out=rr[:], in_=p[:], func=AF.Relu), reads=[p], writes=[rr])
                    k.op("pool", lambda e: e.tensor_tensor(out=h1T[:, f2 * 2:f2 * 2 + 2, :], in0=rr[:], in1=rr[:], op=ALU.mult),
                         reads=[rr], writes=[h1T])
                for f in pend:
                    f()
                pend.clear()
                for i in range(2):
                    t0 = (tg * 2 + i) * 128
                    o = xo.next()
                    for half in range(2):
                        p = pY.next()
                        for f in range(32):
                            k.op("pe", lambda e, f=f: e.matmul(out=p[:], lhsT=h1T[:, f, i * 128:(i + 1) * 128],
                                                               rhs=W2[:, f, half * 512:(half + 1) * 512],
                                                               start=(f == 0), stop=(f == 31)), reads=[h1T, W2], writes=[p])
                        k.op("dve", lambda e: e.tensor_tensor(out=o[:, half * 512:(half + 1) * 512], in0=p[:],
                                                              in1=xt2[i][:, half * 512:(half + 1) * 512], op=ALU.add),
                             reads=[p, xt2[i]], writes=[o])
                    k.store("sp", o, dsts[s][t0:t0 + 128, :], o[:])
        ph.end()

    U = UPTO
    if U < 4:
        prep_weights([(l, "all") for l in range(nL)])
    else:
        prep_weights([(0, "first")])
    for l in range(nL):
        for s in range(nS):
            xsrc = x_in[s] if l == 0 else xres[s]
            if U >= 1:
                phase1(l, s, xsrc, 0)
            if U >= 2:
                phase1(l, s, xsrc, 1)
            if U >= 3:
                phase2()
            if U >= 4:
                pl = []
                if l == 0 and s == 0:
                    pl.append((0, "rest"))
                if s == nS - 1 and l + 1 < nL:
                    pl.append((l + 1, "all"))
                phase3(prep_l=pl)
            if U >= 5:
                phase4()
            if U >= 6:
                phase5(l, s, xsrc)
        dsts = [out[s] if l == nL - 1 else xres[s] for s in range(nS)]
        if U >= 7:
            phase6(l, dsts)
    k.barrier()
    print("sbuf max bytes/partition:", k.sb_max, "instructions:", k.ninstr, {e: k.cnt[e] for e in k.cnt})
    k.close()
    return nc


def host_consts(S):
    NT = S // 128
    p = np.arange(128)[:, None]
    c = np.arange(128)[None, :]
    consts = np.zeros((128, 5, 128), np.float32)
    consts[:, 0, :] = (p == c)
    consts[:, 1, :] = (c >= p)
    consts[:, 2, :] = (c > p)
    consts[:, 3, :] = (c <= p)
    consts[:, 4, :] = -1.0 * (p >= c)

    def tables(dim):
        pos = np.arange(S, dtype=np.float32)
        inv = (np.float32(500000.0) ** (-np.arange(0, dim, 2, dtype=np.float32) / np.float32(dim))).astype(np.float32)
        ang = (pos[:, None] * inv[None, :]).astype(np.float32)
        t = np.concatenate([np.cos(ang), np.sin(ang)], axis=1).astype(np.float32)
        return np.ascontiguousarray(t.reshape(NT, 128, dim).transpose(1, 0, 2))
    return consts, tables(16), tables(32)


def make_inputs(x_sh, l0, l1, S, attn_norm, w_in, a_q_norm, a_k_norm, b_q_a_norm, w_q_b, b_kv_a_norm,
                w_kv_b, b_q_norm, b_k_norm, w_branch, w_out, mlp_norm, w_ff1, w_ff2):
    nL = l1 - l0
    sl = slice(l0, l1)
    f = lambda a: np.ascontiguousarray(np.asarray(a, dtype=np.float32))
    col = lambda g, c: np.ascontiguousarray(np.asarray(g, np.float32)[sl].reshape(nL, c, 128).transpose(0, 2, 1))
    rep = lambda a, b: np.ascontiguousarray(np.broadcast_to(
        np.concatenate([np.asarray(a, np.float32)[sl], np.asarray(b, np.float32)[sl]], axis=1)[:, None, :],
        (nL, 128, a.shape[1] + b.shape[1])))
    consts, rA, rM = host_consts(S)
    return {
        "x": f(x_sh), "w_in": f(w_in[sl]), "w_q_b": f(np.asarray(w_q_b)[sl].reshape(nL, 256, 768)),
        "w_kv_b": f(np.asarray(w_kv_b)[sl].reshape(nL, 128, 1024)), "w_branch": f(w_branch[sl]),
        "w_out": f(w_out[sl]), "w_ff1": f(w_ff1[sl]), "w_ff2": f(w_ff2[sl]),
        "g_attn": col(attn_norm, 8), "g_mlp": col(mlp_norm, 8), "g_qa": col(b_q_a_norm, 2),
        "g_kva": col(b_kv_a_norm, 1), "g_aqk": rep(a_q_norm, a_k_norm), "g_bqk": rep(b_q_norm, b_k_norm),
        "ropeA": rA, "ropeM": rM, "consts": consts,
    }


def kernel(x, attn_norm, w_in, a_q_norm, a_k_norm, b_q_a_norm, w_q_b, b_kv_a_norm,
           w_kv_b, b_q_norm, b_k_norm, w_branch, w_out, mlp_norm, w_ff1, w_ff2):
    x = np.asarray(x, dtype=np.float32)
    B, S, _ = x.shape
    nL = np.asarray(attn_norm).shape[0]
    ncores = 8
    nS = B // ncores
    nc = build(nL, nS, S)
    in_maps = []
    for c in range(ncores):
        in_maps.append(make_inputs(x[c * nS:(c + 1) * nS], 0, nL, S, attn_norm, w_in, a_q_norm, a_k_norm, b_q_a_norm,
                                   w_q_b, b_kv_a_norm, w_kv_b, b_q_norm, b_k_norm, w_branch, w_out, mlp_norm,
                                   w_ff1, w_ff2))
    res = run_bass_kernel_spmd(nc, in_maps, core_ids=list(range(ncores)))
    return np.concatenate([np.asarray(r["out"], dtype=np.float32) for r in res.results], axis=0)
```

```python
import contextlib
import numpy as np
import concourse.bass as bass
import concourse.mybir as mybir
from concourse.alu_op_type import AluOpType as ALU
from concourse.bass_utils import run_bass_kernel_spmd

AF = mybir.ActivationFunctionType
AX = mybir.AxisListType
F32 = mybir.dt.float32
BF16 = mybir.dt.bfloat16

SAME_ENG_SYNC = True
UPTO = 99
STAGE1 = 99
TLIM = 9999
EPS = 1e-6
D = 1024
INC = 7328
DFF = 4096


class T:
    _n = 0

    def __init__(self, t, name=None):
        self.t = t
        T._n += 1
        self.id = "t%d" % T._n
        self.name = name or self.id
        self.w = None
        self.r = {}
        self.dsem = None
        self.dval = 0
        self.psum = False

    def __getitem__(self, idx):
        return self.t[idx]


class Rot:
    def __init__(self, tiles):
        self.tiles = tiles
        self.i = 0

    def next(self):
        t = self.tiles[self.i % len(self.tiles)]
        self.i += 1
        return t


class K:
    def __init__(self, nc):
        self.nc = nc
        self.es = contextlib.ExitStack()
        self.eng = {"pe": nc.tensor, "act": nc.scalar, "dve": nc.vector,
                    "pool": nc.gpsimd, "sp": nc.sync}
        self.sem = {}
        self.cnt = {}
        self.seen = {}
        for e in self.eng:
            self.sem[e] = self.es.enter_context(nc.semaphore("s_" + e))
            self.cnt[e] = 0
            self.seen[e] = {}
        self.semof = dict(self.sem)
        self.dtiles = []
        self.dsem_pool = []
        self.ninstr = 0
        self.uid = 0

    def sb(self, shape, dtype, name, stack=None):
        st = stack if stack is not None else self.es
        self.uid += 1
        nb = int(np.prod(shape[1:])) * (2 if dtype == BF16 else 4)
        self.sb_bytes = getattr(self, "sb_bytes", 0) + nb
        self.sb_max = max(getattr(self, "sb_max", 0), self.sb_bytes)
        st.callback(self._free, nb)
        t = st.enter_context(self.nc.sbuf_tensor("%s_%d" % (name, self.uid), list(shape), dtype))
        return T(t, name)

    def ps(self, shape, dtype, name, stack=None):
        st = stack if stack is not None else self.es
        self.uid += 1
        t = st.enter_context(self.nc.psum_tensor("%s_%d" % (name, self.uid), list(shape), dtype))
        tt = T(t, name)
        tt.psum = True
        return tt

    def rot(self, n, shape, dtype, name, stack=None, psum=False):
        f = self.ps if psum else self.sb
        return Rot([f(shape, dtype, "%s%d" % (name, i), stack) for i in range(n)])

    def _free(self, nb):
        self.sb_bytes -= nb

    def _wait(self, e, deps):
        eng = self.eng[e]
        seen = self.seen[e]
        for key, val in sorted(deps, key=lambda d: str(d[0])):
            if key == e and (e in ("pe", "sp") or not SAME_ENG_SYNC):
                continue
            if seen.get(key, 0) >= val:
                continue
            eng.wait_ge(self.semof[key], val)
            seen[key] = val
            self.ninstr += 1

    def _deps(self, reads, writes):
        deps = set()
        for t in reads:
            if t.w:
                deps.add(t.w)
            if t.psum:
                for d in t.r.values():
                    deps.add(d)
        for t in writes:
            if t.w:
                deps.add(t.w)
            for d in t.r.values():
                deps.add(d)
        return deps

    def op(self, e, fn, reads=(), writes=()):
        self._wait(e, self._deps(reads, writes))
        ins = fn(self.eng[e])
        self.cnt[e] += 1
        ins.then_inc(self.sem[e], 1)
        self.ninstr += 1
        me = (e, self.cnt[e])
        for t in reads:
            t.r[e] = me
        for t in writes:
            t.w = me
            t.r = {}
        return ins

    def _dsem(self, tile):
        if tile.dsem is None:
            if self.dsem_pool:
                key, sem, val = self.dsem_pool.pop()
                tile.dkey, tile.dsem, tile.dval = key, sem, val
            else:
                tile.dkey = "d" + tile.id
                tile.dsem = self.es.enter_context(self.nc.semaphore(tile.dkey))
                self.semof[tile.dkey] = tile.dsem
            self.dtiles.append(tile)

    def load(self, q, tile, out_ap, in_ap, **kw):
        self._dsem(tile)
        self._wait(q, self._deps((), (tile,)))
        ins = self.eng[q].dma_start(out=out_ap, in_=in_ap, **kw)
        tile.dval += 16
        ins.then_inc(tile.dsem, 16)
        self.ninstr += 1
        tile.w = (tile.dkey, tile.dval)
        tile.r = {}

    def store(self, q, tile, out_ap, in_ap, **kw):
        self._dsem(tile)
        self._wait(q, self._deps((tile,), ()))
        ins = self.eng[q].dma_start(out=out_ap, in_=in_ap, **kw)
        tile.dval += 16
        ins.then_inc(tile.dsem, 16)
        self.ninstr += 1
        tile.r["dma"] = (tile.dkey, tile.dval)

    def barrier(self, release=True):
        deps = set()
        for e in self.eng:
            if self.cnt[e]:
                deps.add((e, self.cnt[e]))
        for t in self.dtiles:
            if t.dval:
                deps.add((t.dkey, t.dval))
        for e in self.eng:
            self._wait(e, deps)

    def release(self, tiles):
        for t in tiles:
            if t.dsem is not None:
                self.dsem_pool.append((t.dkey, t.dsem, t.dval))
                self.dtiles.remove(t)
                t.dsem = None

    def close(self):
        self.es.close()


class Phase:
    def __init__(self, k):
        self.k = k
        self.st = contextlib.ExitStack()
        self.tiles = []

    def sb(self, shape, dtype, name):
        t = self.k.sb(shape, dtype, name, self.st)
        self.tiles.append(t)
        return t

    def ps(self, shape, dtype, name):
        t = self.k.ps(shape, dtype, name, self.st)
        self.tiles.append(t)
        return t

    def rot(self, n, shape, dtype, name, psum=False):
        f = self.ps if psum else self.sb
        return Rot([f(shape, dtype, "%s%d" % (name, i)) for i in range(n)])

    def end(self):
        self.k.barrier()
        self.k.release(self.tiles)
        self.st.close()


def bc(ap, axis, shape):
    return ap.unsqueeze(axis).broadcast_to(list(shape))


def build(nL, nS, S, debug=False):
    nc = bass.Bass("TRN2", target_bir_lowering=False)
    NT = S // 128
    NQG = S // 512

    def din(name, shape, dt=F32):
        return nc.dram_tensor(name, list(shape), dt, kind="ExternalInput").ap()

    def dscr(name, shape, dt):
        if debug:
            return nc.dram_tensor(name, list(shape), dt, kind="ExternalOutput").ap()
        return nc.dram_tensor(name, list(shape), dt).ap()

    x_in = din("x", [nS, S, D])
    w_in = din("w_in", [nL, D, INC])
    w_qb = din("w_q_b", [nL, 256, 768])
    w_kvb = din("w_kv_b", [nL, 128, 1024])
    w_br = din("w_branch", [nL, 1280, D])
    w_out = din("w_out", [nL, D, D])
    w_ff1 = din("w_ff1", [nL, D, DFF])
    w_ff2 = din("w_ff2", [nL, DFF, D])
    g_attn = din("g_attn", [nL, 128, 8])
    g_mlp = din("g_mlp", [nL, 128, 8])
    g_qa = din("g_qa", [nL, 128, 2])
    g_kva = din("g_kva", [nL, 128, 1])
    g_aqk = din("g_aqk", [nL, 128, 128])
    g_bqk = din("g_bqk", [nL, 128, 192])
    ropeA = din("ropeA", [128, NT, 16])
    ropeM = din("ropeM", [128, NT, 32])
    consts = din("consts", [128, 5, 128])
    out = nc.dram_tensor("out", [nS, S, D], F32, kind="ExternalOutput").ap()

    Wb_in = dscr("Wb_in", [nL, D, INC], BF16)
    Wb_qb = dscr("Wb_qb", [nL, 256, 768], BF16)
    Wb_kvb = dscr("Wb_kvb", [nL, 128, 1024], BF16)
    Wb_br = dscr("Wb_br", [nL, 1280, D], BF16)
    Wb_out = dscr("Wb_out", [nL, D, D], BF16)
    Wb_ff1 = dscr("Wb_ff1", [nL, D, DFF], BF16)
    Wb_ff2 = dscr("Wb_ff2", [nL, DFF, D], BF16)
    xres = dscr("xres", [nS, S, D], F32)
    xmid = dscr("xmid", [nS, S, D], F32)
    AqT = dscr("AqT", [768, S], BF16)
    AkT = dscr("AkT", [768, S], BF16)
    Av = dscr("Av", [S, 780], BF16)
    BqT = dscr("BqT", [768, S], BF16)
    BkT = dscr("BkT", [768, S], BF16)
    Bv = dscr("Bv", [S, 520], BF16)
    CqT = dscr("CqT", [512, S], BF16)
    CkT = dscr("CkT", [512, S], BF16)
    Cv = dscr("Cv", [S, 512], BF16)
    Gt = dscr("Gt", [S, 3072], BF16)
    accA = dscr("accA", [3, S, 260], F32)
    oB = dscr("oB", [S, 512], BF16)
    oC = dscr("oC", [S, 512], BF16)

    k = K(nc)
    cf = k.sb([128, 5, 128], F32, "cf")
    cb = k.sb([128, 5, 128], BF16, "cb")
    k.load("sp", cf, cf[:], consts)
    k.op("dve", lambda e: e.tensor_copy(out=cb[:], in_=cf[:]), reads=[cf], writes=[cb])
    identb = cb[:, 0, :]
    identf = cf[:, 0, :]
    m_ge = cb[:, 1, :]
    m_gt = cb[:, 2, :]
    m_le = cb[:, 3, :]
    negU = cb[:, 4, :]
    onesb = k.sb([128, 1], BF16, "onesb")
    k.op("dve", lambda e: e.memset(onesb[:], 1.0), writes=[onesb])
    rA = k.sb([128, NT, 16], F32, "rA")
    rM = k.sb([128, NT, 32], F32, "rM")
    k.load("sp", rA, rA[:], ropeA)
    k.load("sp", rM, rM[:], ropeM)

    neghalf = k.sb([128, 24], F32, "neghalf")
    k.op("dve", lambda e: e.memset(neghalf[:], -0.5), writes=[neghalf])

    def rsq(rs_ap, ss_ap, n, inv_n, rs_t, ss_t):
        k.op("dve", lambda e: e.tensor_scalar(out=rs_ap, in0=ss_ap, scalar1=inv_n, scalar2=EPS, op0=ALU.mult, op1=ALU.add),
             reads=[ss_t], writes=[rs_t])
        k.op("pool", lambda e: e.tensor_tensor(out=rs_ap, in0=rs_ap, in1=neghalf[:, 0:n], op=ALU.pow),
             reads=[rs_t, neghalf], writes=[rs_t])

    ceng = Rot(["dve", "act", "pool"])

    def cast(e, out_ap, in_ap, reads, writes, scale=None):
        if e == "act":
            if scale is None:
                k.op("act", lambda g: g.activation(out=out_ap, in_=in_ap, func=AF.Copy), reads=reads, writes=writes)
            else:
                k.op("act", lambda g: g.activation(out=out_ap, in_=in_ap, func=AF.Copy, scale=scale),
                     reads=reads, writes=writes)
        else:
            if scale is None:
                k.op(e, lambda g: g.tensor_copy(out=out_ap, in_=in_ap), reads=reads, writes=writes)
            else:
                k.op(e, lambda g: g.tensor_scalar(out=out_ap, in0=in_ap, scalar1=scale, scalar2=None, op0=ALU.mult),
                     reads=reads, writes=writes)

    gcol = k.sb([128, nL, 19], F32, "gcol")
    for l in range(nL):
        k.load("sp", gcol, gcol[:, l, 0:8], g_attn[l])
        k.load("sp", gcol, gcol[:, l, 8:16], g_mlp[l])
        k.load("sp", gcol, gcol[:, l, 16:18], g_qa[l])
        k.load("sp", gcol, gcol[:, l, 18:19], g_kva[l])
    CB = 2048

    def prep_gen(ph, plist, engines):
        wf = ph.rot(4, [128, CB], F32, "wf")
        wb = ph.rot(3, [128, CB], BF16, "wb")
        items = []
        for (l, part) in plist:
            first = [(w_in[l], Wb_in[l], D, INC, 0), (w_qb[l], Wb_qb[l], 256, 768, 16),
                     (w_kvb[l], Wb_kvb[l], 128, 1024, 18)]
            rest = [(w_br[l], Wb_br[l], 1280, D, None),
                    (w_out[l], Wb_out[l], D, D, None), (w_ff1[l], Wb_ff1[l], D, DFF, 8),
                    (w_ff2[l], Wb_ff2[l], DFF, D, None)]
            jobs = first if part == "first" else rest if part == "rest" else first + rest
            for src, dst, R, C, gc in jobs:
                for c in range(R // 128):
                    for c0 in range(0, C, CB):
                        items.append((l, src, dst, c, c0, min(CB, C - c0), gc))
        loaded = {}

        def ld(i):
            l, src, dst, c, c0, n, gc = items[i]
            f = wf.next()
            k.load("sp", f, f[:, 0:n], src[c * 128:(c + 1) * 128, c0:c0 + n])
            loaded[i] = f

        for i in range(min(2, len(items))):
            ld(i)
        for i in range(len(items)):
            if i + 2 < len(items):
                ld(i + 2)
            l, src, dst, c, c0, n, gc = items[i]
            f = loaded.pop(i)
            b = wb.next()
            sc = None if gc is None else gcol[:, l, gc + c:gc + c + 1]
            e = engines.next()
            rd = [f] if gc is None else [f, gcol]
            cast(e, b[:, 0:n], f[:, 0:n], rd, [b], scale=sc)
            k.store("sp", b, dst[c * 128:(c + 1) * 128, c0:c0 + n], b[:, 0:n])
            yield

    def prep_weights(plist):
        ph = Phase(k)
        for _ in prep_gen(ph, plist, ceng):
            pass
        ph.end()

    def norm_transpose(ph, r, xsrc_rows, xt, hT_out, defer=False):
        ss = r["ss"].next()
        rstd = r["rstd"].next()
        junk = r["junk"]
        xn = r["xn"].next()
        k.op("act", lambda e: e.activation(out=junk[:], in_=xt[:], func=AF.Square, accum_out=ss[:]),
             reads=[xt], writes=[junk, ss])
        rsq(rstd[:], ss[:], 1, 1.0 / D, rstd, ss)
        k.op("act", lambda e: e.activation(out=xn[:], in_=xt[:], func=AF.Copy, scale=rstd[:]),
             reads=[xt, rstd], writes=[xn])
        hTt = r["hTtile"]

        def partB():
            pT = r["pT"].next()
            for c in range(8):
                k.op("pe", lambda e, c=c: e.transpose(out=pT[:, c, :], in_=xn[:, c * 128:(c + 1) * 128], identity=identb),
                     reads=[xn, cb], writes=[pT])
            k.op("dve", lambda e: e.tensor_copy(out=hT_out, in_=pT[:]), reads=[pT], writes=[hTt])
        if defer:
            return partB
        partB()

    def norm_res(ph, n=2):
        return {"ss": ph.rot(n, [128, 1], F32, "ss"), "rstd": ph.rot(n, [128, 1], F32, "rstd"),
                "junk": ph.sb([128, D], BF16, "junk"), "xn": ph.rot(2, [128, D], BF16, "xn"),
                "pT": ph.rot(1, [128, 8, 128], BF16, "pT", psum=True)}

    def rms_small(src_ap, nh, hd, reads_t, sq_t, ss_t, rs_t, extra_add=None):
        k.op("act", lambda e: e.activation(out=sq_t[:, 0:nh * hd].rearrange("p (h d) -> p h d", d=hd),
                                           in_=src_ap, func=AF.Square), reads=reads_t, writes=[sq_t])
        k.op("dve", lambda e: e.tensor_reduce(out=ss_t[:, 0:nh], in_=sq_t[:, 0:nh * hd].rearrange("p (h d) -> p h d", d=hd),
                                              axis=AX.X, op=ALU.add), reads=[sq_t], writes=[ss_t])

    def interleave(gens):
        gens = [g for g in gens if g is not None]
        while gens:
            for g in list(gens):
                try:
                    next(g)
                except StopIteration:
                    gens.remove(g)

    def phase1(l, s, xsrc, pas):
        ph = Phase(k)
        if pas == 0:
            c_lo, c_hi = 0, 3744
        else:
            c_lo, c_hi = 3744, INC
        NW = c_hi - c_lo
        segb = [0, 1536, 2720, 3744] if pas == 0 else [0, 1024, 2048, NW]
        Wsegs = []
        for si in range(3):
            a, b = segb[si], segb[si + 1]
            Wt = ph.sb([128, 8, b - a], BF16, "W%d" % si)
            k.load("sp", Wt, Wt[:], Wb_in[l][:, c_lo + a:c_lo + b].rearrange("(c p) n -> p c n", p=128))
            Wsegs.append((a, b, Wt))

        def wsl(c, col0, n):
            for a, b, Wt in Wsegs:
                if a <= col0 and col0 + n <= b:
                    return Wt[:, c, col0 - a:col0 - a + n], Wt
            raise AssertionError("chunk crosses W segment")
        r = norm_res(ph, 3)
        xts = ph.rot(3, [128, D], F32, "xt")
        hTs = ph.rot(2, [128, 8, 512], BF16, "hT4") if pas == 0 else ph.rot(3, [128, 8, 128], BF16, "hT")
        pj = ph.rot(2 if pas == 0 else 6, [128, 512], F32, "pj", psum=True)
        hT_of = {}
        x_of = {}
        if pas == 0:
            gaq = ph.sb([128, 128], F32, "gaq")
            gbq = ph.sb([128, 192], F32, "gbq")
            k.load("sp", gaq, gaq[:], g_aqk[l])
            k.load("sp", gbq, gbq[:], g_bqk[l])
            wqb = ph.sb([128, 2, 768], BF16, "wqb")
            wkvb = ph.sb([128, 1024], BF16, "wkvb")
            k.load("sp", wqb, wqb[:], Wb_qb[l].rearrange("(c p) n -> p c n", p=128))
            k.load("sp", wkvb, wkvb[:], Wb_kvb[l])
            qks = ph.rot(2, [128, 1536], F32, "qk")
            lats = ph.rot(2, [128, 416], F32, "lat")
            sqA = ph.sb([128, 1536], BF16, "sqA")
            ss24 = ph.sb([128, 24], F32, "ss24")
            rs24 = ph.sb([128, 24], F32, "rs24")
            qkb = ph.rot(2, [128, 1536], BF16, "qkb")
            rtA = [ph.sb([128, 24, 8], F32, "rtA%d" % i) for i in range(4)]
            x16 = ph.sb([128, 24, 16], F32, "x16")
            x32 = ph.sb([128, 8, 32], F32, "x32")
            rtK = [ph.sb([128, 1, 16], F32, "rtK%d" % i) for i in range(4)]
            sqK = ph.sb([128, 512], BF16, "sqK")
            ss8k = ph.sb([128, 8], F32, "ss8k")
            rs8k = ph.sb([128, 8], F32, "rs8k")
            rtM = [ph.sb([128, 8, 16], F32, "rtM%d" % i) for i in range(4)]
            vAs = ph.rot(2, [128, 12, 65], BF16, "vA")
            for t in vAs.tiles:
                k.op("dve", lambda e, t=t: e.memset(t[:], 1.0), writes=[t])
            pTA = ph.ps([128, 16, 128], BF16, "pTA")
            ATs = ph.rot(2, [128, 12, 128], BF16, "AT")
            sqM = ph.sb([128, 768], BF16, "sqM")
            ssl = ph.sb([128, 3], F32, "ssl")
            rsl = ph.sb([128, 3], F32, "rsl")
            latn = ph.sb([128, 384], BF16, "latn")
            pT2 = ph.ps([128, 8, 128], BF16, "pT2")
            latT = ph.sb([128, 3, 128], BF16, "latT")
            pM = ph.ps([128, 1024], F32, "pM")
            bq = ph.sb([128, 768], F32, "bq")
            bk = ph.sb([128, 8, 96], F32, "bk")
            ss8 = ph.sb([128, 8], F32, "ss8")
            rs8 = ph.sb([128, 8], F32, "rs8")
            bqbs = ph.rot(2, [128, 8, 96], BF16, "bqb")
            bkbs = ph.rot(2, [128, 8, 96], BF16, "bkb")
            tails = {}
            krg = ph.sb([128, 32], F32, "krg")
            krr = ph.sb([128, 32], F32, "krr")
            vBs = ph.rot(2, [128, 8, 65], BF16, "vB")
            for t in vBs.tiles:
                k.op("dve", lambda e, t=t: e.memset(t[:], 1.0), writes=[t])
            BTq = ph.rot(2, [96, 8, 128], BF16, "BTq")
            BTk = ph.rot(2, [96, 8, 128], BF16, "BTk")
            CTs = ph.rot(1, [128, 8, 512], BF16, "CT")
            qk_of, lat_of = {}, {}
        else:
            cvs = ph.rot(2, [128, 512], BF16, "cv")
            gts = ph.rot(2, [128, 3072], BF16, "gt")

        def proj_chunk(hTo, col0, n):
            hT, off = hTo
            p = pj.next()
            for c in range(8):
                wap, Wt = wsl(c, col0, n)
                k.op("pe", lambda e, c=c, wap=wap: e.matmul(out=p[:, 0:n], lhsT=hT[:, c, off:off + 128], rhs=wap,
                                                            start=(c == 0), stop=(c == 7)), reads=[hT, Wt], writes=[p])
            return p

        def rope(src, dst, nh, off, half, cs, sn, reads_t, dst_t, tmp):
            x1 = src[:, :, off:off + half]
            x2 = src[:, :, off + half:off + 2 * half]
            cB = bc(cs, 1, [128, nh, half])
            sB = bc(sn, 1, [128, nh, half])
            t1, t2, t3, t4 = [t[:, 0:nh, 0:half] for t in tmp]
            k.op("pool", lambda e: e.tensor_tensor(out=t1, in0=x1, in1=cB, op=ALU.mult), reads=reads_t, writes=[tmp[0]])
            k.op("pool", lambda e: e.tensor_tensor(out=t2, in0=x2, in1=sB, op=ALU.mult), reads=reads_t, writes=[tmp[1]])
            yield
            k.op("pool", lambda e: e.tensor_tensor(out=t3, in0=x2, in1=cB, op=ALU.mult), reads=reads_t, writes=[tmp[2]])
            k.op("pool", lambda e: e.tensor_tensor(out=t4, in0=x1, in1=sB, op=ALU.mult), reads=reads_t, writes=[tmp[3]])
            yield
            k.op("dve", lambda e: e.tensor_tensor(out=dst[:, :, off:off + half], in0=t1, in1=t2, op=ALU.subtract),
                 reads=[tmp[0], tmp[1]], writes=[dst_t])
            k.op("dve", lambda e: e.tensor_tensor(out=dst[:, :, off + half:off + 2 * half], in0=t3, in1=t4, op=ALU.add),
                 reads=[tmp[2], tmp[3]], writes=[dst_t])
            yield

        def Lx(tt):
            t0 = tt * 128
            xt = xts.next()
            k.load("sp", xt, xt[:], xsrc[t0:t0 + 128, :])
            x_of[tt] = xt

        def Nn(tt):
            xt = x_of.pop(tt)
            if pas == 0:
                if tt % 4 == 0:
                    hT_of["cur"] = hTs.next()
                hT = hT_of["cur"]
                off = (tt % 4) * 128
            else:
                hT = hTs.next()
                off = 0
            hT_of[tt] = (hT, off)
            r["hTtile"] = hT
            return norm_transpose(ph, r, None, xt, hT[:, :, off:off + 128], defer=True)

        def P0(tt):
            t0 = tt * 128
            hT = hT_of[tt]
            qk = qks.next()
            lat = lats.next()
            qk_of[tt] = qk
            lat_of[tt] = lat
            for j in range(3):
                p = proj_chunk(hT, j * 512, 512)
                cast("act", qk[:, j * 512:(j + 1) * 512], p[:, :], [p], [qk])
                yield
            p = proj_chunk(hT, 2304, 416)
            cast("dve", lat[:, :], p[:, 0:416], [p], [lat])
            yield
            vA = vAs.next()
            p = proj_chunk(hT, 1536, 512)
            cast("act", vA[:, 0:8, 0:64], p[:, :].rearrange("p (h d) -> p h d", d=64), [p], [vA])
            yield
            p = proj_chunk(hT, 2048, 256)
            cast("dve", vA[:, 8:12, 0:64], p[:, 0:256].rearrange("p (h d) -> p h d", d=64), [p], [vA])
            k.store("sp", vA, Av[t0:t0 + 128, :], vA[:].rearrange("p h d -> p (h d)"))
            yield

        def Cg(g):
            hT = hT_of[4 * g][0]
            CT = CTs.next()
            for j in range(8):
                col0 = 2720 + j * 128
                p = pj.next()
                for c in range(8):
                    wap, Wt = wsl(c, col0, 128)
                    k.op("pe", lambda e, c=c, wap=wap: e.matmul(out=p[:, 0:512], lhsT=wap, rhs=hT[:, c, 0:512],
                                                                start=(c == 0), stop=(c == 7)), reads=[hT, Wt], writes=[p])
                if j < 4:
                    k.op("act", lambda e, p=p, j=j: e.activation(out=CT[:, j, :], in_=p[:, :], func=AF.Copy, scale=0.125),
                         reads=[p], writes=[CT])
                else:
                    cast("dve", CT[:, j, :], p[:, :], [p], [CT])
                yield
            k.store("sp", CT, CqT.rearrange("(j p) s -> p j s", p=128)[:, :, g * 512:(g + 1) * 512], CT[:, 0:4, :])
            k.store("sp", CT, CkT.rearrange("(j p) s -> p j s", p=128)[:, :, g * 512:(g + 1) * 512], CT[:, 4:8, :])
            yield

        def P1(tt):
            t0 = tt * 128
            hT = hT_of[tt]
            cv = cvs.next()
            p = proj_chunk(hT, 0, 512)
            cast("dve", cv[:], p[:], [p], [cv])
            k.store("sp", cv, Cv[t0:t0 + 128, :], cv[:])
            gt = gts.next()
            for j in range(6):
                p = proj_chunk(hT, 512 + j * 512, 512)
                k.op("act", lambda e, p=p, j=j: e.activation(out=gt[:, j * 512:(j + 1) * 512], in_=p[:], func=AF.Sigmoid),
                     reads=[p], writes=[gt])
            k.store("sp", gt, Gt[t0:t0 + 128, :], gt[:])

        def YA(tt):
            t0 = tt * 128
            qk = qk_of[tt]
            qk3 = qk[:, :].rearrange("p (h d) -> p h d", d=64)
            rms_small(qk3, 24, 64, [qk], sqA, ss24, rs24)
            yield
            rsq(rs24[:], ss24[:], 24, 1.0 / 64, rs24, ss24)
            yield
            k.op("dve", lambda e: e.tensor_tensor(out=qk3, in0=qk3, in1=bc(rs24[:, :], 2, [128, 24, 64]), op=ALU.mult),
                 reads=[qk, rs24], writes=[qk])
            yield
            qk4 = qk[:, :].rearrange("p (a h d) -> p a h d", a=2, d=64)
            g4 = bc(gaq[:, :].rearrange("p (a d) -> p a d", a=2), 2, [128, 2, 12, 64])
            qb = qkb.next()
            qb4 = qb[:, :].rearrange("p (a h d) -> p a h d", a=2, d=64)
            k.op("dve", lambda e: e.tensor_tensor(out=qb4, in0=qk4, in1=g4, op=ALU.mult), reads=[qk, gaq], writes=[qb])
            x16v = x16[:, :, :].rearrange("p (a h) d -> p a h d", a=2)
            k.op("pool", lambda e: e.tensor_tensor(out=x16v, in0=qk4[:, :, :, 0:16], in1=g4[:, :, :, 0:16], op=ALU.mult),
                 reads=[qk, gaq], writes=[x16])
            yield
            qb3 = qb[:, :].rearrange("p (h d) -> p h d", d=64)
            yield from rope(x16[:, :, :], qb3, 24, 0, 8, rA[:, tt, 0:8], rA[:, tt, 8:16], [x16, rA], qb, rtA)

            def tailA():
                for j in range(12):
                    k.op("pe", lambda e, j=j: e.transpose(out=pTA[:, j, :], in_=qb[:, j * 128:(j + 1) * 128], identity=identb),
                         reads=[qb, cb], writes=[pTA])
                AT = ATs.next()
                cast("act", AT[:], pTA[:, 0:12, :], [pTA], [AT])
                k.store("sp", AT, AqT.rearrange("(j p) s -> p j s", p=128)[:, :, t0:t0 + 128], AT[:, 0:6, :])
                k.store("sp", AT, AkT.rearrange("(j p) s -> p j s", p=128)[:, :, t0:t0 + 128], AT[:, 6:12, :])
            tails.setdefault(tt, []).append(tailA)

        def YM(tt):
            t0 = tt * 128
            lat = lat_of[tt]
            k.op("act", lambda e: e.activation(out=sqM[:, 0:256], in_=lat[:, 0:256], func=AF.Square, accum_out=ssl[:, 0:1]),
                 reads=[lat], writes=[sqM, ssl])
            k.op("act", lambda e: e.activation(out=sqM[:, 256:384], in_=lat[:, 256:384], func=AF.Square, accum_out=ssl[:, 1:2]),
                 reads=[lat], writes=[sqM, ssl])
            k.op("act", lambda e: e.activation(out=sqM[:, 384:416], in_=lat[:, 384:416], func=AF.Square, accum_out=ssl[:, 2:3]),
                 reads=[lat], writes=[sqM, ssl])
            yield
            k.op("dve", lambda e: e.tensor_scalar(out=rsl[:, 0:1], in0=ssl[:, 0:1], scalar1=1.0 / 256, scalar2=EPS,
                                                  op0=ALU.mult, op1=ALU.add), reads=[ssl], writes=[rsl])
            k.op("dve", lambda e: e.tensor_scalar(out=rsl[:, 1:2], in0=ssl[:, 1:2], scalar1=1.0 / 128, scalar2=EPS,
                                                  op0=ALU.mult, op1=ALU.add), reads=[ssl], writes=[rsl])
            k.op("pool", lambda e: e.tensor_tensor(out=rsl[:, 0:2], in0=rsl[:, 0:2], in1=neghalf[:, 0:2], op=ALU.pow),
                 reads=[rsl, neghalf], writes=[rsl])
            yield
            cast("dve", latn[:, 0:256], lat[:, 0:256], [lat, rsl], [latn], scale=rsl[:, 0:1])
            cast("dve", latn[:, 256:384], lat[:, 256:384], [lat, rsl], [latn], scale=rsl[:, 1:2])
            yield
            for j in range(3):
                k.op("pe", lambda e, j=j: e.transpose(out=pT2[:, j, :], in_=latn[:, j * 128:(j + 1) * 128], identity=identb),
                     reads=[latn, cb], writes=[pT2])
            cast("dve", latT[:], pT2[:, 0:3, :], [pT2], [latT])
            yield
            for (c0, n) in ((0, 512), (512, 256)):
                for c in range(2):
                    k.op("pe", lambda e, c=c, c0=c0, n=n: e.matmul(out=pM[:, c0:c0 + n], lhsT=latT[:, c, :],
                                                                  rhs=wqb[:, c, c0:c0 + n], start=(c == 0), stop=(c == 1)),
                         reads=[latT, wqb], writes=[pM])
            cast("act", bq[:, :], pM[:, 0:768], [pM], [bq])
            yield
            for c0 in (0, 512):
                k.op("pe", lambda e, c0=c0: e.matmul(out=pM[:, c0:c0 + 512], lhsT=latT[:, 2, :], rhs=wkvb[:, c0:c0 + 512],
                                                    start=True, stop=True), reads=[latT, wkvb], writes=[pM])
            vB = vBs.next()
            pM3 = pM[:, :].rearrange("p (h d) -> p h d", d=128)
            cast("act", vB[:, :, 0:64], pM3[:, :, 64:128], [pM], [vB])
            cast("dve", bk[:, :, 0:64], pM3[:, :, 0:64], [pM], [bk])
            k.store("sp", vB, Bv[t0:t0 + 128, :], vB[:].rearrange("p h d -> p (h d)"))
            yield
            yield from rr2(Qc(tt), Kc(tt, lat))

        def rr2(g1, g2):
            gens = [g1, g2]
            while gens:
                for g in list(gens):
                    try:
                        next(g)
                        yield
                    except StopIteration:
                        gens.remove(g)

        def Qc(tt):
            t0 = tt * 128
            bqb = bqbs.next()
            bq3 = bq[:, :].rearrange("p (h d) -> p h d", d=96)
            rms_small(bq3, 8, 96, [bq], sqM, ss8, rs8)
            yield
            rsq(rs8[:], ss8[:], 8, 1.0 / 96, rs8, ss8)
            yield
            k.op("dve", lambda e: e.tensor_tensor(out=bq3, in0=bq3, in1=bc(rs8[:, :], 2, [128, 8, 96]), op=ALU.mult),
                 reads=[bq, rs8], writes=[bq])
            yield
            k.op("dve", lambda e: e.tensor_tensor(out=bqb[:, :, 0:64], in0=bq3[:, :, 0:64],
                                                  in1=bc(gbq[:, 0:64], 1, [128, 8, 64]), op=ALU.mult),
                 reads=[bq, gbq], writes=[bqb])
            k.op("pool", lambda e: e.tensor_tensor(out=x32[:, :, :], in0=bq3[:, :, 64:96],
                                                   in1=bc(gbq[:, 64:96], 1, [128, 8, 32]), op=ALU.mult),
                 reads=[bq, gbq], writes=[x32])
            yield
            yield from rope(x32[:, :, :], bqb[:, :, 64:96], 8, 0, 16, rM[:, tt, 0:16], rM[:, tt, 16:32], [x32, rM], bqb, rtM)

            def tailQ():
                for h in range(8):
                    k.op("pe", lambda e, h=h: e.transpose(out=pT2[0:96, h, :], in_=bqb[:, h, :], identity=identb),
                         reads=[bqb, cb], writes=[pT2])
                Bq = BTq.next()
                cast("act", Bq[:], pT2[0:96, :, :], [pT2], [Bq])
                k.store("sp", Bq, BqT.rearrange("(h f) s -> f h s", f=96)[:, :, t0:t0 + 128], Bq[:])
            tails.setdefault(tt, []).append(tailQ)

        def Kc(tt, lat):
            t0 = tt * 128
            bkb = bkbs.next()
            rms_small(bk[:, :, 0:64], 8, 64, [bk], sqK, ss8k, rs8k)
            yield
            k.op("dve", lambda e: e.tensor_scalar(out=ss8k[:], in0=ss8k[:], scalar1=ssl[:, 2:3], scalar2=None, op0=ALU.add),
                 reads=[ss8k, ssl], writes=[ss8k])
            rsq(rs8k[:], ss8k[:], 8, 1.0 / 96, rs8k, ss8k)
            yield
            k.op("dve", lambda e: e.tensor_tensor(out=krg[:, :], in0=lat[:, 384:416], in1=gbq[:, 160:192], op=ALU.mult),
                 reads=[lat, gbq], writes=[krg])
            yield from rope(krg[:, :].unsqueeze(1), krr[:, :].unsqueeze(1), 1, 0, 16, rM[:, tt, 0:16], rM[:, tt, 16:32],
                            [krg, rM], krr, rtK)
            k.op("dve", lambda e: e.tensor_tensor(out=bk[:, :, 0:64], in0=bk[:, :, 0:64],
                                                  in1=bc(gbq[:, 96:160], 1, [128, 8, 64]), op=ALU.mult),
                 reads=[bk, gbq], writes=[bk])
            yield
            k.op("dve", lambda e: e.tensor_tensor(out=bkb[:, :, 0:64], in0=bk[:, :, 0:64],
                                                  in1=bc(rs8k[:, :], 2, [128, 8, 64]), op=ALU.mult),
                 reads=[bk, rs8k], writes=[bkb])
            k.op("dve", lambda e: e.tensor_tensor(out=bkb[:, :, 64:96], in0=bc(krr[:, :], 1, [128, 8, 32]),
                                                  in1=bc(rs8k[:, :], 2, [128, 8, 32]), op=ALU.mult),
                 reads=[krr, rs8k], writes=[bkb])
            yield

            def tailK():
                for h in range(8):
                    k.op("pe", lambda e, h=h: e.transpose(out=pT2[0:96, h, :], in_=bkb[:, h, :], identity=identb),
                         reads=[bkb, cb], writes=[pT2])
                Bk = BTk.next()
                cast("act", Bk[:], pT2[0:96, :, :], [pT2], [Bk])
                k.store("sp", Bk, BkT.rearrange("(h f) s -> f h s", f=96)[:, :, t0:t0 + 128], Bk[:])
            tails.setdefault(tt, []).append(tailK)

        nt = min(NT, TLIM)
        for tt in range(min(2, nt)):
            Lx(tt)
        Nn(0)()
        if nt > 1:
            if nt > 2:
                Lx(2)
            Nn(1)()
        if pas == 0:
            interleave([P0(0)])
        for tt in range(nt):
            if tt + 3 < nt:
                Lx(tt + 3)
            nB = None
            if tt + 2 < nt:
                nB = Nn(tt + 2)
            if pas == 0:
                def P0n(tt=tt, nB=nB):
                    if tt + 1 < nt:
                        yield from P0(tt + 1)
                    if nB:
                        nB()
                    yield
                interleave([P0n(), YA(tt), YM(tt), Cg(tt // 4) if tt % 4 == 3 else None])
                for f in tails.pop(tt - 1, []):
                    f()
            else:
                P1(tt)
                if nB:
                    nB()
        if pas == 0:
            for f in tails.pop(nt - 1, []):
                f()
        ph.end()

    def phase2():
        ph = Phase(k)
        qn = [ph.sb([64, S], BF16, "qn%d" % i) for i in range(4)]
        kn = [ph.sb([64, S], BF16, "kn%d" % i) for i in range(4)]
        qd = [ph.sb([64, S], BF16, "qd%d" % i) for i in range(4)]
        kd = [ph.sb([64, S], BF16, "kd%d" % i) for i in range(4)]
        vA = ph.sb([128, NT, 4, 65], BF16, "vAall")
        mb = ph.sb([128, 2, 256], BF16, "mband")
        for hi in range(2):
            k.op("dve", lambda e, hi=hi: e.tensor_copy(out=mb[:, hi, 0:128], in_=m_ge), reads=[cb], writes=[mb])
            k.op("dve", lambda e, hi=hi: e.tensor_copy(out=mb[:, hi, 128:256], in_=m_le), reads=[cb], writes=[mb])
        S2 = ph.rot(6, [128, 2, 256], F32, "S2", psum=True)
        Pt = [ph.rot(5, [128, 2, 256], BF16, "Pt%d_" % hp) for hp in range(2)]
        O4 = ph.rot(2, [128, 512], F32, "O4", psum=True)
        osb = ph.rot(2, [128, 260], F32, "osb")
        for g, d in enumerate((1, 4, 16)):
            L = S // d
            nb = L // 128
            for hs in range(4):
                h = g * 4 + hs
                k.load("sp", qn[hs], qn[hs][:], AqT[h * 64:(h + 1) * 64, :])
                k.load("sp", kn[hs], kn[hs][:], AkT[h * 64:(h + 1) * 64, :])
                if d > 1:
                    k.op("pool" if hs < 2 else "dve",
                         lambda e, hs=hs: e.tensor_copy(out=qd[hs][:, :].rearrange("p (r u) -> p r u", r=d),
                                                        in_=qn[hs][:, :].rearrange("p (u r) -> p r u", r=d)),
                         reads=[qn[hs]], writes=[qd[hs]])
                    k.op("dve", lambda e, hs=hs: e.tensor_copy(out=kd[hs][:, :].rearrange("p (r u) -> p r u", r=d),
                                                               in_=kn[hs][:, :].rearrange("p (u r) -> p r u", r=d)),
                         reads=[kn[hs]], writes=[kd[hs]])
            Q = qd if d > 1 else qn
            Kk = kd if d > 1 else kn
            for r in range(d):
                src = Av.rearrange("(n p r) c -> r p n c", p=128, r=d)[r][:, :, g * 260:(g + 1) * 260]
                k.load("sp", vA, vA[:, r * nb:(r + 1) * nb, :, :].rearrange("p n h d -> p n (h d)"), src)
            blocks = [(r, n) for r in range(d) for n in range(nb)]
            Pof = {}

            def QE(i):
                r, n = blocks[i]
                col0 = r * L + n * 128
                nq = 256 if n < nb - 1 else 128
                cur = []
                for hp in range(2):
                    s2 = S2.next()
                    for hi in range(2):
                        hs = hp * 2 + hi
                        k.op("pe", lambda e, hs=hs, hi=hi, s2=s2: e.matmul(
                            out=s2[:, hi, 0:nq], lhsT=Kk[hs][:, col0:col0 + 128], rhs=Q[hs][:, col0:col0 + nq],
                            start=True, stop=True), reads=[Kk[hs], Q[hs]], writes=[s2])
                    pt = Pt[hp].next()
                    k.op("act", lambda e, s2=s2, pt=pt: e.activation(out=pt[:, :, 0:nq], in_=s2[:, :, 0:nq], func=AF.Exp,
                                                                     scale=0.125), reads=[s2], writes=[pt])
                    k.op("pool", lambda e, pt=pt: e.tensor_tensor(out=pt[:, :, 0:nq], in0=pt[:, :, 0:nq],
                                                                  in1=mb[:, :, 0:nq], op=ALU.mult),
                         reads=[pt, mb], writes=[pt])
                    cur.append(pt)
                Pof[i] = cur

            def PVs(i):
                r, n = blocks[i]
                b = r * nb + n
                curP = Pof[i]
                prevP = Pof.get(i - 1) if n > 0 else None
                o4t = O4.next()
                o4 = o4t[:, 0:260].rearrange("p (h d) -> p h d", d=65)
                for hs in range(4):
                    hp, hi = hs // 2, hs % 2
                    if n > 0:
                        k.op("pe", lambda e, hs=hs, hp=hp, hi=hi: e.matmul(
                            out=o4[:, hs, :], lhsT=prevP[hp][:, hi, 128:256], rhs=vA[:, b - 1, hs, :],
                            start=True, stop=False), reads=[prevP[hp], vA], writes=[o4t])
                    k.op("pe", lambda e, hs=hs, hp=hp, hi=hi: e.matmul(
                        out=o4[:, hs, :], lhsT=curP[hp][:, hi, 0:128], rhs=vA[:, b, hs, :],
                        start=(n == 0), stop=True), reads=[curP[hp], vA], writes=[o4t])
                ob = osb.next()
                cast("act", ob[:, :], o4t[:, 0:260], [o4t], [ob])
                dst = accA[g].rearrange("(n p r) c -> r n p c", p=128, r=d)[r, n]
                k.store("sp", ob, dst, ob[:, :])
                Pof.pop(i - 1, None)

            QE(0)
            if len(blocks) > 1:
                QE(1)
            for i in range(len(blocks)):
                if i + 2 < len(blocks):
                    QE(i + 2)
                PVs(i)
        ph.end()

    def phase3(prep_l=None):
        ph = Phase(k)
        pg = prep_gen(ph, prep_l, Rot(["dve"])) if prep_l else None
        qTs = ph.rot(2, [96, S], BF16, "bqT")
        kTs = ph.rot(2, [96, S], BF16, "bkT")
        VCH = NT // 4
        vBs = [ph.sb([128, VCH, 520], BF16, "vB%d" % i) for i in range(4)]
        oB_sb = ph.sb([128, NT, 512], BF16, "oB_sb")
        Sb = ph.rot(4, [128, 512], F32, "Sb", psum=True)
        Pts = ph.rot(5, [128, 512], BF16, "Ptb")
        Ob = ph.rot(2, [65, 512], F32, "Ob", psum=True)
        Osb = ph.rot(2, [65, 512], F32, "Osb")
        pTo = ph.rot(1, [128, 512], F32, "pTo", psum=True)
        rden = ph.rot(2, [128, 4], F32, "rden")
        sc = 96 ** -0.5
        heads = {}

        def load_head(h):
            qT = qTs.next()
            kT = kTs.next()
            k.load("sp", qT, qT[:], BqT[h * 96:(h + 1) * 96, :])
            k.load("sp", kT, kT[:], BkT[h * 96:(h + 1) * 96, :])
            heads[h] = (qT, kT)

        load_head(0)
        for i in range(4):
            k.load("sp", vBs[i], vBs[i][:], Bv.rearrange("(t p) c -> p t c", p=128)[:, i * VCH:(i + 1) * VCH, :])
        units = []
        for h in range(8):
            for qg in range(NQG):
                for kb in range(4 * qg + 4):
                    units.append({"h": h, "qg": qg, "kb": kb, "c0": max(0, kb - 4 * qg) * 128, "last": kb == 4 * qg + 3})
        grp = {}

        def A(u):
            h, qg, kb, c0 = u["h"], u["qg"], u["kb"], u["c0"]
            if h not in heads:
                load_head(h)
            if kb == 0 and qg == 0 and h + 1 < 8 and (h + 1) not in heads:
                load_head(h + 1)
            qT, kT = heads[h]
            sb_ = Sb.next()
            u["sb"] = sb_
            k.op("pe", lambda e: e.matmul(out=sb_[:, c0:512], lhsT=kT[:, kb * 128:(kb + 1) * 128],
                                          rhs=qT[:, qg * 512 + c0:(qg + 1) * 512], start=True, stop=True),
                 reads=[kT, qT], writes=[sb_])

        def B(u):
            c0, sb_ = u["c0"], u["sb"]
            pt = Pts.next()
            u["pt"] = pt
            k.op("act", lambda e: e.activation(out=pt[:, c0:512], in_=sb_[:, c0:512], func=AF.Exp, scale=sc),
                 reads=[sb_], writes=[pt])
            if u["kb"] >= 4 * u["qg"]:
                k.op("pool", lambda e: e.tensor_tensor(out=pt[:, c0:c0 + 128], in0=pt[:, c0:c0 + 128], in1=m_ge,
                                                       op=ALU.mult), reads=[pt, cb], writes=[pt])

        def F(u):
            h, qg, kb, c0, pt = u["h"], u["qg"], u["kb"], u["c0"], u["pt"]
            if kb == 0:
                grp[(h, qg)] = Ob.next()
            ob = grp[(h, qg)]
            vBc = vBs[kb // VCH]
            k.op("pe", lambda e: e.matmul(out=ob[:, c0:512], lhsT=vBc[:, kb % VCH, h * 65:(h + 1) * 65], rhs=pt[:, c0:512],
                                          start=(kb == 0), stop=u["last"]), reads=[vBc, pt], writes=[ob])
            if u["last"]:
                osb_ = Osb.next()
                cast("act", osb_[:, :], ob[:, :], [ob], [osb_])
                ptot = pTo.next()
                pto = ptot[:, 0:260].rearrange("p (j d) -> p j d", d=65)
                for j in range(4):
                    k.op("pe", lambda e, j=j: e.transpose(out=pto[:, j, :], in_=osb_[:, j * 128:(j + 1) * 128],
                                                          identity=identf[0:65, 0:65]), reads=[osb_, cf], writes=[ptot])
                rd = rden.next()
                k.op("dve", lambda e: e.reciprocal(out=rd[:, :], in_=pto[:, :, 64]), reads=[ptot], writes=[rd])
                k.op("dve", lambda e: e.tensor_tensor(out=oB_sb[:, 4 * qg:4 * qg + 4, h * 64:(h + 1) * 64], in0=pto[:, :, 0:64],
                                                      in1=bc(rd[:, :], 2, [128, 4, 64]), op=ALU.mult),
                     reads=[ptot, rd], writes=[oB_sb])

        N = len(units)
        A(units[0])
        for t in range(N + 1):
            if t + 1 < N:
                A(units[t + 1])
            if t >= 1:
                F(units[t - 1])
            if t < N:
                B(units[t])
            if pg is not None and t % 6 == 3:
                try:
                    next(pg)
                except StopIteration:
                    pg = None
        if pg is not None:
            for _ in pg:
                pass
        k.store("sp", oB_sb, oB.rearrange("(t p) c -> p t c", p=128), oB_sb[:])
        ph.end()

    def phase4():
        ph = Phase(k)
        qTs = ph.rot(2, [64, S], BF16, "cqT")
        kTs = ph.rot(2, [64, S], BF16, "ckT")
        VCH = NT // 4
        vCs = [ph.sb([128, VCH, 512], BF16, "vC%d" % i) for i in range(4)]
        oC_sb = ph.sb([128, NT, 512], BF16, "oC_sb")
        Zb = ph.rot(5, [128, 512], F32, "Zb", psum=True)
        Nb = ph.rot(2, [128, 512], F32, "Nb", psum=True)
        es = ph.rot(2, [128, 512], F32, "e_")
        sps = ph.rot(4, [128, 512], BF16, "sp_")
        Es = ph.rot(3, [128, 512], BF16, "E_")
        gts = ph.rot(2, [128, 4], F32, "g_")
        accs = ph.rot(2, [128, 4, 64], F32, "acc")
        heads = {}

        def load_head(h):
            qT = qTs.next()
            kT = kTs.next()
            k.load("sp", qT, qT[:], CqT[h * 64:(h + 1) * 64, :])
            k.load("sp", kT, kT[:], CkT[h * 64:(h + 1) * 64, :])
            heads[h] = (qT, kT)

        load_head(0)
        for i in range(4):
            k.load("sp", vCs[i], vCs[i][:], Cv.rearrange("(t p) c -> p t c", p=128)[:, i * VCH:(i + 1) * VCH, :])
        units = []
        for h in range(8):
            for qg in range(NQG):
                for kb in range(4 * qg + 4):
                    j0 = max(0, kb - 4 * qg)
                    units.append({"h": h, "qg": qg, "kb": kb, "j0": j0, "c0": j0 * 128, "diag": kb >= 4 * qg,
                                  "last": kb == 4 * qg + 3})
        grp = {}

        def A(u):
            h, qg, kb, c0 = u["h"], u["qg"], u["kb"], u["c0"]
            if h not in heads:
                load_head(h)
            if kb == 0 and qg == 0 and h + 1 < 8 and (h + 1) not in heads:
                load_head(h + 1)
            qT, kT = heads[h]
            zb = Zb.next()
            u["zb"] = zb
            k.op("pe", lambda e: e.matmul(out=zb[:, c0:512], lhsT=kT[:, kb * 128:(kb + 1) * 128],
                                          rhs=qT[:, qg * 512 + c0:(qg + 1) * 512], start=True, stop=True),
                 reads=[kT, qT], writes=[zb])

        def B(u):
            c0, zb = u["c0"], u["zb"]
            ee = es.next()
            k.op("act", lambda e: e.activation(out=ee[:, c0:512], in_=zb[:, c0:512], func=AF.Exp), reads=[zb], writes=[ee])
            sp = sps.next()
            u["sp"] = sp
            k.op("act", lambda e: e.activation(out=sp[:, c0:512], in_=ee[:, c0:512], func=AF.Ln, bias=1.0),
                 reads=[ee], writes=[sp])
            if u["diag"]:
                k.op("pool", lambda e: e.tensor_tensor(out=sp[:, c0:c0 + 128], in0=sp[:, c0:c0 + 128], in1=m_gt,
                                                       op=ALU.mult), reads=[sp, cb], writes=[sp])

        def C(u):
            c0, zb, sp = u["c0"], u["zb"], u["sp"]
            k.op("pe", lambda e: e.matmul(out=zb[:, c0:512], lhsT=negU, rhs=sp[:, c0:512], start=False, stop=True,
                                          skip_group_check=True), reads=[cb, sp], writes=[zb])

        def Dd(u):
            c0, zb = u["c0"], u["zb"]
            E = Es.next()
            u["E"] = E
            k.op("act", lambda e: e.activation(out=E[:, c0:512], in_=zb[:, c0:512], func=AF.Exp), reads=[zb], writes=[E])
            if u["diag"]:
                k.op("pool", lambda e: e.tensor_tensor(out=E[:, c0:c0 + 128], in0=E[:, c0:c0 + 128], in1=m_gt,
                                                       op=ALU.mult), reads=[E, cb], writes=[E])

        def F(u):
            h, kb, j0, E, sp = u["h"], u["kb"], u["j0"], u["E"], u["sp"]
            nbt = Nb.next()
            u["nbt"] = nbt
            nbk = nbt[:, 0:260].rearrange("p (j d) -> p j d", d=65)
            vCc = vCs[kb // VCH]
            for j in range(j0, 4):
                k.op("pe", lambda e, j=j: e.matmul(out=nbk[:, j, 0:64], lhsT=E[:, j * 128:(j + 1) * 128],
                                                   rhs=vCc[:, kb % VCH, h * 64:(h + 1) * 64], start=True, stop=True,
                                                   skip_group_check=True), reads=[E, vCc], writes=[nbt])
                k.op("pe", lambda e, j=j: e.matmul(out=nbk[:, j, 64:65], lhsT=sp[:, j * 128:(j + 1) * 128],
                                                   rhs=onesb[:, 0:1], start=True, stop=True,
                                                   skip_group_check=True), reads=[sp, onesb], writes=[nbt])

        def G(u):
            h, qg, kb, j0, nbt = u["h"], u["qg"], u["kb"], u["j0"], u["nbt"]
            nbk = nbt[:, 0:260].rearrange("p (j d) -> p j d", d=65)
            if kb == 0:
                acc = accs.next()
                grp[(h, qg)] = acc
                k.op("dve", lambda e: e.tensor_copy(out=acc[:], in_=nbk[:, :, 0:64]), reads=[nbt], writes=[acc])
            else:
                acc = grp[(h, qg)]
                gt = gts.next()
                k.op("act", lambda e: e.activation(out=gt[:, j0:4], in_=nbk[:, j0:4, 64], func=AF.Exp, scale=-1.0),
                     reads=[nbt], writes=[gt])
                for j in range(j0, 4):
                    k.op("dve", lambda e, j=j: e.scalar_tensor_tensor(out=acc[:, j, :], in0=acc[:, j, :], scalar=gt[:, j:j + 1],
                                                                      in1=nbk[:, j, 0:64], op0=ALU.mult, op1=ALU.add),
                         reads=[acc, gt, nbt], writes=[acc])
            if u["last"]:
                k.op("pool", lambda e: e.tensor_copy(out=oC_sb[:, 4 * qg:4 * qg + 4, h * 64:(h + 1) * 64], in_=acc[:]),
                     reads=[acc], writes=[oC_sb])

        N = len(units)
        A(units[0])
        for t in range(N + 2):
            if t + 1 < N:
                A(units[t + 1])
            if t < N:
                B(units[t])
            if 1 <= t <= N:
                C(units[t - 1])
                Dd(units[t - 1])
            if t >= 2:
                F(units[t - 2])
                G(units[t - 2])
        k.store("sp", oC_sb, oC.rearrange("(t p) c -> p t c", p=128), oC_sb[:])
        ph.end()

    def phase5(l, s, xsrc):
        ph = Phase(k)
        Wbr = ph.sb([128, 10, D], BF16, "Wbr")
        Wo = ph.sb([128, 8, D], BF16, "Wo")
        k.load("sp", Wbr, Wbr[:], Wb_br[l].rearrange("(c p) n -> p c n", p=128))
        k.load("sp", Wo, Wo[:], Wb_out[l].rearrange("(c p) n -> p c n", p=128))
        xts = ph.rot(3, [128, D], F32, "xt")
        a3s = ph.rot(3, [128, 3, 260], F32, "a3")
        ocs = ph.rot(3, [128, 1280], BF16, "ocat")
        gts = ph.rot(3, [128, 3072], BF16, "gt")
        rds = ph.rot(2, [128, 4], F32, "rd")
        pT = ph.ps([128, 16, 128], BF16, "pT")
        pTm = ph.ps([128, 8, 128], BF16, "pTm")
        oTs = ph.rot(2, [128, 10, 128], BF16, "oT")
        PP = ph.rot(2, [128, D], F32, "PP", psum=True)
        tf = ph.sb([128, D], F32, "tf")
        uf = ph.sb([128, D], F32, "uf")
        u2 = ph.sb([128, D], F32, "u2")
        mbfs = ph.rot(2, [128, D], BF16, "mbf")
        mTs = ph.rot(2, [128, 8, 128], BF16, "mT")
        xo = ph.rot(2, [128, D], F32, "xo")
        st = {}

        def L(tt):
            t0 = tt * 128
            xt, a3, oc, gt = xts.next(), a3s.next(), ocs.next(), gts.next()
            k.load("sp", xt, xt[:], xsrc[t0:t0 + 128, :])
            k.load("sp", a3, a3[:], accA[:, t0:t0 + 128, :].rearrange("g p c -> p g c"))
            k.load("sp", oc, oc[:, 256:768], oB[t0:t0 + 128, :])
            k.load("sp", oc, oc[:, 768:1280], oC[t0:t0 + 128, :])
            k.load("sp", gt, gt[:], Gt[t0:t0 + 128, :])
            st[tt] = {"xt": xt, "a3": a3, "oc": oc, "gt": gt}

        def branch(P, oT, ca, cbn):
            for half in range(2):
                for c in range(ca, cbn):
                    k.op("pe", lambda e, c=c, half=half: e.matmul(
                        out=P[:, half * 512:(half + 1) * 512], lhsT=oT[:, c, :], rhs=Wbr[:, c, half * 512:(half + 1) * 512],
                        start=(c == ca), stop=(c == cbn - 1)), reads=[oT, Wbr], writes=[P])

        def S1(tt):
            d = st[tt]
            a3, oc, gt = d["a3"], d["oc"], d["gt"]
            k.op("pool", lambda e: e.tensor_tensor(out=a3[:, 0, :], in0=a3[:, 0, :], in1=a3[:, 1, :], op=ALU.add),
                 reads=[a3], writes=[a3])
            k.op("pool", lambda e: e.tensor_tensor(out=a3[:, 0, :], in0=a3[:, 0, :], in1=a3[:, 2, :], op=ALU.add),
                 reads=[a3], writes=[a3])
            n4 = a3[:, 0, :].rearrange("p (h d) -> p h d", d=65)
            rd = rds.next()
            k.op("dve", lambda e: e.reciprocal(out=rd[:, :], in_=n4[:, :, 64]), reads=[a3], writes=[rd])
            k.op("dve", lambda e: e.tensor_tensor(out=oc[:, 0:256].rearrange("p (h d) -> p h d", d=64), in0=n4[:, :, 0:64],
                                                  in1=bc(rd[:, :], 2, [128, 4, 64]), op=ALU.mult), reads=[a3, rd], writes=[oc])
            for j in range(10):
                k.op("pe", lambda e, j=j: e.transpose(out=pT[:, j, :], in_=oc[:, j * 128:(j + 1) * 128], identity=identb),
                     reads=[oc, cb], writes=[pT])
            oT = oTs.next()
            cast("act", oT[:], pT[:, 0:10, :], [pT], [oT])
            Pa = PP.next()
            branch(Pa, oT, 0, 2)
            k.op("dve", lambda e: e.tensor_tensor(out=tf[:], in0=Pa[:], in1=gt[:, 0:1024], op=ALU.mult),
                 reads=[Pa, gt], writes=[tf])
            Pb = PP.next()
            branch(Pb, oT, 2, 6)
            k.op("dve", lambda e: e.tensor_tensor(out=uf[:], in0=Pb[:], in1=gt[:, 1024:2048], op=ALU.mult),
                 reads=[Pb, gt], writes=[uf])
            k.op("pool", lambda e: e.tensor_tensor(out=tf[:], in0=tf[:], in1=uf[:], op=ALU.add), reads=[tf, uf], writes=[tf])
            Pc = PP.next()
            branch(Pc, oT, 6, 10)
            k.op("dve", lambda e: e.tensor_tensor(out=u2[:], in0=Pc[:], in1=gt[:, 2048:3072], op=ALU.mult),
                 reads=[Pc, gt], writes=[u2])
            mbf = mbfs.next()
            k.op("pool", lambda e: e.tensor_tensor(out=mbf[:], in0=tf[:], in1=u2[:], op=ALU.add), reads=[tf, u2], writes=[mbf])
            d["mbf"] = mbf

        def S2(tt):
            t0 = tt * 128
            d = st.pop(tt)
            mbf, xt = d["mbf"], d["xt"]
            for j in range(8):
                k.op("pe", lambda e, j=j: e.transpose(out=pTm[:, j, :], in_=mbf[:, j * 128:(j + 1) * 128], identity=identb),
                     reads=[mbf, cb], writes=[pTm])
            mT = mTs.next()
            cast("act", mT[:], pTm[:], [pTm], [mT])
            P = PP.next()
            for half in range(2):
                for c in range(8):
                    k.op("pe", lambda e, c=c, half=half: e.matmul(
                        out=P[:, half * 512:(half + 1) * 512], lhsT=mT[:, c, :], rhs=Wo[:, c, half * 512:(half + 1) * 512],
                        start=(c == 0), stop=(c == 7)), reads=[mT, Wo], writes=[P])
            o = xo.next()
            k.op("dve", lambda e: e.tensor_tensor(out=o[:], in0=P[:], in1=xt[:], op=ALU.add), reads=[P, xt], writes=[o])
            k.store("sp", o, xmid[s][t0:t0 + 128, :], o[:])

        L(0)
        L(1)
        S1(0)
        for tt in range(NT):
            if tt + 2 < NT:
                L(tt + 2)
            if tt + 1 < NT:
                S1(tt + 1)
            S2(tt)
        ph.end()

    def phase6(l, dsts):
        ph = Phase(k)
        W1q = [ph.sb([128, 8, 1024], BF16, "W1q%d" % i) for i in range(4)]
        W2 = ph.sb([128, 32, D], BF16, "W2")
        for i in range(4):
            k.load("sp", W1q[i], W1q[i][:], Wb_ff1[l][:, i * 1024:(i + 1) * 1024].rearrange("(c p) n -> p c n", p=128))
        for c in range(4):
            k.load("sp", W2, W2[:, c * 8:(c + 1) * 8, :],
                   Wb_ff2[l][c * 1024:(c + 1) * 1024, :].rearrange("(c p) n -> p c n", p=128))
        r = norm_res(ph)
        xts = ph.rot(4, [128, D], F32, "xt")
        h2T = ph.rot(2, [128, 8, 256], BF16, "h2T")
        h1T = ph.sb([128, 32, 256], BF16, "h1T")
        rl = ph.rot(2, [128, 2, 256], F32, "rl")
        pF = ph.rot(2, [128, 2, 256], F32, "pF", psum=True)
        pY = ph.rot(3, [128, 512], F32, "pY", psum=True)
        xo = ph.rot(1, [128, D], F32, "xo")
        groups = [(s, tg) for s in range(nS) for tg in range(NT // 2)]
        gstate = {}

        def Ng(gi):
            s, tg = groups[gi]
            hT = h2T.next()
            xt2 = []
            for i in range(2):
                t0 = (tg * 2 + i) * 128
                xt = xts.next()
                k.load("sp", xt, xt[:], xmid[s][t0:t0 + 128, :])
                r["hTtile"] = hT
                pend.append(norm_transpose(ph, r, None, xt, hT[:, :, i * 128:(i + 1) * 128], defer=True))
                xt2.append(xt)
            gstate[gi] = (hT, xt2)

        pend = []
        Ng(0)
        for f in pend:
            f()
        pend.clear()
        for gi, (s, tg) in enumerate(groups):
            if True:
                if gi + 1 < len(groups):
                    Ng(gi + 1)
                hT, xt2 = gstate.pop(gi)
                for f2 in range(16):
                    p = pF.next()
                    for fi in range(2):
                        f = f2 * 2 + fi
                        W1t = W1q[f // 8]
                        fo = (f % 8) * 128
                        for c in range(8):
                            k.op("pe", lambda e, c=c, fi=fi: e.matmul(
                                out=p[:, fi, :], lhsT=W1t[:, c, fo:fo + 128], rhs=hT[:, c, :],
                                start=(c == 0), stop=(c == 7)), reads=[W1t, hT], writes=[p])
                    rr = rl.next()
                    k.op("act", lambda e: e.activation(out=rr[:], in_=p[:], func=AF.Relu), reads=[p], writes=[rr])
                    k.op("dve", lambda e: e.tensor_tensor(out=h1T[:, f2 * 2:f2 * 2 + 2, :], in0=rr[:], in1=rr[:], op=ALU.mult),
                         reads=[rr], writes=[h1T])
                for f in pend:
                    f()
                pend.clear()
                for i in range(2):
                    t0 = (tg * 2 + i) * 128
                    o = xo.next()
                    for half in range(2):
                        p = pY.next()
                        for f in range(32):
                            k.op("pe", lambda e, f=f: e.matmul(out=p[:], lhsT=h1T[:, f, i * 128:(i + 1) * 128],
                                                               rhs=W2[:, f, half * 512:(half + 1) * 512],
                                                               start=(f == 0), stop=(f == 31)), reads=[h1T, W2], writes=[p])
                        k.op("dve", lambda e: e.tensor_tensor(out=o[:, half * 512:(half + 1) * 512], in0=p[:],
                                                              in1=xt2[i][:, half * 512:(half + 1) * 512], op=ALU.add),
                             reads=[p, xt2[i]], writes=[o])
                    k.store("sp", o, dsts[s][t0:t0 + 128, :], o[:])
        ph.end()

    U = UPTO
    if U < 4:
        prep_weights([(l, "all") for l in range(nL)])
    else:
        prep_weights([(0, "first")])
    for l in range(nL):
        for s in range(nS):
            xsrc = x_in[s] if l == 0 else xres[s]
            if U >= 1:
                phase1(l, s, xsrc, 0)
            if U >= 2:
                phase1(l, s, xsrc, 1)
            if U >= 3:
                phase2()
            if U >= 4:
                pl = []
                if l == 0 and s == 0:
                    pl.append((0, "rest"))
                if s == nS - 1 and l + 1 < nL:
                    pl.append((l + 1, "all"))
                phase3(prep_l=pl)
            if U >= 5:
                phase4()
            if U >= 6:
                phase5(l, s, xsrc)
        dsts = [out[s] if l == nL - 1 else xres[s] for s in range(nS)]
        if U >= 7:
            phase6(l, dsts)
    k.barrier()
    print("sbuf max bytes/partition:", k.sb_max, "instructions:", k.ninstr, {e: k.cnt[e] for e in k.cnt})
    k.close()
    return nc


def host_consts(S):
    NT = S // 128
    p = np.arange(128)[:, None]
    c = np.arange(128)[None, :]
    consts = np.zeros((128, 5, 128), np.float32)
    consts[:, 0, :] = (p == c)
    consts[:, 1, :] = (c >= p)
    consts[:, 2, :] = (c > p)
    consts[:, 3, :] = (c <= p)
    consts[:, 4, :] = -1.0 * (p >= c)

    def tables(dim):
        pos = np.arange(S, dtype=np.float32)
        inv = (np.float32(500000.0) ** (-np.arange(0, dim, 2, dtype=np.float32) / np.float32(dim))).astype(np.float32)
        ang = (pos[:, None] * inv[None, :]).astype(np.float32)
        t = np.concatenate([np.cos(ang), np.sin(ang)], axis=1).astype(np.float32)
        return np.ascontiguousarray(t.reshape(NT, 128, dim).transpose(1, 0, 2))
    return consts, tables(16), tables(32)


def make_inputs(x_sh, l0, l1, S, attn_norm, w_in, a_q_norm, a_k_norm, b_q_a_norm, w_q_b, b_kv_a_norm,
                w_kv_b, b_q_norm, b_k_norm, w_branch, w_out, mlp_norm, w_ff1, w_ff2):
    nL = l1 - l0
    sl = slice(l0, l1)
    f = lambda a: np.ascontiguousarray(np.asarray(a, dtype=np.float32))
    col = lambda g, c: np.ascontiguousarray(np.asarray(g, np.float32)[sl].reshape(nL, c, 128).transpose(0, 2, 1))
    rep = lambda a, b: np.ascontiguousarray(np.broadcast_to(
        np.concatenate([np.asarray(a, np.float32)[sl], np.asarray(b, np.float32)[sl]], axis=1)[:, None, :],
        (nL, 128, a.shape[1] + b.shape[1])))
    consts, rA, rM = host_consts(S)
    return {
        "x": f(x_sh), "w_in": f(w_in[sl]), "w_q_b": f(np.asarray(w_q_b)[sl].reshape(nL, 256, 768)),
        "w_kv_b": f(np.asarray(w_kv_b)[sl].reshape(nL, 128, 1024)), "w_branch": f(w_branch[sl]),
        "w_out": f(w_out[sl]), "w_ff1": f(w_ff1[sl]), "w_ff2": f(w_ff2[sl]),
        "g_attn": col(attn_norm, 8), "g_mlp": col(mlp_norm, 8), "g_qa": col(b_q_a_norm, 2),
        "g_kva": col(b_kv_a_norm, 1), "g_aqk": rep(a_q_norm, a_k_norm), "g_bqk": rep(b_q_norm, b_k_norm),
        "ropeA": rA, "ropeM": rM, "consts": consts,
    }


def kernel(x, attn_norm, w_in, a_q_norm, a_k_norm, b_q_a_norm, w_q_b, b_kv_a_norm,
           w_kv_b, b_q_norm, b_k_norm, w_branch, w_out, mlp_norm, w_ff1, w_ff2):
    x = np.asarray(x, dtype=np.float32)
    B, S, _ = x.shape
    nL = np.asarray(attn_norm).shape[0]
    ncores = 8
    nS = B // ncores
    nc = build(nL, nS, S)
    in_maps = []
    for c in range(ncores):
        in_maps.append(make_inputs(x[c * nS:(c + 1) * nS], 0, nL, S, attn_norm, w_in, a_q_norm, a_k_norm, b_q_a_norm,
                                   w_q_b, b_kv_a_norm, w_kv_b, b_q_norm, b_k_norm, w_branch, w_out, mlp_norm,
                                   w_ff1, w_ff2))
    res = run_bass_kernel_spmd(nc, in_maps, core_ids=list(range(ncores)))
    return np.concatenate([np.asarray(r["out"], dtype=np.float32) for r in res.results], axis=0)
```
